# Optimizing a Trainium2 kernel written in Bass

```python
import math
import jax, jax.numpy as jnp
from jax import lax
import numpy as np


D_MODEL = 1024
BATCH = 8
SEQ = 4096
DEPTH = 2

RWKV_HEADS = 8
RWKV_HEAD_DIM = 64
RWKV_WIDTH = RWKV_HEADS * RWKV_HEAD_DIM
DECAY_LORA = 64
ICLR_LORA = 64
GN_EPS = 64e-5
ATT_HEADS = 8
ATT_HEAD_DIM = 64
ATT_WIDTH = ATT_HEADS * ATT_HEAD_DIM
KV_LATENT = 128
IDX_HEADS = 8
IDX_HEAD_DIM = 64
TOPK_MAX = 256
Q_BLOCK = 128
NUM_BUCKETS = 32
MAX_DISTANCE = 128
NORM_EPS = 1e-6
COL_SIZES = (RWKV_WIDTH, RWKV_WIDTH, RWKV_WIDTH, DECAY_LORA, ICLR_LORA, RWKV_WIDTH,
             ATT_WIDTH, KV_LATENT, ATT_WIDTH, IDX_HEADS * IDX_HEAD_DIM, IDX_HEAD_DIM, IDX_HEADS,
             D_MODEL, D_MODEL)
N_SHIFT = 3 * RWKV_WIDTH + DECAY_LORA + ICLR_LORA
N_IN = sum(COL_SIZES)

kernel_name = "hybrid_rwkv7_dsa_gated_block"


def rms_norm(x, g):
    xf = x.astype(jnp.float32)
    y = xf * lax.rsqrt(jnp.mean(xf * xf, axis=-1, keepdims=True) + NORM_EPS)
    return (y * g.astype(jnp.float32)).astype(x.dtype)


def token_shift(u):
    return jnp.pad(u, ((0, 0), (1, 0), (0, 0)))[:, :-1]


def t5_bucket(dist):
    max_exact = NUM_BUCKETS // 2
    is_small = dist < max_exact
    d = jnp.maximum(dist, 1).astype(jnp.float32)
    large = max_exact + (jnp.log(d / max_exact) / math.log(MAX_DISTANCE / max_exact)
                         * (NUM_BUCKETS - max_exact)).astype(jnp.int32)
    large = jnp.minimum(large, NUM_BUCKETS - 1)
    return jnp.where(is_small, dist, large)


def rwkv7_scan(r, w, k, v, a_vec, b_vec):
    B, S, H, N = r.shape

    def step(state, inp):
        r_t, w_t, k_t, v_t, a_t, b_t = inp
        sa = jnp.einsum('bhvk,bhk->bhv', state, a_t)
        state = (state * w_t[:, :, None, :] + sa[..., None] * b_t[:, :, None, :]
                 + v_t[..., None] * k_t[:, :, None, :])
        y_t = jnp.einsum('bhvk,bhk->bhv', state, r_t)
        return state, y_t

    xs = tuple(jnp.moveaxis(t, 1, 0) for t in (r, w, k, v, a_vec, b_vec))
    s0 = jnp.zeros((B, H, N, N), jnp.float32)
    _, y = lax.scan(step, s0, xs)
    return jnp.moveaxis(y, 0, 1)


def rwkv7_branch(pr, pk, pv, pwd, pad_, w0, w2, a0, a2, k_k, k_a, r_k, lnx_g, lnx_b):
    B, S, _ = pr.shape
    f32 = jnp.float32
    heads = lambda t: t.astype(f32).reshape(B, S, RWKV_HEADS, RWKV_HEAD_DIM)
    w_log = -jax.nn.softplus(-(w0 + jnp.tanh(pwd) @ w2).astype(f32)) - 0.5
    decay = jnp.exp(-jnp.exp(w_log))
    a = jax.nn.sigmoid((a0 + pad_ @ a2).astype(f32))
    kk = heads(pk * k_k)
    kk = kk / jnp.maximum(jnp.sqrt(jnp.sum(kk * kk, axis=-1, keepdims=True)), 1e-12)
    k = pk.astype(f32) * (1.0 + (a - 1.0) * k_a)
    r_h, k_h, v_h, a_h, w_h = heads(pr), heads(k), heads(pv), heads(a), heads(decay)
    y = rwkv7_scan(r_h, w_h, k_h, v_h, -kk, kk * a_h)
    mu = jnp.mean(y, axis=-1, keepdims=True)
    var = jnp.mean(jnp.square(y - mu), axis=-1, keepdims=True)
    yn = ((y - mu) * lax.rsqrt(var + GN_EPS)).reshape(B, S, RWKV_WIDTH) * lnx_g + lnx_b
    bonus = jnp.sum(r_h * k_h * r_k.astype(f32), axis=-1, keepdims=True) * v_h
    out = yn + bonus.reshape(B, S, RWKV_WIDTH)
    return out.astype(pr.dtype)


def dsa_branch(q, ckv, qi, ki, wi, kv_norm_g, w_uk, w_uv, rel_bias):
    B, S, _ = q.shape
    topk = min(TOPK_MAX, S // 4)
    nb = S // Q_BLOCK
    q = q.reshape(B, S, ATT_HEADS, ATT_HEAD_DIM)
    lat = rms_norm(ckv, kv_norm_g)
    q_abs = jnp.einsum('bshd,chd->bshc', q, w_uk) * (ATT_HEAD_DIM ** -0.5)
    qi = qi.reshape(B, S, IDX_HEADS, IDX_HEAD_DIM)
    key_pos = jnp.arange(S, dtype=jnp.int32)
    qpos = key_pos.reshape(nb, Q_BLOCK)
    b_idx = jnp.arange(B)[:, None, None]

    def blocks(t):
        return jnp.moveaxis(t.reshape((B, nb, Q_BLOCK) + t.shape[2:]), 1, 0)

    def one_block(args):
        qa_b, qi_b, wi_b, pos_b = args
        s_idx = jax.nn.relu(jnp.einsum('bqhd,bsd->bqhs', qi_b, ki))
        score = jnp.einsum('bqh,bqhs->bqs', wi_b, s_idx).astype(jnp.float32)
        causal = key_pos[None, :] <= pos_b[:, None]
        score = jnp.where(causal[None], score, -jnp.inf)
        _, sel = lax.top_k(score, topk)
        lat_sel = lat[b_idx, sel]
        dist = pos_b[None, :, None] - sel
        valid = dist >= 0
        bias = rel_bias[t5_bucket(jnp.maximum(dist, 0))]
        logits = (jnp.einsum('bqhc,bqkc->bqhk', qa_b, lat_sel).astype(jnp.float32)
                  + jnp.moveaxis(bias, -1, 2).astype(jnp.float32))
        logits = jnp.where(valid[:, :, None, :], logits, -jnp.inf)
        p = jax.nn.softmax(logits, axis=-1)
        return jnp.einsum('bqhk,bqkc->bqhc', p.astype(lat_sel.dtype), lat_sel)

    o_lat = lax.map(one_block, (blocks(q_abs), blocks(qi), blocks(wi), qpos))
    o_lat = jnp.moveaxis(o_lat, 0, 1).reshape(B, S, ATT_HEADS, KV_LATENT)
    o = jnp.einsum('bshc,chd->bshd', o_lat, w_uv)
    return o.reshape(B, S, ATT_WIDTH)


def setup_inputs(seed: int = 0) -> dict:
    key = jax.random.key(seed)
    ks = jax.random.split(key, 24)
    f32 = jnp.float32

    def nrm(k, shape, s):
        return jax.random.normal(k, shape, f32) * s

    return {
        "x": nrm(ks[0], (BATCH, SEQ, D_MODEL), 1.0),
        "c": nrm(ks[1], (BATCH, D_MODEL), 1.0),
        "ada_w": nrm(ks[2], (DEPTH, D_MODEL, 3 * D_MODEL), 0.5 * D_MODEL ** -0.5),
        "ada_b": nrm(ks[3], (DEPTH, 3 * D_MODEL), 0.02),
        "norm_g": 1.0 + nrm(ks[4], (DEPTH, D_MODEL), 0.05),
        "w_in": nrm(ks[5], (DEPTH, D_MODEL, N_IN), D_MODEL ** -0.5),
        "shift_mu": jax.random.uniform(ks[6], (DEPTH, N_SHIFT), f32),
        "w0": jax.random.uniform(ks[7], (DEPTH, RWKV_WIDTH), f32, -6.0, -1.0),
        "w2": nrm(ks[8], (DEPTH, DECAY_LORA, RWKV_WIDTH), 0.1),
        "a0": nrm(ks[9], (DEPTH, RWKV_WIDTH), 0.5),
        "a2": nrm(ks[10], (DEPTH, ICLR_LORA, RWKV_WIDTH), 0.1),
        "k_k": 0.85 + nrm(ks[11], (DEPTH, RWKV_WIDTH), 0.05),
        "k_a": 1.0 + nrm(ks[12], (DEPTH, RWKV_WIDTH), 0.05),
        "r_k": nrm(ks[13], (DEPTH, RWKV_HEADS, RWKV_HEAD_DIM), 0.1),
        "lnx_g": 1.0 + nrm(ks[14], (DEPTH, RWKV_WIDTH), 0.05),
        "lnx_b": nrm(ks[15], (DEPTH, RWKV_WIDTH), 0.02),
        "kv_norm_g": 1.0 + nrm(ks[16], (DEPTH, KV_LATENT), 0.05),
        "w_uk": nrm(ks[17], (DEPTH, KV_LATENT, ATT_HEADS, ATT_HEAD_DIM), KV_LATENT ** -0.5),
        "w_uv": nrm(ks[18], (DEPTH, KV_LATENT, ATT_HEADS, ATT_HEAD_DIM), KV_LATENT ** -0.5),
        "w_pa": nrm(ks[19], (DEPTH, RWKV_WIDTH, D_MODEL), RWKV_WIDTH ** -0.5),
        "w_pb": nrm(ks[20], (DEPTH, ATT_WIDTH, D_MODEL), ATT_WIDTH ** -0.5),
        "w_o": nrm(ks[21], (DEPTH, D_MODEL, D_MODEL), D_MODEL ** -0.5),
        "rel_bias": nrm(ks[22], (NUM_BUCKETS, ATT_HEADS), 0.5),
        "final_g": 1.0 + nrm(ks[23], (D_MODEL,), 0.05),
    }


def reference(x, c, ada_w, ada_b, norm_g, w_in, shift_mu, w0, w2, a0, a2, k_k, k_a, r_k,
              lnx_g, lnx_b, kv_norm_g, w_uk, w_uv, w_pa, w_pb, w_o, rel_bias, final_g):
    split_at = np.cumsum(np.array(COL_SIZES))[:-1].tolist()
    c_act = jax.nn.silu(c)
    for l in range(DEPTH):
        mod = c_act @ ada_w[l] + ada_b[l]
        shift, scale, gate = jnp.split(mod, 3, axis=-1)
        h = rms_norm(x, norm_g[l]) * (1.0 + scale[:, None, :]) + shift[:, None, :]
        p = h @ w_in[l]
        ps = p[..., :N_SHIFT]
        ps = ps + (token_shift(ps) - ps) * shift_mu[l]
        p = jnp.concatenate([ps, p[..., N_SHIFT:]], axis=-1)
        (pr, pk, pv, pwd, pad_, z_a, q, ckv, z_b, qi, ki, wi, g_a, g_b) = jnp.split(p, split_at, axis=-1)
        y_a = rwkv7_branch(pr, pk, pv, pwd, pad_, w0[l], w2[l], a0[l], a2[l], k_k[l], k_a[l],
                           r_k[l], lnx_g[l], lnx_b[l])
        y_b = dsa_branch(q, ckv, qi, ki, wi, kv_norm_g[l], w_uk[l], w_uv[l], rel_bias)
        br_a = (y_a * jax.nn.silu(z_a)) @ w_pa[l]
        br_b = (y_b * jax.nn.silu(z_b)) @ w_pb[l]
        merged = jax.nn.sigmoid(g_a) * br_a + jax.nn.sigmoid(g_b) * br_b
        x = x + gate[:, None, :] * (merged @ w_o[l])
    return rms_norm(x, final_g)
```

```python
import numpy as np
import concourse.bass as bass
import concourse.mybir as mybir
from concourse.bass_utils import run_bass_kernel_spmd

F32 = mybir.dt.float32
BF16 = mybir.dt.bfloat16
AF = mybir.ActivationFunctionType
ALU = mybir.AluOpType
AX = mybir.AxisListType


class Prog:
    NDSEM = 14
    PSUM_PREFIXES = ("mps", "pps", "pst", "ppq", "rps", "dps", "ops", "fps")

    def __init__(self, nc):
        self.nc = nc
        self.eng = dict(pe=nc.tensor, act=nc.scalar, dve=nc.vector, pool=nc.gpsimd, sp=nc.sync)
        self.csem = {k: nc.alloc_semaphore(name=f"c_{k}") for k in self.eng}
        self.cnt = {k: 0 for k in self.eng}
        self.seen = {k: {} for k in self.eng}
        self.lastw = {}
        self.readers = {}
        self.dsem = {q: [nc.alloc_semaphore(name=f"d_{q}{i}") for i in range(self.NDSEM)]
                     for q in ("sp", "pool", "act")}
        self.dcnt = {q: [0] * self.NDSEM for q in self.dsem}
        self.drr = {q: 0 for q in self.dsem}
        self.uid = 0
        self.out_tokens = []
        self.ninstr = 0

    def _wait(self, e, tok):
        sem, val = tok
        if self.seen[e].get(sem.num, 0) >= val:
            return
        self.eng[e].wait_ge(sem, val)
        self.seen[e][sem.num] = val

    def _deps(self, e, r, w):
        for b in r:
            lw = self.lastw.get(b)
            if lw is not None and not (lw[0] == e == "pe"):
                self._wait(e, lw[1])
            if b.startswith(self.PSUM_PREFIXES):
                for re_, tok in self.readers.get(b, {}).items():
                    if re_ != e:
                        self._wait(e, tok)
        for b in w:
            lw = self.lastw.get(b)
            if lw is not None and not (lw[0] == e == "pe"):
                self._wait(e, lw[1])
            for re_, tok in self.readers.get(b, {}).items():
                if re_ != e or e == "pool":
                    self._wait(e, tok)

    def _commit(self, key, tok, r, w):
        for b in r:
            self.readers.setdefault(b, {})[key] = tok
        for b in w:
            self.lastw[b] = (key, tok)
            self.readers[b] = {}

    def op(self, e, fn, r=(), w=()):
        self._deps(e, r, w)
        ins = fn(self.eng[e])
        self.cnt[e] += 1
        ins.then_inc(self.csem[e], 1)
        self._commit(e, (self.csem[e], self.cnt[e]), r, w)
        self.ninstr += 1

    def dma(self, q, out, in_, r=(), w=(), is_output=False, **kw):
        i = self.drr[q]
        self.drr[q] = (i + 1) % self.NDSEM
        sem = self.dsem[q][i]
        if self.dcnt[q][i] > 0:
            self._wait(q, (sem, self.dcnt[q][i]))
        self._deps(q, r, w)
        ins = self.eng[q].dma_start(out=out, in_=in_, **kw)
        self.dcnt[q][i] += 16
        ins.then_inc(sem, 16)
        tok = (sem, self.dcnt[q][i])
        self.uid += 1
        self._commit(f"dma{self.uid}", tok, r, w)
        if is_output:
            self.out_tokens.append(tok)
        self.ninstr += 1

    def barrier(self):
        for e in self.eng:
            for o in self.eng:
                if o != e and self.cnt[o] > 0:
                    self._wait(e, (self.csem[o], self.cnt[o]))
            for q in self.dsem:
                for i, sem in enumerate(self.dsem[q]):
                    if self.dcnt[q][i] > 0:
                        self._wait(e, (sem, self.dcnt[q][i]))

    def finish(self):
        for q in self.dsem:
            for i, sem in enumerate(self.dsem[q]):
                if self.dcnt[q][i] > 0:
                    self._wait("sp", (sem, self.dcnt[q][i]))
        for e in ("pe", "act", "dve", "pool"):
            if self.cnt[e] > 0:
                self._wait("sp", (self.csem[e], self.cnt[e]))


D = 1024
NIN = 5960
C_R, C_K, C_V, C_WD, C_AD, C_ZA, C_Q, C_CKV, C_ZB, C_QI, C_KI, C_WI, C_GA, C_GB = (
    0, 512, 1024, 1536, 1600, 1664, 2176, 2688, 2816, 3328, 3840, 3904, 3912, 4936)
C0 = 0.6065306597126334
NEG = -1.0e30

COLS = dict(mu=(0, 13), w0=(13, 4), a0=(17, 4), kk=(21, 4), ka=(25, 4), rk=(29, 4),
            lg=(33, 4), lb=(37, 4), kvg=(41, 1))
NCOL = 42


def t5_bucket_np(dist):
    import math
    max_exact = 16
    d = np.maximum(dist, 1).astype(np.float32)
    large = max_exact + (np.log(d / np.float32(max_exact)) / np.float32(math.log(128 / max_exact))
                         * np.float32(32 - max_exact)).astype(np.int32)
    large = np.minimum(large, 31)
    return np.where(dist < max_exact, dist, large)


class Ctx:
    pass


_UNIQ = [0]


def _sbm(nc, name, shape, dt):
    _UNIQ[0] += 1
    return nc.sbuf_tensor(f"{name}_u{_UNIQ[0]}", shape, dt)


def _psm(nc, name, shape, dt):
    _UNIQ[0] += 1
    return nc.psum_tensor(f"{name}_u{_UNIQ[0]}", shape, dt)


def build(S, TOPK, depth=2, dbg=(), stop_after=None, stop_layer=0):
    from contextlib import ExitStack
    nc = bass.Bass("TRN2", target_bir_lowering=False)
    P = Prog(nc)
    NT = S // 128
    TC = min(512, S)
    NTC = S // TC
    EPS = 1e-6

    def din(name, shape, dt=F32):
        return nc.dram_tensor(name, list(shape), dt, kind="ExternalInput").ap()

    def scr(name, shape, dt=F32):
        kind = "ExternalOutput" if name in dbg else "Internal"
        return nc.dram_tensor(name, list(shape), dt, kind=kind).ap()

    x_in = din("x", [S, D])
    cT = din("cT", [128, 8])
    final_g = din("final_g", [1, D])
    ebias = din("ebias", [2, 128, 8, 128])
    eb31 = din("eb31", [128, 8, 128])
    L = []
    for l in range(depth):
        L.append(dict(
            ada_w=din(f"ada_w{l}", [D, 3 * D]), ada_b=din(f"ada_b{l}", [1, 3 * D]),
            norm_g=din(f"norm_g{l}", [1, D]), w_in=din(f"w_in{l}", [D, NIN]),
            cols=din(f"cols{l}", [128, NCOL]), w2=din(f"w2{l}", [64, 512]), a2=din(f"a2{l}", [64, 512]),
            wukT=din(f"wukT{l}", [128, 4, 128]), wuv=din(f"wuv{l}", [128, 512]),
            w_pa=din(f"w_pa{l}", [512, D]), w_pb=din(f"w_pb{l}", [512, D]), w_o=din(f"w_o{l}", [D, D])))
    out = nc.dram_tensor("out", [S, D], F32, kind="ExternalOutput").ap()
    xs = [scr(f"xs{i}", [S, D]) for i in range(2)]
    psT = scr("psT", [1664, S])
    szaT = scr("szaT", [512, S], BF16)
    szbT = scr("szbT", [512, S], BF16)
    sgaT = scr("sgaT", [D, S], BF16)
    sgbT = scr("sgbT", [D, S], BF16)
    qabsT = scr("qabsT", [8, 128, S], BF16)
    latT = scr("latT", [128, S], BF16)
    vext = scr("vext", [NT, 128, 520], BF16)
    qiT = scr("qiT", [512, S])
    kiT = scr("kiT", [64, S])
    witm = scr("witm", [128, NT * 8])
    yagT = scr("yagT", [512, S], BF16)
    ybgT = scr("ybgT", [512, S], BF16)

    def sb(name, shape, dt=F32):
        return nc.alloc_sbuf_tensor(name, list(shape), dt).ap()

    ones = sb("ones", [128, 128])
    identf = sb("identf", [128, 128])
    identb = sb("identb", [128, 128], BF16)
    bdiag = sb("bdiag", [128, 128])
    u2m = sb("u2m", [64, 128])
    slm = sb("slm", [64, 64])
    cact = sb("cact", [128, 8])
    cbc = sb("cbc", [128, 8, 128])
    modB = sb("modB", [128, 3 * D])
    Gt = sb("Gt", [128, D])
    cols = sb("cols", [128, NCOL])
    omka = sb("omka", [128, 4])
    row = sb("row", [1, 3 * D])

    reg_zero = nc.gpsimd.to_reg(0.0)
    reg_neg = nc.gpsimd.to_reg(NEG)
    P.op("pool", lambda e: e.memset(ones, 1.0), w=["ones"])
    P.op("pool", lambda e: e.affine_select(out=identf, in_=ones, pattern=[[-1, 128]], compare_op=ALU.is_equal,
                                           fill=reg_zero, base=0, channel_multiplier=1), r=["ones"], w=["identf"])
    P.op("pool", lambda e: e.tensor_copy(out=identb, in_=identf), r=["identf"], w=["identb"])
    P.op("pool", lambda e: e.memset(bdiag, 0.0), w=["bdiag"])
    P.op("pool", lambda e: e.memset(bdiag[0:64, 0:64], 1.0), w=["bdiag"])
    P.op("pool", lambda e: e.memset(bdiag[64:128, 64:128], 1.0), w=["bdiag"])
    P.op("pool", lambda e: e.affine_select(out=u2m[:, 0:64], in_=ones[0:64, 0:64], pattern=[[1, 64]], compare_op=ALU.is_ge,
                                           fill=reg_zero, base=-1, channel_multiplier=-1), r=["ones"], w=["u2m"])
    P.op("pool", lambda e: e.affine_select(out=u2m[:, 64:128], in_=ones[0:64, 0:64], pattern=[[1, 64]], compare_op=ALU.is_ge,
                                           fill=reg_zero, base=0, channel_multiplier=-1), r=["ones"], w=["u2m"])
    P.op("pool", lambda e: e.affine_select(out=slm, in_=ones[0:64, 0:64], pattern=[[-1, 64]], compare_op=ALU.is_ge,
                                           fill=reg_zero, base=-1, channel_multiplier=1), r=["ones"], w=["slm"])
    P.dma("sp", cact, cT, w=["cact"])
    P.op("act", lambda e: e.activation(out=cact, in_=cact, func=AF.Silu), r=["cact"], w=["cact"])
    for j in range(8):
        P.op("dve", lambda e, j=j: e.tensor_scalar(out=cbc[:, j, :], in0=ones, scalar1=cact[:, j:j + 1], scalar2=None,
                                                   op0=ALU.mult), r=["cact", "ones"], w=["cbc"])

    K = Ctx()
    K.__dict__.update(locals())
    K.D = D
    K.COLS = COLS
    for l in range(depth):
        xsrc = x_in if l == 0 else xs[(l - 1) % 2]
        xdst = xs[l % 2]
        phase_mod(K, l)
        if stop_after == "mod" and l == stop_layer:
            break
        phase_proj(K, l, xsrc)
        if stop_after == f"B{l}":
            break
        if stop_after == "proj" and l == stop_layer:
            break
        phase_rwkv(K, l)
        if stop_after == "rwkv" and l == stop_layer:
            break
        phase_dsa(K, l)
        if stop_after == "dsa" and l == stop_layer:
            break
        phase_out(K, l, xsrc, xdst)
        if stop_after == "out" and l == stop_layer:
            break
    else:
        phase_final(K, xs[(depth - 1) % 2])
    P.finish()
    return nc, P


def phase_mod(K, l):
    from contextlib import ExitStack
    nc, P, W = K.nc, K.P, K.L[l]
    with ExitStack() as st:
        wb = [st.enter_context(_sbm(nc, f"mw{i}", [128, 8, 512], F32)).ap() for i in range(2)]
        ps = [st.enter_context(_psm(nc, f"mps{i}", [128, 512], F32)).ap() for i in range(2)]
        P.dma("sp", K.cols, W["cols"], w=["cols"])
        P.dma("sp", K.row, W["ada_b"], w=["row"])
        P.op("dve", lambda e: e.tensor_scalar(out=K.omka, in0=K.cols[:, 25:29], scalar1=-1.0, scalar2=1.0,
                                              op0=ALU.mult, op1=ALU.add), r=["cols"], w=["omka"])
        for nb in range(6):
            b = nb % 2
            P.dma("sp", wb[b], W["ada_w"][:, nb * 512:(nb + 1) * 512].rearrange("(j p) n -> p j n", p=128), w=[f"mw{b}"])
            for j in range(8):
                P.op("pe", lambda e, j=j: e.matmul(ps[b], lhsT=K.cbc[:, j, :], rhs=wb[b][:, j, :], start=(j == 0), stop=False),
                     r=["cbc", f"mw{b}"], w=[f"mps{b}"])
            P.op("pe", lambda e: e.matmul(ps[b], lhsT=K.ones[0:1, :], rhs=K.row[0:1, nb * 512:(nb + 1) * 512], start=False, stop=True),
                 r=["ones", "row"], w=[f"mps{b}"])
            P.op("act", lambda e: e.activation(out=K.modB[:, nb * 512:(nb + 1) * 512], in_=ps[b], func=AF.Copy),
                 r=[f"mps{b}"], w=["modB"])
        P.dma("sp", K.row[0:1, 0:K.D], W["norm_g"], r=["row"], w=["row"])
        for hf in range(2):
            P.op("pe", lambda e: e.matmul(ps[hf], lhsT=K.ones[0:1, :], rhs=K.row[0:1, hf * 512:(hf + 1) * 512], start=True, stop=True),
                 r=["ones", "row"], w=[f"mps{hf}"])
            P.op("dve", lambda e: e.scalar_tensor_tensor(out=K.Gt[:, hf * 512:(hf + 1) * 512],
                                                         in0=K.modB[:, K.D + hf * 512:K.D + (hf + 1) * 512], scalar=1.0,
                                                         in1=ps[hf], op0=ALU.add, op1=ALU.mult),
                 r=["modB", f"mps{hf}"], w=["Gt"])


def proj_blocks():
    blks = []
    for i in range(13):
        blks.append((i * 128, 128, "shift", i))
    for i in range(4):
        blks.append((C_ZA + i * 128, 128, "sza", i))
    for i in range(4):
        blks.append((C_Q + i * 128, 128, "q", i))
    blks.append((C_CKV, 128, "ckv", 0))
    for i in range(4):
        blks.append((C_ZB + i * 128, 128, "szb", i))
    for i in range(8):
        blks.append((C_GA + i * 128, 128, "sga", i))
    for i in range(8):
        blks.append((C_GB + i * 128, 128, "sgb", i))
    return blks


def phase_proj(K, l, xsrc):
    from contextlib import ExitStack
    nc, P, W, S, NT, TC, NTC = K.nc, K.P, K.L[l], K.S, K.NT, K.TC, K.NTC
    with ExitStack() as st:
        def sbt(name, shape, dt=F32):
            return st.enter_context(_sbm(nc, name, list(shape), dt)).ap()
        hT = sbt("hT", [128, 8, S], BF16)
        ps = [st.enter_context(_psm(nc, f"pps{i}", [128, 512], F32)).ap() for i in range(4)]
        with ExitStack() as st2:
            def sb2(name, shape, dt=F32):
                return st2.enter_context(_sbm(nc, name, list(shape), dt)).ap()
            pst = [st2.enter_context(_psm(nc, f"pst{i}", [128, 512], F32)).ap() for i in range(2)]
            pq = [st2.enter_context(_psm(nc, f"ppq{i}", [128, 512], F32)).ap() for i in range(2)]
            xb = [sb2(f"xb{i}", [128, K.D]) for i in range(2)]
            t1 = sb2("t1", [128, K.D])
            sqj = sb2("sqj", [128, K.D])
            ss = sb2("ss", [128, 2])
            hTf = sb2("hTf", [128, 8, 128])
            wq = sb2("wq", [128, 8, 584])
            pqs = [sb2(f"pqs{i}", [128, 5, 128]) for i in range(2)]
            wit = sb2("wit", [128, NT * 8])
            P.dma("sp", wq, W["w_in"][:, C_QI:C_QI + 584].rearrange("(j p) n -> p j n", p=128), w=["wq"])
            for i in range(NT):
                b = i % 2
                tcs = slice(i * 128, (i + 1) * 128)
                P.dma("sp", xb[b], xsrc[tcs, :], w=[f"xb{b}"])
                P.op("act", lambda e: e.activation(out=sqj, in_=xb[b], func=AF.Square, accum_out=ss[:, 0:1]),
                     r=[f"xb{b}"], w=["sqj", "ss"])
                P.op("dve", lambda e: e.tensor_scalar(out=ss[:, 1:2], in0=ss[:, 0:1], scalar1=1.0 / K.D, scalar2=K.EPS,
                                                      op0=ALU.mult, op1=ALU.add), r=["ss"], w=["ss"])
                P.op("act", lambda e: e.activation(out=ss[:, 1:2], in_=ss[:, 1:2], func=AF.Sqrt), r=["ss"], w=["ss"])
                P.op("dve", lambda e: e.reciprocal(out=ss[:, 1:2], in_=ss[:, 1:2]), r=["ss"], w=["ss"])
                P.op("dve", lambda e: e.scalar_tensor_tensor(out=t1, in0=xb[b], scalar=ss[:, 1:2], in1=K.Gt,
                                                             op0=ALU.mult, op1=ALU.mult), r=[f"xb{b}", "ss", "Gt"], w=["t1"])
                P.op("dve", lambda e: e.tensor_tensor(out=t1, in0=t1, in1=K.modB[:, 0:K.D], op=ALU.add),
                     r=["t1", "modB"], w=["t1"])
                for j in range(8):
                    P.op("pe", lambda e: e.matmul(pst[j // 4][:, (j % 4) * 128:(j % 4 + 1) * 128], lhsT=t1[:, j * 128:(j + 1) * 128],
                                                  rhs=K.identf, start=True, stop=True), r=["t1", "identf"], w=[f"pst{j // 4}"])
                for hh in range(2):
                    src = pst[hh].rearrange("p (j t) -> p j t", j=4)
                    P.op("act", lambda e: e.activation(out=hT[:, 4 * hh:4 * hh + 4, tcs], in_=src, func=AF.Copy), r=[f"pst{hh}"], w=["hT"])
                    P.op("dve", lambda e: e.tensor_copy(out=hTf[:, 4 * hh:4 * hh + 4, :], in_=src), r=[f"pst{hh}"], w=["hTf"])
                for qb_ in range(5):
                    n = 128 if qb_ < 4 else 72
                    pb = qb_ % 2
                    for j in range(8):
                        P.op("pe", lambda e: e.matmul(pq[pb][0:n, 0:128], lhsT=wq[:, j, qb_ * 128:qb_ * 128 + n], rhs=hTf[:, j, :],
                                                      start=(j == 0), stop=(j == 7)), r=["wq", "hTf"], w=[f"ppq{pb}"])
                    P.op("act", lambda e: e.activation(out=pqs[b][0:n, qb_, :], in_=pq[pb][0:n, 0:128], func=AF.Copy),
                         r=[f"ppq{pb}"], w=[f"pqs{b}"])
                P.dma("sp", K.qiT.rearrange("(q p) t -> p q t", p=128)[:, :, tcs], pqs[b][:, 0:4, :], r=[f"pqs{b}"], w=["qiT"])
                P.dma("sp", K.kiT[:, tcs], pqs[b][0:64, 4, :], r=[f"pqs{b}"], w=["kiT"])
                P.op("pe", lambda e: e.transpose(out=pq[0][:, 256:264], in_=pqs[b][64:72, 4, :], identity=K.identf[64:72, 64:72]),
                     r=[f"pqs{b}", "identf"], w=["ppq0"])
                P.op("act", lambda e: e.activation(out=wit[:, i * 8:(i + 1) * 8], in_=pq[0][:, 256:264], func=AF.Copy), r=["ppq0"], w=["wit"])
            P.dma("sp", K.witm, wit, r=["wit"], w=["witm"])
            P.barrier()
        if K.stop_after == f"B{l}":
            return
        wf = [sbt(f"wf{i}", [128, 8, 128]) for i in range(2)]
        wbf = [sbt(f"wbf{i}", [128, 8, 128], BF16) for i in range(2)]
        blkf = [sbt(f"blkf{i}", [128, S]) for i in range(2)]
        tmpf = [sbt(f"tmpf{i}", [128, S]) for i in range(2)]
        blkb = [sbt(f"blkb{i}", [128, S], BF16) for i in range(2)]
        wuk = sbt("wuk", [128, 4, 128])
        wukb = sbt("wukb", [128, 4, 128], BF16)
        wuv = sbt("wuv", [128, 512])
        wuvb = sbt("wuvb", [128, 512], BF16)
        qa = [sbt(f"qa{i}", [128, TC], BF16) for i in range(2)]
        vx = [sbt(f"vx{i}", [128, 8, 65], BF16) for i in range(2)]
        P.dma("sp", wuk, W["wukT"], w=["wuk"])
        P.op("pool", lambda e: e.tensor_copy(out=wukb, in_=wuk), r=["wuk"], w=["wukb"])
        P.dma("sp", wuv, W["wuv"], w=["wuv"])
        P.op("pool", lambda e: e.tensor_copy(out=wuvb, in_=wuv), r=["wuv"], w=["wuvb"])
        for i in range(2):
            P.op("pool", lambda e: e.memset(vx[i], 1.0), w=[f"vx{i}"])
        pcount = 0
        qcount = 0
        for bi, (cs, n, kind, idx) in enumerate(proj_blocks()):
            b = bi % 2
            P.dma("sp", wf[b][:, :, 0:n], W["w_in"][:, cs:cs + n].rearrange("(j p) n -> p j n", p=128), w=[f"wf{b}"])
            P.op("pool", lambda e: e.tensor_copy(out=wbf[b][:, :, 0:n], in_=wf[b][:, :, 0:n]), r=[f"wf{b}"], w=[f"wbf{b}"])
            isb = kind in ("sza", "szb", "sga", "sgb", "q")
            dst, dname = (blkb[b], f"blkb{b}") if isb else (blkf[b], f"blkf{b}")
            func = AF.Silu if kind in ("sza", "szb") else (AF.Sigmoid if kind in ("sga", "sgb") else AF.Copy)
            for tc in range(NTC):
                pb = pcount % 2
                pcount += 1
                for j in range(8):
                    P.op("pe", lambda e: e.matmul(ps[pb][0:n, 0:TC], lhsT=wbf[b][:, j, 0:n], rhs=hT[:, j, tc * TC:(tc + 1) * TC],
                                                  start=(j == 0), stop=(j == 7)), r=[f"wbf{b}", "hT"], w=[f"pps{pb}"])
                P.op("act", lambda e: e.activation(out=dst[0:n, tc * TC:(tc + 1) * TC], in_=ps[pb][0:n, 0:TC], func=func),
                     r=[f"pps{pb}"], w=[dname])
            if kind == "shift":
                tm, tn = tmpf[b], f"tmpf{b}"
                P.op("dve", lambda e: e.tensor_tensor(out=tm[:, 1:S], in0=dst[:, 0:S - 1], in1=dst[:, 1:S], op=ALU.subtract),
                     r=[dname], w=[tn])
                P.op("dve", lambda e: e.tensor_scalar(out=tm[:, 0:1], in0=dst[:, 0:1], scalar1=-1.0, scalar2=None, op0=ALU.mult),
                     r=[dname], w=[tn])
                P.op("dve", lambda e: e.scalar_tensor_tensor(out=tm, in0=tm, scalar=K.cols[:, idx:idx + 1], in1=dst,
                                                             op0=ALU.mult, op1=ALU.add), r=[tn, dname, "cols"], w=[tn])
                P.dma("sp", K.psT[cs:cs + 128, :], tm, r=[tn], w=["psT"])
            elif kind in ("sza", "szb", "sga", "sgb"):
                tgt = dict(sza=K.szaT, szb=K.szbT, sga=K.sgaT, sgb=K.sgbT)[kind]
                P.dma("sp", tgt[idx * 128:(idx + 1) * 128, :], dst, r=[dname], w=[kind + "T"])
            elif kind == "q":
                for hp in range(2):
                    for tc in range(NTC):
                        pb = 2 + qcount % 2
                        qb = qcount % 2
                        qcount += 1
                        P.op("pe", lambda e: e.matmul(ps[pb][:, 0:TC], lhsT=wukb[64 * hp:64 * hp + 64, idx, :],
                                                      rhs=dst[64 * hp:64 * hp + 64, tc * TC:(tc + 1) * TC], start=True, stop=True),
                             r=["wukb", dname], w=[f"pps{pb}"])
                        P.op("act", lambda e: e.activation(out=qa[qb], in_=ps[pb][:, 0:TC], func=AF.Copy, scale=0.125),
                             r=[f"pps{pb}"], w=[f"qa{qb}"])
                        P.dma("sp", K.qabsT[2 * idx + hp, :, tc * TC:(tc + 1) * TC], qa[qb], r=[f"qa{qb}"], w=["qabsT"])
            elif kind == "ckv":
                tm, tn = tmpf[b], f"tmpf{b}"
                rs, rn = tmpf[1 - b], f"tmpf{1 - b}"
                P.op("act", lambda e: e.activation(out=tm, in_=dst, func=AF.Square), r=[dname], w=[tn])
                for tc in range(NTC):
                    pb = 2 + tc % 2
                    P.op("pe", lambda e: e.matmul(ps[pb][:, 0:TC], lhsT=K.ones, rhs=tm[:, tc * TC:(tc + 1) * TC], start=True, stop=True),
                         r=["ones", tn], w=[f"pps{pb}"])
                    P.op("act", lambda e: e.activation(out=rs[:, tc * TC:(tc + 1) * TC], in_=ps[pb][:, 0:TC], func=AF.Sqrt,
                                                       scale=1.0 / 128, bias=K.EPS), r=[f"pps{pb}"], w=[rn])
                P.op("dve", lambda e: e.reciprocal(out=rs, in_=rs), r=[rn], w=[rn])
                lb_, ln_ = blkb[b], f"blkb{b}"
                P.op("dve", lambda e: e.scalar_tensor_tensor(out=lb_, in0=dst, scalar=K.cols[:, 41:42], in1=rs,
                                                             op0=ALU.mult, op1=ALU.mult), r=[dname, rn, "cols"], w=[ln_])
                P.dma("sp", K.latT, lb_, r=[ln_], w=["latT"])
                for i in range(NT):
                    pb = 2 + i % 2
                    vb = i % 2
                    P.op("pe", lambda e: e.matmul(ps[pb], lhsT=lb_[:, i * 128:(i + 1) * 128], rhs=wuvb, start=True, stop=True),
                         r=[ln_, "wuvb"], w=[f"pps{pb}"])
                    P.op("act", lambda e: e.activation(out=vx[vb][:, :, 0:64], in_=ps[pb].rearrange("p (h d) -> p h d", h=8),
                                                       func=AF.Copy), r=[f"pps{pb}"], w=[f"vx{vb}"])
                    P.dma("sp", K.vext[i], vx[vb].rearrange("p h d -> p (h d)"), r=[f"vx{vb}"], w=["vext"])
        P.barrier()


def colpack(v):
    return np.ascontiguousarray(np.asarray(v, np.float32).reshape(-1, 128).T)


def shared_maps(inp, depth):
    m = {}
    m["final_g"] = np.asarray(inp["final_g"], np.float32).reshape(1, D)
    rb = np.asarray(inp["rel_bias"], np.float32)
    s_ = np.arange(128)[:, None]
    t_ = np.arange(128)[None, :]
    eb = np.zeros((2, 128, 8, 128), np.float32)
    for kind, off in ((0, 0), (1, 128)):
        dist = t_ - s_ + off
        bk = t5_bucket_np(np.maximum(dist, 0))
        eb[kind] = np.transpose(rb[bk], (0, 2, 1))
    m["ebias"] = eb
    m["eb31"] = np.ascontiguousarray(np.broadcast_to(rb[31][None, :, None], (128, 8, 128))).astype(np.float32)
    for l in range(depth):
        g = lambda k: np.asarray(inp[k][l], np.float32)
        m[f"ada_w{l}"] = g("ada_w")
        m[f"ada_b{l}"] = g("ada_b").reshape(1, -1)
        m[f"norm_g{l}"] = g("norm_g").reshape(1, -1)
        m[f"w_in{l}"] = g("w_in")
        m[f"cols{l}"] = np.ascontiguousarray(np.concatenate([
            colpack(g("shift_mu")), colpack(g("w0")), colpack(g("a0")), colpack(g("k_k")), colpack(g("k_a")),
            colpack(g("r_k").reshape(-1)), colpack(g("lnx_g")), colpack(g("lnx_b")), colpack(g("kv_norm_g"))], axis=1))
        m[f"w2{l}"] = g("w2")
        m[f"a2{l}"] = g("a2")
        wuk = g("w_uk")
        m[f"wukT{l}"] = np.ascontiguousarray(wuk.reshape(128, 4, 2, 64).transpose(2, 3, 1, 0).reshape(128, 4, 128))
        m[f"wuv{l}"] = g("w_uv").reshape(128, 512)
        m[f"w_pa{l}"] = g("w_pa")
        m[f"w_pb{l}"] = g("w_pb")
        m[f"w_o{l}"] = g("w_o")
    return m


def core_map(inp, shared, b):
    m = dict(shared)
    m["x"] = np.ascontiguousarray(np.asarray(inp["x"][b], np.float32))
    m["cT"] = colpack(np.asarray(inp["c"][b], np.float32))
    return m


def phase_rwkv(K, l):
    from contextlib import ExitStack
    nc, P, W, S = K.nc, K.P, K.L[l], K.S
    SEG = min(1024, S)
    NSEG = S // SEG
    NCH = SEG // 64
    CC = min(512, SEG)
    NCC = SEG // CC
    GN_EPS = 64e-5
    with ExitStack() as st:
        def sbt(name, shape, dt=F32):
            return st.enter_context(_sbm(nc, name, list(shape), dt)).ap()
        names = ["rT", "kT", "vT", "lwp", "ai", "kkn", "km", "bb", "Lp", "Lpe", "E1", "E2", "E3", "E4", "bonT", "tmpA", "OG", "cmask"]
        T = {n: sbt("r_" + n, [128, SEG]) for n in names}
        AR = sbt("r_AR", [128, 2, SEG])
        BK = sbt("r_BK", [128, 2, SEG])
        BKh = sbt("r_BKh", [128, 2, SEG])
        wdT = sbt("r_wdT", [64, SEG])
        adT = sbt("r_adT", [64, SEG])
        w2 = sbt("r_w2", [64, 512])
        a2 = sbt("r_a2", [64, 512])
        gz = sbt("r_gz", [128, SEG], BF16)
        ogb = sbt("r_ogb", [128, SEG], BF16)
        u2 = sbt("r_u2", [128, 128])
        sl = sbt("r_sl", [128, 64])
        tm = sbt("r_tm", [128, 192])
        MAKA = sbt("r_MAKA", [128, 2, 128])
        Qm = sbt("r_Qm", [128, 64])
        Pk = [sbt(f"r_P{i}", [128, 64]) for i in range(2)]
        Qk = [sbt(f"r_Q{i}", [128, 64]) for i in range(2)]
        TT = [sbt(f"r_TT{i}", [128, 64]) for i in range(2)]
        Xs = sbt("r_X", [128, 64])
        Us = sbt("r_U", [128, 64])
        Ytm = sbt("r_Y", [128, 64])
        cen = sbt("r_cen", [128, 64])
        sq = sbt("r_sq", [128, 64])
        stt = sbt("r_st", [128, 4])
        S0 = [sbt(f"r_S{i}", [128, 64]) for i in range(2)]
        psA = [st.enter_context(_psm(nc, f"rpsA{i}", [128, 512], F32)).ap() for i in range(2)]
        psN = [st.enter_context(_psm(nc, f"rpsN{i}", [128, 512], F32)).ap() for i in range(2)]
        psX = [st.enter_context(_psm(nc, f"rpsX{i}", [128, 512], F32)).ap() for i in range(2)]
        psP = [st.enter_context(_psm(nc, f"rpsP{i}", [128, 512], F32)).ap() for i in range(2)]

        P.dma("sp", w2, W["w2"], w=["r_w2"])
        P.dma("sp", a2, W["a2"], w=["r_a2"])
        for par in range(2):
            pr = slice(64 * par, 64 * par + 64)
            P.dma("sp", u2[pr, :], K.u2m, r=["u2m"], w=["r_u2"])
            P.dma("sp", sl[pr, :], K.slm, r=["slm"], w=["r_sl"])
        cm = T["cmask"]
        P.op("pool", lambda e: e.memset(cm, 1.0), w=["r_cmask"])
        P.op("pool", lambda e: e.memset(cm.rearrange("p (c t) -> p c t", t=64)[:, :, 0:1], 0.0), w=["r_cmask"])

        def tt(out, in0, in1, op, r, w, eng="dve"):
            P.op(eng, lambda e: e.tensor_tensor(out=out, in0=in0, in1=in1, op=op), r=r, w=w)

        for hp in range(4):
            col = lambda nm: K.cols[:, K.COLS[nm][0] + hp:K.COLS[nm][0] + hp + 1]
            hc = slice(hp * 128, (hp + 1) * 128)
            for par in range(2):
                P.op("pool", lambda e: e.memset(S0[0][64 * par:64 * par + 64, :], 0.0), w=[f"r_S0_{par}"])
            for sg in range(NSEG):
                sc = slice(sg * SEG, (sg + 1) * SEG)
                P.dma("sp", T["rT"], K.psT[C_R + hp * 128:C_R + (hp + 1) * 128, sc], r=["psT"], w=["r_rT"])
                P.dma("sp", T["kT"], K.psT[C_K + hp * 128:C_K + (hp + 1) * 128, sc], r=["psT"], w=["r_kT"])
                P.dma("sp", T["vT"], K.psT[C_V + hp * 128:C_V + (hp + 1) * 128, sc], r=["psT"], w=["r_vT"])
                P.dma("sp", wdT, K.psT[C_WD:C_WD + 64, sc], r=["psT"], w=["r_wdT"])
                P.dma("sp", adT, K.psT[C_AD:C_AD + 64, sc], r=["psT"], w=["r_adT"])
                P.dma("sp", gz, K.szaT[hc, sc], r=["szaT"], w=["r_gz"])
                P.op("act", lambda e: e.activation(out=wdT, in_=wdT, func=AF.Tanh), r=["r_wdT"], w=["r_wdT"])
                for cc in range(NCC):
                    cs_ = slice(cc * CC, (cc + 1) * CC)
                    pb = cc % 2
                    P.op("pe", lambda e: e.matmul(psP[pb][:, 0:CC], lhsT=w2[:, hc], rhs=wdT[:, cs_], start=True, stop=True),
                         r=["r_w2", "r_wdT"], w=[f"rpsP{pb}"])
                    P.op("act", lambda e: e.activation(out=T["lwp"][:, cs_], in_=psP[pb][:, 0:CC], func=AF.Sigmoid, bias=col("w0")),
                         r=[f"rpsP{pb}", "cols"], w=["r_lwp"])
                    P.op("pe", lambda e: e.matmul(psP[pb][:, 0:CC], lhsT=a2[:, hc], rhs=adT[:, cs_], start=True, stop=True),
                         r=["r_a2", "r_adT"], w=[f"rpsP{pb}"])
                    P.op("act", lambda e: e.activation(out=T["ai"][:, cs_], in_=psP[pb][:, 0:CC], func=AF.Sigmoid, bias=col("a0")),
                         r=[f"rpsP{pb}", "cols"], w=["r_ai"])
                P.op("dve", lambda e: e.tensor_scalar(out=T["kkn"], in0=T["kT"], scalar1=col("kk"), scalar2=None, op0=ALU.mult),
                     r=["r_kT", "cols"], w=["r_kkn"])
                tt(T["tmpA"], T["kkn"], T["kkn"], ALU.mult, ["r_kkn"], ["r_tmpA"])
                for cc in range(NCC):
                    cs_ = slice(cc * CC, (cc + 1) * CC)
                    pb = cc % 2
                    P.op("pe", lambda e: e.matmul(psP[pb][:, 0:CC], lhsT=K.bdiag, rhs=T["tmpA"][:, cs_], start=True, stop=True),
                         r=["bdiag", "r_tmpA"], w=[f"rpsP{pb}"])
                    P.op("act", lambda e: e.activation(out=T["E4"][:, cs_], in_=psP[pb][:, 0:CC], func=AF.Sqrt),
                         r=[f"rpsP{pb}"], w=["r_E4"])
                P.op("dve", lambda e: e.tensor_scalar(out=T["E4"], in0=T["E4"], scalar1=1e-12, scalar2=None, op0=ALU.max),
                     r=["r_E4"], w=["r_E4"])
                P.op("dve", lambda e: e.reciprocal(out=T["E4"], in_=T["E4"]), r=["r_E4"], w=["r_E4"])
                tt(T["kkn"], T["kkn"], T["E4"], ALU.mult, ["r_kkn", "r_E4"], ["r_kkn"])
                P.op("dve", lambda e: e.tensor_scalar(out=T["tmpA"], in0=T["ai"], scalar1=col("ka"), scalar2=K.omka[:, hp:hp + 1],
                                                      op0=ALU.mult, op1=ALU.add), r=["r_ai", "cols", "omka"], w=["r_tmpA"])
                tt(T["km"], T["kT"], T["tmpA"], ALU.mult, ["r_kT", "r_tmpA"], ["r_km"])
                tt(T["bb"], T["kkn"], T["ai"], ALU.mult, ["r_kkn", "r_ai"], ["r_bb"])
                P.op("dve", lambda e: e.tensor_tensor_scan(out=T["Lp"], data0=cm, data1=T["lwp"], initial=0.0, op0=ALU.mult, op1=ALU.add),
                     r=["r_cmask", "r_lwp"], w=["r_Lp"])
                tt(T["Lpe"], T["Lp"], T["lwp"], ALU.subtract, ["r_Lp", "r_lwp"], ["r_Lpe"])
                Lp3 = T["Lp"].rearrange("p (c t) -> p c t", t=64)
                tt(T["tmpA"].rearrange("p (c t) -> p c t", t=64), Lp3[:, :, 63:64].broadcast_to([128, NCH, 64]), Lp3, ALU.subtract,
                   ["r_Lp"], ["r_tmpA"])
                P.op("act", lambda e: e.activation(out=T["E1"], in_=T["Lp"], func=AF.Exp, scale=-C0), r=["r_Lp"], w=["r_E1"])
                P.op("act", lambda e: e.activation(out=T["E2"], in_=T["Lp"], func=AF.Exp, scale=C0), r=["r_Lp"], w=["r_E2"])
                P.op("act", lambda e: e.activation(out=T["E3"], in_=T["Lpe"], func=AF.Exp, scale=-C0), r=["r_Lpe"], w=["r_E3"])
                P.op("act", lambda e: e.activation(out=T["E4"], in_=T["tmpA"], func=AF.Exp, scale=-C0), r=["r_tmpA"], w=["r_E4"])
                P.op("dve", lambda e: e.scalar_tensor_tensor(out=AR[:, 0, :], in0=T["kkn"], scalar=-1.0, in1=T["E3"], op0=ALU.mult, op1=ALU.mult),
                     r=["r_kkn", "r_E3"], w=["r_AR"])
                tt(AR[:, 1, :], T["rT"], T["E1"], ALU.mult, ["r_rT", "r_E1"], ["r_AR"])
                tt(BK[:, 0, :], T["bb"], T["E2"], ALU.mult, ["r_bb", "r_E2"], ["r_BK"])
                tt(BK[:, 1, :], T["km"], T["E2"], ALU.mult, ["r_km", "r_E2"], ["r_BK"])
                tt(BKh[:, 0, :], T["bb"], T["E4"], ALU.mult, ["r_bb", "r_E4"], ["r_BKh"])
                tt(BKh[:, 1, :], T["km"], T["E4"], ALU.mult, ["r_km", "r_E4"], ["r_BKh"])
                P.op("dve", lambda e: e.scalar_tensor_tensor(out=T["tmpA"], in0=T["rT"], scalar=col("rk"), in1=T["km"], op0=ALU.mult, op1=ALU.mult),
                     r=["r_rT", "r_km", "cols", "r_tmpA"], w=["r_tmpA"])
                for cc in range(NCC):
                    cs_ = slice(cc * CC, (cc + 1) * CC)
                    pb = cc % 2
                    P.op("pe", lambda e: e.matmul(psP[pb][:, 0:CC], lhsT=K.bdiag, rhs=T["tmpA"][:, cs_], start=True, stop=True),
                         r=["bdiag", "r_tmpA"], w=[f"rpsP{pb}"])
                    tt(T["bonT"][:, cs_], psP[pb][:, 0:CC], T["vT"][:, cs_], ALU.mult, [f"rpsP{pb}", "r_vT"], ["r_bonT"])
                for c in range(NCH):
                    gi = sg * NCH + c
                    cs_ = slice(c * 64, (c + 1) * 64)
                    for par in range(2):
                        pr = slice(64 * par, 64 * par + 64)
                        A_, N_, X_ = psA[par], psN[par], psX[par]
                        nA, nN, nX = f"rpsA{par}", f"rpsN{par}", f"rpsX{par}"
                        s_old, s_new = S0[gi % 2], S0[(gi + 1) % 2]
                        ns_old, ns_new = f"r_S{gi % 2}_{par}", f"r_S{(gi + 1) % 2}_{par}"
                        sfx = f"_{par}"
                        idm = K.identf[pr, pr]
                        for q_, src, sn in ((0, T["vT"][pr, cs_], "r_vT"), (1, BKh[pr, 0, cs_], "r_BKh"), (2, BKh[pr, 1, cs_], "r_BKh")):
                            P.op("pe", lambda e: e.matmul(X_[pr, 256 + 64 * q_:320 + 64 * q_], lhsT=src, rhs=idm, start=True, stop=True),
                                 r=[sn, "identf"], w=[nX])
                        P.op("act", lambda e: e.activation(out=tm[pr, :], in_=X_[pr, 256:448], func=AF.Copy), r=[nX], w=["r_tm" + sfx])
                        P.op("pe", lambda e: e.matmul(A_[pr, 0:128], lhsT=BK[pr, 0, cs_], rhs=AR[pr, :, cs_], start=True, stop=True),
                             r=["r_BK", "r_AR"], w=[nA])
                        P.op("pe", lambda e: e.matmul(A_[pr, 128:256], lhsT=BK[pr, 1, cs_], rhs=AR[pr, :, cs_], start=True, stop=True),
                             r=["r_BK", "r_AR"], w=[nA])
                        P.op("pe", lambda e: e.matmul(A_[pr, 256:320], lhsT=AR[pr, 0, cs_], rhs=BK[pr, 0, cs_], start=True, stop=True),
                             r=["r_BK", "r_AR"], w=[nA])
                        tt(MAKA[pr], A_[pr, 0:256].rearrange("p (a t) -> p a t", a=2), u2[pr, :].unsqueeze(1).broadcast_to([64, 2, 128]),
                           ALU.mult, [nA, "r_u2"], ["r_MAKA" + sfx])
                        tt(Qm[pr], A_[pr, 256:320], sl[pr, :], ALU.mult, [nA, "r_sl"], ["r_Qm" + sfx])
                        MA, KA = MAKA[pr, 0, :], MAKA[pr, 1, :]
                        tt(TT[0][pr], MA[:, 0:64], idm, ALU.add, ["r_MAKA" + sfx, "identf"], ["r_TT0" + sfx])
                        p_prev, np_prev = MA[:, 0:64], "r_MAKA" + sfx
                        q_prev, nq_prev = Qm[pr], "r_Qm" + sfx
                        for kq in range(1, 6):
                            pp = kq % 2
                            if kq < 5:
                                P.op("pe", lambda e: e.matmul(N_[pr, 0:64], lhsT=q_prev, rhs=p_prev, start=True, stop=True),
                                     r=[nq_prev, np_prev], w=[nN])
                                P.op("act", lambda e: e.activation(out=Pk[pp][pr], in_=N_[pr, 0:64], func=AF.Copy),
                                     r=[nN], w=[f"r_P{pp}" + sfx])
                            P.op("pe", lambda e: e.matmul(N_[pr, 64:128], lhsT=p_prev, rhs=q_prev, start=True, stop=True),
                                 r=[nq_prev, np_prev], w=[nN])
                            P.op("act", lambda e: e.activation(out=Qk[pp][pr], in_=N_[pr, 64:128], func=AF.Copy),
                                 r=[nN], w=[f"r_Q{pp}" + sfx])
                            t_old, t_new = TT[(kq - 1) % 2], TT[kq % 2]
                            P.op("pe", lambda e: e.matmul(A_[pr, 320:384], lhsT=Qk[pp][pr], rhs=t_old[pr], start=True, stop=True),
                                 r=[f"r_Q{pp}" + sfx, f"r_TT{(kq - 1) % 2}" + sfx], w=[nA])
                            tt(t_new[pr], A_[pr, 320:384], t_old[pr], ALU.add, [nA, f"r_TT{(kq - 1) % 2}" + sfx], [f"r_TT{kq % 2}" + sfx])
                            p_prev, np_prev = Pk[pp][pr], f"r_P{pp}" + sfx
                            q_prev, nq_prev = Qk[pp][pr], f"r_Q{pp}" + sfx
                        TTf, nTT = TT[5 % 2][pr], f"r_TT{5 % 2}" + sfx
                        Vh, Bh, Kh = tm[pr, 0:64], tm[pr, 64:128], tm[pr, 128:192]
                        P.op("pe", lambda e: e.matmul(X_[pr, 0:64], lhsT=AR[pr, 0, cs_], rhs=s_old[pr], start=True, stop=False),
                             r=["r_AR", ns_old], w=[nX])
                        P.op("pe", lambda e: e.matmul(X_[pr, 0:64], lhsT=KA[:, 0:64], rhs=Vh, start=False, stop=True),
                             r=["r_MAKA" + sfx, "r_tm" + sfx], w=[nX])
                        P.op("act", lambda e: e.activation(out=Xs[pr], in_=X_[pr, 0:64], func=AF.Copy), r=[nX], w=["r_X" + sfx])
                        P.op("pe", lambda e: e.matmul(X_[pr, 64:128], lhsT=TTf, rhs=Xs[pr], start=True, stop=True),
                             r=[nTT, "r_X" + sfx], w=[nX])
                        P.op("act", lambda e: e.activation(out=Us[pr], in_=X_[pr, 64:128], func=AF.Copy), r=[nX], w=["r_U" + sfx])
                        P.op("pe", lambda e: e.matmul(X_[pr, 128:192], lhsT=AR[pr, 1, cs_], rhs=s_old[pr], start=True, stop=False),
                             r=["r_AR", ns_old], w=[nX])
                        P.op("pe", lambda e: e.matmul(X_[pr, 128:192], lhsT=MA[:, 64:128], rhs=Us[pr], start=False, stop=False),
                             r=["r_MAKA" + sfx, "r_U" + sfx], w=[nX])
                        P.op("pe", lambda e: e.matmul(X_[pr, 128:192], lhsT=KA[:, 64:128], rhs=Vh, start=False, stop=True),
                             r=["r_MAKA" + sfx, "r_tm" + sfx], w=[nX])
                        P.op("act", lambda e: e.activation(out=Ytm[pr], in_=X_[pr, 128:192], func=AF.Copy), r=[nX], w=["r_Y"])
                        P.op("pe", lambda e: e.matmul(X_[pr, 192:256], lhsT=Bh, rhs=Us[pr], start=True, stop=False),
                             r=["r_tm" + sfx, "r_U" + sfx], w=[nX])
                        P.op("pe", lambda e: e.matmul(X_[pr, 192:256], lhsT=Kh, rhs=Vh, start=False, stop=True),
                             r=["r_tm" + sfx], w=[nX])
                        P.op("dve", lambda e: e.scalar_tensor_tensor(out=s_new[pr], in0=s_old[pr], scalar=T["E1"][pr, c * 64 + 63:c * 64 + 64],
                                                                     in1=X_[pr, 192:256], op0=ALU.mult, op1=ALU.add),
                             r=[ns_old, "r_E1", nX], w=[ns_new])
                    P.op("dve", lambda e: e.tensor_reduce(out=stt[:, 0:1], in_=Ytm, axis=AX.X, op=ALU.add), r=["r_Y"], w=["r_st"])
                    P.op("dve", lambda e: e.tensor_scalar(out=stt[:, 1:2], in0=stt[:, 0:1], scalar1=-1.0 / 64, scalar2=None, op0=ALU.mult),
                         r=["r_st"], w=["r_st"])
                    P.op("dve", lambda e: e.tensor_scalar(out=cen, in0=Ytm, scalar1=stt[:, 1:2], scalar2=None, op0=ALU.add),
                         r=["r_Y", "r_st"], w=["r_cen"])
                    P.op("act", lambda e: e.activation(out=sq, in_=cen, func=AF.Square, accum_out=stt[:, 2:3]), r=["r_cen"], w=["r_sq", "r_st"])
                    P.op("dve", lambda e: e.tensor_scalar(out=stt[:, 3:4], in0=stt[:, 2:3], scalar1=1.0 / 64, scalar2=GN_EPS, op0=ALU.mult, op1=ALU.add),
                         r=["r_st"], w=["r_st"])
                    P.op("act", lambda e: e.activation(out=stt[:, 3:4], in_=stt[:, 3:4], func=AF.Sqrt), r=["r_st"], w=["r_st"])
                    P.op("dve", lambda e: e.reciprocal(out=stt[:, 3:4], in_=stt[:, 3:4]), r=["r_st"], w=["r_st"])
                    P.op("dve", lambda e: e.tensor_scalar(out=cen, in0=cen, scalar1=stt[:, 3:4], scalar2=None, op0=ALU.mult),
                         r=["r_cen", "r_st"], w=["r_cen"])
                    for par in range(2):
                        pr = slice(64 * par, 64 * par + 64)
                        P.op("pe", lambda e: e.matmul(psX[par][pr, 448:512], lhsT=cen[pr], rhs=K.identf[pr, pr], start=True, stop=True),
                             r=["r_cen", "identf"], w=[f"rpsX{par}"])
                        P.op("act", lambda e: e.activation(out=T["OG"][pr, cs_], in_=psX[par][pr, 448:512], func=AF.Identity,
                                                           scale=col("lg")[pr], bias=col("lb")[pr]),
                             r=[f"rpsX{par}", "cols"], w=["r_OG"])
                tt(T["OG"], T["OG"], T["bonT"], ALU.add, ["r_OG", "r_bonT"], ["r_OG"])
                tt(ogb, T["OG"], gz, ALU.mult, ["r_OG", "r_gz"], ["r_ogb"])
                P.dma("sp", K.yagT[hc, sc], ogb, r=["r_ogb"], w=["yagT"])
        P.barrier()


def phase_dsa(K, l):
    from contextlib import ExitStack
    nc, P, W, S, NT, TOPK = K.nc, K.P, K.L[l], K.S, K.NT, K.TOPK
    NR = TOPK // 8
    with ExitStack() as st:
        def sbt(name, shape, dt=F32):
            return st.enter_context(_sbm(nc, name, list(shape), dt)).ap()
        latS = sbt("d_lat", [128, S], BF16)
        ki2 = sbt("d_ki2", [128, S])
        vxs = sbt("d_vxs", [128, NT, 520], BF16)
        wis = sbt("d_wis", [128, NT * 8])
        EBM = [sbt(f"d_EBM{i}", [128, 8, 128]) for i in range(2)]
        b31 = sbt("d_b31", [128, 8, 128])
        qij = [sbt(f"d_qij{i}", [128, 4, 128]) for i in range(2)]
        qab = [sbt(f"d_qab{i}", [128, 8, 128], BF16) for i in range(2)]
        zbj = [sbt(f"d_zbj{i}", [128, 4, 128], BF16) for i in range(2)]
        acc = sbt("d_acc", [128, S])
        work = sbt("d_work", [128, S])
        maskf = sbt("d_maskf", [128, S], BF16)
        maskT = sbt("d_maskT", [128, NT, 128], BF16)
        Pt = [sbt(f"d_Pt{i}", [128, 8, 128], BF16) for i in range(2)]
        rtmp = [sbt(f"d_rt{i}", [128, 512]) for i in range(2)]
        m8 = sbt("d_m8", [128, 8])
        rec = sbt("d_rec", [128, 8])
        yb = sbt("d_yb", [128, 512])
        ybg = [sbt(f"d_ybg{i}", [128, 4, 128], BF16) for i in range(2)]
        psI = [st.enter_context(_psm(nc, f"dpsI{i}", [128, 512], F32)).ap() for i in range(2)]
        psM = st.enter_context(_psm(nc, "dpsM", [128, 1024], BF16)).ap()
        psQ = [st.enter_context(_psm(nc, f"dpsQ{i}", [128, 512], F32)).ap() for i in range(2)]
        psO = [st.enter_context(_psm(nc, f"dpsO{i}", [128, 512], F32)).ap() for i in range(2)]

        P.dma("sp", latS, K.latT, r=["latT"], w=["d_lat"])
        P.dma("sp", ki2[0:64, :], K.kiT, r=["kiT"], w=["d_ki2"])
        P.dma("sp", ki2[64:128, :], K.kiT, r=["kiT"], w=["d_ki2"])
        for i in range(NT):
            P.dma("sp", vxs[:, i, :], K.vext[i], r=["vext"], w=["d_vxs"])
        P.dma("sp", wis, K.witm, r=["witm"], w=["d_wis"])
        P.dma("sp", b31, K.eb31, w=["d_b31"])
        for kd in range(2):
            P.dma("sp", EBM[kd], K.ebias[kd], w=[f"d_EBM{kd}"])
            P.op("dve", lambda e: e.tensor_tensor(out=EBM[kd], in0=EBM[kd], in1=b31, op=ALU.subtract), r=[f"d_EBM{kd}", "d_b31"], w=[f"d_EBM{kd}"])
            P.op("act", lambda e: e.activation(out=EBM[kd], in_=EBM[kd], func=AF.Exp), r=[f"d_EBM{kd}"], w=[f"d_EBM{kd}"])
        P.op("pool", lambda e: e.affine_select(out=EBM[0], in_=EBM[0], pattern=[[0, 8], [1, 128]], compare_op=ALU.is_ge, fill=K.reg_zero,
                                               base=0, channel_multiplier=-1), r=["d_EBM0"], w=["d_EBM0"])
        icount = 0
        for j in range(NT):
            b = j % 2
            jc = slice(j * 128, (j + 1) * 128)
            Lk = (j + 1) * 128
            P.dma("sp", qij[b], K.qiT.rearrange("(hq p) t -> p hq t", p=128)[:, :, jc], r=["qiT"], w=[f"d_qij{b}"])
            P.dma("sp", qab[b], K.qabsT.rearrange("h c t -> c h t")[:, :, jc], r=["qabsT"], w=[f"d_qab{b}"])
            P.dma("sp", zbj[b], K.szbT.rearrange("(fb p) t -> p fb t", p=128)[:, :, jc], r=["szbT"], w=[f"d_zbj{b}"])
            if Lk > TOPK:
                for k0 in range(0, Lk, 512):
                    k1 = min(Lk, k0 + 512)
                    wd = k1 - k0
                    for h in range(8):
                        par, hq = h % 2, h // 2
                        pr = slice(64 * par, 64 * par + 64)
                        ib = icount % 2
                        icount += 1
                        P.op("pe", lambda e: e.matmul(psI[ib][:, 0:wd], lhsT=qij[b][pr, hq, :], rhs=ki2[pr, k0:k1], start=True, stop=True),
                             r=[f"d_qij{b}", "d_ki2"], w=[f"dpsI{ib}"])
                        P.op("act", lambda e: e.activation(out=rtmp[ib][:, 0:wd], in_=psI[ib][:, 0:wd], func=AF.Relu),
                             r=[f"dpsI{ib}"], w=[f"d_rt{ib}"])
                        wcol = wis[:, j * 8 + h:j * 8 + h + 1]
                        if h == 0:
                            P.op("dve", lambda e: e.tensor_scalar(out=acc[:, k0:k1], in0=rtmp[ib][:, 0:wd], scalar1=wcol, scalar2=None, op0=ALU.mult),
                                 r=[f"d_rt{ib}", "d_wis"], w=["d_acc"])
                        else:
                            P.op("dve", lambda e: e.scalar_tensor_tensor(out=acc[:, k0:k1], in0=rtmp[ib][:, 0:wd], scalar=wcol, in1=acc[:, k0:k1],
                                                                         op0=ALU.mult, op1=ALU.add), r=[f"d_rt{ib}", "d_wis", "d_acc"], w=["d_acc"])
                P.op("pool", lambda e: e.affine_select(out=acc[:, jc], in_=acc[:, jc], pattern=[[-1, 128]], compare_op=ALU.is_ge, fill=K.reg_neg,
                                                       base=0, channel_multiplier=1), r=["d_acc"], w=["d_acc"])
                for rd in range(NR):
                    src, sn = (acc, "d_acc") if rd == 0 else (work, "d_work")
                    P.op("dve", lambda e: e.max(out=m8, in_=src[:, 0:Lk]), r=[sn], w=["d_m8"])
                    if rd < NR - 1:
                        P.op("dve", lambda e: e.match_replace(out=work[:, 0:Lk], in_to_replace=m8, in_values=src[:, 0:Lk], imm_value=NEG),
                             r=[sn, "d_m8"], w=["d_work"])
                P.op("dve", lambda e: e.tensor_scalar(out=maskf[:, 0:Lk], in0=acc[:, 0:Lk], scalar1=m8[:, 7:8], scalar2=None, op0=ALU.is_ge),
                     r=["d_acc", "d_m8"], w=["d_maskf"])
                for i0 in range(0, j + 1, 8):
                    i1 = min(j + 1, i0 + 8)
                    for i in range(i0, i1):
                        P.op("pe", lambda e: e.transpose(out=psM[:, (i - i0) * 128:(i - i0 + 1) * 128], in_=maskf[:, i * 128:(i + 1) * 128],
                                                         identity=K.identb), r=["d_maskf", "identb"], w=["dpsM"])
                    P.op("act", lambda e: e.activation(out=maskT[:, i0:i1, :], in_=psM[:, 0:(i1 - i0) * 128].rearrange("p (i t) -> p i t", t=128),
                                                       func=AF.Copy), r=["dpsM"], w=["d_maskT"])
            else:
                P.op("pool", lambda e: e.memset(maskT[:, 0:j + 1, :], 1.0), w=["d_maskT"])
            def qk(i):
                for hh in range(2):
                    P.op("pe", lambda e: e.matmul(psQ[hh], lhsT=latS[:, i * 128:(i + 1) * 128], rhs=qab[b][:, 4 * hh:4 * hh + 4, :], start=True, stop=True),
                         r=["d_lat", f"d_qab{b}"], w=[f"dpsQ{hh}"])
            qk(0)
            for i in range(j + 1):
                a = i % 2
                for hh in range(2):
                    P.op("act", lambda e: e.activation(out=Pt[a][:, 4 * hh:4 * hh + 4, :], in_=psQ[hh].rearrange("p (h t) -> p h t", h=4), func=AF.Exp),
                         r=[f"dpsQ{hh}"], w=[f"d_Pt{a}"])
                kd = j - i
                if kd <= 1:
                    P.op("dve", lambda e: e.tensor_tensor(out=Pt[a], in0=Pt[a], in1=EBM[kd], op=ALU.mult), r=[f"d_Pt{a}", f"d_EBM{kd}"], w=[f"d_Pt{a}"])
                P.op("dve", lambda e: e.tensor_tensor(out=Pt[a], in0=Pt[a], in1=maskT[:, i:i + 1, :].broadcast_to([128, 8, 128]), op=ALU.mult),
                     r=[f"d_Pt{a}", "d_maskT"], w=[f"d_Pt{a}"])
                if i + 1 <= j:
                    qk(i + 1)
                for h in range(8):
                    ob = h // 4
                    o0 = (h % 4) * 65
                    P.op("pe", lambda e: e.matmul(psO[ob][:, o0:o0 + 65], lhsT=Pt[a][:, h, :], rhs=vxs[:, i, h * 65:(h + 1) * 65],
                                                  start=(i == 0 and h % 4 == 0), stop=(i == j), skip_group_check=True),
                         r=[f"d_Pt{a}", "d_vxs"], w=[f"dpsO{ob}"])
            for ob in range(2):
                o3 = psO[ob][:, 0:260].rearrange("p (h d) -> p h d", d=65)
                P.op("dve", lambda e: e.reciprocal(out=rec[:, 4 * ob:4 * ob + 4].unsqueeze(2), in_=o3[:, :, 64:65]), r=[f"dpsO{ob}"], w=["d_rec"])
                P.op("dve", lambda e: e.tensor_tensor(out=yb[:, 256 * ob:256 * ob + 256].rearrange("p (h d) -> p h d", d=64), in0=o3[:, :, 0:64],
                                                      in1=rec[:, 4 * ob:4 * ob + 4].unsqueeze(2).broadcast_to([128, 4, 64]), op=ALU.mult),
                     r=[f"dpsO{ob}", "d_rec"], w=["d_yb"])
            for fb in range(4):
                P.op("pe", lambda e: e.transpose(out=psI[0][:, fb * 128:(fb + 1) * 128], in_=yb[:, fb * 128:(fb + 1) * 128], identity=K.identf),
                     r=["d_yb", "identf"], w=["dpsI0"])
            P.op("dve", lambda e: e.tensor_tensor(out=ybg[b], in0=psI[0].rearrange("p (f t) -> p f t", f=4), in1=zbj[b], op=ALU.mult),
                 r=["dpsI0", f"d_zbj{b}"], w=[f"d_ybg{b}"])
            P.dma("sp", K.ybgT.rearrange("(fb p) t -> p fb t", p=128)[:, :, jc], ybg[b], r=[f"d_ybg{b}"], w=["ybgT"])
        P.barrier()


def phase_out(K, l, xsrc, xdst):
    from contextlib import ExitStack
    nc, P, W, S, TC, NTC = K.nc, K.P, K.L[l], K.S, K.TC, K.NTC
    with ExitStack() as st:
        def sbt(name, shape, dt=F32):
            return st.enter_context(_sbm(nc, name, list(shape), dt)).ap()
        stg = sbt("o_stg", [128, 4, K.D])
        wpa = sbt("o_wpa", [128, 4, K.D], BF16)
        wpb = sbt("o_wpb", [128, 4, K.D], BF16)
        wo = sbt("o_wo", [128, 8, K.D], BF16)
        ya = [sbt(f"o_ya{i}", [128, 4, TC], BF16) for i in range(2)]
        yb = [sbt(f"o_yb{i}", [128, 4, TC], BF16) for i in range(2)]
        ga = [sbt(f"o_ga{i}", [128, 8, TC], BF16) for i in range(2)]
        gb = [sbt(f"o_gb{i}", [128, 8, TC], BF16) for i in range(2)]
        t1 = [sbt(f"o_t1{i}", [128, TC]) for i in range(2)]
        t2 = [sbt(f"o_t2{i}", [128, TC]) for i in range(2)]
        mg = sbt("o_mg", [128, 8, TC], BF16)
        xt = [sbt(f"o_xt{i}", [128, K.D]) for i in range(2)]
        xo = [sbt(f"o_xo{i}", [128, K.D]) for i in range(2)]
        psA = [st.enter_context(_psm(nc, f"opsA{i}", [128, 512], F32)).ap() for i in range(2)]
        psB = [st.enter_context(_psm(nc, f"opsB{i}", [128, 512], F32)).ap() for i in range(2)]
        psO = [st.enter_context(_psm(nc, f"opsO{i}", [128, 512], F32)).ap() for i in range(2)]
        for src, dst, dn in ((W["w_pa"], wpa, "o_wpa"), (W["w_pb"], wpb, "o_wpb")):
            P.dma("sp", stg, src.rearrange("(j p) n -> p j n", p=128), r=["o_stg"], w=["o_stg"])
            P.op("pool", lambda e: e.tensor_copy(out=dst, in_=stg), r=["o_stg"], w=[dn])
        for hf in range(2):
            P.dma("sp", stg, W["w_o"][hf * 512:(hf + 1) * 512, :].rearrange("(j p) n -> p j n", p=128), r=["o_stg"], w=["o_stg"])
            P.op("pool", lambda e: e.tensor_copy(out=wo[:, 4 * hf:4 * hf + 4, :], in_=stg), r=["o_stg"], w=["o_wo"])
        gate = K.modB[:, 2 * K.D:3 * K.D]
        cnt = 0
        xcnt = 0
        for tc in range(NTC):
            b = tc % 2
            tcs = slice(tc * TC, (tc + 1) * TC)
            P.dma("sp", ya[b], K.yagT.rearrange("(j p) t -> p j t", p=128)[:, :, tcs], r=["yagT"], w=[f"o_ya{b}"])
            P.dma("sp", yb[b], K.ybgT.rearrange("(j p) t -> p j t", p=128)[:, :, tcs], r=["ybgT"], w=[f"o_yb{b}"])
            P.dma("sp", ga[b], K.sgaT.rearrange("(j p) t -> p j t", p=128)[:, :, tcs], r=["sgaT"], w=[f"o_ga{b}"])
            P.dma("sp", gb[b], K.sgbT.rearrange("(j p) t -> p j t", p=128)[:, :, tcs], r=["sgbT"], w=[f"o_gb{b}"])
            for ob in range(8):
                pb = cnt % 2
                cnt += 1
                oc = slice(ob * 128, (ob + 1) * 128)
                for j in range(4):
                    P.op("pe", lambda e: e.matmul(psA[pb][:, 0:TC], lhsT=wpa[:, j, oc], rhs=ya[b][:, j, :], start=(j == 0), stop=(j == 3)),
                         r=["o_wpa", f"o_ya{b}"], w=[f"opsA{pb}"])
                for j in range(4):
                    P.op("pe", lambda e: e.matmul(psB[pb][:, 0:TC], lhsT=wpb[:, j, oc], rhs=yb[b][:, j, :], start=(j == 0), stop=(j == 3)),
                         r=["o_wpb", f"o_yb{b}"], w=[f"opsB{pb}"])
                P.op("dve", lambda e: e.tensor_tensor(out=t1[pb], in0=psA[pb][:, 0:TC], in1=ga[b][:, ob, :], op=ALU.mult),
                     r=[f"opsA{pb}", f"o_ga{b}"], w=[f"o_t1{pb}"])
                P.op("dve", lambda e: e.tensor_tensor(out=t2[pb], in0=psB[pb][:, 0:TC], in1=gb[b][:, ob, :], op=ALU.mult),
                     r=[f"opsB{pb}", f"o_gb{b}"], w=[f"o_t2{pb}"])
                P.op("pool", lambda e: e.tensor_tensor(out=mg[:, ob, :], in0=t1[pb], in1=t2[pb], op=ALU.add),
                     r=[f"o_t1{pb}", f"o_t2{pb}"], w=["o_mg"])
            for ts in range(TC // 128):
                xb_ = xcnt % 2
                xcnt += 1
                rows = slice(tc * TC + ts * 128, tc * TC + (ts + 1) * 128)
                P.dma("sp", xt[xb_], xsrc[rows, :], w=[f"o_xt{xb_}"])
                for hf in range(2):
                    hc = slice(hf * 512, (hf + 1) * 512)
                    for ob in range(8):
                        P.op("pe", lambda e: e.matmul(psO[hf], lhsT=mg[:, ob, ts * 128:(ts + 1) * 128], rhs=wo[:, ob, hc], start=(ob == 0), stop=(ob == 7)),
                             r=["o_mg", "o_wo"], w=[f"opsO{hf}"])
                    P.op("dve", lambda e: e.tensor_tensor(out=xo[xb_][:, hc], in0=psO[hf], in1=gate[:, hc], op=ALU.mult),
                         r=[f"opsO{hf}", "modB"], w=[f"o_xo{xb_}"])
                P.op("pool", lambda e: e.tensor_tensor(out=xo[xb_], in0=xo[xb_], in1=xt[xb_], op=ALU.add),
                     r=[f"o_xo{xb_}", f"o_xt{xb_}"], w=[f"o_xo{xb_}"])
                P.dma("sp", xdst[rows, :], xo[xb_], r=[f"o_xo{xb_}"], w=["xs"])
        P.barrier()


def phase_final(K, xsrc):
    from contextlib import ExitStack
    nc, P, S, NT = K.nc, K.P, K.S, K.NT
    with ExitStack() as st:
        def sbt(name, shape, dt=F32):
            return st.enter_context(_sbm(nc, name, list(shape), dt)).ap()
        xt = [sbt(f"f_xt{i}", [128, K.D]) for i in range(2)]
        xo = [sbt(f"f_xo{i}", [128, K.D]) for i in range(2)]
        sq = sbt("f_sq", [128, K.D])
        fg = sbt("f_fg", [128, K.D])
        ss = sbt("f_ss", [128, 2])
        ps = [st.enter_context(_psm(nc, f"fps{i}", [128, 512], F32)).ap() for i in range(2)]
        P.dma("sp", K.row[0:1, 0:K.D], K.final_g, r=["row"], w=["row"])
        for hf in range(2):
            P.op("pe", lambda e: e.matmul(ps[hf], lhsT=K.ones[0:1, :], rhs=K.row[0:1, hf * 512:(hf + 1) * 512], start=True, stop=True),
                 r=["ones", "row"], w=[f"fps{hf}"])
            P.op("act", lambda e: e.activation(out=fg[:, hf * 512:(hf + 1) * 512], in_=ps[hf], func=AF.Copy), r=[f"fps{hf}"], w=["f_fg"])
        for i in range(NT):
            b = i % 2
            rows = slice(i * 128, (i + 1) * 128)
            P.dma("sp", xt[b], xsrc[rows, :], r=["xs"], w=[f"f_xt{b}"])
            P.op("act", lambda e: e.activation(out=sq, in_=xt[b], func=AF.Square, accum_out=ss[:, 0:1]), r=[f"f_xt{b}"], w=["f_sq", "f_ss"])
            P.op("dve", lambda e: e.tensor_scalar(out=ss[:, 1:2], in0=ss[:, 0:1], scalar1=1.0 / K.D, scalar2=K.EPS, op0=ALU.mult, op1=ALU.add),
                 r=["f_ss"], w=["f_ss"])
            P.op("act", lambda e: e.activation(out=ss[:, 1:2], in_=ss[:, 1:2], func=AF.Sqrt), r=["f_ss"], w=["f_ss"])
            P.op("dve", lambda e: e.reciprocal(out=ss[:, 1:2], in_=ss[:, 1:2]), r=["f_ss"], w=["f_ss"])
            P.op("dve", lambda e: e.scalar_tensor_tensor(out=xo[b], in0=xt[b], scalar=ss[:, 1:2], in1=fg, op0=ALU.mult, op1=ALU.mult),
                 r=[f"f_xt{b}", "f_ss", "f_fg"], w=[f"f_xo{b}"])
            P.dma("sp", K.out[rows, :], xo[b], r=[f"f_xo{b}"], is_output=True)
        P.barrier()


_CACHE = {}


def kernel(**inputs):
    S = int(np.asarray(inputs["x"]).shape[1])
    B = int(np.asarray(inputs["x"]).shape[0])
    depth = int(np.asarray(inputs["w_in"]).shape[0])
    topk = min(256, S // 4)
    key = (S, depth, topk)
    if key not in _CACHE:
        _CACHE[key] = build(S, topk, depth=depth)[0]
    nc = _CACHE[key]
    shared = shared_maps(inputs, depth)
    in_maps = [core_map(inputs, shared, b) for b in range(B)]
    res = run_bass_kernel_spmd(nc, in_maps, core_ids=list(range(B)))
    return np.stack([np.asarray(r["out"], np.float32) for r in res.results], axis=0)
```

```python
import numpy as np
import concourse.bass as bass
import concourse.mybir as mybir
from concourse.bass_utils import run_bass_kernel_spmd

F32 = mybir.dt.float32
BF16 = mybir.dt.bfloat16
AF = mybir.ActivationFunctionType
ALU = mybir.AluOpType
AX = mybir.AxisListType


class Prog:
    NDSEM = 14
    PSUM_PREFIXES = ("mps", "pps", "pst", "ppq", "rps", "dps", "ops", "fps")

    def __init__(self, nc):
        self.nc = nc
        self.eng = dict(pe=nc.tensor, act=nc.scalar, dve=nc.vector, pool=nc.gpsimd, sp=nc.sync)
        self.csem = {k: nc.alloc_semaphore(name=f"c_{k}") for k in self.eng}
        self.cnt = {k: 0 for k in self.eng}
        self.seen = {k: {} for k in self.eng}
        self.lastw = {}
        self.readers = {}
        self.dsem = {q: [nc.alloc_semaphore(name=f"d_{q}{i}") for i in range(self.NDSEM)]
                     for q in ("sp", "pool", "act")}
        self.dcnt = {q: [0] * self.NDSEM for q in self.dsem}
        self.drr = {q: 0 for q in self.dsem}
        self.uid = 0
        self.out_tokens = []
        self.ninstr = 0

    def _wait(self, e, tok):
        sem, val = tok
        if self.seen[e].get(sem.num, 0) >= val:
            return
        self.eng[e].wait_ge(sem, val)
        self.seen[e][sem.num] = val

    def _deps(self, e, r, w):
        for b in r:
            lw = self.lastw.get(b)
            if lw is not None and not (lw[0] == e == "pe"):
                self._wait(e, lw[1])
            if b.startswith(self.PSUM_PREFIXES):
                for re_, tok in self.readers.get(b, {}).items():
                    if re_ != e:
                        self._wait(e, tok)
        for b in w:
            lw = self.lastw.get(b)
            if lw is not None and not (lw[0] == e == "pe"):
                self._wait(e, lw[1])
            for re_, tok in self.readers.get(b, {}).items():
                if re_ != e or e == "pool":
                    self._wait(e, tok)

    def _commit(self, key, tok, r, w):
        for b in r:
            self.readers.setdefault(b, {})[key] = tok
        for b in w:
            self.lastw[b] = (key, tok)
            self.readers[b] = {}

    def op(self, e, fn, r=(), w=()):
        self._deps(e, r, w)
        ins = fn(self.eng[e])
        self.cnt[e] += 1
        ins.then_inc(self.csem[e], 1)
        self._commit(e, (self.csem[e], self.cnt[e]), r, w)
        self.ninstr += 1

    def dma(self, q, out, in_, r=(), w=(), is_output=False, **kw):
        i = self.drr[q]
        self.drr[q] = (i + 1) % self.NDSEM
        sem = self.dsem[q][i]
        if self.dcnt[q][i] > 0:
            self._wait(q, (sem, self.dcnt[q][i]))
        self._deps(q, r, w)
        ins = self.eng[q].dma_start(out=out, in_=in_, **kw)
        self.dcnt[q][i] += 16
        ins.then_inc(sem, 16)
        tok = (sem, self.dcnt[q][i])
        self.uid += 1
        self._commit(f"dma{self.uid}", tok, r, w)
        if is_output:
            self.out_tokens.append(tok)
        self.ninstr += 1

    def barrier(self):
        for e in self.eng:
            for o in self.eng:
                if o != e and self.cnt[o] > 0:
                    self._wait(e, (self.csem[o], self.cnt[o]))
            for q in self.dsem:
                for i, sem in enumerate(self.dsem[q]):
                    if self.dcnt[q][i] > 0:
                        self._wait(e, (sem, self.dcnt[q][i]))

    def finish(self):
        for q in self.dsem:
            for i, sem in enumerate(self.dsem[q]):
                if self.dcnt[q][i] > 0:
                    self._wait("sp", (sem, self.dcnt[q][i]))
        for e in ("pe", "act", "dve", "pool"):
            if self.cnt[e] > 0:
                self._wait("sp", (self.csem[e], self.cnt[e]))


D = 1024
NIN = 5960
C_R, C_K, C_V, C_WD, C_AD, C_ZA, C_Q, C_CKV, C_ZB, C_QI, C_KI, C_WI, C_GA, C_GB = (
    0, 512, 1024, 1536, 1600, 1664, 2176, 2688, 2816, 3328, 3840, 3904, 3912, 4936)
C0 = 0.6065306597126334
NEG = -1.0e30

COLS = dict(mu=(0, 13), w0=(13, 4), a0=(17, 4), kk=(21, 4), ka=(25, 4), rk=(29, 4),
            lg=(33, 4), lb=(37, 4), kvg=(41, 1))
NCOL = 42


def t5_bucket_np(dist):
    import math
    max_exact = 16
    d = np.maximum(dist, 1).astype(np.float32)
    large = max_exact + (np.log(d / np.float32(max_exact)) / np.float32(math.log(128 / max_exact))
                         * np.float32(32 - max_exact)).astype(np.int32)
    large = np.minimum(large, 31)
    return np.where(dist < max_exact, dist, large)


class Ctx:
    pass


_UNIQ = [0]


def _sbm(nc, name, shape, dt):
    _UNIQ[0] += 1
    return nc.sbuf_tensor(f"{name}_u{_UNIQ[0]}", shape, dt)


def _psm(nc, name, shape, dt):
    _UNIQ[0] += 1
    return nc.psum_tensor(f"{name}_u{_UNIQ[0]}", shape, dt)


def build(S, TOPK, depth=2, dbg=(), stop_after=None, stop_layer=0):
    from contextlib import ExitStack
    nc = bass.Bass("TRN2", target_bir_lowering=False)
    P = Prog(nc)
    NT = S // 128
    TC = min(512, S)
    NTC = S // TC
    EPS = 1e-6

    def din(name, shape, dt=F32):
        return nc.dram_tensor(name, list(shape), dt, kind="ExternalInput").ap()

    def scr(name, shape, dt=F32):
        kind = "ExternalOutput" if name in dbg else "Internal"
        return nc.dram_tensor(name, list(shape), dt, kind=kind).ap()

    x_in = din("x", [S, D])
    cT = din("cT", [128, 8])
    final_g = din("final_g", [1, D])
    ebias = din("ebias", [2, 128, 8, 128])
    eb31 = din("eb31", [128, 8, 128])
    L = []
    for l in range(depth):
        L.append(dict(
            ada_w=din(f"ada_w{l}", [D, 3 * D]), ada_b=din(f"ada_b{l}", [1, 3 * D]),
            norm_g=din(f"norm_g{l}", [1, D]), w_in=din(f"w_in{l}", [D, NIN]),
            cols=din(f"cols{l}", [128, NCOL]), w2=din(f"w2{l}", [64, 512]), a2=din(f"a2{l}", [64, 512]),
            wukT=din(f"wukT{l}", [128, 4, 128]), wuv=din(f"wuv{l}", [128, 512]),
            w_pa=din(f"w_pa{l}", [512, D]), w_pb=din(f"w_pb{l}", [512, D]), w_o=din(f"w_o{l}", [D, D])))
    out = nc.dram_tensor("out", [S, D], F32, kind="ExternalOutput").ap()
    xs = [scr(f"xs{i}", [S, D]) for i in range(2)]
    psT = scr("psT", [1664, S])
    szaT = scr("szaT", [512, S], BF16)
    szbT = scr("szbT", [512, S], BF16)
    sgaT = scr("sgaT", [D, S], BF16)
    sgbT = scr("sgbT", [D, S], BF16)
    qabsT = scr("qabsT", [8, 128, S], BF16)
    latT = scr("latT", [128, S], BF16)
    vext = scr("vext", [NT, 128, 520], BF16)
    qiT = scr("qiT", [512, S])
    kiT = scr("kiT", [64, S])
    witm = scr("witm", [128, NT * 8])
    yagT = scr("yagT", [512, S], BF16)
    ybgT = scr("ybgT", [512, S], BF16)

    def sb(name, shape, dt=F32):
        return nc.alloc_sbuf_tensor(name, list(shape), dt).ap()

    ones = sb("ones", [128, 128])
    identf = sb("identf", [128, 128])
    identb = sb("identb", [128, 128], BF16)
    bdiag = sb("bdiag", [128, 128])
    u2m = sb("u2m", [64, 128])
    slm = sb("slm", [64, 64])
    cact = sb("cact", [128, 8])
    cbc = sb("cbc", [128, 8, 128])
    modB = sb("modB", [128, 3 * D])
    Gt = sb("Gt", [128, D])
    cols = sb("cols", [128, NCOL])
    omka = sb("omka", [128, 4])
    row = sb("row", [1, 3 * D])

    reg_zero = nc.gpsimd.to_reg(0.0)
    reg_neg = nc.gpsimd.to_reg(NEG)
    P.op("pool", lambda e: e.memset(ones, 1.0), w=["ones"])
    P.op("pool", lambda e: e.affine_select(out=identf, in_=ones, pattern=[[-1, 128]], compare_op=ALU.is_equal,
                                           fill=reg_zero, base=0, channel_multiplier=1), r=["ones"], w=["identf"])
    P.op("pool", lambda e: e.tensor_copy(out=identb, in_=identf), r=["identf"], w=["identb"])
    P.op("pool", lambda e: e.memset(bdiag, 0.0), w=["bdiag"])
    P.op("pool", lambda e: e.memset(bdiag[0:64, 0:64], 1.0), w=["bdiag"])
    P.op("pool", lambda e: e.memset(bdiag[64:128, 64:128], 1.0), w=["bdiag"])
    P.op("pool", lambda e: e.affine_select(out=u2m[:, 0:64], in_=ones[0:64, 0:64], pattern=[[1, 64]], compare_op=ALU.is_ge,
                                           fill=reg_zero, base=-1, channel_multiplier=-1), r=["ones"], w=["u2m"])
    P.op("pool", lambda e: e.affine_select(out=u2m[:, 64:128], in_=ones[0:64, 0:64], pattern=[[1, 64]], compare_op=ALU.is_ge,
                                           fill=reg_zero, base=0, channel_multiplier=-1), r=["ones"], w=["u2m"])
    P.op("pool", lambda e: e.affine_select(out=slm, in_=ones[0:64, 0:64], pattern=[[-1, 64]], compare_op=ALU.is_ge,
                                           fill=reg_zero, base=-1, channel_multiplier=1), r=["ones"], w=["slm"])
    P.dma("sp", cact, cT, w=["cact"])
    P.op("act", lambda e: e.activation(out=cact, in_=cact, func=AF.Silu), r=["cact"], w=["cact"])
    for j in range(8):
        P.op("dve", lambda e, j=j: e.tensor_scalar(out=cbc[:, j, :], in0=ones, scalar1=cact[:, j:j + 1], scalar2=None,
                                                   op0=ALU.mult), r=["cact", "ones"], w=["cbc"])

    K = Ctx()
    K.__dict__.update(locals())
    K.D = D
    K.COLS = COLS
    for l in range(depth):
        xsrc = x_in if l == 0 else xs[(l - 1) % 2]
        xdst = xs[l % 2]
        phase_mod(K, l)
        if stop_after == "mod" and l == stop_layer:
            break
        phase_proj(K, l, xsrc)
        if stop_after == f"B{l}":
            break
        if stop_after == "proj" and l == stop_layer:
            break
        phase_rwkv(K, l)
        if stop_after == "rwkv" and l == stop_layer:
            break
        phase_dsa(K, l)
        if stop_after == "dsa" and l == stop_layer:
            break
        phase_out(K, l, xsrc, xdst)
        if stop_after == "out" and l == stop_layer:
            break
    else:
        phase_final(K, xs[(depth - 1) % 2])
    P.finish()
    return nc, P


def phase_mod(K, l):
    from contextlib import ExitStack
    nc, P, W = K.nc, K.P, K.L[l]
    with ExitStack() as st:
        wb = [st.enter_context(_sbm(nc, f"mw{i}", [128, 8, 512], F32)).ap() for i in range(2)]
        ps = [st.enter_context(_psm(nc, f"mps{i}", [128, 512], F32)).ap() for i in range(2)]
        P.dma("sp", K.cols, W["cols"], w=["cols"])
        P.dma("sp", K.row, W["ada_b"], w=["row"])
        P.op("dve", lambda e: e.tensor_scalar(out=K.omka, in0=K.cols[:, 25:29], scalar1=-1.0, scalar2=1.0,
                                              op0=ALU.mult, op1=ALU.add), r=["cols"], w=["omka"])
        for nb in range(6):
            b = nb % 2
            P.dma("sp", wb[b], W["ada_w"][:, nb * 512:(nb + 1) * 512].rearrange("(j p) n -> p j n", p=128), w=[f"mw{b}"])
            for j in range(8):
                P.op("pe", lambda e, j=j: e.matmul(ps[b], lhsT=K.cbc[:, j, :], rhs=wb[b][:, j, :], start=(j == 0), stop=False),
                     r=["cbc", f"mw{b}"], w=[f"mps{b}"])
            P.op("pe", lambda e: e.matmul(ps[b], lhsT=K.ones[0:1, :], rhs=K.row[0:1, nb * 512:(nb + 1) * 512], start=False, stop=True),
                 r=["ones", "row"], w=[f"mps{b}"])
            P.op("act", lambda e: e.activation(out=K.modB[:, nb * 512:(nb + 1) * 512], in_=ps[b], func=AF.Copy),
                 r=[f"mps{b}"], w=["modB"])
        P.dma("sp", K.row[0:1, 0:K.D], W["norm_g"], r=["row"], w=["row"])
        for hf in range(2):
            P.op("pe", lambda e: e.matmul(ps[hf], lhsT=K.ones[0:1, :], rhs=K.row[0:1, hf * 512:(hf + 1) * 512], start=True, stop=True),
                 r=["ones", "row"], w=[f"mps{hf}"])
            P.op("dve", lambda e: e.scalar_tensor_tensor(out=K.Gt[:, hf * 512:(hf + 1) * 512],
                                                         in0=K.modB[:, K.D + hf * 512:K.D + (hf + 1) * 512], scalar=1.0,
                                                         in1=ps[hf], op0=ALU.add, op1=ALU.mult),
                 r=["modB", f"mps{hf}"], w=["Gt"])


def proj_blocks():
    blks = []
    for i in range(13):
        blks.append((i * 128, 128, "shift", i))
    for i in range(4):
        blks.append((C_ZA + i * 128, 128, "sza", i))
    for i in range(4):
        blks.append((C_Q + i * 128, 128, "q", i))
    blks.append((C_CKV, 128, "ckv", 0))
    for i in range(4):
        blks.append((C_ZB + i * 128, 128, "szb", i))
    for i in range(8):
        blks.append((C_GA + i * 128, 128, "sga", i))
    for i in range(8):
        blks.append((C_GB + i * 128, 128, "sgb", i))
    return blks


def phase_proj(K, l, xsrc):
    from contextlib import ExitStack
    nc, P, W, S, NT, TC, NTC = K.nc, K.P, K.L[l], K.S, K.NT, K.TC, K.NTC
    with ExitStack() as st:
        def sbt(name, shape, dt=F32):
            return st.enter_context(_sbm(nc, name, list(shape), dt)).ap()
        hT = sbt("hT", [128, 8, S], BF16)
        ps = [st.enter_context(_psm(nc, f"pps{i}", [128, 512], F32)).ap() for i in range(4)]
        with ExitStack() as st2:
            def sb2(name, shape, dt=F32):
                return st2.enter_context(_sbm(nc, name, list(shape), dt)).ap()
            pst = [st2.enter_context(_psm(nc, f"pst{i}", [128, 512], F32)).ap() for i in range(2)]
            pq = [st2.enter_context(_psm(nc, f"ppq{i}", [128, 512], F32)).ap() for i in range(2)]
            xb = [sb2(f"xb{i}", [128, K.D]) for i in range(2)]
            t1 = sb2("t1", [128, K.D])
            sqj = sb2("sqj", [128, K.D])
            ss = sb2("ss", [128, 2])
            hTf = sb2("hTf", [128, 8, 128])
            wq = sb2("wq", [128, 8, 584])
            pqs = [sb2(f"pqs{i}", [128, 5, 128]) for i in range(2)]
            wit = sb2("wit", [128, NT * 8])
            P.dma("sp", wq, W["w_in"][:, C_QI:C_QI + 584].rearrange("(j p) n -> p j n", p=128), w=["wq"])
            for i in range(NT):
                b = i % 2
                tcs = slice(i * 128, (i + 1) * 128)
                P.dma("sp", xb[b], xsrc[tcs, :], w=[f"xb{b}"])
                P.op("act", lambda e: e.activation(out=sqj, in_=xb[b], func=AF.Square, accum_out=ss[:, 0:1]),
                     r=[f"xb{b}"], w=["sqj", "ss"])
                P.op("dve", lambda e: e.tensor_scalar(out=ss[:, 1:2], in0=ss[:, 0:1], scalar1=1.0 / K.D, scalar2=K.EPS,
                                                      op0=ALU.mult, op1=ALU.add), r=["ss"], w=["ss"])
                P.op("act", lambda e: e.activation(out=ss[:, 1:2], in_=ss[:, 1:2], func=AF.Sqrt), r=["ss"], w=["ss"])
                P.op("dve", lambda e: e.reciprocal(out=ss[:, 1:2], in_=ss[:, 1:2]), r=["ss"], w=["ss"])
                P.op("dve", lambda e: e.scalar_tensor_tensor(out=t1, in0=xb[b], scalar=ss[:, 1:2], in1=K.Gt,
                                                             op0=ALU.mult, op1=ALU.mult), r=[f"xb{b}", "ss", "Gt"], w=["t1"])
                P.op("dve", lambda e: e.tensor_tensor(out=t1, in0=t1, in1=K.modB[:, 0:K.D], op=ALU.add),
                     r=["t1", "modB"], w=["t1"])
                for j in range(8):
                    P.op("pe", lambda e: e.matmul(pst[j // 4][:, (j % 4) * 128:(j % 4 + 1) * 128], lhsT=t1[:, j * 128:(j + 1) * 128],
                                                  rhs=K.identf, start=True, stop=True), r=["t1", "identf"], w=[f"pst{j // 4}"])
                for hh in range(2):
                    src = pst[hh].rearrange("p (j t) -> p j t", j=4)
                    P.op("act", lambda e: e.activation(out=hT[:, 4 * hh:4 * hh + 4, tcs], in_=src, func=AF.Copy), r=[f"pst{hh}"], w=["hT"])
                    P.op("dve", lambda e: e.tensor_copy(out=hTf[:, 4 * hh:4 * hh + 4, :], in_=src), r=[f"pst{hh}"], w=["hTf"])
                for qb_ in range(5):
                    n = 128 if qb_ < 4 else 72
                    pb = qb_ % 2
                    for j in range(8):
                        P.op("pe", lambda e: e.matmul(pq[pb][0:n, 0:128], lhsT=wq[:, j, qb_ * 128:qb_ * 128 + n], rhs=hTf[:, j, :],
                                                      start=(j == 0), stop=(j == 7)), r=["wq", "hTf"], w=[f"ppq{pb}"])
                    P.op("act", lambda e: e.activation(out=pqs[b][0:n, qb_, :], in_=pq[pb][0:n, 0:128], func=AF.Copy),
                         r=[f"ppq{pb}"], w=[f"pqs{b}"])
                P.dma("sp", K.qiT.rearrange("(q p) t -> p q t", p=128)[:, :, tcs], pqs[b][:, 0:4, :], r=[f"pqs{b}"], w=["qiT"])
                P.dma("sp", K.kiT[:, tcs], pqs[b][0:64, 4, :], r=[f"pqs{b}"], w=["kiT"])
                P.op("pe", lambda e: e.transpose(out=pq[0][:, 256:264], in_=pqs[b][64:72, 4, :], identity=K.identf[64:72, 64:72]),
                     r=[f"pqs{b}", "identf"], w=["ppq0"])
                P.op("act", lambda e: e.activation(out=wit[:, i * 8:(i + 1) * 8], in_=pq[0][:, 256:264], func=AF.Copy), r=["ppq0"], w=["wit"])
            P.dma("sp", K.witm, wit, r=["wit"], w=["witm"])
            P.barrier()
        if K.stop_after == f"B{l}":
            return
        wf = [sbt(f"wf{i}", [128, 8, 128]) for i in range(2)]
        wbf = [sbt(f"wbf{i}", [128, 8, 128], BF16) for i in range(2)]
        blkf = [sbt(f"blkf{i}", [128, S]) for i in range(2)]
        tmpf = [sbt(f"tmpf{i}", [128, S]) for i in range(2)]
        blkb = [sbt(f"blkb{i}", [128, S], BF16) for i in range(2)]
        wuk = sbt("wuk", [128, 4, 128])
        wukb = sbt("wukb", [128, 4, 128], BF16)
        wuv = sbt("wuv", [128, 512])
        wuvb = sbt("wuvb", [128, 512], BF16)
        qa = [sbt(f"qa{i}", [128, TC], BF16) for i in range(2)]
        vx = [sbt(f"vx{i}", [128, 8, 65], BF16) for i in range(2)]
        P.dma("sp", wuk, W["wukT"], w=["wuk"])
        P.op("pool", lambda e: e.tensor_copy(out=wukb, in_=wuk), r=["wuk"], w=["wukb"])
        P.dma("sp", wuv, W["wuv"], w=["wuv"])
        P.op("pool", lambda e: e.tensor_copy(out=wuvb, in_=wuv), r=["wuv"], w=["wuvb"])
        for i in range(2):
            P.op("pool", lambda e: e.memset(vx[i], 1.0), w=[f"vx{i}"])
        pcount = 0
        qcount = 0
        for bi, (cs, n, kind, idx) in enumerate(proj_blocks()):
            b = bi % 2
            P.dma("sp", wf[b][:, :, 0:n], W["w_in"][:, cs:cs + n].rearrange("(j p) n -> p j n", p=128), w=[f"wf{b}"])
            P.op("pool", lambda e: e.tensor_copy(out=wbf[b][:, :, 0:n], in_=wf[b][:, :, 0:n]), r=[f"wf{b}"], w=[f"wbf{b}"])
            isb = kind in ("sza", "szb", "sga", "sgb", "q")
            dst, dname = (blkb[b], f"blkb{b}") if isb else (blkf[b], f"blkf{b}")
            func = AF.Silu if kind in ("sza", "szb") else (AF.Sigmoid if kind in ("sga", "sgb") else AF.Copy)
            for tc in range(NTC):
                pb = pcount % 2
                pcount += 1
                for j in range(8):
                    P.op("pe", lambda e: e.matmul(ps[pb][0:n, 0:TC], lhsT=wbf[b][:, j, 0:n], rhs=hT[:, j, tc * TC:(tc + 1) * TC],
                                                  start=(j == 0), stop=(j == 7)), r=[f"wbf{b}", "hT"], w=[f"pps{pb}"])
                P.op("act", lambda e: e.activation(out=dst[0:n, tc * TC:(tc + 1) * TC], in_=ps[pb][0:n, 0:TC], func=func),
                     r=[f"pps{pb}"], w=[dname])
            if kind == "shift":
                tm, tn = tmpf[b], f"tmpf{b}"
                P.op("dve", lambda e: e.tensor_tensor(out=tm[:, 1:S], in0=dst[:, 0:S - 1], in1=dst[:, 1:S], op=ALU.subtract),
                     r=[dname], w=[tn])
                P.op("dve", lambda e: e.tensor_scalar(out=tm[:, 0:1], in0=dst[:, 0:1], scalar1=-1.0, scalar2=None, op0=ALU.mult),
                     r=[dname], w=[tn])
                P.op("dve", lambda e: e.scalar_tensor_tensor(out=tm, in0=tm, scalar=K.cols[:, idx:idx + 1], in1=dst,
                                                             op0=ALU.mult, op1=ALU.add), r=[tn, dname, "cols"], w=[tn])
                P.dma("sp", K.psT[cs:cs + 128, :], tm, r=[tn], w=["psT"])
            elif kind in ("sza", "szb", "sga", "sgb"):
                tgt = dict(sza=K.szaT, szb=K.szbT, sga=K.sgaT, sgb=K.sgbT)[kind]
                P.dma("sp", tgt[idx * 128:(idx + 1) * 128, :], dst, r=[dname], w=[kind + "T"])
            elif kind == "q":
                for hp in range(2):
                    for tc in range(NTC):
                        pb = 2 + qcount % 2
                        qb = qcount % 2
                        qcount += 1
                        P.op("pe", lambda e: e.matmul(ps[pb][:, 0:TC], lhsT=wukb[64 * hp:64 * hp + 64, idx, :],
                                                      rhs=dst[64 * hp:64 * hp + 64, tc * TC:(tc + 1) * TC], start=True, stop=True),
                             r=["wukb", dname], w=[f"pps{pb}"])
                        P.op("act", lambda e: e.activation(out=qa[qb], in_=ps[pb][:, 0:TC], func=AF.Copy, scale=0.125),
                             r=[f"pps{pb}"], w=[f"qa{qb}"])
                        P.dma("sp", K.qabsT[2 * idx + hp, :, tc * TC:(tc + 1) * TC], qa[qb], r=[f"qa{qb}"], w=["qabsT"])
            elif kind == "ckv":
                tm, tn = tmpf[b], f"tmpf{b}"
                rs, rn = tmpf[1 - b], f"tmpf{1 - b}"
                P.op("act", lambda e: e.activation(out=tm, in_=dst, func=AF.Square), r=[dname], w=[tn])
                for tc in range(NTC):
                    pb = 2 + tc % 2
                    P.op("pe", lambda e: e.matmul(ps[pb][:, 0:TC], lhsT=K.ones, rhs=tm[:, tc * TC:(tc + 1) * TC], start=True, stop=True),
                         r=["ones", tn], w=[f"pps{pb}"])
                    P.op("act", lambda e: e.activation(out=rs[:, tc * TC:(tc + 1) * TC], in_=ps[pb][:, 0:TC], func=AF.Sqrt,
                                                       scale=1.0 / 128, bias=K.EPS), r=[f"pps{pb}"], w=[rn])
                P.op("dve", lambda e: e.reciprocal(out=rs, in_=rs), r=[rn], w=[rn])
                lb_, ln_ = blkb[b], f"blkb{b}"
                P.op("dve", lambda e: e.scalar_tensor_tensor(out=lb_, in0=dst, scalar=K.cols[:, 41:42], in1=rs,
                                                             op0=ALU.mult, op1=ALU.mult), r=[dname, rn, "cols"], w=[ln_])
                P.dma("sp", K.latT, lb_, r=[ln_], w=["latT"])
                for i in range(NT):
                    pb = 2 + i % 2
                    vb = i % 2
                    P.op("pe", lambda e: e.matmul(ps[pb], lhsT=lb_[:, i * 128:(i + 1) * 128], rhs=wuvb, start=True, stop=True),
                         r=[ln_, "wuvb"], w=[f"pps{pb}"])
                    P.op("act", lambda e: e.activation(out=vx[vb][:, :, 0:64], in_=ps[pb].rearrange("p (h d) -> p h d", h=8),
                                                       func=AF.Copy), r=[f"pps{pb}"], w=[f"vx{vb}"])
                    P.dma("sp", K.vext[i], vx[vb].rearrange("p h d -> p (h d)"), r=[f"vx{vb}"], w=["vext"])
        P.barrier()


def colpack(v):
    return np.ascontiguousarray(np.asarray(v, np.float32).reshape(-1, 128).T)


def shared_maps(inp, depth):
    m = {}
    m["final_g"] = np.asarray(inp["final_g"], np.float32).reshape(1, D)
    rb = np.asarray(inp["rel_bias"], np.float32)
    s_ = np.arange(128)[:, None]
    t_ = np.arange(128)[None, :]
    eb = np.zeros((2, 128, 8, 128), np.float32)
    for kind, off in ((0, 0), (1, 128)):
        dist = t_ - s_ + off
        bk = t5_bucket_np(np.maximum(dist, 0))
        eb[kind] = np.transpose(rb[bk], (0, 2, 1))
    m["ebias"] = eb
    m["eb31"] = np.ascontiguousarray(np.broadcast_to(rb[31][None, :, None], (128, 8, 128))).astype(np.float32)
    for l in range(depth):
        g = lambda k: np.asarray(inp[k][l], np.float32)
        m[f"ada_w{l}"] = g("ada_w")
        m[f"ada_b{l}"] = g("ada_b").reshape(1, -1)
        m[f"norm_g{l}"] = g("norm_g").reshape(1, -1)
        m[f"w_in{l}"] = g("w_in")
        m[f"cols{l}"] = np.ascontiguousarray(np.concatenate([
            colpack(g("shift_mu")), colpack(g("w0")), colpack(g("a0")), colpack(g("k_k")), colpack(g("k_a")),
            colpack(g("r_k").reshape(-1)), colpack(g("lnx_g")), colpack(g("lnx_b")), colpack(g("kv_norm_g"))], axis=1))
        m[f"w2{l}"] = g("w2")
        m[f"a2{l}"] = g("a2")
        wuk = g("w_uk")
        m[f"wukT{l}"] = np.ascontiguousarray(wuk.reshape(128, 4, 2, 64).transpose(2, 3, 1, 0).reshape(128, 4, 128))
        m[f"wuv{l}"] = g("w_uv").reshape(128, 512)
        m[f"w_pa{l}"] = g("w_pa")
        m[f"w_pb{l}"] = g("w_pb")
        m[f"w_o{l}"] = g("w_o")
    return m


def core_map(inp, shared, b):
    m = dict(shared)
    m["x"] = np.ascontiguousarray(np.asarray(inp["x"][b], np.float32))
    m["cT"] = colpack(np.asarray(inp["c"][b], np.float32))
    return m


def phase_rwkv(K, l):
    from contextlib import ExitStack
    nc, P, W, S = K.nc, K.P, K.L[l], K.S
    SEG = min(1024, S)
    NSEG = S // SEG
    NCH = SEG // 64
    CC = min(512, SEG)
    NCC = SEG // CC
    GN_EPS = 64e-5
    with ExitStack() as st:
        def sbt(name, shape, dt=F32):
            return st.enter_context(_sbm(nc, name, list(shape), dt)).ap()
        names = ["rT", "kT", "vT", "lwp", "ai", "kkn", "km", "bb", "Lp", "Lpe", "E1", "E2", "E3", "E4", "bonT", "tmpA", "OG", "cmask"]
        T = {n: sbt("r_" + n, [128, SEG]) for n in names}
        AR = sbt("r_AR", [128, 2, SEG])
        BK = sbt("r_BK", [128, 2, SEG])
        BKh = sbt("r_BKh", [128, 2, SEG])
        wdT = sbt("r_wdT", [64, SEG])
        adT = sbt("r_adT", [64, SEG])
        w2 = sbt("r_w2", [64, 512])
        a2 = sbt("r_a2", [64, 512])
        gz = sbt("r_gz", [128, SEG], BF16)
        ogb = sbt("r_ogb", [128, SEG], BF16)
        u2 = sbt("r_u2", [128, 128])
        sl = sbt("r_sl", [128, 64])
        NSLOT = 6
        tms = sbt("r_tms", [128, NCH, 192])
        MAKAs = sbt("r_MAKAs", [128, NCH, 256])
        TTs = sbt("r_TTs", [128, NCH, 64])
        Yall = sbt("r_Yall", [128, NCH, 64])
        Ysq = sbt("r_Ysq", [128, NCH, 64])
        gst = sbt("r_gst", [128, 4, NCH])
        Ps = [[sbt(f"r_P{i}s{k}", [128, 64]) for i in range(2)] for k in range(NSLOT)]
        Qs = [[sbt(f"r_Q{i}s{k}", [128, 64]) for i in range(2)] for k in range(NSLOT)]
        TTk = [[sbt(f"r_TT{i}s{k}", [128, 64]) for i in range(2)] for k in range(NSLOT)]
        Xs = sbt("r_X", [128, 64])
        Us = sbt("r_U", [128, 64])
        S0 = [sbt(f"r_S{i}", [128, 64]) for i in range(2)]
        psB = [st.enter_context(_psm(nc, f"rpsB{i}", [128, 512], F32)).ap() for i in range(NSLOT)]
        psX = [st.enter_context(_psm(nc, f"rpsX{i}", [128, 512], F32)).ap() for i in range(2)]
        psP = psB

        P.dma("sp", w2, W["w2"], w=["r_w2"])
        P.dma("sp", a2, W["a2"], w=["r_a2"])
        for par in range(2):
            pr = slice(64 * par, 64 * par + 64)
            P.dma("sp", u2[pr, :], K.u2m, r=["u2m"], w=["r_u2"])
            P.dma("sp", sl[pr, :], K.slm, r=["slm"], w=["r_sl"])
        cm = T["cmask"]
        P.op("pool", lambda e: e.memset(cm, 1.0), w=["r_cmask"])
        P.op("pool", lambda e: e.memset(cm.rearrange("p (c t) -> p c t", t=64)[:, :, 0:1], 0.0), w=["r_cmask"])

        def tt(out, in0, in1, op, r, w, eng="dve"):
            P.op(eng, lambda e: e.tensor_tensor(out=out, in0=in0, in1=in1, op=op), r=r, w=w)

        for hp in range(4):
            col = lambda nm: K.cols[:, K.COLS[nm][0] + hp:K.COLS[nm][0] + hp + 1]
            hc = slice(hp * 128, (hp + 1) * 128)
            for par in range(2):
                P.op("pool", lambda e: e.memset(S0[0][64 * par:64 * par + 64, :], 0.0), w=[f"r_S0_{par}"])
            for sg in range(NSEG):
                sc = slice(sg * SEG, (sg + 1) * SEG)
                P.dma("sp", T["rT"], K.psT[C_R + hp * 128:C_R + (hp + 1) * 128, sc], r=["psT"], w=["r_rT"])
                P.dma("sp", T["kT"], K.psT[C_K + hp * 128:C_K + (hp + 1) * 128, sc], r=["psT"], w=["r_kT"])
                P.dma("sp", T["vT"], K.psT[C_V + hp * 128:C_V + (hp + 1) * 128, sc], r=["psT"], w=["r_vT"])
                P.dma("sp", wdT, K.psT[C_WD:C_WD + 64, sc], r=["psT"], w=["r_wdT"])
                P.dma("sp", adT, K.psT[C_AD:C_AD + 64, sc], r=["psT"], w=["r_adT"])
                P.dma("sp", gz, K.szaT[hc, sc], r=["szaT"], w=["r_gz"])
                P.op("act", lambda e: e.activation(out=wdT, in_=wdT, func=AF.Tanh), r=["r_wdT"], w=["r_wdT"])
                for cc in range(NCC):
                    cs_ = slice(cc * CC, (cc + 1) * CC)
                    pb = cc % 2
                    P.op("pe", lambda e: e.matmul(psP[pb][:, 0:CC], lhsT=w2[:, hc], rhs=wdT[:, cs_], start=True, stop=True),
                         r=["r_w2", "r_wdT"], w=[f"rpsB{pb}"])
                    P.op("act", lambda e: e.activation(out=T["lwp"][:, cs_], in_=psP[pb][:, 0:CC], func=AF.Sigmoid, bias=col("w0")),
                         r=[f"rpsB{pb}", "cols"], w=["r_lwp"])
                    P.op("pe", lambda e: e.matmul(psP[pb][:, 0:CC], lhsT=a2[:, hc], rhs=adT[:, cs_], start=True, stop=True),
                         r=["r_a2", "r_adT"], w=[f"rpsB{pb}"])
                    P.op("act", lambda e: e.activation(out=T["ai"][:, cs_], in_=psP[pb][:, 0:CC], func=AF.Sigmoid, bias=col("a0")),
                         r=[f"rpsB{pb}", "cols"], w=["r_ai"])
                P.op("dve", lambda e: e.tensor_scalar(out=T["kkn"], in0=T["kT"], scalar1=col("kk"), scalar2=None, op0=ALU.mult),
                     r=["r_kT", "cols"], w=["r_kkn"])
                tt(T["tmpA"], T["kkn"], T["kkn"], ALU.mult, ["r_kkn"], ["r_tmpA"])
                for cc in range(NCC):
                    cs_ = slice(cc * CC, (cc + 1) * CC)
                    pb = cc % 2
                    P.op("pe", lambda e: e.matmul(psP[pb][:, 0:CC], lhsT=K.bdiag, rhs=T["tmpA"][:, cs_], start=True, stop=True),
                         r=["bdiag", "r_tmpA"], w=[f"rpsB{pb}"])
                    P.op("act", lambda e: e.activation(out=T["E4"][:, cs_], in_=psP[pb][:, 0:CC], func=AF.Sqrt),
                         r=[f"rpsB{pb}"], w=["r_E4"])
                P.op("dve", lambda e: e.tensor_scalar(out=T["E4"], in0=T["E4"], scalar1=1e-12, scalar2=None, op0=ALU.max),
                     r=["r_E4"], w=["r_E4"])
                P.op("dve", lambda e: e.reciprocal(out=T["E4"], in_=T["E4"]), r=["r_E4"], w=["r_E4"])
                tt(T["kkn"], T["kkn"], T["E4"], ALU.mult, ["r_kkn", "r_E4"], ["r_kkn"])
                P.op("dve", lambda e: e.tensor_scalar(out=T["tmpA"], in0=T["ai"], scalar1=col("ka"), scalar2=K.omka[:, hp:hp + 1],
                                                      op0=ALU.mult, op1=ALU.add), r=["r_ai", "cols", "omka"], w=["r_tmpA"])
                tt(T["km"], T["kT"], T["tmpA"], ALU.mult, ["r_kT", "r_tmpA"], ["r_km"])
                tt(T["bb"], T["kkn"], T["ai"], ALU.mult, ["r_kkn", "r_ai"], ["r_bb"])
                P.op("dve", lambda e: e.tensor_tensor_scan(out=T["Lp"], data0=cm, data1=T["lwp"], initial=0.0, op0=ALU.mult, op1=ALU.add),
                     r=["r_cmask", "r_lwp"], w=["r_Lp"])
                tt(T["Lpe"], T["Lp"], T["lwp"], ALU.subtract, ["r_Lp", "r_lwp"], ["r_Lpe"])
                Lp3 = T["Lp"].rearrange("p (c t) -> p c t", t=64)
                tt(T["tmpA"].rearrange("p (c t) -> p c t", t=64), Lp3[:, :, 63:64].broadcast_to([128, NCH, 64]), Lp3, ALU.subtract,
                   ["r_Lp"], ["r_tmpA"])
                P.op("act", lambda e: e.activation(out=T["E1"], in_=T["Lp"], func=AF.Exp, scale=-C0), r=["r_Lp"], w=["r_E1"])
                P.op("act", lambda e: e.activation(out=T["E2"], in_=T["Lp"], func=AF.Exp, scale=C0), r=["r_Lp"], w=["r_E2"])
                P.op("act", lambda e: e.activation(out=T["E3"], in_=T["Lpe"], func=AF.Exp, scale=-C0), r=["r_Lpe"], w=["r_E3"])
                P.op("act", lambda e: e.activation(out=T["E4"], in_=T["tmpA"], func=AF.Exp, scale=-C0), r=["r_tmpA"], w=["r_E4"])
                P.op("dve", lambda e: e.scalar_tensor_tensor(out=AR[:, 0, :], in0=T["kkn"], scalar=-1.0, in1=T["E3"], op0=ALU.mult, op1=ALU.mult),
                     r=["r_kkn", "r_E3"], w=["r_AR"])
                tt(AR[:, 1, :], T["rT"], T["E1"], ALU.mult, ["r_rT", "r_E1"], ["r_AR"])
                tt(BK[:, 0, :], T["bb"], T["E2"], ALU.mult, ["r_bb", "r_E2"], ["r_BK"])
                tt(BK[:, 1, :], T["km"], T["E2"], ALU.mult, ["r_km", "r_E2"], ["r_BK"])
                tt(BKh[:, 0, :], T["bb"], T["E4"], ALU.mult, ["r_bb", "r_E4"], ["r_BKh"])
                tt(BKh[:, 1, :], T["km"], T["E4"], ALU.mult, ["r_km", "r_E4"], ["r_BKh"])
                P.op("dve", lambda e: e.scalar_tensor_tensor(out=T["tmpA"], in0=T["rT"], scalar=col("rk"), in1=T["km"], op0=ALU.mult, op1=ALU.mult),
                     r=["r_rT", "r_km", "cols", "r_tmpA"], w=["r_tmpA"])
                for cc in range(NCC):
                    cs_ = slice(cc * CC, (cc + 1) * CC)
                    pb = cc % 2
                    P.op("pe", lambda e: e.matmul(psP[pb][:, 0:CC], lhsT=K.bdiag, rhs=T["tmpA"][:, cs_], start=True, stop=True),
                         r=["bdiag", "r_tmpA"], w=[f"rpsB{pb}"])
                    tt(T["bonT"][:, cs_], psP[pb][:, 0:CC], T["vT"][:, cs_], ALU.mult, [f"rpsB{pb}", "r_vT"], ["r_bonT"])
                done = [0, 0]

                def chain1(c, par, slot):
                    pr = slice(64 * par, 64 * par + 64)
                    cs_ = slice(c * 64, (c + 1) * 64)
                    B_, nB = psB[slot], f"rpsB{slot}"
                    sfx = f"_{par}_{c}"
                    idm = K.identf[pr, pr]
                    for q_, src, sn in ((0, T["vT"][pr, cs_], "r_vT"), (1, BKh[pr, 0, cs_], "r_BKh"), (2, BKh[pr, 1, cs_], "r_BKh")):
                        P.op("pe", lambda e: e.matmul(B_[pr, 320 + 64 * q_:384 + 64 * q_], lhsT=src, rhs=idm, start=True, stop=True),
                             r=[sn, "identf"], w=[nB])
                    P.op("pe", lambda e: e.matmul(B_[pr, 0:128], lhsT=BK[pr, 0, cs_], rhs=AR[pr, :, cs_], start=True, stop=True),
                         r=["r_BK", "r_AR"], w=[nB])
                    P.op("pe", lambda e: e.matmul(B_[pr, 128:256], lhsT=BK[pr, 1, cs_], rhs=AR[pr, :, cs_], start=True, stop=True),
                         r=["r_BK", "r_AR"], w=[nB])
                    P.op("pe", lambda e: e.matmul(B_[pr, 256:320], lhsT=AR[pr, 0, cs_], rhs=BK[pr, 0, cs_], start=True, stop=True),
                         r=["r_BK", "r_AR"], w=[nB])
                    yield
                    P.op("act", lambda e: e.activation(out=tms[pr, c, :], in_=B_[pr, 320:512], func=AF.Copy), r=[nB], w=["r_tms" + sfx])
                    tt(MAKAs[pr, c, :].rearrange("p (a t) -> p a t", a=2), B_[pr, 0:256].rearrange("p (a t) -> p a t", a=2),
                       u2[pr, :].unsqueeze(1).broadcast_to([64, 2, 128]), ALU.mult, [nB, "r_u2"], ["r_MAKA" + sfx])
                    tt(Qs[slot][0][pr], B_[pr, 256:320], sl[pr, :], ALU.mult, [nB, "r_sl"], [f"r_Q0s{slot}"])
                    tt(TTk[slot][0][pr], MAKAs[pr, c, 0:64], idm, ALU.add, ["r_MAKA" + sfx, "identf"], [f"r_TT0s{slot}"])
                    yield
                    p_prev, np_prev = MAKAs[pr, c, 0:64], "r_MAKA" + sfx
                    q_prev, nq_prev = Qs[slot][0][pr], f"r_Q0s{slot}"
                    for kq in range(1, 6):
                        pp = kq % 2
                        if kq < 5:
                            P.op("pe", lambda e: e.matmul(B_[pr, 0:64], lhsT=q_prev, rhs=p_prev, start=True, stop=True),
                                 r=[nq_prev, np_prev], w=[nB])
                        P.op("pe", lambda e: e.matmul(B_[pr, 64:128], lhsT=p_prev, rhs=q_prev, start=True, stop=True),
                             r=[nq_prev, np_prev], w=[nB])
                        yield
                        if kq < 5:
                            P.op("act", lambda e: e.activation(out=Ps[slot][pp][pr], in_=B_[pr, 0:64], func=AF.Copy),
                                 r=[nB], w=[f"r_P{pp}s{slot}"])
                        P.op("act", lambda e: e.activation(out=Qs[slot][pp][pr], in_=B_[pr, 64:128], func=AF.Copy),
                             r=[nB], w=[f"r_Q{pp}s{slot}"])
                        yield
                        t_old, nt_old = TTk[slot][(kq - 1) % 2][pr], f"r_TT{(kq - 1) % 2}s{slot}"
                        P.op("pe", lambda e: e.matmul(B_[pr, 128:192], lhsT=Qs[slot][pp][pr], rhs=t_old, start=True, stop=True),
                             r=[f"r_Q{pp}s{slot}", nt_old], w=[nB])
                        yield
                        if kq < 5:
                            tt(TTk[slot][kq % 2][pr], B_[pr, 128:192], t_old, ALU.add, [nB, nt_old], [f"r_TT{kq % 2}s{slot}"])
                        else:
                            tt(TTs[pr, c, :], B_[pr, 128:192], t_old, ALU.add, [nB, nt_old], ["r_TTs" + sfx])
                        yield
                        p_prev, np_prev = Ps[slot][pp][pr], f"r_P{pp}s{slot}"
                        q_prev, nq_prev = Qs[slot][pp][pr], f"r_Q{pp}s{slot}"
                    done[par] = max(done[par], c + 1)

                def chain2(par):
                    pr = slice(64 * par, 64 * par + 64)
                    X_, nX = psX[par], f"rpsX{par}"
                    for c in range(NCH):
                        while done[par] < min(NCH, c + 2):
                            yield
                        gi = sg * NCH + c
                        cs_ = slice(c * 64, (c + 1) * 64)
                        sfx = f"_{par}_{c}"
                        s_old, s_new = S0[gi % 2], S0[(gi + 1) % 2]
                        ns_old, ns_new = f"r_S{gi % 2}_{par}", f"r_S{(gi + 1) % 2}_{par}"
                        Vh, Bh, Kh = tms[pr, c, 0:64], tms[pr, c, 64:128], tms[pr, c, 128:192]
                        ArbT, AakT, ArkT = MAKAs[pr, c, 64:128], MAKAs[pr, c, 128:192], MAKAs[pr, c, 192:256]
                        P.op("pe", lambda e: e.matmul(X_[pr, 0:64], lhsT=AR[pr, 0, cs_], rhs=s_old[pr], start=True, stop=False),
                             r=["r_AR", ns_old], w=[nX])
                        P.op("pe", lambda e: e.matmul(X_[pr, 0:64], lhsT=AakT, rhs=Vh, start=False, stop=True),
                             r=["r_MAKA" + sfx, "r_tms" + sfx], w=[nX])
                        yield
                        P.op("act", lambda e: e.activation(out=Xs[pr], in_=X_[pr, 0:64], func=AF.Copy), r=[nX], w=[f"r_X_{par}"])
                        yield
                        P.op("pe", lambda e: e.matmul(X_[pr, 64:128], lhsT=TTs[pr, c, :], rhs=Xs[pr], start=True, stop=True),
                             r=["r_TTs" + sfx, f"r_X_{par}"], w=[nX])
                        yield
                        P.op("act", lambda e: e.activation(out=Us[pr], in_=X_[pr, 64:128], func=AF.Copy), r=[nX], w=[f"r_U_{par}"])
                        yield
                        P.op("pe", lambda e: e.matmul(X_[pr, 128:192], lhsT=AR[pr, 1, cs_], rhs=s_old[pr], start=True, stop=False),
                             r=["r_AR", ns_old], w=[nX])
                        P.op("pe", lambda e: e.matmul(X_[pr, 128:192], lhsT=ArbT, rhs=Us[pr], start=False, stop=False),
                             r=["r_MAKA" + sfx, f"r_U_{par}"], w=[nX])
                        P.op("pe", lambda e: e.matmul(X_[pr, 128:192], lhsT=ArkT, rhs=Vh, start=False, stop=True),
                             r=["r_MAKA" + sfx, "r_tms" + sfx], w=[nX])
                        P.op("pe", lambda e: e.matmul(X_[pr, 192:256], lhsT=Bh, rhs=Us[pr], start=True, stop=False),
                             r=["r_tms" + sfx, f"r_U_{par}"], w=[nX])
                        P.op("pe", lambda e: e.matmul(X_[pr, 192:256], lhsT=Kh, rhs=Vh, start=False, stop=True),
                             r=["r_tms" + sfx], w=[nX])
                        yield
                        P.op("dve", lambda e: e.scalar_tensor_tensor(out=s_new[pr], in0=s_old[pr], scalar=T["E1"][pr, c * 64 + 63:c * 64 + 64],
                                                                     in1=X_[pr, 192:256], op0=ALU.mult, op1=ALU.add),
                             r=[ns_old, "r_E1", nX], w=[ns_new])
                        P.op("dve", lambda e: e.tensor_copy(out=Yall[pr, c, :], in_=X_[pr, 128:192]), r=[nX], w=["r_Yall"])
                        yield

                pending = [(c, par) for c in range(NCH) for par in range(2)]
                free_slots = list(range(NSLOT))
                active = []
                for par in range(2):
                    active.append((chain2(par), None))
                while pending or active:
                    while pending and free_slots:
                        c, par = pending.pop(0)
                        sl_ = free_slots.pop(0)
                        active.append((chain1(c, par, sl_), sl_))
                    for g in list(active):
                        try:
                            next(g[0])
                        except StopIteration:
                            active.remove(g)
                            if g[1] is not None:
                                free_slots.append(g[1])
                P.op("dve", lambda e: e.tensor_reduce(out=gst[:, 0, :], in_=Yall, axis=AX.X, op=ALU.add), r=["r_Yall"], w=["r_gst"])
                P.op("dve", lambda e: e.tensor_scalar(out=gst[:, 1, :], in0=gst[:, 0, :], scalar1=-1.0 / 64, scalar2=None, op0=ALU.mult),
                     r=["r_gst"], w=["r_gst"])
                tt(Yall, Yall, gst[:, 1, :].unsqueeze(2).broadcast_to([128, NCH, 64]), ALU.add, ["r_Yall", "r_gst"], ["r_Yall"])
                tt(Ysq, Yall, Yall, ALU.mult, ["r_Yall"], ["r_Ysq"])
                P.op("dve", lambda e: e.tensor_reduce(out=gst[:, 2, :], in_=Ysq, axis=AX.X, op=ALU.add), r=["r_Ysq"], w=["r_gst"])
                P.op("dve", lambda e: e.tensor_scalar(out=gst[:, 3, :], in0=gst[:, 2, :], scalar1=1.0 / 64, scalar2=GN_EPS, op0=ALU.mult, op1=ALU.add),
                     r=["r_gst"], w=["r_gst"])
                P.op("act", lambda e: e.activation(out=gst[:, 3, :], in_=gst[:, 3, :], func=AF.Sqrt), r=["r_gst"], w=["r_gst"])
                P.op("dve", lambda e: e.reciprocal(out=gst[:, 3, :], in_=gst[:, 3, :]), r=["r_gst"], w=["r_gst"])
                tt(Yall, Yall, gst[:, 3, :].unsqueeze(2).broadcast_to([128, NCH, 64]), ALU.mult, ["r_Yall", "r_gst"], ["r_Yall"])
                for c in range(NCH):
                    for par in range(2):
                        pr = slice(64 * par, 64 * par + 64)
                        slot = (2 * c + par) % NSLOT
                        P.op("pe", lambda e: e.matmul(psB[slot][pr, 0:64], lhsT=Yall[pr, c, :], rhs=K.identf[pr, pr], start=True, stop=True),
                             r=["r_Yall", "identf"], w=[f"rpsB{slot}"])
                        P.op("act", lambda e: e.activation(out=T["OG"][pr, c * 64:(c + 1) * 64], in_=psB[slot][pr, 0:64], func=AF.Identity,
                                                           scale=col("lg")[pr], bias=col("lb")[pr]),
                             r=[f"rpsB{slot}", "cols"], w=["r_OG"])
                tt(T["OG"], T["OG"], T["bonT"], ALU.add, ["r_OG", "r_bonT"], ["r_OG"])
                tt(ogb, T["OG"], gz, ALU.mult, ["r_OG", "r_gz"], ["r_ogb"])
                P.dma("sp", K.yagT[hc, sc], ogb, r=["r_ogb"], w=["yagT"])
        P.barrier()


def phase_dsa(K, l):
    from contextlib import ExitStack
    nc, P, W, S, NT, TOPK = K.nc, K.P, K.L[l], K.S, K.NT, K.TOPK
    NR = TOPK // 8
    with ExitStack() as st:
        def sbt(name, shape, dt=F32):
            return st.enter_context(_sbm(nc, name, list(shape), dt)).ap()
        latS = sbt("d_lat", [128, S], BF16)
        ki2 = sbt("d_ki2", [128, S])
        vxs = sbt("d_vxs", [128, NT, 520], BF16)
        wis = sbt("d_wis", [128, NT * 8])
        EBM = [sbt(f"d_EBM{i}", [128, 8, 128]) for i in range(2)]
        b31 = sbt("d_b31", [128, 8, 128])
        qij = [sbt(f"d_qij{i}", [128, 4, 128]) for i in range(2)]
        qab = [sbt(f"d_qab{i}", [128, 8, 128], BF16) for i in range(2)]
        zbj = [sbt(f"d_zbj{i}", [128, 4, 128], BF16) for i in range(2)]
        acc = sbt("d_acc", [128, S])
        work = sbt("d_work", [128, S])
        maskf = sbt("d_maskf", [128, S], BF16)
        maskT = sbt("d_maskT", [128, NT, 128], BF16)
        Pt = [sbt(f"d_Pt{i}", [128, 8, 128], BF16) for i in range(2)]
        rtmp = [sbt(f"d_rt{i}", [128, 512]) for i in range(2)]
        m8 = sbt("d_m8", [128, 8])
        rec = sbt("d_rec", [128, 8])
        yb = sbt("d_yb", [128, 512])
        ybg = [sbt(f"d_ybg{i}", [128, 4, 128], BF16) for i in range(2)]
        psI = [st.enter_context(_psm(nc, f"dpsI{i}", [128, 512], F32)).ap() for i in range(2)]
        psM = st.enter_context(_psm(nc, "dpsM", [128, 1024], BF16)).ap()
        psQ = [st.enter_context(_psm(nc, f"dpsQ{i}", [128, 512], F32)).ap() for i in range(2)]
        psO = [st.enter_context(_psm(nc, f"dpsO{i}", [128, 512], F32)).ap() for i in range(2)]

        P.dma("sp", latS, K.latT, r=["latT"], w=["d_lat"])
        P.dma("sp", ki2[0:64, :], K.kiT, r=["kiT"], w=["d_ki2"])
        P.dma("sp", ki2[64:128, :], K.kiT, r=["kiT"], w=["d_ki2"])
        for i in range(NT):
            P.dma("sp", vxs[:, i, :], K.vext[i], r=["vext"], w=["d_vxs"])
        P.dma("sp", wis, K.witm, r=["witm"], w=["d_wis"])
        P.dma("sp", b31, K.eb31, w=["d_b31"])
        for kd in range(2):
            P.dma("sp", EBM[kd], K.ebias[kd], w=[f"d_EBM{kd}"])
            P.op("dve", lambda e: e.tensor_tensor(out=EBM[kd], in0=EBM[kd], in1=b31, op=ALU.subtract), r=[f"d_EBM{kd}", "d_b31"], w=[f"d_EBM{kd}"])
            P.op("act", lambda e: e.activation(out=EBM[kd], in_=EBM[kd], func=AF.Exp), r=[f"d_EBM{kd}"], w=[f"d_EBM{kd}"])
        P.op("pool", lambda e: e.affine_select(out=EBM[0], in_=EBM[0], pattern=[[0, 8], [1, 128]], compare_op=ALU.is_ge, fill=K.reg_zero,
                                               base=0, channel_multiplier=-1), r=["d_EBM0"], w=["d_EBM0"])
        icount = 0
        for j in range(NT):
            b = j % 2
            jc = slice(j * 128, (j + 1) * 128)
            Lk = (j + 1) * 128
            P.dma("sp", qij[b], K.qiT.rearrange("(hq p) t -> p hq t", p=128)[:, :, jc], r=["qiT"], w=[f"d_qij{b}"])
            P.dma("sp", qab[b], K.qabsT.rearrange("h c t -> c h t")[:, :, jc], r=["qabsT"], w=[f"d_qab{b}"])
            P.dma("sp", zbj[b], K.szbT.rearrange("(fb p) t -> p fb t", p=128)[:, :, jc], r=["szbT"], w=[f"d_zbj{b}"])
            if Lk > TOPK:
                for k0 in range(0, Lk, 512):
                    k1 = min(Lk, k0 + 512)
                    wd = k1 - k0
                    for h in range(8):
                        par, hq = h % 2, h // 2
                        pr = slice(64 * par, 64 * par + 64)
                        ib = icount % 2
                        icount += 1
                        P.op("pe", lambda e: e.matmul(psI[ib][:, 0:wd], lhsT=qij[b][pr, hq, :], rhs=ki2[pr, k0:k1], start=True, stop=True),
                             r=[f"d_qij{b}", "d_ki2"], w=[f"dpsI{ib}"])
                        P.op("act", lambda e: e.activation(out=rtmp[ib][:, 0:wd], in_=psI[ib][:, 0:wd], func=AF.Relu),
                             r=[f"dpsI{ib}"], w=[f"d_rt{ib}"])
                        wcol = wis[:, j * 8 + h:j * 8 + h + 1]
                        if h == 0:
                            P.op("dve", lambda e: e.tensor_scalar(out=acc[:, k0:k1], in0=rtmp[ib][:, 0:wd], scalar1=wcol, scalar2=None, op0=ALU.mult),
                                 r=[f"d_rt{ib}", "d_wis"], w=["d_acc"])
                        else:
                            P.op("dve", lambda e: e.scalar_tensor_tensor(out=acc[:, k0:k1], in0=rtmp[ib][:, 0:wd], scalar=wcol, in1=acc[:, k0:k1],
                                                                         op0=ALU.mult, op1=ALU.add), r=[f"d_rt{ib}", "d_wis", "d_acc"], w=["d_acc"])
                P.op("pool", lambda e: e.affine_select(out=acc[:, jc], in_=acc[:, jc], pattern=[[-1, 128]], compare_op=ALU.is_ge, fill=K.reg_neg,
                                                       base=0, channel_multiplier=1), r=["d_acc"], w=["d_acc"])
                for rd in range(NR):
                    src, sn = (acc, "d_acc") if rd == 0 else (work, "d_work")
                    P.op("dve", lambda e: e.max(out=m8, in_=src[:, 0:Lk]), r=[sn], w=["d_m8"])
                    if rd < NR - 1:
                        P.op("dve", lambda e: e.match_replace(out=work[:, 0:Lk], in_to_replace=m8, in_values=src[:, 0:Lk], imm_value=NEG),
                             r=[sn, "d_m8"], w=["d_work"])
                P.op("dve", lambda e: e.tensor_scalar(out=maskf[:, 0:Lk], in0=acc[:, 0:Lk], scalar1=m8[:, 7:8], scalar2=None, op0=ALU.is_ge),
                     r=["d_acc", "d_m8"], w=["d_maskf"])
                for i0 in range(0, j + 1, 8):
                    i1 = min(j + 1, i0 + 8)
                    for i in range(i0, i1):
                        P.op("pe", lambda e: e.transpose(out=psM[:, (i - i0) * 128:(i - i0 + 1) * 128], in_=maskf[:, i * 128:(i + 1) * 128],
                                                         identity=K.identb), r=["d_maskf", "identb"], w=["dpsM"])
                    P.op("act", lambda e: e.activation(out=maskT[:, i0:i1, :], in_=psM[:, 0:(i1 - i0) * 128].rearrange("p (i t) -> p i t", t=128),
                                                       func=AF.Copy), r=["dpsM"], w=["d_maskT"])
            else:
                P.op("pool", lambda e: e.memset(maskT[:, 0:j + 1, :], 1.0), w=["d_maskT"])
            def qk(i):
                for hh in range(2):
                    P.op("pe", lambda e: e.matmul(psQ[hh], lhsT=latS[:, i * 128:(i + 1) * 128], rhs=qab[b][:, 4 * hh:4 * hh + 4, :], start=True, stop=True),
                         r=["d_lat", f"d_qab{b}"], w=[f"dpsQ{hh}"])
            qk(0)
            for i in range(j + 1):
                a = i % 2
                for hh in range(2):
                    P.op("act", lambda e: e.activation(out=Pt[a][:, 4 * hh:4 * hh + 4, :], in_=psQ[hh].rearrange("p (h t) -> p h t", h=4), func=AF.Exp),
                         r=[f"dpsQ{hh}"], w=[f"d_Pt{a}"])
                kd = j - i
                if kd <= 1:
                    P.op("dve", lambda e: e.tensor_tensor(out=Pt[a], in0=Pt[a], in1=EBM[kd], op=ALU.mult), r=[f"d_Pt{a}", f"d_EBM{kd}"], w=[f"d_Pt{a}"])
                P.op("dve", lambda e: e.tensor_tensor(out=Pt[a], in0=Pt[a], in1=maskT[:, i:i + 1, :].broadcast_to([128, 8, 128]), op=ALU.mult),
                     r=[f"d_Pt{a}", "d_maskT"], w=[f"d_Pt{a}"])
                if i + 1 <= j:
                    qk(i + 1)
                for h in range(8):
                    ob = h // 4
                    o0 = (h % 4) * 65
                    P.op("pe", lambda e: e.matmul(psO[ob][:, o0:o0 + 65], lhsT=Pt[a][:, h, :], rhs=vxs[:, i, h * 65:(h + 1) * 65],
                                                  start=(i == 0 and h % 4 == 0), stop=(i == j), skip_group_check=True),
                         r=[f"d_Pt{a}", "d_vxs"], w=[f"dpsO{ob}"])
            for ob in range(2):
                o3 = psO[ob][:, 0:260].rearrange("p (h d) -> p h d", d=65)
                P.op("dve", lambda e: e.reciprocal(out=rec[:, 4 * ob:4 * ob + 4].unsqueeze(2), in_=o3[:, :, 64:65]), r=[f"dpsO{ob}"], w=["d_rec"])
                P.op("dve", lambda e: e.tensor_tensor(out=yb[:, 256 * ob:256 * ob + 256].rearrange("p (h d) -> p h d", d=64), in0=o3[:, :, 0:64],
                                                      in1=rec[:, 4 * ob:4 * ob + 4].unsqueeze(2).broadcast_to([128, 4, 64]), op=ALU.mult),
                     r=[f"dpsO{ob}", "d_rec"], w=["d_yb"])
            for fb in range(4):
                P.op("pe", lambda e: e.transpose(out=psI[0][:, fb * 128:(fb + 1) * 128], in_=yb[:, fb * 128:(fb + 1) * 128], identity=K.identf),
                     r=["d_yb", "identf"], w=["dpsI0"])
            P.op("dve", lambda e: e.tensor_tensor(out=ybg[b], in0=psI[0].rearrange("p (f t) -> p f t", f=4), in1=zbj[b], op=ALU.mult),
                 r=["dpsI0", f"d_zbj{b}"], w=[f"d_ybg{b}"])
            P.dma("sp", K.ybgT.rearrange("(fb p) t -> p fb t", p=128)[:, :, jc], ybg[b], r=[f"d_ybg{b}"], w=["ybgT"])
        P.barrier()


def phase_out(K, l, xsrc, xdst):
    from contextlib import ExitStack
    nc, P, W, S, TC, NTC = K.nc, K.P, K.L[l], K.S, K.TC, K.NTC
    with ExitStack() as st:
        def sbt(name, shape, dt=F32):
            return st.enter_context(_sbm(nc, name, list(shape), dt)).ap()
        stg = sbt("o_stg", [128, 4, K.D])
        wpa = sbt("o_wpa", [128, 4, K.D], BF16)
        wpb = sbt("o_wpb", [128, 4, K.D], BF16)
        wo = sbt("o_wo", [128, 8, K.D], BF16)
        ya = [sbt(f"o_ya{i}", [128, 4, TC], BF16) for i in range(2)]
        yb = [sbt(f"o_yb{i}", [128, 4, TC], BF16) for i in range(2)]
        ga = [sbt(f"o_ga{i}", [128, 8, TC], BF16) for i in range(2)]
        gb = [sbt(f"o_gb{i}", [128, 8, TC], BF16) for i in range(2)]
        t1 = [sbt(f"o_t1{i}", [128, TC]) for i in range(2)]
        t2 = [sbt(f"o_t2{i}", [128, TC]) for i in range(2)]
        mg = sbt("o_mg", [128, 8, TC], BF16)
        xt = [sbt(f"o_xt{i}", [128, K.D]) for i in range(2)]
        xo = [sbt(f"o_xo{i}", [128, K.D]) for i in range(2)]
        psA = [st.enter_context(_psm(nc, f"opsA{i}", [128, 512], F32)).ap() for i in range(2)]
        psB = [st.enter_context(_psm(nc, f"opsB{i}", [128, 512], F32)).ap() for i in range(2)]
        psO = [st.enter_context(_psm(nc, f"opsO{i}", [128, 512], F32)).ap() for i in range(2)]
        for src, dst, dn in ((W["w_pa"], wpa, "o_wpa"), (W["w_pb"], wpb, "o_wpb")):
            P.dma("sp", stg, src.rearrange("(j p) n -> p j n", p=128), r=["o_stg"], w=["o_stg"])
            P.op("pool", lambda e: e.tensor_copy(out=dst, in_=stg), r=["o_stg"], w=[dn])
        for hf in range(2):
            P.dma("sp", stg, W["w_o"][hf * 512:(hf + 1) * 512, :].rearrange("(j p) n -> p j n", p=128), r=["o_stg"], w=["o_stg"])
            P.op("pool", lambda e: e.tensor_copy(out=wo[:, 4 * hf:4 * hf + 4, :], in_=stg), r=["o_stg"], w=["o_wo"])
        gate = K.modB[:, 2 * K.D:3 * K.D]
        cnt = 0
        xcnt = 0
        for tc in range(NTC):
            b = tc % 2
            tcs = slice(tc * TC, (tc + 1) * TC)
            P.dma("sp", ya[b], K.yagT.rearrange("(j p) t -> p j t", p=128)[:, :, tcs], r=["yagT"], w=[f"o_ya{b}"])
            P.dma("sp", yb[b], K.ybgT.rearrange("(j p) t -> p j t", p=128)[:, :, tcs], r=["ybgT"], w=[f"o_yb{b}"])
            P.dma("sp", ga[b], K.sgaT.rearrange("(j p) t -> p j t", p=128)[:, :, tcs], r=["sgaT"], w=[f"o_ga{b}"])
            P.dma("sp", gb[b], K.sgbT.rearrange("(j p) t -> p j t", p=128)[:, :, tcs], r=["sgbT"], w=[f"o_gb{b}"])
            for ob in range(8):
                pb = cnt % 2
                cnt += 1
                oc = slice(ob * 128, (ob + 1) * 128)
                for j in range(4):
                    P.op("pe", lambda e: e.matmul(psA[pb][:, 0:TC], lhsT=wpa[:, j, oc], rhs=ya[b][:, j, :], start=(j == 0), stop=(j == 3)),
                         r=["o_wpa", f"o_ya{b}"], w=[f"opsA{pb}"])
                for j in range(4):
                    P.op("pe", lambda e: e.matmul(psB[pb][:, 0:TC], lhsT=wpb[:, j, oc], rhs=yb[b][:, j, :], start=(j == 0), stop=(j == 3)),
                         r=["o_wpb", f"o_yb{b}"], w=[f"opsB{pb}"])
                P.op("dve", lambda e: e.tensor_tensor(out=t1[pb], in0=psA[pb][:, 0:TC], in1=ga[b][:, ob, :], op=ALU.mult),
                     r=[f"opsA{pb}", f"o_ga{b}"], w=[f"o_t1{pb}"])
                P.op("dve", lambda e: e.tensor_tensor(out=t2[pb], in0=psB[pb][:, 0:TC], in1=gb[b][:, ob, :], op=ALU.mult),
                     r=[f"opsB{pb}", f"o_gb{b}"], w=[f"o_t2{pb}"])
                P.op("pool", lambda e: e.tensor_tensor(out=mg[:, ob, :], in0=t1[pb], in1=t2[pb], op=ALU.add),
                     r=[f"o_t1{pb}", f"o_t2{pb}"], w=["o_mg"])
            for ts in range(TC // 128):
                xb_ = xcnt % 2
                xcnt += 1
                rows = slice(tc * TC + ts * 128, tc * TC + (ts + 1) * 128)
                P.dma("sp", xt[xb_], xsrc[rows, :], w=[f"o_xt{xb_}"])
                for hf in range(2):
                    hc = slice(hf * 512, (hf + 1) * 512)
                    for ob in range(8):
                        P.op("pe", lambda e: e.matmul(psO[hf], lhsT=mg[:, ob, ts * 128:(ts + 1) * 128], rhs=wo[:, ob, hc], start=(ob == 0), stop=(ob == 7)),
                             r=["o_mg", "o_wo"], w=[f"opsO{hf}"])
                    P.op("dve", lambda e: e.tensor_tensor(out=xo[xb_][:, hc], in0=psO[hf], in1=gate[:, hc], op=ALU.mult),
                         r=[f"opsO{hf}", "modB"], w=[f"o_xo{xb_}"])
                P.op("pool", lambda e: e.tensor_tensor(out=xo[xb_], in0=xo[xb_], in1=xt[xb_], op=ALU.add),
                     r=[f"o_xo{xb_}", f"o_xt{xb_}"], w=[f"o_xo{xb_}"])
                P.dma("sp", xdst[rows, :], xo[xb_], r=[f"o_xo{xb_}"], w=["xs"])
        P.barrier()


def phase_final(K, xsrc):
    from contextlib import ExitStack
    nc, P, S, NT = K.nc, K.P, K.S, K.NT
    with ExitStack() as st:
        def sbt(name, shape, dt=F32):
            return st.enter_context(_sbm(nc, name, list(shape), dt)).ap()
        xt = [sbt(f"f_xt{i}", [128, K.D]) for i in range(2)]
        xo = [sbt(f"f_xo{i}", [128, K.D]) for i in range(2)]
        sq = sbt("f_sq", [128, K.D])
        fg = sbt("f_fg", [128, K.D])
        ss = sbt("f_ss", [128, 2])
        ps = [st.enter_context(_psm(nc, f"fps{i}", [128, 512], F32)).ap() for i in range(2)]
        P.dma("sp", K.row[0:1, 0:K.D], K.final_g, r=["row"], w=["row"])
        for hf in range(2):
            P.op("pe", lambda e: e.matmul(ps[hf], lhsT=K.ones[0:1, :], rhs=K.row[0:1, hf * 512:(hf + 1) * 512], start=True, stop=True),
                 r=["ones", "row"], w=[f"fps{hf}"])
            P.op("act", lambda e: e.activation(out=fg[:, hf * 512:(hf + 1) * 512], in_=ps[hf], func=AF.Copy), r=[f"fps{hf}"], w=["f_fg"])
        for i in range(NT):
            b = i % 2
            rows = slice(i * 128, (i + 1) * 128)
            P.dma("sp", xt[b], xsrc[rows, :], r=["xs"], w=[f"f_xt{b}"])
            P.op("act", lambda e: e.activation(out=sq, in_=xt[b], func=AF.Square, accum_out=ss[:, 0:1]), r=[f"f_xt{b}"], w=["f_sq", "f_ss"])
            P.op("dve", lambda e: e.tensor_scalar(out=ss[:, 1:2], in0=ss[:, 0:1], scalar1=1.0 / K.D, scalar2=K.EPS, op0=ALU.mult, op1=ALU.add),
                 r=["f_ss"], w=["f_ss"])
            P.op("act", lambda e: e.activation(out=ss[:, 1:2], in_=ss[:, 1:2], func=AF.Sqrt), r=["f_ss"], w=["f_ss"])
            P.op("dve", lambda e: e.reciprocal(out=ss[:, 1:2], in_=ss[:, 1:2]), r=["f_ss"], w=["f_ss"])
            P.op("dve", lambda e: e.scalar_tensor_tensor(out=xo[b], in0=xt[b], scalar=ss[:, 1:2], in1=fg, op0=ALU.mult, op1=ALU.mult),
                 r=[f"f_xt{b}", "f_ss", "f_fg"], w=[f"f_xo{b}"])
            P.dma("sp", K.out[rows, :], xo[b], r=[f"f_xo{b}"], is_output=True)
        P.barrier()


_CACHE = {}


def kernel(**inputs):
    S = int(np.asarray(inputs["x"]).shape[1])
    B = int(np.asarray(inputs["x"]).shape[0])
    depth = int(np.asarray(inputs["w_in"]).shape[0])
    topk = min(256, S // 4)
    key = (S, depth, topk)
    if key not in _CACHE:
        _CACHE[key] = build(S, topk, depth=depth)[0]
    nc = _CACHE[key]
    shared = shared_maps(inputs, depth)
    in_maps = [core_map(inputs, shared, b) for b in range(B)]
    res = run_bass_kernel_spmd(nc, in_maps, core_ids=list(range(B)))
    return np.stack([np.asarray(r["out"], np.float32) for r in res.results], axis=0)
```

```python
import numpy as np
import concourse.bass as bass
import concourse.mybir as mybir
from concourse.bass_utils import run_bass_kernel_spmd

F32 = mybir.dt.float32
BF16 = mybir.dt.bfloat16
AF = mybir.ActivationFunctionType
ALU = mybir.AluOpType
AX = mybir.AxisListType


class Prog:
    NDSEM = 14
    PSUM_PREFIXES = ("mps", "pps", "pst", "ppq", "rps", "dps", "ops", "fps")

    def __init__(self, nc):
        self.nc = nc
        self.eng = dict(pe=nc.tensor, act=nc.scalar, dve=nc.vector, pool=nc.gpsimd, sp=nc.sync)
        self.csem = {k: nc.alloc_semaphore(name=f"c_{k}") for k in self.eng}
        self.cnt = {k: 0 for k in self.eng}
        self.seen = {k: {} for k in self.eng}
        self.lastw = {}
        self.readers = {}
        self.dsem = {q: [nc.alloc_semaphore(name=f"d_{q}{i}") for i in range(self.NDSEM)]
                     for q in ("sp", "pool", "act")}
        self.dcnt = {q: [0] * self.NDSEM for q in self.dsem}
        self.drr = {q: 0 for q in self.dsem}
        self.uid = 0
        self.out_tokens = []
        self.ninstr = 0

    def _wait(self, e, tok):
        sem, val = tok
        if self.seen[e].get(sem.num, 0) >= val:
            return
        self.eng[e].wait_ge(sem, val)
        self.seen[e][sem.num] = val

    def _deps(self, e, r, w):
        for b in r:
            lw = self.lastw.get(b)
            if lw is not None and not (lw[0] == e == "pe"):
                self._wait(e, lw[1])
            if b.startswith(self.PSUM_PREFIXES):
                for re_, tok in self.readers.get(b, {}).items():
                    if re_ != e:
                        self._wait(e, tok)
        for b in w:
            lw = self.lastw.get(b)
            if lw is not None and not (lw[0] == e == "pe"):
                self._wait(e, lw[1])
            for re_, tok in self.readers.get(b, {}).items():
                if re_ != e or e == "pool":
                    self._wait(e, tok)

    def _commit(self, key, tok, r, w):
        for b in r:
            self.readers.setdefault(b, {})[key] = tok
        for b in w:
            self.lastw[b] = (key, tok)
            self.readers[b] = {}

    def op(self, e, fn, r=(), w=()):
        self._deps(e, r, w)
        ins = fn(self.eng[e])
        self.cnt[e] += 1
        ins.then_inc(self.csem[e], 1)
        self._commit(e, (self.csem[e], self.cnt[e]), r, w)
        self.ninstr += 1

    def dma(self, q, out, in_, r=(), w=(), is_output=False, **kw):
        i = self.drr[q]
        self.drr[q] = (i + 1) % self.NDSEM
        sem = self.dsem[q][i]
        if self.dcnt[q][i] > 0:
            self._wait(q, (sem, self.dcnt[q][i]))
        self._deps(q, r, w)
        ins = self.eng[q].dma_start(out=out, in_=in_, **kw)
        self.dcnt[q][i] += 16
        ins.then_inc(sem, 16)
        tok = (sem, self.dcnt[q][i])
        self.uid += 1
        self._commit(f"dma{self.uid}", tok, r, w)
        if is_output:
            self.out_tokens.append(tok)
        self.ninstr += 1

    def barrier(self):
        for e in self.eng:
            for o in self.eng:
                if o != e and self.cnt[o] > 0:
                    self._wait(e, (self.csem[o], self.cnt[o]))
            for q in self.dsem:
                for i, sem in enumerate(self.dsem[q]):
                    if self.dcnt[q][i] > 0:
                        self._wait(e, (sem, self.dcnt[q][i]))

    def finish(self):
        for q in self.dsem:
            for i, sem in enumerate(self.dsem[q]):
                if self.dcnt[q][i] > 0:
                    self._wait("sp", (sem, self.dcnt[q][i]))
        for e in ("pe", "act", "dve", "pool"):
            if self.cnt[e] > 0:
                self._wait("sp", (self.csem[e], self.cnt[e]))


D = 1024
NIN = 5960
C_R, C_K, C_V, C_WD, C_AD, C_ZA, C_Q, C_CKV, C_ZB, C_QI, C_KI, C_WI, C_GA, C_GB = (
    0, 512, 1024, 1536, 1600, 1664, 2176, 2688, 2816, 3328, 3840, 3904, 3912, 4936)
C0 = 0.6065306597126334
NEG = -1.0e30

COLS = dict(mu=(0, 13), w0=(13, 4), a0=(17, 4), kk=(21, 4), ka=(25, 4), rk=(29, 4),
            lg=(33, 4), lb=(37, 4), kvg=(41, 1))
NCOL = 42


def t5_bucket_np(dist):
    import math
    max_exact = 16
    d = np.maximum(dist, 1).astype(np.float32)
    large = max_exact + (np.log(d / np.float32(max_exact)) / np.float32(math.log(128 / max_exact))
                         * np.float32(32 - max_exact)).astype(np.int32)
    large = np.minimum(large, 31)
    return np.where(dist < max_exact, dist, large)


class Ctx:
    pass


_UNIQ = [0]


def _sbm(nc, name, shape, dt):
    _UNIQ[0] += 1
    return nc.sbuf_tensor(f"{name}_u{_UNIQ[0]}", shape, dt)


def _psm(nc, name, shape, dt):
    _UNIQ[0] += 1
    return nc.psum_tensor(f"{name}_u{_UNIQ[0]}", shape, dt)


def build(S, TOPK, depth=2, dbg=(), stop_after=None, stop_layer=0):
    from contextlib import ExitStack
    nc = bass.Bass("TRN2", target_bir_lowering=False)
    P = Prog(nc)
    NT = S // 128
    TC = min(512, S)
    NTC = S // TC
    EPS = 1e-6

    def din(name, shape, dt=F32):
        return nc.dram_tensor(name, list(shape), dt, kind="ExternalInput").ap()

    def scr(name, shape, dt=F32):
        kind = "ExternalOutput" if name in dbg else "Internal"
        return nc.dram_tensor(name, list(shape), dt, kind=kind).ap()

    x_in = din("x", [S, D])
    cT = din("cT", [128, 8])
    final_g = din("final_g", [1, D])
    ebias = din("ebias", [2, 128, 8, 128])
    eb31 = din("eb31", [128, 8, 128])
    L = []
    for l in range(depth):
        L.append(dict(
            ada_w=din(f"ada_w{l}", [D, 3 * D]), ada_b=din(f"ada_b{l}", [1, 3 * D]),
            norm_g=din(f"norm_g{l}", [1, D]), w_in=din(f"w_in{l}", [D, NIN]),
            cols=din(f"cols{l}", [128, NCOL]), w2=din(f"w2{l}", [64, 512]), a2=din(f"a2{l}", [64, 512]),
            wukT=din(f"wukT{l}", [128, 4, 128]), wuv=din(f"wuv{l}", [128, 512]),
            w_pa=din(f"w_pa{l}", [512, D]), w_pb=din(f"w_pb{l}", [512, D]), w_o=din(f"w_o{l}", [D, D])))
    out = nc.dram_tensor("out", [S, D], F32, kind="ExternalOutput").ap()
    xs = [scr(f"xs{i}", [S, D]) for i in range(2)]
    psT = scr("psT", [1664, S])
    szaT = scr("szaT", [512, S], BF16)
    szbT = scr("szbT", [512, S], BF16)
    sgaT = scr("sgaT", [D, S], BF16)
    sgbT = scr("sgbT", [D, S], BF16)
    qabsT = scr("qabsT", [8, 128, S], BF16)
    latT = scr("latT", [128, S], BF16)
    vext = scr("vext", [NT, 128, 520], BF16)
    qiT = scr("qiT", [512, S])
    kiT = scr("kiT", [64, S])
    witm = scr("witm", [128, NT * 8])
    yagT = scr("yagT", [512, S], BF16)
    ybgT = scr("ybgT", [512, S], BF16)

    def sb(name, shape, dt=F32):
        return nc.alloc_sbuf_tensor(name, list(shape), dt).ap()

    ones = sb("ones", [128, 128])
    identf = sb("identf", [128, 128])
    identb = sb("identb", [128, 128], BF16)
    bdiag = sb("bdiag", [128, 128])
    u2m = sb("u2m", [64, 128])
    slm = sb("slm", [64, 64])
    cact = sb("cact", [128, 8])
    cbc = sb("cbc", [128, 8, 128])
    modB = sb("modB", [128, 3 * D])
    Gt = sb("Gt", [128, D])
    cols = sb("cols", [128, NCOL])
    omka = sb("omka", [128, 4])
    row = sb("row", [1, 3 * D])

    reg_zero = nc.gpsimd.to_reg(0.0)
    reg_neg = nc.gpsimd.to_reg(NEG)
    P.op("pool", lambda e: e.memset(ones, 1.0), w=["ones"])
    P.op("pool", lambda e: e.affine_select(out=identf, in_=ones, pattern=[[-1, 128]], compare_op=ALU.is_equal,
                                           fill=reg_zero, base=0, channel_multiplier=1), r=["ones"], w=["identf"])
    P.op("pool", lambda e: e.tensor_copy(out=identb, in_=identf), r=["identf"], w=["identb"])
    P.op("pool", lambda e: e.memset(bdiag, 0.0), w=["bdiag"])
    P.op("pool", lambda e: e.memset(bdiag[0:64, 0:64], 1.0), w=["bdiag"])
    P.op("pool", lambda e: e.memset(bdiag[64:128, 64:128], 1.0), w=["bdiag"])
    P.op("pool", lambda e: e.affine_select(out=u2m[:, 0:64], in_=ones[0:64, 0:64], pattern=[[1, 64]], compare_op=ALU.is_ge,
                                           fill=reg_zero, base=-1, channel_multiplier=-1), r=["ones"], w=["u2m"])
    P.op("pool", lambda e: e.affine_select(out=u2m[:, 64:128], in_=ones[0:64, 0:64], pattern=[[1, 64]], compare_op=ALU.is_ge,
                                           fill=reg_zero, base=0, channel_multiplier=-1), r=["ones"], w=["u2m"])
    P.op("pool", lambda e: e.affine_select(out=slm, in_=ones[0:64, 0:64], pattern=[[-1, 64]], compare_op=ALU.is_ge,
                                           fill=reg_zero, base=-1, channel_multiplier=1), r=["ones"], w=["slm"])
    P.dma("sp", cact, cT, w=["cact"])
    P.op("act", lambda e: e.activation(out=cact, in_=cact, func=AF.Silu), r=["cact"], w=["cact"])
    for j in range(8):
        P.op("dve", lambda e, j=j: e.tensor_scalar(out=cbc[:, j, :], in0=ones, scalar1=cact[:, j:j + 1], scalar2=None,
                                                   op0=ALU.mult), r=["cact", "ones"], w=["cbc"])

    K = Ctx()
    K.__dict__.update(locals())
    K.D = D
    K.COLS = COLS
    for l in range(depth):
        xsrc = x_in if l == 0 else xs[(l - 1) % 2]
        xdst = xs[l % 2]
        phase_mod(K, l)
        if stop_after == "mod" and l == stop_layer:
            break
        phase_proj(K, l, xsrc)
        if stop_after == f"B{l}":
            break
        if stop_after == "proj" and l == stop_layer:
            break
        phase_rwkv(K, l)
        if stop_after == "rwkv" and l == stop_layer:
            break
        phase_dsa(K, l)
        if stop_after == "dsa" and l == stop_layer:
            break
        phase_out(K, l, xsrc, xdst)
        if stop_after == "out" and l == stop_layer:
            break
    else:
        phase_final(K, xs[(depth - 1) % 2])
    P.finish()
    return nc, P


def phase_mod(K, l):
    from contextlib import ExitStack
    nc, P, W = K.nc, K.P, K.L[l]
    with ExitStack() as st:
        wb = [st.enter_context(_sbm(nc, f"mw{i}", [128, 8, 512], F32)).ap() for i in range(2)]
        ps = [st.enter_context(_psm(nc, f"mps{i}", [128, 512], F32)).ap() for i in range(2)]
        P.dma("sp", K.cols, W["cols"], w=["cols"])
        P.dma("sp", K.row, W["ada_b"], w=["row"])
        P.op("dve", lambda e: e.tensor_scalar(out=K.omka, in0=K.cols[:, 25:29], scalar1=-1.0, scalar2=1.0,
                                              op0=ALU.mult, op1=ALU.add), r=["cols"], w=["omka"])
        for nb in range(6):
            b = nb % 2
            P.dma("sp", wb[b], W["ada_w"][:, nb * 512:(nb + 1) * 512].rearrange("(j p) n -> p j n", p=128), w=[f"mw{b}"])
            for j in range(8):
                P.op("pe", lambda e, j=j: e.matmul(ps[b], lhsT=K.cbc[:, j, :], rhs=wb[b][:, j, :], start=(j == 0), stop=False),
                     r=["cbc", f"mw{b}"], w=[f"mps{b}"])
            P.op("pe", lambda e: e.matmul(ps[b], lhsT=K.ones[0:1, :], rhs=K.row[0:1, nb * 512:(nb + 1) * 512], start=False, stop=True),
                 r=["ones", "row"], w=[f"mps{b}"])
            P.op("act", lambda e: e.activation(out=K.modB[:, nb * 512:(nb + 1) * 512], in_=ps[b], func=AF.Copy),
                 r=[f"mps{b}"], w=["modB"])
        P.dma("sp", K.row[0:1, 0:K.D], W["norm_g"], r=["row"], w=["row"])
        for hf in range(2):
            P.op("pe", lambda e: e.matmul(ps[hf], lhsT=K.ones[0:1, :], rhs=K.row[0:1, hf * 512:(hf + 1) * 512], start=True, stop=True),
                 r=["ones", "row"], w=[f"mps{hf}"])
            P.op("dve", lambda e: e.scalar_tensor_tensor(out=K.Gt[:, hf * 512:(hf + 1) * 512],
                                                         in0=K.modB[:, K.D + hf * 512:K.D + (hf + 1) * 512], scalar=1.0,
                                                         in1=ps[hf], op0=ALU.add, op1=ALU.mult),
                 r=["modB", f"mps{hf}"], w=["Gt"])


def proj_blocks():
    blks = []
    for i in range(13):
        blks.append((i * 128, 128, "shift", i))
    for i in range(4):
        blks.append((C_ZA + i * 128, 128, "sza", i))
    for i in range(4):
        blks.append((C_Q + i * 128, 128, "q", i))
    blks.append((C_CKV, 128, "ckv", 0))
    for i in range(4):
        blks.append((C_ZB + i * 128, 128, "szb", i))
    for i in range(8):
        blks.append((C_GA + i * 128, 128, "sga", i))
    for i in range(8):
        blks.append((C_GB + i * 128, 128, "sgb", i))
    return blks


def phase_proj(K, l, xsrc):
    from contextlib import ExitStack
    nc, P, W, S, NT, TC, NTC = K.nc, K.P, K.L[l], K.S, K.NT, K.TC, K.NTC
    with ExitStack() as st:
        def sbt(name, shape, dt=F32):
            return st.enter_context(_sbm(nc, name, list(shape), dt)).ap()
        hT = sbt("hT", [128, 8, S], BF16)
        ps = [st.enter_context(_psm(nc, f"pps{i}", [128, 512], F32)).ap() for i in range(4)]
        with ExitStack() as st2:
            def sb2(name, shape, dt=F32):
                return st2.enter_context(_sbm(nc, name, list(shape), dt)).ap()
            pst = [st2.enter_context(_psm(nc, f"pst{i}", [128, 512], F32)).ap() for i in range(2)]
            pq = [st2.enter_context(_psm(nc, f"ppq{i}", [128, 512], F32)).ap() for i in range(2)]
            xb = [sb2(f"xb{i}", [128, K.D]) for i in range(2)]
            t1 = sb2("t1", [128, K.D])
            sqj = sb2("sqj", [128, K.D])
            ss = sb2("ss", [128, 2])
            hTf = sb2("hTf", [128, 8, 128])
            wq = sb2("wq", [128, 8, 584])
            pqs = [sb2(f"pqs{i}", [128, 5, 128]) for i in range(2)]
            wit = sb2("wit", [128, NT * 8])
            P.dma("sp", wq, W["w_in"][:, C_QI:C_QI + 584].rearrange("(j p) n -> p j n", p=128), w=["wq"])
            for i in range(NT):
                b = i % 2
                tcs = slice(i * 128, (i + 1) * 128)
                P.dma("sp", xb[b], xsrc[tcs, :], w=[f"xb{b}"])
                P.op("act", lambda e: e.activation(out=sqj, in_=xb[b], func=AF.Square, accum_out=ss[:, 0:1]),
                     r=[f"xb{b}"], w=["sqj", "ss"])
                P.op("dve", lambda e: e.tensor_scalar(out=ss[:, 1:2], in0=ss[:, 0:1], scalar1=1.0 / K.D, scalar2=K.EPS,
                                                      op0=ALU.mult, op1=ALU.add), r=["ss"], w=["ss"])
                P.op("act", lambda e: e.activation(out=ss[:, 1:2], in_=ss[:, 1:2], func=AF.Sqrt), r=["ss"], w=["ss"])
                P.op("dve", lambda e: e.reciprocal(out=ss[:, 1:2], in_=ss[:, 1:2]), r=["ss"], w=["ss"])
                P.op("dve", lambda e: e.scalar_tensor_tensor(out=t1, in0=xb[b], scalar=ss[:, 1:2], in1=K.Gt,
                                                             op0=ALU.mult, op1=ALU.mult), r=[f"xb{b}", "ss", "Gt"], w=["t1"])
                P.op("dve", lambda e: e.tensor_tensor(out=t1, in0=t1, in1=K.modB[:, 0:K.D], op=ALU.add),
                     r=["t1", "modB"], w=["t1"])
                for j in range(8):
                    P.op("pe", lambda e: e.matmul(pst[j // 4][:, (j % 4) * 128:(j % 4 + 1) * 128], lhsT=t1[:, j * 128:(j + 1) * 128],
                                                  rhs=K.identf, start=True, stop=True), r=["t1", "identf"], w=[f"pst{j // 4}"])
                for hh in range(2):
                    src = pst[hh].rearrange("p (j t) -> p j t", j=4)
                    P.op("act", lambda e: e.activation(out=hT[:, 4 * hh:4 * hh + 4, tcs], in_=src, func=AF.Copy), r=[f"pst{hh}"], w=["hT"])
                    P.op("dve", lambda e: e.tensor_copy(out=hTf[:, 4 * hh:4 * hh + 4, :], in_=src), r=[f"pst{hh}"], w=["hTf"])
                for qb_ in range(5):
                    n = 128 if qb_ < 4 else 72
                    pb = qb_ % 2
                    for j in range(8):
                        P.op("pe", lambda e: e.matmul(pq[pb][0:n, 0:128], lhsT=wq[:, j, qb_ * 128:qb_ * 128 + n], rhs=hTf[:, j, :],
                                                      start=(j == 0), stop=(j == 7)), r=["wq", "hTf"], w=[f"ppq{pb}"])
                    P.op("act", lambda e: e.activation(out=pqs[b][0:n, qb_, :], in_=pq[pb][0:n, 0:128], func=AF.Copy),
                         r=[f"ppq{pb}"], w=[f"pqs{b}"])
                P.dma("sp", K.qiT.rearrange("(q p) t -> p q t", p=128)[:, :, tcs], pqs[b][:, 0:4, :], r=[f"pqs{b}"], w=["qiT"])
                P.dma("sp", K.kiT[:, tcs], pqs[b][0:64, 4, :], r=[f"pqs{b}"], w=["kiT"])
                P.op("pe", lambda e: e.transpose(out=pq[0][:, 256:264], in_=pqs[b][64:72, 4, :], identity=K.identf[64:72, 64:72]),
                     r=[f"pqs{b}", "identf"], w=["ppq0"])
                P.op("act", lambda e: e.activation(out=wit[:, i * 8:(i + 1) * 8], in_=pq[0][:, 256:264], func=AF.Copy), r=["ppq0"], w=["wit"])
            P.dma("sp", K.witm, wit, r=["wit"], w=["witm"])
            P.barrier()
        if K.stop_after == f"B{l}":
            return
        wf = [sbt(f"wf{i}", [128, 8, 128]) for i in range(2)]
        wbf = [sbt(f"wbf{i}", [128, 8, 128], BF16) for i in range(2)]
        blkf = [sbt(f"blkf{i}", [128, S]) for i in range(2)]
        tmpf = [sbt(f"tmpf{i}", [128, S]) for i in range(2)]
        blkb = [sbt(f"blkb{i}", [128, S], BF16) for i in range(2)]
        wuk = sbt("wuk", [128, 4, 128])
        wukb = sbt("wukb", [128, 4, 128], BF16)
        wuv = sbt("wuv", [128, 512])
        wuvb = sbt("wuvb", [128, 512], BF16)
        qa = [sbt(f"qa{i}", [128, TC], BF16) for i in range(2)]
        vx = [sbt(f"vx{i}", [128, 8, 65], BF16) for i in range(2)]
        P.dma("sp", wuk, W["wukT"], w=["wuk"])
        P.op("pool", lambda e: e.tensor_copy(out=wukb, in_=wuk), r=["wuk"], w=["wukb"])
        P.dma("sp", wuv, W["wuv"], w=["wuv"])
        P.op("pool", lambda e: e.tensor_copy(out=wuvb, in_=wuv), r=["wuv"], w=["wuvb"])
        for i in range(2):
            P.op("pool", lambda e: e.memset(vx[i], 1.0), w=[f"vx{i}"])
        pcount = 0
        qcount = 0
        for bi, (cs, n, kind, idx) in enumerate(proj_blocks()):
            b = bi % 2
            P.dma("sp", wf[b][:, :, 0:n], W["w_in"][:, cs:cs + n].rearrange("(j p) n -> p j n", p=128), w=[f"wf{b}"])
            P.op("pool", lambda e: e.tensor_copy(out=wbf[b][:, :, 0:n], in_=wf[b][:, :, 0:n]), r=[f"wf{b}"], w=[f"wbf{b}"])
            isb = kind in ("sza", "szb", "sga", "sgb", "q")
            dst, dname = (blkb[b], f"blkb{b}") if isb else (blkf[b], f"blkf{b}")
            func = AF.Silu if kind in ("sza", "szb") else (AF.Sigmoid if kind in ("sga", "sgb") else AF.Copy)
            for tc in range(NTC):
                pb = pcount % 2
                pcount += 1
                for j in range(8):
                    P.op("pe", lambda e: e.matmul(ps[pb][0:n, 0:TC], lhsT=wbf[b][:, j, 0:n], rhs=hT[:, j, tc * TC:(tc + 1) * TC],
                                                  start=(j == 0), stop=(j == 7)), r=[f"wbf{b}", "hT"], w=[f"pps{pb}"])
                P.op("act", lambda e: e.activation(out=dst[0:n, tc * TC:(tc + 1) * TC], in_=ps[pb][0:n, 0:TC], func=func),
                     r=[f"pps{pb}"], w=[dname])
            if kind == "shift":
                tm, tn = tmpf[b], f"tmpf{b}"
                P.op("dve", lambda e: e.tensor_tensor(out=tm[:, 1:S], in0=dst[:, 0:S - 1], in1=dst[:, 1:S], op=ALU.subtract),
                     r=[dname], w=[tn])
                P.op("dve", lambda e: e.tensor_scalar(out=tm[:, 0:1], in0=dst[:, 0:1], scalar1=-1.0, scalar2=None, op0=ALU.mult),
                     r=[dname], w=[tn])
                P.op("dve", lambda e: e.scalar_tensor_tensor(out=tm, in0=tm, scalar=K.cols[:, idx:idx + 1], in1=dst,
                                                             op0=ALU.mult, op1=ALU.add), r=[tn, dname, "cols"], w=[tn])
                P.dma("sp", K.psT[cs:cs + 128, :], tm, r=[tn], w=["psT"])
            elif kind in ("sza", "szb", "sga", "sgb"):
                tgt = dict(sza=K.szaT, szb=K.szbT, sga=K.sgaT, sgb=K.sgbT)[kind]
                P.dma("sp", tgt[idx * 128:(idx + 1) * 128, :], dst, r=[dname], w=[kind + "T"])
            elif kind == "q":
                for hp in range(2):
                    for tc in range(NTC):
                        pb = 2 + qcount % 2
                        qb = qcount % 2
                        qcount += 1
                        P.op("pe", lambda e: e.matmul(ps[pb][:, 0:TC], lhsT=wukb[64 * hp:64 * hp + 64, idx, :],
                                                      rhs=dst[64 * hp:64 * hp + 64, tc * TC:(tc + 1) * TC], start=True, stop=True),
                             r=["wukb", dname], w=[f"pps{pb}"])
                        P.op("act", lambda e: e.activation(out=qa[qb], in_=ps[pb][:, 0:TC], func=AF.Copy, scale=0.125),
                             r=[f"pps{pb}"], w=[f"qa{qb}"])
                        P.dma("sp", K.qabsT[2 * idx + hp, :, tc * TC:(tc + 1) * TC], qa[qb], r=[f"qa{qb}"], w=["qabsT"])
            elif kind == "ckv":
                tm, tn = tmpf[b], f"tmpf{b}"
                rs, rn = tmpf[1 - b], f"tmpf{1 - b}"
                P.op("act", lambda e: e.activation(out=tm, in_=dst, func=AF.Square), r=[dname], w=[tn])
                for tc in range(NTC):
                    pb = 2 + tc % 2
                    P.op("pe", lambda e: e.matmul(ps[pb][:, 0:TC], lhsT=K.ones, rhs=tm[:, tc * TC:(tc + 1) * TC], start=True, stop=True),
                         r=["ones", tn], w=[f"pps{pb}"])
                    P.op("act", lambda e: e.activation(out=rs[:, tc * TC:(tc + 1) * TC], in_=ps[pb][:, 0:TC], func=AF.Sqrt,
                                                       scale=1.0 / 128, bias=K.EPS), r=[f"pps{pb}"], w=[rn])
                P.op("dve", lambda e: e.reciprocal(out=rs, in_=rs), r=[rn], w=[rn])
                lb_, ln_ = blkb[b], f"blkb{b}"
                P.op("dve", lambda e: e.scalar_tensor_tensor(out=lb_, in0=dst, scalar=K.cols[:, 41:42], in1=rs,
                                                             op0=ALU.mult, op1=ALU.mult), r=[dname, rn, "cols"], w=[ln_])
                P.dma("sp", K.latT, lb_, r=[ln_], w=["latT"])
                for i in range(NT):
                    pb = 2 + i % 2
                    vb = i % 2
                    P.op("pe", lambda e: e.matmul(ps[pb], lhsT=lb_[:, i * 128:(i + 1) * 128], rhs=wuvb, start=True, stop=True),
                         r=[ln_, "wuvb"], w=[f"pps{pb}"])
                    P.op("act", lambda e: e.activation(out=vx[vb][:, :, 0:64], in_=ps[pb].rearrange("p (h d) -> p h d", h=8),
                                                       func=AF.Copy), r=[f"pps{pb}"], w=[f"vx{vb}"])
                    P.dma("sp", K.vext[i], vx[vb].rearrange("p h d -> p (h d)"), r=[f"vx{vb}"], w=["vext"])
        P.barrier()


def colpack(v):
    return np.ascontiguousarray(np.asarray(v, np.float32).reshape(-1, 128).T)


def shared_maps(inp, depth):
    m = {}
    m["final_g"] = np.asarray(inp["final_g"], np.float32).reshape(1, D)
    rb = np.asarray(inp["rel_bias"], np.float32)
    s_ = np.arange(128)[:, None]
    t_ = np.arange(128)[None, :]
    eb = np.zeros((2, 128, 8, 128), np.float32)
    for kind, off in ((0, 0), (1, 128)):
        dist = t_ - s_ + off
        bk = t5_bucket_np(np.maximum(dist, 0))
        eb[kind] = np.transpose(rb[bk], (0, 2, 1))
    m["ebias"] = eb
    m["eb31"] = np.ascontiguousarray(np.broadcast_to(rb[31][None, :, None], (128, 8, 128))).astype(np.float32)
    for l in range(depth):
        g = lambda k: np.asarray(inp[k][l], np.float32)
        m[f"ada_w{l}"] = g("ada_w")
        m[f"ada_b{l}"] = g("ada_b").reshape(1, -1)
        m[f"norm_g{l}"] = g("norm_g").reshape(1, -1)
        m[f"w_in{l}"] = g("w_in")
        m[f"cols{l}"] = np.ascontiguousarray(np.concatenate([
            colpack(g("shift_mu")), colpack(g("w0")), colpack(g("a0")), colpack(g("k_k")), colpack(g("k_a")),
            colpack(g("r_k").reshape(-1)), colpack(g("lnx_g")), colpack(g("lnx_b")), colpack(g("kv_norm_g"))], axis=1))
        m[f"w2{l}"] = g("w2")
        m[f"a2{l}"] = g("a2")
        wuk = g("w_uk")
        m[f"wukT{l}"] = np.ascontiguousarray(wuk.reshape(128, 4, 2, 64).transpose(2, 3, 1, 0).reshape(128, 4, 128))
        m[f"wuv{l}"] = g("w_uv").reshape(128, 512)
        m[f"w_pa{l}"] = g("w_pa")
        m[f"w_pb{l}"] = g("w_pb")
        m[f"w_o{l}"] = g("w_o")
    return m


def core_map(inp, shared, b):
    m = dict(shared)
    m["x"] = np.ascontiguousarray(np.asarray(inp["x"][b], np.float32))
    m["cT"] = colpack(np.asarray(inp["c"][b], np.float32))
    return m


def phase_rwkv(K, l):
    from contextlib import ExitStack
    nc, P, W, S = K.nc, K.P, K.L[l], K.S
    SEG = min(1024, S)
    NSEG = S // SEG
    NCH = SEG // 64
    CC = min(512, SEG)
    NCC = SEG // CC
    GN_EPS = 64e-5
    with ExitStack() as st:
        def sbt(name, shape, dt=F32):
            return st.enter_context(_sbm(nc, name, list(shape), dt)).ap()
        names = ["rT", "kT", "vT", "lwp", "ai", "kkn", "km", "bb", "Lp", "Lpe", "E1", "E2", "E3", "E4", "bonT", "tmpA", "OG", "cmask"]
        T = {n: sbt("r_" + n, [128, SEG]) for n in names}
        AR = sbt("r_AR", [128, 2, SEG])
        BK = sbt("r_BK", [128, 2, SEG])
        BKh = sbt("r_BKh", [128, 2, SEG])
        wdT = sbt("r_wdT", [64, SEG])
        adT = sbt("r_adT", [64, SEG])
        w2 = sbt("r_w2", [64, 512])
        a2 = sbt("r_a2", [64, 512])
        gz = sbt("r_gz", [128, SEG], BF16)
        ogb = sbt("r_ogb", [128, SEG], BF16)
        u2 = sbt("r_u2", [128, 128])
        sl = sbt("r_sl", [128, 64])
        NSLOT = 6
        tms = sbt("r_tms", [128, NCH, 192])
        MAKAs = sbt("r_MAKAs", [128, NCH, 256])
        TTs = sbt("r_TTs", [128, NCH, 64])
        Yall = sbt("r_Yall", [128, NCH, 64])
        Ysq = sbt("r_Ysq", [128, NCH, 64])
        gst = sbt("r_gst", [128, 4, NCH])
        Ps = [[sbt(f"r_P{i}s{k}", [128, 64]) for i in range(2)] for k in range(NSLOT)]
        Qs = [[sbt(f"r_Q{i}s{k}", [128, 64]) for i in range(2)] for k in range(NSLOT)]
        TTk = [[sbt(f"r_TT{i}s{k}", [128, 64]) for i in range(2)] for k in range(NSLOT)]
        Xs = sbt("r_X", [128, 64])
        Us = sbt("r_U", [128, 64])
        S0 = [sbt(f"r_S{i}", [128, 64]) for i in range(2)]
        psB = [st.enter_context(_psm(nc, f"rpsB{i}", [128, 512], F32)).ap() for i in range(NSLOT)]
        psX = [st.enter_context(_psm(nc, f"rpsX{i}", [128, 512], F32)).ap() for i in range(2)]
        psP = psB

        P.dma("sp", w2, W["w2"], w=["r_w2"])
        P.dma("sp", a2, W["a2"], w=["r_a2"])
        for par in range(2):
            pr = slice(64 * par, 64 * par + 64)
            P.dma("sp", u2[pr, :], K.u2m, r=["u2m"], w=["r_u2"])
            P.dma("sp", sl[pr, :], K.slm, r=["slm"], w=["r_sl"])
        cm = T["cmask"]
        P.op("pool", lambda e: e.memset(cm, 1.0), w=["r_cmask"])
        P.op("pool", lambda e: e.memset(cm.rearrange("p (c t) -> p c t", t=64)[:, :, 0:1], 0.0), w=["r_cmask"])

        def tt(out, in0, in1, op, r, w, eng="dve"):
            P.op(eng, lambda e: e.tensor_tensor(out=out, in0=in0, in1=in1, op=op), r=r, w=w)

        for hp in range(4):
            col = lambda nm: K.cols[:, K.COLS[nm][0] + hp:K.COLS[nm][0] + hp + 1]
            hc = slice(hp * 128, (hp + 1) * 128)
            for par in range(2):
                P.op("pool", lambda e: e.memset(S0[0][64 * par:64 * par + 64, :], 0.0), w=[f"r_S0_{par}"])
            for sg in range(NSEG):
                sc = slice(sg * SEG, (sg + 1) * SEG)
                P.dma("sp", T["rT"], K.psT[C_R + hp * 128:C_R + (hp + 1) * 128, sc], r=["psT"], w=["r_rT"])
                P.dma("sp", T["kT"], K.psT[C_K + hp * 128:C_K + (hp + 1) * 128, sc], r=["psT"], w=["r_kT"])
                P.dma("sp", T["vT"], K.psT[C_V + hp * 128:C_V + (hp + 1) * 128, sc], r=["psT"], w=["r_vT"])
                P.dma("sp", wdT, K.psT[C_WD:C_WD + 64, sc], r=["psT"], w=["r_wdT"])
                P.dma("sp", adT, K.psT[C_AD:C_AD + 64, sc], r=["psT"], w=["r_adT"])
                P.dma("sp", gz, K.szaT[hc, sc], r=["szaT"], w=["r_gz"])
                P.op("act", lambda e: e.activation(out=wdT, in_=wdT, func=AF.Tanh), r=["r_wdT"], w=["r_wdT"])
                for cc in range(NCC):
                    cs_ = slice(cc * CC, (cc + 1) * CC)
                    pb = cc % 2
                    P.op("pe", lambda e: e.matmul(psP[pb][:, 0:CC], lhsT=w2[:, hc], rhs=wdT[:, cs_], start=True, stop=True),
                         r=["r_w2", "r_wdT"], w=[f"rpsB{pb}"])
                    P.op("act", lambda e: e.activation(out=T["lwp"][:, cs_], in_=psP[pb][:, 0:CC], func=AF.Sigmoid, bias=col("w0")),
                         r=[f"rpsB{pb}", "cols"], w=["r_lwp"])
                    P.op("pe", lambda e: e.matmul(psP[pb][:, 0:CC], lhsT=a2[:, hc], rhs=adT[:, cs_], start=True, stop=True),
                         r=["r_a2", "r_adT"], w=[f"rpsB{pb}"])
                    P.op("act", lambda e: e.activation(out=T["ai"][:, cs_], in_=psP[pb][:, 0:CC], func=AF.Sigmoid, bias=col("a0")),
                         r=[f"rpsB{pb}", "cols"], w=["r_ai"])
                P.op("dve", lambda e: e.tensor_scalar(out=T["kkn"], in0=T["kT"], scalar1=col("kk"), scalar2=None, op0=ALU.mult),
                     r=["r_kT", "cols"], w=["r_kkn"])
                tt(T["tmpA"], T["kkn"], T["kkn"], ALU.mult, ["r_kkn"], ["r_tmpA"])
                for cc in range(NCC):
                    cs_ = slice(cc * CC, (cc + 1) * CC)
                    pb = cc % 2
                    P.op("pe", lambda e: e.matmul(psP[pb][:, 0:CC], lhsT=K.bdiag, rhs=T["tmpA"][:, cs_], start=True, stop=True),
                         r=["bdiag", "r_tmpA"], w=[f"rpsB{pb}"])
                    P.op("act", lambda e: e.activation(out=T["E4"][:, cs_], in_=psP[pb][:, 0:CC], func=AF.Sqrt),
                         r=[f"rpsB{pb}"], w=["r_E4"])
                P.op("dve", lambda e: e.tensor_scalar(out=T["E4"], in0=T["E4"], scalar1=1e-12, scalar2=None, op0=ALU.max),
                     r=["r_E4"], w=["r_E4"])
                P.op("dve", lambda e: e.reciprocal(out=T["E4"], in_=T["E4"]), r=["r_E4"], w=["r_E4"])
                tt(T["kkn"], T["kkn"], T["E4"], ALU.mult, ["r_kkn", "r_E4"], ["r_kkn"])
                P.op("dve", lambda e: e.tensor_scalar(out=T["tmpA"], in0=T["ai"], scalar1=col("ka"), scalar2=K.omka[:, hp:hp + 1],
                                                      op0=ALU.mult, op1=ALU.add), r=["r_ai", "cols", "omka"], w=["r_tmpA"])
                tt(T["km"], T["kT"], T["tmpA"], ALU.mult, ["r_kT", "r_tmpA"], ["r_km"])
                tt(T["bb"], T["kkn"], T["ai"], ALU.mult, ["r_kkn", "r_ai"], ["r_bb"])
                P.op("dve", lambda e: e.tensor_tensor_scan(out=T["Lp"], data0=cm, data1=T["lwp"], initial=0.0, op0=ALU.mult, op1=ALU.add),
                     r=["r_cmask", "r_lwp"], w=["r_Lp"])
                tt(T["Lpe"], T["Lp"], T["lwp"], ALU.subtract, ["r_Lp", "r_lwp"], ["r_Lpe"])
                Lp3 = T["Lp"].rearrange("p (c t) -> p c t", t=64)
                tt(T["tmpA"].rearrange("p (c t) -> p c t", t=64), Lp3[:, :, 63:64].broadcast_to([128, NCH, 64]), Lp3, ALU.subtract,
                   ["r_Lp"], ["r_tmpA"])
                P.op("act", lambda e: e.activation(out=T["E1"], in_=T["Lp"], func=AF.Exp, scale=-C0), r=["r_Lp"], w=["r_E1"])
                P.op("act", lambda e: e.activation(out=T["E2"], in_=T["Lp"], func=AF.Exp, scale=C0), r=["r_Lp"], w=["r_E2"])
                P.op("act", lambda e: e.activation(out=T["E3"], in_=T["Lpe"], func=AF.Exp, scale=-C0), r=["r_Lpe"], w=["r_E3"])
                P.op("act", lambda e: e.activation(out=T["E4"], in_=T["tmpA"], func=AF.Exp, scale=-C0), r=["r_tmpA"], w=["r_E4"])
                P.op("dve", lambda e: e.scalar_tensor_tensor(out=AR[:, 0, :], in0=T["kkn"], scalar=-1.0, in1=T["E3"], op0=ALU.mult, op1=ALU.mult),
                     r=["r_kkn", "r_E3"], w=["r_AR"])
                tt(AR[:, 1, :], T["rT"], T["E1"], ALU.mult, ["r_rT", "r_E1"], ["r_AR"])
                tt(BK[:, 0, :], T["bb"], T["E2"], ALU.mult, ["r_bb", "r_E2"], ["r_BK"])
                tt(BK[:, 1, :], T["km"], T["E2"], ALU.mult, ["r_km", "r_E2"], ["r_BK"])
                tt(BKh[:, 0, :], T["bb"], T["E4"], ALU.mult, ["r_bb", "r_E4"], ["r_BKh"])
                tt(BKh[:, 1, :], T["km"], T["E4"], ALU.mult, ["r_km", "r_E4"], ["r_BKh"])
                P.op("dve", lambda e: e.scalar_tensor_tensor(out=T["tmpA"], in0=T["rT"], scalar=col("rk"), in1=T["km"], op0=ALU.mult, op1=ALU.mult),
                     r=["r_rT", "r_km", "cols", "r_tmpA"], w=["r_tmpA"])
                for cc in range(NCC):
                    cs_ = slice(cc * CC, (cc + 1) * CC)
                    pb = cc % 2
                    P.op("pe", lambda e: e.matmul(psP[pb][:, 0:CC], lhsT=K.bdiag, rhs=T["tmpA"][:, cs_], start=True, stop=True),
                         r=["bdiag", "r_tmpA"], w=[f"rpsB{pb}"])
                    tt(T["bonT"][:, cs_], psP[pb][:, 0:CC], T["vT"][:, cs_], ALU.mult, [f"rpsB{pb}", "r_vT"], ["r_bonT"])
                done = [0, 0]

                def chain1(c, par, slot):
                    pr = slice(64 * par, 64 * par + 64)
                    cs_ = slice(c * 64, (c + 1) * 64)
                    B_, nB = psB[slot], f"rpsB{slot}"
                    sfx = f"_{par}_{c}"
                    idm = K.identf[pr, pr]
                    for q_, src, sn in ((0, T["vT"][pr, cs_], "r_vT"), (1, BKh[pr, 0, cs_], "r_BKh"), (2, BKh[pr, 1, cs_], "r_BKh")):
                        P.op("pe", lambda e: e.matmul(B_[pr, 320 + 64 * q_:384 + 64 * q_], lhsT=src, rhs=idm, start=True, stop=True),
                             r=[sn, "identf"], w=[nB])
                    P.op("pe", lambda e: e.matmul(B_[pr, 0:128], lhsT=BK[pr, 0, cs_], rhs=AR[pr, :, cs_], start=True, stop=True),
                         r=["r_BK", "r_AR"], w=[nB])
                    P.op("pe", lambda e: e.matmul(B_[pr, 128:256], lhsT=BK[pr, 1, cs_], rhs=AR[pr, :, cs_], start=True, stop=True),
                         r=["r_BK", "r_AR"], w=[nB])
                    P.op("pe", lambda e: e.matmul(B_[pr, 256:320], lhsT=AR[pr, 0, cs_], rhs=BK[pr, 0, cs_], start=True, stop=True),
                         r=["r_BK", "r_AR"], w=[nB])
                    yield
                    P.op("act", lambda e: e.activation(out=tms[pr, c, :], in_=B_[pr, 320:512], func=AF.Copy), r=[nB], w=["r_tms" + sfx])
                    tt(MAKAs[pr, c, :].rearrange("p (a t) -> p a t", a=2), B_[pr, 0:256].rearrange("p (a t) -> p a t", a=2),
                       u2[pr, :].unsqueeze(1).broadcast_to([64, 2, 128]), ALU.mult, [nB, "r_u2"], ["r_MAKA" + sfx])
                    tt(Qs[slot][0][pr], B_[pr, 256:320], sl[pr, :], ALU.mult, [nB, "r_sl"], [f"r_Q0s{slot}"])
                    tt(TTk[slot][0][pr], MAKAs[pr, c, 0:64], idm, ALU.add, ["r_MAKA" + sfx, "identf"], [f"r_TT0s{slot}"])
                    yield
                    p_prev, np_prev = MAKAs[pr, c, 0:64], "r_MAKA" + sfx
                    q_prev, nq_prev = Qs[slot][0][pr], f"r_Q0s{slot}"
                    for kq in range(1, 6):
                        pp = kq % 2
                        if kq < 5:
                            P.op("pe", lambda e: e.matmul(B_[pr, 0:64], lhsT=q_prev, rhs=p_prev, start=True, stop=True),
                                 r=[nq_prev, np_prev], w=[nB])
                        P.op("pe", lambda e: e.matmul(B_[pr, 64:128], lhsT=p_prev, rhs=q_prev, start=True, stop=True),
                             r=[nq_prev, np_prev], w=[nB])
                        yield
                        if kq < 5:
                            P.op("act", lambda e: e.activation(out=Ps[slot][pp][pr], in_=B_[pr, 0:64], func=AF.Copy),
                                 r=[nB], w=[f"r_P{pp}s{slot}"])
                        P.op("act", lambda e: e.activation(out=Qs[slot][pp][pr], in_=B_[pr, 64:128], func=AF.Copy),
                             r=[nB], w=[f"r_Q{pp}s{slot}"])
                        yield
                        t_old, nt_old = TTk[slot][(kq - 1) % 2][pr], f"r_TT{(kq - 1) % 2}s{slot}"
                        P.op("pe", lambda e: e.matmul(B_[pr, 128:192], lhsT=Qs[slot][pp][pr], rhs=t_old, start=True, stop=True),
                             r=[f"r_Q{pp}s{slot}", nt_old], w=[nB])
                        yield
                        if kq < 5:
                            tt(TTk[slot][kq % 2][pr], B_[pr, 128:192], t_old, ALU.add, [nB, nt_old], [f"r_TT{kq % 2}s{slot}"])
                        else:
                            tt(TTs[pr, c, :], B_[pr, 128:192], t_old, ALU.add, [nB, nt_old], ["r_TTs" + sfx])
                        yield
                        p_prev, np_prev = Ps[slot][pp][pr], f"r_P{pp}s{slot}"
                        q_prev, nq_prev = Qs[slot][pp][pr], f"r_Q{pp}s{slot}"
                    done[par] = max(done[par], c + 1)

                def chain2(par):
                    pr = slice(64 * par, 64 * par + 64)
                    X_, nX = psX[par], f"rpsX{par}"
                    for c in range(NCH):
                        while done[par] < min(NCH, c + 2):
                            yield
                        gi = sg * NCH + c
                        cs_ = slice(c * 64, (c + 1) * 64)
                        sfx = f"_{par}_{c}"
                        s_old, s_new = S0[gi % 2], S0[(gi + 1) % 2]
                        ns_old, ns_new = f"r_S{gi % 2}_{par}", f"r_S{(gi + 1) % 2}_{par}"
                        Vh, Bh, Kh = tms[pr, c, 0:64], tms[pr, c, 64:128], tms[pr, c, 128:192]
                        ArbT, AakT, ArkT = MAKAs[pr, c, 64:128], MAKAs[pr, c, 128:192], MAKAs[pr, c, 192:256]
                        P.op("pe", lambda e: e.matmul(X_[pr, 0:64], lhsT=AR[pr, 0, cs_], rhs=s_old[pr], start=True, stop=False),
                             r=["r_AR", ns_old], w=[nX])
                        P.op("pe", lambda e: e.matmul(X_[pr, 0:64], lhsT=AakT, rhs=Vh, start=False, stop=True),
                             r=["r_MAKA" + sfx, "r_tms" + sfx], w=[nX])
                        yield
                        P.op("act", lambda e: e.activation(out=Xs[pr], in_=X_[pr, 0:64], func=AF.Copy), r=[nX], w=[f"r_X_{par}"])
                        yield
                        P.op("pe", lambda e: e.matmul(X_[pr, 64:128], lhsT=TTs[pr, c, :], rhs=Xs[pr], start=True, stop=True),
                             r=["r_TTs" + sfx, f"r_X_{par}"], w=[nX])
                        yield
                        P.op("act", lambda e: e.activation(out=Us[pr], in_=X_[pr, 64:128], func=AF.Copy), r=[nX], w=[f"r_U_{par}"])
                        yield
                        P.op("pe", lambda e: e.matmul(X_[pr, 128:192], lhsT=AR[pr, 1, cs_], rhs=s_old[pr], start=True, stop=False),
                             r=["r_AR", ns_old], w=[nX])
                        P.op("pe", lambda e: e.matmul(X_[pr, 128:192], lhsT=ArbT, rhs=Us[pr], start=False, stop=False),
                             r=["r_MAKA" + sfx, f"r_U_{par}"], w=[nX])
                        P.op("pe", lambda e: e.matmul(X_[pr, 128:192], lhsT=ArkT, rhs=Vh, start=False, stop=True),
                             r=["r_MAKA" + sfx, "r_tms" + sfx], w=[nX])
                        P.op("pe", lambda e: e.matmul(X_[pr, 192:256], lhsT=Bh, rhs=Us[pr], start=True, stop=False),
                             r=["r_tms" + sfx, f"r_U_{par}"], w=[nX])
                        P.op("pe", lambda e: e.matmul(X_[pr, 192:256], lhsT=Kh, rhs=Vh, start=False, stop=True),
                             r=["r_tms" + sfx], w=[nX])
                        yield
                        P.op("dve", lambda e: e.scalar_tensor_tensor(out=s_new[pr], in0=s_old[pr], scalar=T["E1"][pr, c * 64 + 63:c * 64 + 64],
                                                                     in1=X_[pr, 192:256], op0=ALU.mult, op1=ALU.add),
                             r=[ns_old, "r_E1", nX], w=[ns_new])
                        P.op("dve", lambda e: e.tensor_copy(out=Yall[pr, c, :], in_=X_[pr, 128:192]), r=[nX], w=["r_Yall"])
                        yield

                pending = [(c, par) for c in range(NCH) for par in range(2)]
                free_slots = list(range(NSLOT))
                active = []
                for par in range(2):
                    active.append((chain2(par), None))
                while pending or active:
                    while pending and free_slots:
                        c, par = pending.pop(0)
                        sl_ = free_slots.pop(0)
                        active.append((chain1(c, par, sl_), sl_))
                    for g in list(active):
                        try:
                            next(g[0])
                        except StopIteration:
                            active.remove(g)
                            if g[1] is not None:
                                free_slots.append(g[1])
                P.op("dve", lambda e: e.tensor_reduce(out=gst[:, 0, :], in_=Yall, axis=AX.X, op=ALU.add), r=["r_Yall"], w=["r_gst"])
                P.op("dve", lambda e: e.tensor_scalar(out=gst[:, 1, :], in0=gst[:, 0, :], scalar1=-1.0 / 64, scalar2=None, op0=ALU.mult),
                     r=["r_gst"], w=["r_gst"])
                tt(Yall, Yall, gst[:, 1, :].unsqueeze(2).broadcast_to([128, NCH, 64]), ALU.add, ["r_Yall", "r_gst"], ["r_Yall"])
                tt(Ysq, Yall, Yall, ALU.mult, ["r_Yall"], ["r_Ysq"])
                P.op("dve", lambda e: e.tensor_reduce(out=gst[:, 2, :], in_=Ysq, axis=AX.X, op=ALU.add), r=["r_Ysq"], w=["r_gst"])
                P.op("dve", lambda e: e.tensor_scalar(out=gst[:, 3, :], in0=gst[:, 2, :], scalar1=1.0 / 64, scalar2=GN_EPS, op0=ALU.mult, op1=ALU.add),
                     r=["r_gst"], w=["r_gst"])
                P.op("act", lambda e: e.activation(out=gst[:, 3, :], in_=gst[:, 3, :], func=AF.Sqrt), r=["r_gst"], w=["r_gst"])
                P.op("dve", lambda e: e.reciprocal(out=gst[:, 3, :], in_=gst[:, 3, :]), r=["r_gst"], w=["r_gst"])
                tt(Yall, Yall, gst[:, 3, :].unsqueeze(2).broadcast_to([128, NCH, 64]), ALU.mult, ["r_Yall", "r_gst"], ["r_Yall"])
                for c in range(NCH):
                    for par in range(2):
                        pr = slice(64 * par, 64 * par + 64)
                        slot = (2 * c + par) % NSLOT
                        P.op("pe", lambda e: e.matmul(psB[slot][pr, 0:64], lhsT=Yall[pr, c, :], rhs=K.identf[pr, pr], start=True, stop=True),
                             r=["r_Yall", "identf"], w=[f"rpsB{slot}"])
                        P.op("act", lambda e: e.activation(out=T["OG"][pr, c * 64:(c + 1) * 64], in_=psB[slot][pr, 0:64], func=AF.Identity,
                                                           scale=col("lg")[pr], bias=col("lb")[pr]),
                             r=[f"rpsB{slot}", "cols"], w=["r_OG"])
                tt(T["OG"], T["OG"], T["bonT"], ALU.add, ["r_OG", "r_bonT"], ["r_OG"])
                tt(ogb, T["OG"], gz, ALU.mult, ["r_OG", "r_gz"], ["r_ogb"])
                P.dma("sp", K.yagT[hc, sc], ogb, r=["r_ogb"], w=["yagT"])
        P.barrier()


def phase_dsa(K, l):
    from contextlib import ExitStack
    nc, P, W, S, NT, TOPK = K.nc, K.P, K.L[l], K.S, K.NT, K.TOPK
    NIT = 22
    with ExitStack() as st:
        def sbt(name, shape, dt=F32):
            return st.enter_context(_sbm(nc, name, list(shape), dt)).ap()
        latS = sbt("d_lat", [128, S], BF16)
        ki2 = sbt("d_ki2", [128, S])
        vxs = sbt("d_vxs", [128, NT, 520], BF16)
        wis = sbt("d_wis", [128, NT * 8])
        EBM = [sbt(f"d_EBM{i}", [128, 8, 128]) for i in range(2)]
        b31 = sbt("d_b31", [128, 8, 128])
        qij = [sbt(f"d_qij{i}", [128, 4, 128]) for i in range(3)]
        qab = [sbt(f"d_qab{i}", [128, 8, 128], BF16) for i in range(3)]
        zbj = [sbt(f"d_zbj{i}", [128, 4, 128], BF16) for i in range(3)]
        accs = [sbt(f"d_acc{i}", [128, S]) for i in range(2)]
        maskfs = [sbt(f"d_maskf{i}", [128, S], BF16) for i in range(2)]
        maskT = [sbt(f"d_maskT{i}", [128, NT, 128], BF16) for i in range(3)]
        Pt = [sbt(f"d_Pt{i}", [128, 8, 128], BF16) for i in range(3)]
        rtmp = [sbt(f"d_rt{i}", [128, 512]) for i in range(3)]
        m8s = [sbt(f"d_m8{i}", [128, 8]) for i in range(2)]
        bss = [sbt(f"d_bs{i}", [128, 8]) for i in range(2)]
        p2 = sbt("d_p2", [128, NIT + 1])
        wcolss = [sbt(f"d_wcols{i}", [128, NIT + 1]) for i in range(2)]
        rec = sbt("d_rec", [128, 8])
        yb = sbt("d_yb", [128, 512])
        ybg = [sbt(f"d_ybg{i}", [128, 4, 128], BF16) for i in range(2)]
        psI = [st.enter_context(_psm(nc, f"dpsI{i}", [128, 512], F32)).ap() for i in range(3)]
        psM = st.enter_context(_psm(nc, "dpsM", [128, 1024], BF16)).ap()
        psQ = [st.enter_context(_psm(nc, f"dpsQ{i}", [128, 512], F32)).ap() for i in range(2)]
        psO = [st.enter_context(_psm(nc, f"dpsO{i}", [128, 512], F32)).ap() for i in range(2)]

        P.dma("sp", latS, K.latT, r=["latT"], w=["d_lat"])
        P.dma("sp", ki2[0:64, :], K.kiT, r=["kiT"], w=["d_ki2"])
        P.dma("sp", ki2[64:128, :], K.kiT, r=["kiT"], w=["d_ki2"])
        for i in range(NT):
            P.dma("sp", vxs[:, i, :], K.vext[i], r=["vext"], w=["d_vxs"])
        P.dma("sp", wis, K.witm, r=["witm"], w=["d_wis"])
        P.dma("sp", b31, K.eb31, w=["d_b31"])
        for kd in range(2):
            P.dma("sp", EBM[kd], K.ebias[kd], w=[f"d_EBM{kd}"])
            P.op("dve", lambda e: e.tensor_tensor(out=EBM[kd], in0=EBM[kd], in1=b31, op=ALU.subtract), r=[f"d_EBM{kd}", "d_b31"], w=[f"d_EBM{kd}"])
            P.op("act", lambda e: e.activation(out=EBM[kd], in_=EBM[kd], func=AF.Exp), r=[f"d_EBM{kd}"], w=[f"d_EBM{kd}"])
        P.op("pool", lambda e: e.affine_select(out=EBM[0], in_=EBM[0], pattern=[[0, 8], [1, 128]], compare_op=ALU.is_ge, fill=K.reg_zero,
                                               base=0, channel_multiplier=-1), r=["d_EBM0"], w=["d_EBM0"])
        for n in range(NIT + 1):
            P.op("pool", lambda e: e.memset(p2[:, n:n + 1], 2.0 ** -(n + 1)), w=["d_p2"])
        icount = [0]
        it_done = set()

        def chain_it(j):
            b = j % 3
            u = j % 2
            acc, nacc = accs[u], f"d_acc{u}"
            maskf, nmf = maskfs[u], f"d_maskf{u}"
            m8, nm8 = m8s[u], f"d_m8{u}"
            bs, nbs = bss[u], f"d_bs{u}"
            wcols, nwc = wcolss[u], f"d_wcols{u}"
            jc = slice(j * 128, (j + 1) * 128)
            Lk = (j + 1) * 128
            mT, nmT = maskT[b], f"d_maskT{b}"
            P.dma("sp", qij[b], K.qiT.rearrange("(hq p) t -> p hq t", p=128)[:, :, jc], r=["qiT"], w=[f"d_qij{b}"])
            P.dma("sp", qab[b], K.qabsT.rearrange("h c t -> c h t")[:, :, jc], r=["qabsT"], w=[f"d_qab{b}"])
            P.dma("sp", zbj[b], K.szbT.rearrange("(fb p) t -> p fb t", p=128)[:, :, jc], r=["szbT"], w=[f"d_zbj{b}"])
            if Lk <= TOPK:
                P.op("pool", lambda e: e.memset(mT[:, 0:j + 1, :], 1.0), w=[nmT])
                it_done.add(j)
                return
            for k0 in range(0, Lk, 512):
                k1 = min(Lk, k0 + 512)
                wd = k1 - k0
                for h in range(8):
                    par, hq = h % 2, h // 2
                    pr = slice(64 * par, 64 * par + 64)
                    ib = icount[0] % 3
                    icount[0] += 1
                    P.op("pe", lambda e: e.matmul(psI[ib][:, 0:wd], lhsT=qij[b][pr, hq, :], rhs=ki2[pr, k0:k1], start=True, stop=True),
                         r=[f"d_qij{b}", "d_ki2"], w=[f"dpsI{ib}"])
                    P.op("act", lambda e: e.activation(out=rtmp[ib][:, 0:wd], in_=psI[ib][:, 0:wd], func=AF.Relu),
                         r=[f"dpsI{ib}"], w=[f"d_rt{ib}"])
                    wcol = wis[:, j * 8 + h:j * 8 + h + 1]
                    if h == 0:
                        P.op("dve", lambda e: e.tensor_scalar(out=acc[:, k0:k1], in0=rtmp[ib][:, 0:wd], scalar1=wcol, scalar2=None, op0=ALU.mult),
                             r=[f"d_rt{ib}", "d_wis"], w=[nacc])
                    else:
                        P.op("dve", lambda e: e.scalar_tensor_tensor(out=acc[:, k0:k1], in0=rtmp[ib][:, 0:wd], scalar=wcol, in1=acc[:, k0:k1],
                                                                     op0=ALU.mult, op1=ALU.add), r=[f"d_rt{ib}", "d_wis", nacc], w=[nacc])
                    yield
            P.op("pool", lambda e: e.affine_select(out=acc[:, jc], in_=acc[:, jc], pattern=[[-1, 128]], compare_op=ALU.is_ge, fill=K.reg_neg,
                                                   base=0, channel_multiplier=1), r=[nacc], w=[nacc])
            lo0, w0, mid, cnt, sg = (bs[:, i:i + 1] for i in range(5))
            P.op("dve", lambda e: e.max(out=m8, in_=acc[:, 0:Lk]), r=[nacc], w=[nm8])
            P.op("dve", lambda e: e.tensor_reduce(out=lo0, in_=acc[:, 0:j * 128], axis=AX.X, op=ALU.min), r=[nacc], w=[nbs])
            yield
            P.op("dve", lambda e: e.tensor_tensor(out=w0, in0=m8[:, 0:1], in1=lo0, op=ALU.subtract), r=[nm8, nbs], w=[nbs])
            P.op("dve", lambda e: e.tensor_scalar(out=wcols, in0=p2, scalar1=w0, scalar2=None, op0=ALU.mult), r=["d_p2", nbs], w=[nwc])
            P.op("dve", lambda e: e.tensor_tensor(out=mid, in0=lo0, in1=wcols[:, 0:1], op=ALU.add), r=[nbs, nwc], w=[nbs])
            for n in range(NIT):
                P.op("dve", lambda e: e.tensor_scalar(out=maskf[:, 0:Lk], in0=acc[:, 0:Lk], scalar1=mid, scalar2=None, op0=ALU.is_ge, op1=ALU.add,
                                                      accum_out=cnt), r=[nacc, nbs], w=[nmf, nbs])
                P.op("dve", lambda e: e.tensor_scalar(out=sg, in0=cnt, scalar1=TOPK - 0.5, scalar2=wcols[:, n:n + 1], op0=ALU.is_ge, op1=ALU.mult),
                     r=[nbs, nwc], w=[nbs])
                sub = wcols[:, n + 1:n + 2] if n + 1 < NIT else wcols[:, n:n + 1]
                P.op("dve", lambda e: e.scalar_tensor_tensor(out=mid, in0=sg, scalar=sub, in1=mid, op0=ALU.subtract, op1=ALU.add),
                     r=[nbs, nwc], w=[nbs])
                yield
            P.op("dve", lambda e: e.tensor_scalar(out=maskf[:, 0:Lk], in0=acc[:, 0:Lk], scalar1=mid, scalar2=None, op0=ALU.is_ge),
                 r=[nacc, nbs], w=[nmf])
            yield
            yield
            yield
            for i0 in range(0, j + 1, 8):
                i1 = min(j + 1, i0 + 8)
                for i in range(i0, i1):
                    P.op("pe", lambda e: e.transpose(out=psM[:, (i - i0) * 128:(i - i0 + 1) * 128], in_=maskf[:, i * 128:(i + 1) * 128],
                                                     identity=K.identb), r=[nmf, "identb"], w=["dpsM"])
                P.op("act", lambda e: e.activation(out=mT[:, i0:i1, :], in_=psM[:, 0:(i1 - i0) * 128].rearrange("p (i t) -> p i t", t=128),
                                                   func=AF.Copy), r=["dpsM"], w=[nmT])
                yield
            it_done.add(j)

        def chain_at(j):
            b = j % 3
            jc = slice(j * 128, (j + 1) * 128)
            mT, nmT = maskT[b], f"d_maskT{b}"
            while j not in it_done:
                yield

            def qk(i):
                for hh in range(2):
                    P.op("pe", lambda e: e.matmul(psQ[hh], lhsT=latS[:, i * 128:(i + 1) * 128], rhs=qab[b][:, 4 * hh:4 * hh + 4, :], start=True, stop=True),
                         r=["d_lat", f"d_qab{b}"], w=[f"dpsQ{hh}"])
            def pv(i):
                a = i % 3
                for h in range(8):
                    ob = h // 4
                    o0 = (h % 4) * 65
                    P.op("pe", lambda e: e.matmul(psO[ob][:, o0:o0 + 65], lhsT=Pt[a][:, h, :], rhs=vxs[:, i, h * 65:(h + 1) * 65],
                                                  start=(i == 0 and h % 4 == 0), stop=(i == j), skip_group_check=True),
                         r=[f"d_Pt{a}", "d_vxs"], w=[f"dpsO{ob}"])
            qk(0)
            for i in range(j + 1):
                a = i % 3
                for hh in range(2):
                    P.op("act", lambda e: e.activation(out=Pt[a][:, 4 * hh:4 * hh + 4, :], in_=psQ[hh].rearrange("p (h t) -> p h t", h=4), func=AF.Exp),
                         r=[f"dpsQ{hh}"], w=[f"d_Pt{a}"])
                kd = j - i
                if kd <= 1:
                    P.op("pool", lambda e: e.tensor_tensor(out=Pt[a], in0=Pt[a], in1=EBM[kd], op=ALU.mult), r=[f"d_Pt{a}", f"d_EBM{kd}"], w=[f"d_Pt{a}"])
                P.op("pool", lambda e: e.tensor_tensor(out=Pt[a], in0=Pt[a], in1=mT[:, i:i + 1, :].broadcast_to([128, 8, 128]), op=ALU.mult),
                     r=[f"d_Pt{a}", nmT], w=[f"d_Pt{a}"])
                if i >= 1:
                    pv(i - 1)
                if i + 1 <= j:
                    qk(i + 1)
                yield
            pv(j)
            yield
            for ob in range(2):
                o3 = psO[ob][:, 0:260].rearrange("p (h d) -> p h d", d=65)
                P.op("dve", lambda e: e.reciprocal(out=rec[:, 4 * ob:4 * ob + 4].unsqueeze(2), in_=o3[:, :, 64:65]), r=[f"dpsO{ob}"], w=["d_rec"])
                P.op("dve", lambda e: e.tensor_tensor(out=yb[:, 256 * ob:256 * ob + 256].rearrange("p (h d) -> p h d", d=64), in0=o3[:, :, 0:64],
                                                      in1=rec[:, 4 * ob:4 * ob + 4].unsqueeze(2).broadcast_to([128, 4, 64]), op=ALU.mult),
                     r=[f"dpsO{ob}", "d_rec"], w=["d_yb"])
            yield
            for fb in range(4):
                P.op("pe", lambda e: e.matmul(psI[0][:, fb * 128:(fb + 1) * 128], lhsT=yb[:, fb * 128:(fb + 1) * 128], rhs=K.identf, start=True, stop=True),
                     r=["d_yb", "identf"], w=["dpsI0"])
            P.op("dve", lambda e: e.tensor_tensor(out=ybg[j % 2], in0=psI[0].rearrange("p (f t) -> p f t", f=4), in1=zbj[b], op=ALU.mult),
                 r=["dpsI0", f"d_zbj{b}"], w=[f"d_ybg{j % 2}"])
            P.dma("sp", K.ybgT.rearrange("(fb p) t -> p fb t", p=128)[:, :, jc], ybg[j % 2], r=[f"d_ybg{j % 2}"], w=["ybgT"])

        it_gens = {}
        next_it = 0
        at_j = 0
        at_gen = None
        while at_j < NT:
            while next_it < NT and len(it_gens) < 2 and next_it <= at_j + 2:
                it_gens[next_it] = chain_it(next_it)
                next_it += 1
            for jj in sorted(it_gens):
                try:
                    next(it_gens[jj])
                except StopIteration:
                    del it_gens[jj]
            if at_gen is None and at_j in it_done:
                at_gen = chain_at(at_j)
            if at_gen is not None:
                try:
                    next(at_gen)
                except StopIteration:
                    at_gen = None
                    at_j += 1
        P.barrier()


def phase_out(K, l, xsrc, xdst):
    from contextlib import ExitStack
    nc, P, W, S, TC, NTC = K.nc, K.P, K.L[l], K.S, K.TC, K.NTC
    with ExitStack() as st:
        def sbt(name, shape, dt=F32):
            return st.enter_context(_sbm(nc, name, list(shape), dt)).ap()
        stg = sbt("o_stg", [128, 4, K.D])
        wpa = sbt("o_wpa", [128, 4, K.D], BF16)
        wpb = sbt("o_wpb", [128, 4, K.D], BF16)
        wo = sbt("o_wo", [128, 8, K.D], BF16)
        ya = [sbt(f"o_ya{i}", [128, 4, TC], BF16) for i in range(2)]
        yb = [sbt(f"o_yb{i}", [128, 4, TC], BF16) for i in range(2)]
        ga = [sbt(f"o_ga{i}", [128, 8, TC], BF16) for i in range(2)]
        gb = [sbt(f"o_gb{i}", [128, 8, TC], BF16) for i in range(2)]
        t1 = [sbt(f"o_t1{i}", [128, TC]) for i in range(2)]
        t2 = [sbt(f"o_t2{i}", [128, TC]) for i in range(2)]
        mg = sbt("o_mg", [128, 8, TC], BF16)
        xt = [sbt(f"o_xt{i}", [128, K.D]) for i in range(2)]
        xo = [sbt(f"o_xo{i}", [128, K.D]) for i in range(2)]
        psA = [st.enter_context(_psm(nc, f"opsA{i}", [128, 512], F32)).ap() for i in range(2)]
        psB = [st.enter_context(_psm(nc, f"opsB{i}", [128, 512], F32)).ap() for i in range(2)]
        psO = [st.enter_context(_psm(nc, f"opsO{i}", [128, 512], F32)).ap() for i in range(2)]
        for src, dst, dn in ((W["w_pa"], wpa, "o_wpa"), (W["w_pb"], wpb, "o_wpb")):
            P.dma("sp", stg, src.rearrange("(j p) n -> p j n", p=128), r=["o_stg"], w=["o_stg"])
            P.op("pool", lambda e: e.tensor_copy(out=dst, in_=stg), r=["o_stg"], w=[dn])
        for hf in range(2):
            P.dma("sp", stg, W["w_o"][hf * 512:(hf + 1) * 512, :].rearrange("(j p) n -> p j n", p=128), r=["o_stg"], w=["o_stg"])
            P.op("pool", lambda e: e.tensor_copy(out=wo[:, 4 * hf:4 * hf + 4, :], in_=stg), r=["o_stg"], w=["o_wo"])
        gate = K.modB[:, 2 * K.D:3 * K.D]
        cnt = 0
        xcnt = 0
        for tc in range(NTC):
            b = tc % 2
            tcs = slice(tc * TC, (tc + 1) * TC)
            P.dma("sp", ya[b], K.yagT.rearrange("(j p) t -> p j t", p=128)[:, :, tcs], r=["yagT"], w=[f"o_ya{b}"])
            P.dma("sp", yb[b], K.ybgT.rearrange("(j p) t -> p j t", p=128)[:, :, tcs], r=["ybgT"], w=[f"o_yb{b}"])
            P.dma("sp", ga[b], K.sgaT.rearrange("(j p) t -> p j t", p=128)[:, :, tcs], r=["sgaT"], w=[f"o_ga{b}"])
            P.dma("sp", gb[b], K.sgbT.rearrange("(j p) t -> p j t", p=128)[:, :, tcs], r=["sgbT"], w=[f"o_gb{b}"])
            for ob in range(8):
                pb = cnt % 2
                cnt += 1
                oc = slice(ob * 128, (ob + 1) * 128)
                for j in range(4):
                    P.op("pe", lambda e: e.matmul(psA[pb][:, 0:TC], lhsT=wpa[:, j, oc], rhs=ya[b][:, j, :], start=(j == 0), stop=(j == 3)),
                         r=["o_wpa", f"o_ya{b}"], w=[f"opsA{pb}"])
                for j in range(4):
                    P.op("pe", lambda e: e.matmul(psB[pb][:, 0:TC], lhsT=wpb[:, j, oc], rhs=yb[b][:, j, :], start=(j == 0), stop=(j == 3)),
                         r=["o_wpb", f"o_yb{b}"], w=[f"opsB{pb}"])
                P.op("dve", lambda e: e.tensor_tensor(out=t1[pb], in0=psA[pb][:, 0:TC], in1=ga[b][:, ob, :], op=ALU.mult),
                     r=[f"opsA{pb}", f"o_ga{b}"], w=[f"o_t1{pb}"])
                P.op("dve", lambda e: e.tensor_tensor(out=t2[pb], in0=psB[pb][:, 0:TC], in1=gb[b][:, ob, :], op=ALU.mult),
                     r=[f"opsB{pb}", f"o_gb{b}"], w=[f"o_t2{pb}"])
                P.op("pool", lambda e: e.tensor_tensor(out=mg[:, ob, :], in0=t1[pb], in1=t2[pb], op=ALU.add),
                     r=[f"o_t1{pb}", f"o_t2{pb}"], w=["o_mg"])
            for ts in range(TC // 128):
                xb_ = xcnt % 2
                xcnt += 1
                rows = slice(tc * TC + ts * 128, tc * TC + (ts + 1) * 128)
                P.dma("sp", xt[xb_], xsrc[rows, :], w=[f"o_xt{xb_}"])
                for hf in range(2):
                    hc = slice(hf * 512, (hf + 1) * 512)
                    for ob in range(8):
                        P.op("pe", lambda e: e.matmul(psO[hf], lhsT=mg[:, ob, ts * 128:(ts + 1) * 128], rhs=wo[:, ob, hc], start=(ob == 0), stop=(ob == 7)),
                             r=["o_mg", "o_wo"], w=[f"opsO{hf}"])
                    P.op("dve", lambda e: e.tensor_tensor(out=xo[xb_][:, hc], in0=psO[hf], in1=gate[:, hc], op=ALU.mult),
                         r=[f"opsO{hf}", "modB"], w=[f"o_xo{xb_}"])
                P.op("pool", lambda e: e.tensor_tensor(out=xo[xb_], in0=xo[xb_], in1=xt[xb_], op=ALU.add),
                     r=[f"o_xo{xb_}", f"o_xt{xb_}"], w=[f"o_xo{xb_}"])
                P.dma("sp", xdst[rows, :], xo[xb_], r=[f"o_xo{xb_}"], w=["xs"])
        P.barrier()


def phase_final(K, xsrc):
    from contextlib import ExitStack
    nc, P, S, NT = K.nc, K.P, K.S, K.NT
    with ExitStack() as st:
        def sbt(name, shape, dt=F32):
            return st.enter_context(_sbm(nc, name, list(shape), dt)).ap()
        xt = [sbt(f"f_xt{i}", [128, K.D]) for i in range(2)]
        xo = [sbt(f"f_xo{i}", [128, K.D]) for i in range(2)]
        sq = sbt("f_sq", [128, K.D])
        fg = sbt("f_fg", [128, K.D])
        ss = sbt("f_ss", [128, 2])
        ps = [st.enter_context(_psm(nc, f"fps{i}", [128, 512], F32)).ap() for i in range(2)]
        P.dma("sp", K.row[0:1, 0:K.D], K.final_g, r=["row"], w=["row"])
        for hf in range(2):
            P.op("pe", lambda e: e.matmul(ps[hf], lhsT=K.ones[0:1, :], rhs=K.row[0:1, hf * 512:(hf + 1) * 512], start=True, stop=True),
                 r=["ones", "row"], w=[f"fps{hf}"])
            P.op("act", lambda e: e.activation(out=fg[:, hf * 512:(hf + 1) * 512], in_=ps[hf], func=AF.Copy), r=[f"fps{hf}"], w=["f_fg"])
        for i in range(NT):
            b = i % 2
            rows = slice(i * 128, (i + 1) * 128)
            P.dma("sp", xt[b], xsrc[rows, :], r=["xs"], w=[f"f_xt{b}"])
            P.op("act", lambda e: e.activation(out=sq, in_=xt[b], func=AF.Square, accum_out=ss[:, 0:1]), r=[f"f_xt{b}"], w=["f_sq", "f_ss"])
            P.op("dve", lambda e: e.tensor_scalar(out=ss[:, 1:2], in0=ss[:, 0:1], scalar1=1.0 / K.D, scalar2=K.EPS, op0=ALU.mult, op1=ALU.add),
                 r=["f_ss"], w=["f_ss"])
            P.op("act", lambda e: e.activation(out=ss[:, 1:2], in_=ss[:, 1:2], func=AF.Sqrt), r=["f_ss"], w=["f_ss"])
            P.op("dve", lambda e: e.reciprocal(out=ss[:, 1:2], in_=ss[:, 1:2]), r=["f_ss"], w=["f_ss"])
            P.op("dve", lambda e: e.scalar_tensor_tensor(out=xo[b], in0=xt[b], scalar=ss[:, 1:2], in1=fg, op0=ALU.mult, op1=ALU.mult),
                 r=[f"f_xt{b}", "f_ss", "f_fg"], w=[f"f_xo{b}"])
            P.dma("sp", K.out[rows, :], xo[b], r=[f"f_xo{b}"], is_output=True)
        P.barrier()


_CACHE = {}


def kernel(**inputs):
    S = int(np.asarray(inputs["x"]).shape[1])
    B = int(np.asarray(inputs["x"]).shape[0])
    depth = int(np.asarray(inputs["w_in"]).shape[0])
    topk = min(256, S // 4)
    key = (S, depth, topk)
    if key not in _CACHE:
        _CACHE[key] = build(S, topk, depth=depth)[0]
    nc = _CACHE[key]
    shared = shared_maps(inputs, depth)
    in_maps = [core_map(inputs, shared, b) for b in range(B)]
    res = run_bass_kernel_spmd(nc, in_maps, core_ids=list(range(B)))
    return np.stack([np.asarray(r["out"], np.float32) for r in res.results], axis=0)
```

```python
import numpy as np
import concourse.bass as bass
import concourse.mybir as mybir
from concourse.bass_utils import run_bass_kernel_spmd

F32 = mybir.dt.float32
BF16 = mybir.dt.bfloat16
AF = mybir.ActivationFunctionType
ALU = mybir.AluOpType
AX = mybir.AxisListType


class Prog:
    NDSEM = 14
    PSUM_PREFIXES = ("mps", "pps", "pst", "ppq", "rps", "dps", "ops", "fps")

    def __init__(self, nc):
        self.nc = nc
        self.eng = dict(pe=nc.tensor, act=nc.scalar, dve=nc.vector, pool=nc.gpsimd, sp=nc.sync)
        self.csem = {k: nc.alloc_semaphore(name=f"c_{k}") for k in self.eng}
        self.cnt = {k: 0 for k in self.eng}
        self.seen = {k: {} for k in self.eng}
        self.lastw = {}
        self.readers = {}
        self.dsem = {q: [nc.alloc_semaphore(name=f"d_{q}{i}") for i in range(self.NDSEM)]
                     for q in ("sp", "pool", "act")}
        self.dcnt = {q: [0] * self.NDSEM for q in self.dsem}
        self.drr = {q: 0 for q in self.dsem}
        self.uid = 0
        self.out_tokens = []
        self.ninstr = 0

    def _wait(self, e, tok):
        sem, val = tok
        if self.seen[e].get(sem.num, 0) >= val:
            return
        self.eng[e].wait_ge(sem, val)
        self.seen[e][sem.num] = val

    def _deps(self, e, r, w):
        for b in r:
            lw = self.lastw.get(b)
            if lw is not None and not (lw[0] == e == "pe"):
                self._wait(e, lw[1])
            if b.startswith(self.PSUM_PREFIXES):
                for re_, tok in self.readers.get(b, {}).items():
                    if re_ != e:
                        self._wait(e, tok)
        for b in w:
            lw = self.lastw.get(b)
            if lw is not None and not (lw[0] == e == "pe"):
                self._wait(e, lw[1])
            for re_, tok in self.readers.get(b, {}).items():
                if re_ != e or e == "pool":
                    self._wait(e, tok)

    def _commit(self, key, tok, r, w):
        for b in r:
            self.readers.setdefault(b, {})[key] = tok
        for b in w:
            self.lastw[b] = (key, tok)
            self.readers[b] = {}

    def op(self, e, fn, r=(), w=()):
        self._deps(e, r, w)
        ins = fn(self.eng[e])
        self.cnt[e] += 1
        ins.then_inc(self.csem[e], 1)
        self._commit(e, (self.csem[e], self.cnt[e]), r, w)
        self.ninstr += 1

    def dma(self, q, out, in_, r=(), w=(), is_output=False, **kw):
        i = self.drr[q]
        self.drr[q] = (i + 1) % self.NDSEM
        sem = self.dsem[q][i]
        if self.dcnt[q][i] > 0:
            self._wait(q, (sem, self.dcnt[q][i]))
        self._deps(q, r, w)
        ins = self.eng[q].dma_start(out=out, in_=in_, **kw)
        self.dcnt[q][i] += 16
        ins.then_inc(sem, 16)
        tok = (sem, self.dcnt[q][i])
        self.uid += 1
        self._commit(f"dma{self.uid}", tok, r, w)
        if is_output:
            self.out_tokens.append(tok)
        self.ninstr += 1

    def barrier(self):
        for e in self.eng:
            for o in self.eng:
                if o != e and self.cnt[o] > 0:
                    self._wait(e, (self.csem[o], self.cnt[o]))
            for q in self.dsem:
                for i, sem in enumerate(self.dsem[q]):
                    if self.dcnt[q][i] > 0:
                        self._wait(e, (sem, self.dcnt[q][i]))

    def finish(self):
        for q in self.dsem:
            for i, sem in enumerate(self.dsem[q]):
                if self.dcnt[q][i] > 0:
                    self._wait("sp", (sem, self.dcnt[q][i]))
        for e in ("pe", "act", "dve", "pool"):
            if self.cnt[e] > 0:
                self._wait("sp", (self.csem[e], self.cnt[e]))


D = 1024
NIN = 5960
C_R, C_K, C_V, C_WD, C_AD, C_ZA, C_Q, C_CKV, C_ZB, C_QI, C_KI, C_WI, C_GA, C_GB = (
    0, 512, 1024, 1536, 1600, 1664, 2176, 2688, 2816, 3328, 3840, 3904, 3912, 4936)
C0 = 0.6065306597126334
NEG = -1.0e30

COLS = dict(mu=(0, 13), w0=(13, 4), a0=(17, 4), kk=(21, 4), ka=(25, 4), rk=(29, 4),
            lg=(33, 4), lb=(37, 4), kvg=(41, 1))
NCOL = 42


def t5_bucket_np(dist):
    import math
    max_exact = 16
    d = np.maximum(dist, 1).astype(np.float32)
    large = max_exact + (np.log(d / np.float32(max_exact)) / np.float32(math.log(128 / max_exact))
                         * np.float32(32 - max_exact)).astype(np.int32)
    large = np.minimum(large, 31)
    return np.where(dist < max_exact, dist, large)


class Ctx:
    pass


_UNIQ = [0]


def _sbm(nc, name, shape, dt):
    _UNIQ[0] += 1
    return nc.sbuf_tensor(f"{name}_u{_UNIQ[0]}", shape, dt)


def _psm(nc, name, shape, dt):
    _UNIQ[0] += 1
    return nc.psum_tensor(f"{name}_u{_UNIQ[0]}", shape, dt)


def build(S, TOPK, depth=2, dbg=(), stop_after=None, stop_layer=0):
    from contextlib import ExitStack
    nc = bass.Bass("TRN2", target_bir_lowering=False)
    P = Prog(nc)
    NT = S // 128
    TC = min(512, S)
    NTC = S // TC
    EPS = 1e-6

    def din(name, shape, dt=F32):
        return nc.dram_tensor(name, list(shape), dt, kind="ExternalInput").ap()

    def scr(name, shape, dt=F32):
        kind = "ExternalOutput" if name in dbg else "Internal"
        return nc.dram_tensor(name, list(shape), dt, kind=kind).ap()

    x_in = din("x", [S, D])
    cT = din("cT", [128, 8])
    final_g = din("final_g", [1, D])
    ebias = din("ebias", [2, 128, 8, 128])
    eb31 = din("eb31", [128, 8, 128])
    L = []
    for l in range(depth):
        L.append(dict(
            ada_w=din(f"ada_w{l}", [D, 3 * D]), ada_b=din(f"ada_b{l}", [1, 3 * D]),
            norm_g=din(f"norm_g{l}", [1, D]), w_in=din(f"w_in{l}", [D, NIN]),
            cols=din(f"cols{l}", [128, NCOL]), w2=din(f"w2{l}", [64, 512]), a2=din(f"a2{l}", [64, 512]),
            wukT=din(f"wukT{l}", [128, 4, 128]), wuv=din(f"wuv{l}", [128, 512]),
            w_pa=din(f"w_pa{l}", [512, D]), w_pb=din(f"w_pb{l}", [512, D]), w_o=din(f"w_o{l}", [D, D])))
    out = nc.dram_tensor("out", [S, D], F32, kind="ExternalOutput").ap()
    xs = [scr(f"xs{i}", [S, D]) for i in range(2)]
    psT = scr("psT", [1664, S])
    szaT = scr("szaT", [512, S], BF16)
    szbT = scr("szbT", [512, S], BF16)
    sgaT = scr("sgaT", [D, S], BF16)
    sgbT = scr("sgbT", [D, S], BF16)
    qabsT = scr("qabsT", [8, 128, S], BF16)
    latT = scr("latT", [128, S], BF16)
    vext = scr("vext", [NT, 128, 520], BF16)
    qiT = scr("qiT", [512, S])
    kiT = scr("kiT", [64, S])
    witm = scr("witm", [128, NT * 8])
    yagT = scr("yagT", [512, S], BF16)
    ybgT = scr("ybgT", [512, S], BF16)

    def sb(name, shape, dt=F32):
        return nc.alloc_sbuf_tensor(name, list(shape), dt).ap()

    ones = sb("ones", [128, 128])
    identf = sb("identf", [128, 128])
    identb = sb("identb", [128, 128], BF16)
    bdiag = sb("bdiag", [128, 128])
    u2m = sb("u2m", [64, 128])
    slm = sb("slm", [64, 64])
    cact = sb("cact", [128, 8])
    cbc = sb("cbc", [128, 8, 128])
    modB = sb("modB", [128, 3 * D])
    Gt = sb("Gt", [128, D])
    cols = sb("cols", [128, NCOL])
    omka = sb("omka", [128, 4])
    row = sb("row", [1, 3 * D])

    reg_zero = nc.gpsimd.to_reg(0.0)
    reg_neg = nc.gpsimd.to_reg(NEG)
    P.op("pool", lambda e: e.memset(ones, 1.0), w=["ones"])
    P.op("pool", lambda e: e.affine_select(out=identf, in_=ones, pattern=[[-1, 128]], compare_op=ALU.is_equal,
                                           fill=reg_zero, base=0, channel_multiplier=1), r=["ones"], w=["identf"])
    P.op("pool", lambda e: e.tensor_copy(out=identb, in_=identf), r=["identf"], w=["identb"])
    P.op("pool", lambda e: e.memset(bdiag, 0.0), w=["bdiag"])
    P.op("pool", lambda e: e.memset(bdiag[0:64, 0:64], 1.0), w=["bdiag"])
    P.op("pool", lambda e: e.memset(bdiag[64:128, 64:128], 1.0), w=["bdiag"])
    P.op("pool", lambda e: e.affine_select(out=u2m[:, 0:64], in_=ones[0:64, 0:64], pattern=[[1, 64]], compare_op=ALU.is_ge,
                                           fill=reg_zero, base=-1, channel_multiplier=-1), r=["ones"], w=["u2m"])
    P.op("pool", lambda e: e.affine_select(out=u2m[:, 64:128], in_=ones[0:64, 0:64], pattern=[[1, 64]], compare_op=ALU.is_ge,
                                           fill=reg_zero, base=0, channel_multiplier=-1), r=["ones"], w=["u2m"])
    P.op("pool", lambda e: e.affine_select(out=slm, in_=ones[0:64, 0:64], pattern=[[-1, 64]], compare_op=ALU.is_ge,
                                           fill=reg_zero, base=-1, channel_multiplier=1), r=["ones"], w=["slm"])
    P.dma("sp", cact, cT, w=["cact"])
    P.op("act", lambda e: e.activation(out=cact, in_=cact, func=AF.Silu), r=["cact"], w=["cact"])
    for j in range(8):
        P.op("dve", lambda e, j=j: e.tensor_scalar(out=cbc[:, j, :], in0=ones, scalar1=cact[:, j:j + 1], scalar2=None,
                                                   op0=ALU.mult), r=["cact", "ones"], w=["cbc"])

    K = Ctx()
    K.__dict__.update(locals())
    K.D = D
    K.COLS = COLS
    for l in range(depth):
        xsrc = x_in if l == 0 else xs[(l - 1) % 2]
        xdst = xs[l % 2]
        phase_mod(K, l)
        if stop_after == "mod" and l == stop_layer:
            break
        phase_proj(K, l, xsrc)
        if stop_after == f"B{l}":
            break
        if stop_after == "proj" and l == stop_layer:
            break
        phase_rwkv(K, l)
        if stop_after == "rwkv" and l == stop_layer:
            break
        phase_dsa(K, l)
        if stop_after == "dsa" and l == stop_layer:
            break
        phase_out(K, l, xsrc, xdst)
        if stop_after == "out" and l == stop_layer:
            break
    else:
        phase_final(K, xs[(depth - 1) % 2])
    P.finish()
    return nc, P


def phase_mod(K, l):
    from contextlib import ExitStack
    nc, P, W = K.nc, K.P, K.L[l]
    with ExitStack() as st:
        wb = [st.enter_context(_sbm(nc, f"mw{i}", [128, 8, 512], F32)).ap() for i in range(2)]
        ps = [st.enter_context(_psm(nc, f"mps{i}", [128, 512], F32)).ap() for i in range(2)]
        P.dma("sp", K.cols, W["cols"], w=["cols"])
        P.dma("sp", K.row, W["ada_b"], w=["row"])
        P.op("dve", lambda e: e.tensor_scalar(out=K.omka, in0=K.cols[:, 25:29], scalar1=-1.0, scalar2=1.0,
                                              op0=ALU.mult, op1=ALU.add), r=["cols"], w=["omka"])
        for nb in range(6):
            b = nb % 2
            P.dma("sp", wb[b], W["ada_w"][:, nb * 512:(nb + 1) * 512].rearrange("(j p) n -> p j n", p=128), w=[f"mw{b}"])
            for j in range(8):
                P.op("pe", lambda e, j=j: e.matmul(ps[b], lhsT=K.cbc[:, j, :], rhs=wb[b][:, j, :], start=(j == 0), stop=False),
                     r=["cbc", f"mw{b}"], w=[f"mps{b}"])
            P.op("pe", lambda e: e.matmul(ps[b], lhsT=K.ones[0:1, :], rhs=K.row[0:1, nb * 512:(nb + 1) * 512], start=False, stop=True),
                 r=["ones", "row"], w=[f"mps{b}"])
            P.op("act", lambda e: e.activation(out=K.modB[:, nb * 512:(nb + 1) * 512], in_=ps[b], func=AF.Copy),
                 r=[f"mps{b}"], w=["modB"])
        P.dma("sp", K.row[0:1, 0:K.D], W["norm_g"], r=["row"], w=["row"])
        for hf in range(2):
            P.op("pe", lambda e: e.matmul(ps[hf], lhsT=K.ones[0:1, :], rhs=K.row[0:1, hf * 512:(hf + 1) * 512], start=True, stop=True),
                 r=["ones", "row"], w=[f"mps{hf}"])
            P.op("dve", lambda e: e.scalar_tensor_tensor(out=K.Gt[:, hf * 512:(hf + 1) * 512],
                                                         in0=K.modB[:, K.D + hf * 512:K.D + (hf + 1) * 512], scalar=1.0,
                                                         in1=ps[hf], op0=ALU.add, op1=ALU.mult),
                 r=["modB", f"mps{hf}"], w=["Gt"])


def proj_blocks():
    blks = []
    for i in range(13):
        blks.append((i * 128, 128, "shift", i))
    for i in range(4):
        blks.append((C_ZA + i * 128, 128, "sza", i))
    for i in range(4):
        blks.append((C_Q + i * 128, 128, "q", i))
    blks.append((C_CKV, 128, "ckv", 0))
    for i in range(4):
        blks.append((C_ZB + i * 128, 128, "szb", i))
    for i in range(8):
        blks.append((C_GA + i * 128, 128, "sga", i))
    for i in range(8):
        blks.append((C_GB + i * 128, 128, "sgb", i))
    return blks


def phase_proj(K, l, xsrc):
    from contextlib import ExitStack
    nc, P, W, S, NT, TC, NTC = K.nc, K.P, K.L[l], K.S, K.NT, K.TC, K.NTC
    with ExitStack() as st:
        def sbt(name, shape, dt=F32):
            return st.enter_context(_sbm(nc, name, list(shape), dt)).ap()
        hT = sbt("hT", [128, 8, S], BF16)
        ps = [st.enter_context(_psm(nc, f"pps{i}", [128, 512], F32)).ap() for i in range(4)]
        with ExitStack() as st2:
            def sb2(name, shape, dt=F32):
                return st2.enter_context(_sbm(nc, name, list(shape), dt)).ap()
            pst = [st2.enter_context(_psm(nc, f"pst{i}", [128, 512], F32)).ap() for i in range(2)]
            pq = [st2.enter_context(_psm(nc, f"ppq{i}", [128, 512], F32)).ap() for i in range(2)]
            xb = [sb2(f"xb{i}", [128, K.D]) for i in range(2)]
            t1s = [sb2(f"t1{i}", [128, K.D]) for i in range(2)]
            sqj = sb2("sqj", [128, K.D])
            sss = [sb2(f"ss{i}", [128, 2]) for i in range(2)]
            hTfs = [sb2(f"hTf{i}", [128, 8, 128]) for i in range(2)]
            wq = sb2("wq", [128, 8, 584])
            pqs = [sb2(f"pqs{i}", [128, 5, 128]) for i in range(2)]
            wit = sb2("wit", [128, NT * 8])
            tok = [sb2(f"tok{i}", [128, 584]) for i in range(2)]
            P.dma("sp", wq, W["w_in"][:, C_QI:C_QI + 584].rearrange("(j p) n -> p j n", p=128), w=["wq"])
            for i in range(NT):
                b = i % 2
                tcs = slice(i * 128, (i + 1) * 128)
                t1, ss, hTf = t1s[b], sss[b], hTfs[b]
                nt1, nss, nhf = f"t1{b}", f"ss{b}", f"hTf{b}"
                if i == 0:
                    P.dma("sp", xb[0], xsrc[0:128, :], w=["xb0"])
                if i + 1 < NT:
                    P.dma("sp", xb[1 - b], xsrc[(i + 1) * 128:(i + 2) * 128, :], w=[f"xb{1 - b}"])
                P.op("act", lambda e: e.activation(out=sqj, in_=xb[b], func=AF.Square, accum_out=ss[:, 0:1]),
                     r=[f"xb{b}"], w=["sqj", nss])
                P.op("dve", lambda e: e.tensor_scalar(out=ss[:, 1:2], in0=ss[:, 0:1], scalar1=1.0 / K.D, scalar2=K.EPS,
                                                      op0=ALU.mult, op1=ALU.add), r=[nss], w=[nss])
                P.op("act", lambda e: e.activation(out=ss[:, 1:2], in_=ss[:, 1:2], func=AF.Sqrt), r=[nss], w=[nss])
                P.op("dve", lambda e: e.reciprocal(out=ss[:, 1:2], in_=ss[:, 1:2]), r=[nss], w=[nss])
                P.op("dve", lambda e: e.scalar_tensor_tensor(out=t1, in0=xb[b], scalar=ss[:, 1:2], in1=K.Gt,
                                                             op0=ALU.mult, op1=ALU.mult), r=[f"xb{b}", nss, "Gt"], w=[nt1])
                P.op("dve", lambda e: e.tensor_tensor(out=t1, in0=t1, in1=K.modB[:, 0:K.D], op=ALU.add),
                     r=[nt1, "modB"], w=[nt1])
                for j in range(8):
                    P.op("pe", lambda e: e.matmul(pst[j // 4][:, (j % 4) * 128:(j % 4 + 1) * 128], lhsT=t1[:, j * 128:(j + 1) * 128],
                                                  rhs=K.identf, start=True, stop=True), r=[nt1, "identf"], w=[f"pst{j // 4}"])
                for hh in range(2):
                    src = pst[hh].rearrange("p (j t) -> p j t", j=4)
                    P.op("act", lambda e: e.activation(out=hT[:, 4 * hh:4 * hh + 4, tcs], in_=src, func=AF.Copy), r=[f"pst{hh}"], w=["hT"])
                    P.op("dve", lambda e: e.tensor_copy(out=hTf[:, 4 * hh:4 * hh + 4, :], in_=src), r=[f"pst{hh}"], w=[nhf])
                for j in range(8):
                    P.op("pe", lambda e: e.matmul(pq[0][:, 0:512], lhsT=hTf[:, j, :], rhs=wq[:, j, 0:512], start=(j == 0), stop=(j == 7)),
                         r=["wq", nhf], w=["ppq0"])
                    P.op("pe", lambda e: e.matmul(pq[1][:, 0:72], lhsT=hTf[:, j, :], rhs=wq[:, j, 512:584], start=(j == 0), stop=(j == 7)),
                         r=["wq", nhf], w=["ppq1"])
                P.op("act", lambda e: e.activation(out=tok[b][:, 0:512], in_=pq[0][:, 0:512], func=AF.Copy), r=["ppq0"], w=[f"tok{b}"])
                P.op("act", lambda e: e.activation(out=tok[b][:, 512:584], in_=pq[1][:, 0:72], func=AF.Copy), r=["ppq1"], w=[f"tok{b}"])
                P.op("dve", lambda e: e.tensor_copy(out=wit[:, i * 8:(i + 1) * 8], in_=tok[b][:, 576:584]), r=[f"tok{b}"], w=["wit"])
                for q_ in range(4):
                    P.op("pe", lambda e: e.matmul(ps[0][:, q_ * 128:(q_ + 1) * 128], lhsT=tok[b][:, q_ * 128:(q_ + 1) * 128], rhs=K.identf,
                                                  start=True, stop=True), r=[f"tok{b}", "identf"], w=["pps0"])
                P.op("pe", lambda e: e.matmul(ps[1][0:64, 0:128], lhsT=tok[b][:, 512:576], rhs=K.identf, start=True, stop=True),
                     r=[f"tok{b}", "identf"], w=["pps1"])
                P.op("act", lambda e: e.activation(out=pqs[b][:, 0:4, :], in_=ps[0].rearrange("p (q t) -> p q t", q=4), func=AF.Copy),
                     r=["pps0"], w=[f"pqs{b}"])
                P.op("dve", lambda e: e.tensor_copy(out=pqs[b][0:64, 4, :], in_=ps[1][0:64, 0:128]), r=["pps1"], w=[f"pqs{b}"])
                P.dma("sp", K.qiT.rearrange("(q p) t -> p q t", p=128)[:, :, tcs], pqs[b][:, 0:4, :], r=[f"pqs{b}"], w=["qiT"])
                P.dma("sp", K.kiT[:, tcs], pqs[b][0:64, 4, :], r=[f"pqs{b}"], w=["kiT"])
            P.dma("sp", K.witm, wit, r=["wit"], w=["witm"])
            P.barrier()
        if K.stop_after == f"B{l}":
            return
        wfs = [sbt(f"wfs{i}", [128, 8, 512]) for i in range(2)]
        wbf = [sbt(f"wbf{i}", [128, 8, 128], BF16) for i in range(2)]
        blkf = [sbt(f"blkf{i}", [128, S]) for i in range(2)]
        tmpf = [sbt("tmpf0", [128, S])]
        blkb = [sbt(f"blkb{i}", [128, S], BF16) for i in range(2)]
        wuk = blkf[0][:, 0:512].rearrange("p (a c) -> p a c", a=4)
        wukb = sbt("wukb", [128, 4, 128], BF16)
        wuv = blkf[1][:, 0:512]
        wuvb = sbt("wuvb", [128, 512], BF16)
        qa = [sbt(f"qa{i}", [128, TC], BF16) for i in range(2)]
        vx = [sbt(f"vx{i}", [128, 8, 65], BF16) for i in range(2)]
        P.dma("sp", wuk, W["wukT"], w=["blkf0"])
        P.op("pool", lambda e: e.tensor_copy(out=wukb, in_=wuk), r=["blkf0"], w=["wukb"])
        P.dma("sp", wuv, W["wuv"], w=["blkf1"])
        P.op("pool", lambda e: e.tensor_copy(out=wuvb, in_=wuv), r=["blkf1"], w=["wuvb"])
        for i in range(2):
            P.op("pool", lambda e: e.memset(vx[i], 1.0), w=[f"vx{i}"])
        pcount = 0
        qcount = 0
        blocks = proj_blocks()
        supers = []
        for (r0, r1) in ((0, C_QI), (C_GA, NIN)):
            c = r0
            while c < r1:
                supers.append((c, min(512, r1 - c)))
                c += 512
        sup_of = {}
        for si, (c0, cn) in enumerate(supers):
            for (cs, n, kind, idx) in blocks:
                if c0 <= cs < c0 + cn:
                    sup_of[cs] = si
        loaded = set()

        def load_super(si):
            if si in loaded or si >= len(supers):
                return
            loaded.add(si)
            c0, cn = supers[si]
            P.dma("sp", wfs[si % 2][:, :, 0:cn], W["w_in"][:, c0:c0 + cn].rearrange("(j p) n -> p j n", p=128), w=[f"wfs{si % 2}"])
        load_super(0)
        for bi, (cs, n, kind, idx) in enumerate(blocks):
            b = bi % 2
            si = sup_of[cs]
            load_super(si)
            if cs == supers[si][0]:
                load_super(si + 1)
            o = cs - supers[si][0]
            P.op("pool", lambda e: e.tensor_copy(out=wbf[b][:, :, 0:n], in_=wfs[si % 2][:, :, o:o + n]), r=[f"wfs{si % 2}"], w=[f"wbf{b}"])
            isb = kind in ("sza", "szb", "sga", "sgb", "q")
            dst, dname = (blkb[b], f"blkb{b}") if isb else (blkf[b], f"blkf{b}")
            func = AF.Silu if kind in ("sza", "szb") else (AF.Sigmoid if kind in ("sga", "sgb") else AF.Copy)
            for tc in range(NTC):
                pb = pcount % 2
                pcount += 1
                for j in range(8):
                    P.op("pe", lambda e: e.matmul(ps[pb][0:n, 0:TC], lhsT=wbf[b][:, j, 0:n], rhs=hT[:, j, tc * TC:(tc + 1) * TC],
                                                  start=(j == 0), stop=(j == 7)), r=[f"wbf{b}", "hT"], w=[f"pps{pb}"])
                P.op("act", lambda e: e.activation(out=dst[0:n, tc * TC:(tc + 1) * TC], in_=ps[pb][0:n, 0:TC], func=func),
                     r=[f"pps{pb}"], w=[dname])
            if kind == "shift":
                tm, tn = tmpf[0], "tmpf0"
                P.op("dve", lambda e: e.tensor_tensor(out=tm[:, 1:S], in0=dst[:, 0:S - 1], in1=dst[:, 1:S], op=ALU.subtract),
                     r=[dname], w=[tn])
                P.op("dve", lambda e: e.tensor_scalar(out=tm[:, 0:1], in0=dst[:, 0:1], scalar1=-1.0, scalar2=None, op0=ALU.mult),
                     r=[dname], w=[tn])
                P.op("dve", lambda e: e.scalar_tensor_tensor(out=tm, in0=tm, scalar=K.cols[:, idx:idx + 1], in1=dst,
                                                             op0=ALU.mult, op1=ALU.add), r=[tn, dname, "cols"], w=[tn])
                P.dma("sp", K.psT[cs:cs + 128, :], tm, r=[tn], w=["psT"])
            elif kind in ("sza", "szb", "sga", "sgb"):
                tgt = dict(sza=K.szaT, szb=K.szbT, sga=K.sgaT, sgb=K.sgbT)[kind]
                P.dma("sp", tgt[idx * 128:(idx + 1) * 128, :], dst, r=[dname], w=[kind + "T"])
            elif kind == "q":
                for hp in range(2):
                    for tc in range(NTC):
                        pb = 2 + qcount % 2
                        qb = qcount % 2
                        qcount += 1
                        P.op("pe", lambda e: e.matmul(ps[pb][:, 0:TC], lhsT=wukb[64 * hp:64 * hp + 64, idx, :],
                                                      rhs=dst[64 * hp:64 * hp + 64, tc * TC:(tc + 1) * TC], start=True, stop=True),
                             r=["wukb", dname], w=[f"pps{pb}"])
                        P.op("act", lambda e: e.activation(out=qa[qb], in_=ps[pb][:, 0:TC], func=AF.Copy, scale=0.125),
                             r=[f"pps{pb}"], w=[f"qa{qb}"])
                        P.dma("sp", K.qabsT[2 * idx + hp, :, tc * TC:(tc + 1) * TC], qa[qb], r=[f"qa{qb}"], w=["qabsT"])
            elif kind == "ckv":
                tm, tn = tmpf[0], "tmpf0"
                rs, rn = blkf[1 - b], f"blkf{1 - b}"
                P.op("act", lambda e: e.activation(out=tm, in_=dst, func=AF.Square), r=[dname], w=[tn])
                for tc in range(NTC):
                    pb = 2 + tc % 2
                    P.op("pe", lambda e: e.matmul(ps[pb][:, 0:TC], lhsT=K.ones, rhs=tm[:, tc * TC:(tc + 1) * TC], start=True, stop=True),
                         r=["ones", tn], w=[f"pps{pb}"])
                    P.op("act", lambda e: e.activation(out=rs[:, tc * TC:(tc + 1) * TC], in_=ps[pb][:, 0:TC], func=AF.Sqrt,
                                                       scale=1.0 / 128, bias=K.EPS), r=[f"pps{pb}"], w=[rn])
                P.op("dve", lambda e: e.reciprocal(out=rs, in_=rs), r=[rn], w=[rn])
                lb_, ln_ = blkb[b], f"blkb{b}"
                P.op("dve", lambda e: e.scalar_tensor_tensor(out=lb_, in0=dst, scalar=K.cols[:, 41:42], in1=rs,
                                                             op0=ALU.mult, op1=ALU.mult), r=[dname, rn, "cols"], w=[ln_])
                P.dma("sp", K.latT, lb_, r=[ln_], w=["latT"])
                for i in range(NT):
                    pb = 2 + i % 2
                    vb = i % 2
                    P.op("pe", lambda e: e.matmul(ps[pb], lhsT=lb_[:, i * 128:(i + 1) * 128], rhs=wuvb, start=True, stop=True),
                         r=[ln_, "wuvb"], w=[f"pps{pb}"])
                    P.op("act", lambda e: e.activation(out=vx[vb][:, :, 0:64], in_=ps[pb].rearrange("p (h d) -> p h d", h=8),
                                                       func=AF.Copy), r=[f"pps{pb}"], w=[f"vx{vb}"])
                    P.dma("sp", K.vext[i], vx[vb].rearrange("p h d -> p (h d)"), r=[f"vx{vb}"], w=["vext"])
        P.barrier()


def colpack(v):
    return np.ascontiguousarray(np.asarray(v, np.float32).reshape(-1, 128).T)


def shared_maps(inp, depth):
    m = {}
    m["final_g"] = np.asarray(inp["final_g"], np.float32).reshape(1, D)
    rb = np.asarray(inp["rel_bias"], np.float32)
    s_ = np.arange(128)[:, None]
    t_ = np.arange(128)[None, :]
    eb = np.zeros((2, 128, 8, 128), np.float32)
    for kind, off in ((0, 0), (1, 128)):
        dist = t_ - s_ + off
        bk = t5_bucket_np(np.maximum(dist, 0))
        eb[kind] = np.transpose(rb[bk], (0, 2, 1))
    m["ebias"] = eb
    m["eb31"] = np.ascontiguousarray(np.broadcast_to(rb[31][None, :, None], (128, 8, 128))).astype(np.float32)
    for l in range(depth):
        g = lambda k: np.asarray(inp[k][l], np.float32)
        m[f"ada_w{l}"] = g("ada_w")
        m[f"ada_b{l}"] = g("ada_b").reshape(1, -1)
        m[f"norm_g{l}"] = g("norm_g").reshape(1, -1)
        m[f"w_in{l}"] = g("w_in")
        m[f"cols{l}"] = np.ascontiguousarray(np.concatenate([
            colpack(g("shift_mu")), colpack(g("w0")), colpack(g("a0")), colpack(g("k_k")), colpack(g("k_a")),
            colpack(g("r_k").reshape(-1)), colpack(g("lnx_g")), colpack(g("lnx_b")), colpack(g("kv_norm_g"))], axis=1))
        m[f"w2{l}"] = g("w2")
        m[f"a2{l}"] = g("a2")
        wuk = g("w_uk")
        m[f"wukT{l}"] = np.ascontiguousarray(wuk.reshape(128, 4, 2, 64).transpose(2, 3, 1, 0).reshape(128, 4, 128))
        m[f"wuv{l}"] = g("w_uv").reshape(128, 512)
        m[f"w_pa{l}"] = g("w_pa")
        m[f"w_pb{l}"] = g("w_pb")
        m[f"w_o{l}"] = g("w_o")
    return m


def core_map(inp, shared, b):
    m = dict(shared)
    m["x"] = np.ascontiguousarray(np.asarray(inp["x"][b], np.float32))
    m["cT"] = colpack(np.asarray(inp["c"][b], np.float32))
    return m


def phase_rwkv(K, l):
    from contextlib import ExitStack
    nc, P, W, S = K.nc, K.P, K.L[l], K.S
    SEG = min(1024, S)
    NSEG = S // SEG
    NCH = SEG // 64
    CC = min(512, SEG)
    NCC = SEG // CC
    GN_EPS = 64e-5
    with ExitStack() as st:
        def sbt(name, shape, dt=F32):
            return st.enter_context(_sbm(nc, name, list(shape), dt)).ap()
        names = ["rT", "kT", "vT", "lwp", "ai", "kkn", "km", "bb", "Lp", "Lpe", "E1", "E2", "E3", "E4", "bonT", "tmpA", "OG", "cmask"]
        T = {n: sbt("r_" + n, [128, SEG]) for n in names}
        AR = sbt("r_AR", [128, 2, SEG])
        BK = sbt("r_BK", [128, 2, SEG])
        BKh = sbt("r_BKh", [128, 2, SEG])
        wdT = sbt("r_wdT", [64, SEG])
        adT = sbt("r_adT", [64, SEG])
        w2 = sbt("r_w2", [64, 512])
        a2 = sbt("r_a2", [64, 512])
        gz = sbt("r_gz", [128, SEG], BF16)
        ogb = sbt("r_ogb", [128, SEG], BF16)
        u2 = sbt("r_u2", [128, 128])
        sl = sbt("r_sl", [128, 64])
        NSLOT = 6
        tms = sbt("r_tms", [128, NCH, 192])
        MAKAs = sbt("r_MAKAs", [128, NCH, 256])
        TTs = sbt("r_TTs", [128, NCH, 64])
        Yall = sbt("r_Yall", [128, NCH, 64])
        Ysq = sbt("r_Ysq", [128, NCH, 64])
        gst = sbt("r_gst", [128, 4, NCH])
        Ps = [[sbt(f"r_P{i}s{k}", [128, 64]) for i in range(2)] for k in range(NSLOT)]
        Qs = [[sbt(f"r_Q{i}s{k}", [128, 64]) for i in range(2)] for k in range(NSLOT)]
        TTk = [[sbt(f"r_TT{i}s{k}", [128, 64]) for i in range(2)] for k in range(NSLOT)]
        Xs = sbt("r_X", [128, 64])
        Us = sbt("r_U", [128, 64])
        S0 = [sbt(f"r_S{i}", [128, 64]) for i in range(2)]
        psB = [st.enter_context(_psm(nc, f"rpsB{i}", [128, 512], F32)).ap() for i in range(NSLOT)]
        psX = [st.enter_context(_psm(nc, f"rpsX{i}", [128, 512], F32)).ap() for i in range(2)]
        psP = psB

        P.dma("sp", w2, W["w2"], w=["r_w2"])
        P.dma("sp", a2, W["a2"], w=["r_a2"])
        for par in range(2):
            pr = slice(64 * par, 64 * par + 64)
            P.dma("sp", u2[pr, :], K.u2m, r=["u2m"], w=["r_u2"])
            P.dma("sp", sl[pr, :], K.slm, r=["slm"], w=["r_sl"])
        cm = T["cmask"]
        P.op("pool", lambda e: e.memset(cm, 1.0), w=["r_cmask"])
        P.op("pool", lambda e: e.memset(cm.rearrange("p (c t) -> p c t", t=64)[:, :, 0:1], 0.0), w=["r_cmask"])

        def tt(out, in0, in1, op, r, w, eng="dve"):
            P.op(eng, lambda e: e.tensor_tensor(out=out, in0=in0, in1=in1, op=op), r=r, w=w)

        for hp in range(4):
            col = lambda nm: K.cols[:, K.COLS[nm][0] + hp:K.COLS[nm][0] + hp + 1]
            hc = slice(hp * 128, (hp + 1) * 128)
            for par in range(2):
                P.op("pool", lambda e: e.memset(S0[0][64 * par:64 * par + 64, :], 0.0), w=[f"r_S0_{par}"])
            for sg in range(NSEG):
                sc = slice(sg * SEG, (sg + 1) * SEG)
                P.dma("sp", T["rT"], K.psT[C_R + hp * 128:C_R + (hp + 1) * 128, sc], r=["psT"], w=["r_rT"])
                P.dma("sp", T["kT"], K.psT[C_K + hp * 128:C_K + (hp + 1) * 128, sc], r=["psT"], w=["r_kT"])
                P.dma("sp", T["vT"], K.psT[C_V + hp * 128:C_V + (hp + 1) * 128, sc], r=["psT"], w=["r_vT"])
                P.dma("sp", wdT, K.psT[C_WD:C_WD + 64, sc], r=["psT"], w=["r_wdT"])
                P.dma("sp", adT, K.psT[C_AD:C_AD + 64, sc], r=["psT"], w=["r_adT"])
                P.dma("sp", gz, K.szaT[hc, sc], r=["szaT"], w=["r_gz"])
                P.op("act", lambda e: e.activation(out=wdT, in_=wdT, func=AF.Tanh), r=["r_wdT"], w=["r_wdT"])
                for cc in range(NCC):
                    cs_ = slice(cc * CC, (cc + 1) * CC)
                    pb = cc % 2
                    P.op("pe", lambda e: e.matmul(psP[pb][:, 0:CC], lhsT=w2[:, hc], rhs=wdT[:, cs_], start=True, stop=True),
                         r=["r_w2", "r_wdT"], w=[f"rpsB{pb}"])
                    P.op("act", lambda e: e.activation(out=T["lwp"][:, cs_], in_=psP[pb][:, 0:CC], func=AF.Sigmoid, bias=col("w0")),
                         r=[f"rpsB{pb}", "cols"], w=["r_lwp"])
                    P.op("pe", lambda e: e.matmul(psP[pb][:, 0:CC], lhsT=a2[:, hc], rhs=adT[:, cs_], start=True, stop=True),
                         r=["r_a2", "r_adT"], w=[f"rpsB{pb}"])
                    P.op("act", lambda e: e.activation(out=T["ai"][:, cs_], in_=psP[pb][:, 0:CC], func=AF.Sigmoid, bias=col("a0")),
                         r=[f"rpsB{pb}", "cols"], w=["r_ai"])
                P.op("dve", lambda e: e.tensor_scalar(out=T["kkn"], in0=T["kT"], scalar1=col("kk"), scalar2=None, op0=ALU.mult),
                     r=["r_kT", "cols"], w=["r_kkn"])
                tt(T["tmpA"], T["kkn"], T["kkn"], ALU.mult, ["r_kkn"], ["r_tmpA"])
                for cc in range(NCC):
                    cs_ = slice(cc * CC, (cc + 1) * CC)
                    pb = cc % 2
                    P.op("pe", lambda e: e.matmul(psP[pb][:, 0:CC], lhsT=K.bdiag, rhs=T["tmpA"][:, cs_], start=True, stop=True),
                         r=["bdiag", "r_tmpA"], w=[f"rpsB{pb}"])
                    P.op("act", lambda e: e.activation(out=T["E4"][:, cs_], in_=psP[pb][:, 0:CC], func=AF.Sqrt),
                         r=[f"rpsB{pb}"], w=["r_E4"])
                P.op("dve", lambda e: e.tensor_scalar(out=T["E4"], in0=T["E4"], scalar1=1e-12, scalar2=None, op0=ALU.max),
                     r=["r_E4"], w=["r_E4"])
                P.op("dve", lambda e: e.reciprocal(out=T["E4"], in_=T["E4"]), r=["r_E4"], w=["r_E4"])
                tt(T["kkn"], T["kkn"], T["E4"], ALU.mult, ["r_kkn", "r_E4"], ["r_kkn"])
                P.op("dve", lambda e: e.tensor_scalar(out=T["tmpA"], in0=T["ai"], scalar1=col("ka"), scalar2=K.omka[:, hp:hp + 1],
                                                      op0=ALU.mult, op1=ALU.add), r=["r_ai", "cols", "omka"], w=["r_tmpA"])
                tt(T["km"], T["kT"], T["tmpA"], ALU.mult, ["r_kT", "r_tmpA"], ["r_km"])
                tt(T["bb"], T["kkn"], T["ai"], ALU.mult, ["r_kkn", "r_ai"], ["r_bb"])
                P.op("dve", lambda e: e.tensor_tensor_scan(out=T["Lp"], data0=cm, data1=T["lwp"], initial=0.0, op0=ALU.mult, op1=ALU.add),
                     r=["r_cmask", "r_lwp"], w=["r_Lp"])
                tt(T["Lpe"], T["Lp"], T["lwp"], ALU.subtract, ["r_Lp", "r_lwp"], ["r_Lpe"])
                Lp3 = T["Lp"].rearrange("p (c t) -> p c t", t=64)
                tt(T["tmpA"].rearrange("p (c t) -> p c t", t=64), Lp3[:, :, 63:64].broadcast_to([128, NCH, 64]), Lp3, ALU.subtract,
                   ["r_Lp"], ["r_tmpA"])
                P.op("act", lambda e: e.activation(out=T["E1"], in_=T["Lp"], func=AF.Exp, scale=-C0), r=["r_Lp"], w=["r_E1"])
                P.op("act", lambda e: e.activation(out=T["E2"], in_=T["Lp"], func=AF.Exp, scale=C0), r=["r_Lp"], w=["r_E2"])
                P.op("act", lambda e: e.activation(out=T["E3"], in_=T["Lpe"], func=AF.Exp, scale=-C0), r=["r_Lpe"], w=["r_E3"])
                P.op("act", lambda e: e.activation(out=T["E4"], in_=T["tmpA"], func=AF.Exp, scale=-C0), r=["r_tmpA"], w=["r_E4"])
                P.op("dve", lambda e: e.scalar_tensor_tensor(out=AR[:, 0, :], in0=T["kkn"], scalar=-1.0, in1=T["E3"], op0=ALU.mult, op1=ALU.mult),
                     r=["r_kkn", "r_E3"], w=["r_AR"])
                tt(AR[:, 1, :], T["rT"], T["E1"], ALU.mult, ["r_rT", "r_E1"], ["r_AR"])
                tt(BK[:, 0, :], T["bb"], T["E2"], ALU.mult, ["r_bb", "r_E2"], ["r_BK"])
                tt(BK[:, 1, :], T["km"], T["E2"], ALU.mult, ["r_km", "r_E2"], ["r_BK"])
                tt(BKh[:, 0, :], T["bb"], T["E4"], ALU.mult, ["r_bb", "r_E4"], ["r_BKh"])
                tt(BKh[:, 1, :], T["km"], T["E4"], ALU.mult, ["r_km", "r_E4"], ["r_BKh"])
                P.op("dve", lambda e: e.scalar_tensor_tensor(out=T["tmpA"], in0=T["rT"], scalar=col("rk"), in1=T["km"], op0=ALU.mult, op1=ALU.mult),
                     r=["r_rT", "r_km", "cols", "r_tmpA"], w=["r_tmpA"])
                for cc in range(NCC):
                    cs_ = slice(cc * CC, (cc + 1) * CC)
                    pb = cc % 2
                    P.op("pe", lambda e: e.matmul(psP[pb][:, 0:CC], lhsT=K.bdiag, rhs=T["tmpA"][:, cs_], start=True, stop=True),
                         r=["bdiag", "r_tmpA"], w=[f"rpsB{pb}"])
                    tt(T["bonT"][:, cs_], psP[pb][:, 0:CC], T["vT"][:, cs_], ALU.mult, [f"rpsB{pb}", "r_vT"], ["r_bonT"])
                done = [0, 0]

                def chain1(c, par, slot):
                    pr = slice(64 * par, 64 * par + 64)
                    cs_ = slice(c * 64, (c + 1) * 64)
                    B_, nB = psB[slot], f"rpsB{slot}"
                    sfx = f"_{par}_{c}"
                    idm = K.identf[pr, pr]
                    for q_, src, sn in ((0, T["vT"][pr, cs_], "r_vT"), (1, BKh[pr, 0, cs_], "r_BKh"), (2, BKh[pr, 1, cs_], "r_BKh")):
                        P.op("pe", lambda e: e.matmul(B_[pr, 320 + 64 * q_:384 + 64 * q_], lhsT=src, rhs=idm, start=True, stop=True),
                             r=[sn, "identf"], w=[nB])
                    P.op("pe", lambda e: e.matmul(B_[pr, 0:128], lhsT=BK[pr, 0, cs_], rhs=AR[pr, :, cs_], start=True, stop=True),
                         r=["r_BK", "r_AR"], w=[nB])
                    P.op("pe", lambda e: e.matmul(B_[pr, 128:256], lhsT=BK[pr, 1, cs_], rhs=AR[pr, :, cs_], start=True, stop=True),
                         r=["r_BK", "r_AR"], w=[nB])
                    P.op("pe", lambda e: e.matmul(B_[pr, 256:320], lhsT=AR[pr, 0, cs_], rhs=BK[pr, 0, cs_], start=True, stop=True),
                         r=["r_BK", "r_AR"], w=[nB])
                    yield
                    P.op("act", lambda e: e.activation(out=tms[pr, c, :], in_=B_[pr, 320:512], func=AF.Copy), r=[nB], w=["r_tms" + sfx])
                    tt(MAKAs[pr, c, :].rearrange("p (a t) -> p a t", a=2), B_[pr, 0:256].rearrange("p (a t) -> p a t", a=2),
                       u2[pr, :].unsqueeze(1).broadcast_to([64, 2, 128]), ALU.mult, [nB, "r_u2"], ["r_MAKA" + sfx])
                    tt(Qs[slot][0][pr], B_[pr, 256:320], sl[pr, :], ALU.mult, [nB, "r_sl"], [f"r_Q0s{slot}"])
                    tt(TTk[slot][0][pr], MAKAs[pr, c, 0:64], idm, ALU.add, ["r_MAKA" + sfx, "identf"], [f"r_TT0s{slot}"])
                    yield
                    p_prev, np_prev = MAKAs[pr, c, 0:64], "r_MAKA" + sfx
                    q_prev, nq_prev = Qs[slot][0][pr], f"r_Q0s{slot}"
                    for kq in range(1, 6):
                        pp = kq % 2
                        if kq < 5:
                            P.op("pe", lambda e: e.matmul(B_[pr, 0:64], lhsT=q_prev, rhs=p_prev, start=True, stop=True),
                                 r=[nq_prev, np_prev], w=[nB])
                        P.op("pe", lambda e: e.matmul(B_[pr, 64:128], lhsT=p_prev, rhs=q_prev, start=True, stop=True),
                             r=[nq_prev, np_prev], w=[nB])
                        yield
                        if kq < 5:
                            P.op("act", lambda e: e.activation(out=Ps[slot][pp][pr], in_=B_[pr, 0:64], func=AF.Copy),
                                 r=[nB], w=[f"r_P{pp}s{slot}"])
                        P.op("act", lambda e: e.activation(out=Qs[slot][pp][pr], in_=B_[pr, 64:128], func=AF.Copy),
                             r=[nB], w=[f"r_Q{pp}s{slot}"])
                        yield
                        t_old, nt_old = TTk[slot][(kq - 1) % 2][pr], f"r_TT{(kq - 1) % 2}s{slot}"
                        P.op("pe", lambda e: e.matmul(B_[pr, 128:192], lhsT=Qs[slot][pp][pr], rhs=t_old, start=True, stop=True),
                             r=[f"r_Q{pp}s{slot}", nt_old], w=[nB])
                        yield
                        if kq < 5:
                            tt(TTk[slot][kq % 2][pr], B_[pr, 128:192], t_old, ALU.add, [nB, nt_old], [f"r_TT{kq % 2}s{slot}"])
                        else:
                            tt(TTs[pr, c, :], B_[pr, 128:192], t_old, ALU.add, [nB, nt_old], ["r_TTs" + sfx])
                        yield
                        p_prev, np_prev = Ps[slot][pp][pr], f"r_P{pp}s{slot}"
                        q_prev, nq_prev = Qs[slot][pp][pr], f"r_Q{pp}s{slot}"
                    done[par] = max(done[par], c + 1)

                def chain2(par):
                    pr = slice(64 * par, 64 * par + 64)
                    X_, nX = psX[par], f"rpsX{par}"
                    for c in range(NCH):
                        while done[par] < min(NCH, c + 2):
                            yield
                        gi = sg * NCH + c
                        cs_ = slice(c * 64, (c + 1) * 64)
                        sfx = f"_{par}_{c}"
                        s_old, s_new = S0[gi % 2], S0[(gi + 1) % 2]
                        ns_old, ns_new = f"r_S{gi % 2}_{par}", f"r_S{(gi + 1) % 2}_{par}"
                        Vh, Bh, Kh = tms[pr, c, 0:64], tms[pr, c, 64:128], tms[pr, c, 128:192]
                        ArbT, AakT, ArkT = MAKAs[pr, c, 64:128], MAKAs[pr, c, 128:192], MAKAs[pr, c, 192:256]
                        P.op("pe", lambda e: e.matmul(X_[pr, 0:64], lhsT=AR[pr, 0, cs_], rhs=s_old[pr], start=True, stop=False),
                             r=["r_AR", ns_old], w=[nX])
                        P.op("pe", lambda e: e.matmul(X_[pr, 0:64], lhsT=AakT, rhs=Vh, start=False, stop=True),
                             r=["r_MAKA" + sfx, "r_tms" + sfx], w=[nX])
                        yield
                        P.op("act", lambda e: e.activation(out=Xs[pr], in_=X_[pr, 0:64], func=AF.Copy), r=[nX], w=[f"r_X_{par}"])
                        yield
                        P.op("pe", lambda e: e.matmul(X_[pr, 64:128], lhsT=TTs[pr, c, :], rhs=Xs[pr], start=True, stop=True),
                             r=["r_TTs" + sfx, f"r_X_{par}"], w=[nX])
                        yield
                        P.op("act", lambda e: e.activation(out=Us[pr], in_=X_[pr, 64:128], func=AF.Copy), r=[nX], w=[f"r_U_{par}"])
                        yield
                        P.op("pe", lambda e: e.matmul(X_[pr, 128:192], lhsT=AR[pr, 1, cs_], rhs=s_old[pr], start=True, stop=False),
                             r=["r_AR", ns_old], w=[nX])
                        P.op("pe", lambda e: e.matmul(X_[pr, 128:192], lhsT=ArbT, rhs=Us[pr], start=False, stop=False),
                             r=["r_MAKA" + sfx, f"r_U_{par}"], w=[nX])
                        P.op("pe", lambda e: e.matmul(X_[pr, 128:192], lhsT=ArkT, rhs=Vh, start=False, stop=True),
                             r=["r_MAKA" + sfx, "r_tms" + sfx], w=[nX])
                        P.op("pe", lambda e: e.matmul(X_[pr, 192:256], lhsT=Bh, rhs=Us[pr], start=True, stop=False),
                             r=["r_tms" + sfx, f"r_U_{par}"], w=[nX])
                        P.op("pe", lambda e: e.matmul(X_[pr, 192:256], lhsT=Kh, rhs=Vh, start=False, stop=True),
                             r=["r_tms" + sfx], w=[nX])
                        yield
                        P.op("dve", lambda e: e.scalar_tensor_tensor(out=s_new[pr], in0=s_old[pr], scalar=T["E1"][pr, c * 64 + 63:c * 64 + 64],
                                                                     in1=X_[pr, 192:256], op0=ALU.mult, op1=ALU.add),
                             r=[ns_old, "r_E1", nX], w=[ns_new])
                        P.op("dve", lambda e: e.tensor_copy(out=Yall[pr, c, :], in_=X_[pr, 128:192]), r=[nX], w=["r_Yall"])
                        yield

                pending = [(c, par) for c in range(NCH) for par in range(2)]
                free_slots = list(range(NSLOT))
                active = []
                for par in range(2):
                    active.append((chain2(par), None))
                while pending or active:
                    while pending and free_slots:
                        c, par = pending.pop(0)
                        sl_ = free_slots.pop(0)
                        active.append((chain1(c, par, sl_), sl_))
                    for g in list(active):
                        try:
                            next(g[0])
                        except StopIteration:
                            active.remove(g)
                            if g[1] is not None:
                                free_slots.append(g[1])
                P.op("dve", lambda e: e.tensor_reduce(out=gst[:, 0, :], in_=Yall, axis=AX.X, op=ALU.add), r=["r_Yall"], w=["r_gst"])
                P.op("dve", lambda e: e.tensor_scalar(out=gst[:, 1, :], in0=gst[:, 0, :], scalar1=-1.0 / 64, scalar2=None, op0=ALU.mult),
                     r=["r_gst"], w=["r_gst"])
                tt(Yall, Yall, gst[:, 1, :].unsqueeze(2).broadcast_to([128, NCH, 64]), ALU.add, ["r_Yall", "r_gst"], ["r_Yall"])
                tt(Ysq, Yall, Yall, ALU.mult, ["r_Yall"], ["r_Ysq"])
                P.op("dve", lambda e: e.tensor_reduce(out=gst[:, 2, :], in_=Ysq, axis=AX.X, op=ALU.add), r=["r_Ysq"], w=["r_gst"])
                P.op("dve", lambda e: e.tensor_scalar(out=gst[:, 3, :], in0=gst[:, 2, :], scalar1=1.0 / 64, scalar2=GN_EPS, op0=ALU.mult, op1=ALU.add),
                     r=["r_gst"], w=["r_gst"])
                P.op("act", lambda e: e.activation(out=gst[:, 3, :], in_=gst[:, 3, :], func=AF.Sqrt), r=["r_gst"], w=["r_gst"])
                P.op("dve", lambda e: e.reciprocal(out=gst[:, 3, :], in_=gst[:, 3, :]), r=["r_gst"], w=["r_gst"])
                tt(Yall, Yall, gst[:, 3, :].unsqueeze(2).broadcast_to([128, NCH, 64]), ALU.mult, ["r_Yall", "r_gst"], ["r_Yall"])
                for c in range(NCH):
                    for par in range(2):
                        pr = slice(64 * par, 64 * par + 64)
                        slot = (2 * c + par) % NSLOT
                        P.op("pe", lambda e: e.matmul(psB[slot][pr, 0:64], lhsT=Yall[pr, c, :], rhs=K.identf[pr, pr], start=True, stop=True),
                             r=["r_Yall", "identf"], w=[f"rpsB{slot}"])
                        P.op("act", lambda e: e.activation(out=T["OG"][pr, c * 64:(c + 1) * 64], in_=psB[slot][pr, 0:64], func=AF.Identity,
                                                           scale=col("lg")[pr], bias=col("lb")[pr]),
                             r=[f"rpsB{slot}", "cols"], w=["r_OG"])
                tt(T["OG"], T["OG"], T["bonT"], ALU.add, ["r_OG", "r_bonT"], ["r_OG"])
                tt(ogb, T["OG"], gz, ALU.mult, ["r_OG", "r_gz"], ["r_ogb"])
                P.dma("sp", K.yagT[hc, sc], ogb, r=["r_ogb"], w=["yagT"])
        P.barrier()


def phase_dsa(K, l):
    from contextlib import ExitStack
    nc, P, W, S, NT, TOPK = K.nc, K.P, K.L[l], K.S, K.NT, K.TOPK
    NIT = 22
    with ExitStack() as st:
        def sbt(name, shape, dt=F32):
            return st.enter_context(_sbm(nc, name, list(shape), dt)).ap()
        latS = sbt("d_lat", [128, S], BF16)
        ki2 = sbt("d_ki2", [128, S])
        vxs = sbt("d_vxs", [128, NT, 520], BF16)
        wis = sbt("d_wis", [128, NT * 8])
        EBM = [sbt(f"d_EBM{i}", [128, 8, 128]) for i in range(2)]
        b31 = sbt("d_b31", [128, 8, 128])
        qij = [sbt(f"d_qij{i}", [128, 4, 128]) for i in range(3)]
        qab = [sbt(f"d_qab{i}", [128, 8, 128], BF16) for i in range(3)]
        zbj = [sbt(f"d_zbj{i}", [128, 4, 128], BF16) for i in range(3)]
        accs = [sbt(f"d_acc{i}", [128, S]) for i in range(2)]
        maskfs = [sbt(f"d_maskf{i}", [128, S], BF16) for i in range(2)]
        maskT = [sbt(f"d_maskT{i}", [128, NT, 128], BF16) for i in range(3)]
        Pt = [sbt(f"d_Pt{i}", [128, 8, 128], BF16) for i in range(3)]
        rtmp = [sbt(f"d_rt{i}", [128, 512]) for i in range(3)]
        m8s = [sbt(f"d_m8{i}", [128, 8]) for i in range(2)]
        bss = [sbt(f"d_bs{i}", [128, 8]) for i in range(2)]
        p2 = sbt("d_p2", [128, NIT + 1])
        wcolss = [sbt(f"d_wcols{i}", [128, NIT + 1]) for i in range(2)]
        rec = sbt("d_rec", [128, 8])
        yb = sbt("d_yb", [128, 512])
        ybg = [sbt(f"d_ybg{i}", [128, 4, 128], BF16) for i in range(2)]
        psI = [st.enter_context(_psm(nc, f"dpsI{i}", [128, 512], F32)).ap() for i in range(3)]
        psM = st.enter_context(_psm(nc, "dpsM", [128, 1024], BF16)).ap()
        psQ = [st.enter_context(_psm(nc, f"dpsQ{i}", [128, 512], F32)).ap() for i in range(2)]
        psO = [st.enter_context(_psm(nc, f"dpsO{i}", [128, 512], F32)).ap() for i in range(2)]

        P.dma("sp", latS, K.latT, r=["latT"], w=["d_lat"])
        P.dma("sp", ki2[0:64, :], K.kiT, r=["kiT"], w=["d_ki2"])
        P.dma("sp", ki2[64:128, :], K.kiT, r=["kiT"], w=["d_ki2"])
        for i in range(NT):
            P.dma("sp", vxs[:, i, :], K.vext[i], r=["vext"], w=["d_vxs"])
        P.dma("sp", wis, K.witm, r=["witm"], w=["d_wis"])
        P.dma("sp", b31, K.eb31, w=["d_b31"])
        for kd in range(2):
            P.dma("sp", EBM[kd], K.ebias[kd], w=[f"d_EBM{kd}"])
            P.op("dve", lambda e: e.tensor_tensor(out=EBM[kd], in0=EBM[kd], in1=b31, op=ALU.subtract), r=[f"d_EBM{kd}", "d_b31"], w=[f"d_EBM{kd}"])
            P.op("act", lambda e: e.activation(out=EBM[kd], in_=EBM[kd], func=AF.Exp), r=[f"d_EBM{kd}"], w=[f"d_EBM{kd}"])
        P.op("pool", lambda e: e.affine_select(out=EBM[0], in_=EBM[0], pattern=[[0, 8], [1, 128]], compare_op=ALU.is_ge, fill=K.reg_zero,
                                               base=0, channel_multiplier=-1), r=["d_EBM0"], w=["d_EBM0"])
        for n in range(NIT + 1):
            P.op("pool", lambda e: e.memset(p2[:, n:n + 1], 2.0 ** -(n + 1)), w=["d_p2"])
        icount = [0]
        it_done = set()

        def chain_it(j):
            b = j % 3
            u = j % 2
            acc, nacc = accs[u], f"d_acc{u}"
            maskf, nmf = maskfs[u], f"d_maskf{u}"
            m8, nm8 = m8s[u], f"d_m8{u}"
            bs, nbs = bss[u], f"d_bs{u}"
            wcols, nwc = wcolss[u], f"d_wcols{u}"
            jc = slice(j * 128, (j + 1) * 128)
            Lk = (j + 1) * 128
            mT, nmT = maskT[b], f"d_maskT{b}"
            P.dma("sp", qij[b], K.qiT.rearrange("(hq p) t -> p hq t", p=128)[:, :, jc], r=["qiT"], w=[f"d_qij{b}"])
            P.dma("sp", qab[b], K.qabsT.rearrange("h c t -> c h t")[:, :, jc], r=["qabsT"], w=[f"d_qab{b}"])
            P.dma("sp", zbj[b], K.szbT.rearrange("(fb p) t -> p fb t", p=128)[:, :, jc], r=["szbT"], w=[f"d_zbj{b}"])
            if Lk <= TOPK:
                P.op("pool", lambda e: e.memset(mT[:, 0:j + 1, :], 1.0), w=[nmT])
                it_done.add(j)
                return
            for k0 in range(0, Lk, 512):
                k1 = min(Lk, k0 + 512)
                wd = k1 - k0
                for h in range(8):
                    par, hq = h % 2, h // 2
                    pr = slice(64 * par, 64 * par + 64)
                    ib = icount[0] % 3
                    icount[0] += 1
                    P.op("pe", lambda e: e.matmul(psI[ib][:, 0:wd], lhsT=qij[b][pr, hq, :], rhs=ki2[pr, k0:k1], start=True, stop=True),
                         r=[f"d_qij{b}", "d_ki2"], w=[f"dpsI{ib}"])
                    P.op("act", lambda e: e.activation(out=rtmp[ib][:, 0:wd], in_=psI[ib][:, 0:wd], func=AF.Relu),
                         r=[f"dpsI{ib}"], w=[f"d_rt{ib}"])
                    wcol = wis[:, j * 8 + h:j * 8 + h + 1]
                    if h == 0:
                        P.op("dve", lambda e: e.tensor_scalar(out=acc[:, k0:k1], in0=rtmp[ib][:, 0:wd], scalar1=wcol, scalar2=None, op0=ALU.mult),
                             r=[f"d_rt{ib}", "d_wis"], w=[nacc])
                    else:
                        P.op("dve", lambda e: e.scalar_tensor_tensor(out=acc[:, k0:k1], in0=rtmp[ib][:, 0:wd], scalar=wcol, in1=acc[:, k0:k1],
                                                                     op0=ALU.mult, op1=ALU.add), r=[f"d_rt{ib}", "d_wis", nacc], w=[nacc])
                    yield
            P.op("pool", lambda e: e.affine_select(out=acc[:, jc], in_=acc[:, jc], pattern=[[-1, 128]], compare_op=ALU.is_ge, fill=K.reg_neg,
                                                   base=0, channel_multiplier=1), r=[nacc], w=[nacc])
            lo0, w0, mid, cnt, sg = (bs[:, i:i + 1] for i in range(5))
            P.op("dve", lambda e: e.max(out=m8, in_=acc[:, 0:Lk]), r=[nacc], w=[nm8])
            P.op("dve", lambda e: e.tensor_reduce(out=lo0, in_=acc[:, 0:j * 128], axis=AX.X, op=ALU.min), r=[nacc], w=[nbs])
            yield
            P.op("dve", lambda e: e.tensor_tensor(out=w0, in0=m8[:, 0:1], in1=lo0, op=ALU.subtract), r=[nm8, nbs], w=[nbs])
            P.op("dve", lambda e: e.tensor_scalar(out=wcols, in0=p2, scalar1=w0, scalar2=None, op0=ALU.mult), r=["d_p2", nbs], w=[nwc])
            P.op("dve", lambda e: e.tensor_tensor(out=mid, in0=lo0, in1=wcols[:, 0:1], op=ALU.add), r=[nbs, nwc], w=[nbs])
            for n in range(NIT):
                P.op("dve", lambda e: e.tensor_scalar(out=maskf[:, 0:Lk], in0=acc[:, 0:Lk], scalar1=mid, scalar2=None, op0=ALU.is_ge, op1=ALU.add,
                                                      accum_out=cnt), r=[nacc, nbs], w=[nmf, nbs])
                P.op("dve", lambda e: e.tensor_scalar(out=sg, in0=cnt, scalar1=TOPK - 0.5, scalar2=wcols[:, n:n + 1], op0=ALU.is_ge, op1=ALU.mult),
                     r=[nbs, nwc], w=[nbs])
                sub = wcols[:, n + 1:n + 2] if n + 1 < NIT else wcols[:, n:n + 1]
                P.op("dve", lambda e: e.scalar_tensor_tensor(out=mid, in0=sg, scalar=sub, in1=mid, op0=ALU.subtract, op1=ALU.add),
                     r=[nbs, nwc], w=[nbs])
                yield
            P.op("dve", lambda e: e.tensor_scalar(out=maskf[:, 0:Lk], in0=acc[:, 0:Lk], scalar1=mid, scalar2=None, op0=ALU.is_ge),
                 r=[nacc, nbs], w=[nmf])
            yield
            yield
            yield
            for i0 in range(0, j + 1, 8):
                i1 = min(j + 1, i0 + 8)
                for i in range(i0, i1):
                    P.op("pe", lambda e: e.transpose(out=psM[:, (i - i0) * 128:(i - i0 + 1) * 128], in_=maskf[:, i * 128:(i + 1) * 128],
                                                     identity=K.identb), r=[nmf, "identb"], w=["dpsM"])
                P.op("act", lambda e: e.activation(out=mT[:, i0:i1, :], in_=psM[:, 0:(i1 - i0) * 128].rearrange("p (i t) -> p i t", t=128),
                                                   func=AF.Copy), r=["dpsM"], w=[nmT])
                yield
            it_done.add(j)

        def chain_at(j):
            b = j % 3
            jc = slice(j * 128, (j + 1) * 128)
            mT, nmT = maskT[b], f"d_maskT{b}"
            while j not in it_done:
                yield

            def qk(i):
                for hh in range(2):
                    P.op("pe", lambda e: e.matmul(psQ[hh], lhsT=latS[:, i * 128:(i + 1) * 128], rhs=qab[b][:, 4 * hh:4 * hh + 4, :], start=True, stop=True),
                         r=["d_lat", f"d_qab{b}"], w=[f"dpsQ{hh}"])
            def pv(i):
                a = i % 3
                for h in range(8):
                    ob = h // 4
                    o0 = (h % 4) * 65
                    P.op("pe", lambda e: e.matmul(psO[ob][:, o0:o0 + 65], lhsT=Pt[a][:, h, :], rhs=vxs[:, i, h * 65:(h + 1) * 65],
                                                  start=(i == 0 and h % 4 == 0), stop=(i == j), skip_group_check=True),
                         r=[f"d_Pt{a}", "d_vxs"], w=[f"dpsO{ob}"])
            qk(0)
            for i in range(j + 1):
                a = i % 3
                for hh in range(2):
                    P.op("act", lambda e: e.activation(out=Pt[a][:, 4 * hh:4 * hh + 4, :], in_=psQ[hh].rearrange("p (h t) -> p h t", h=4), func=AF.Exp),
                         r=[f"dpsQ{hh}"], w=[f"d_Pt{a}"])
                kd = j - i
                if kd <= 1:
                    P.op("pool", lambda e: e.tensor_tensor(out=Pt[a], in0=Pt[a], in1=EBM[kd], op=ALU.mult), r=[f"d_Pt{a}", f"d_EBM{kd}"], w=[f"d_Pt{a}"])
                P.op("pool", lambda e: e.tensor_tensor(out=Pt[a], in0=Pt[a], in1=mT[:, i:i + 1, :].broadcast_to([128, 8, 128]), op=ALU.mult),
                     r=[f"d_Pt{a}", nmT], w=[f"d_Pt{a}"])
                if i >= 1:
                    pv(i - 1)
                if i + 1 <= j:
                    qk(i + 1)
                yield
            pv(j)
            yield
            for ob in range(2):
                o3 = psO[ob][:, 0:260].rearrange("p (h d) -> p h d", d=65)
                P.op("dve", lambda e: e.reciprocal(out=rec[:, 4 * ob:4 * ob + 4].unsqueeze(2), in_=o3[:, :, 64:65]), r=[f"dpsO{ob}"], w=["d_rec"])
                P.op("dve", lambda e: e.tensor_tensor(out=yb[:, 256 * ob:256 * ob + 256].rearrange("p (h d) -> p h d", d=64), in0=o3[:, :, 0:64],
                                                      in1=rec[:, 4 * ob:4 * ob + 4].unsqueeze(2).broadcast_to([128, 4, 64]), op=ALU.mult),
                     r=[f"dpsO{ob}", "d_rec"], w=["d_yb"])
            yield
            for fb in range(4):
                P.op("pe", lambda e: e.matmul(psI[0][:, fb * 128:(fb + 1) * 128], lhsT=yb[:, fb * 128:(fb + 1) * 128], rhs=K.identf, start=True, stop=True),
                     r=["d_yb", "identf"], w=["dpsI0"])
            P.op("dve", lambda e: e.tensor_tensor(out=ybg[j % 2], in0=psI[0].rearrange("p (f t) -> p f t", f=4), in1=zbj[b], op=ALU.mult),
                 r=["dpsI0", f"d_zbj{b}"], w=[f"d_ybg{j % 2}"])
            P.dma("sp", K.ybgT.rearrange("(fb p) t -> p fb t", p=128)[:, :, jc], ybg[j % 2], r=[f"d_ybg{j % 2}"], w=["ybgT"])

        it_gens = {}
        next_it = 0
        at_j = 0
        at_gen = None
        while at_j < NT:
            while next_it < NT and len(it_gens) < 2 and next_it <= at_j + 2:
                it_gens[next_it] = chain_it(next_it)
                next_it += 1
            for jj in sorted(it_gens):
                try:
                    next(it_gens[jj])
                except StopIteration:
                    del it_gens[jj]
            if at_gen is None and at_j in it_done:
                at_gen = chain_at(at_j)
            if at_gen is not None:
                try:
                    next(at_gen)
                except StopIteration:
                    at_gen = None
                    at_j += 1
        P.barrier()


def phase_out(K, l, xsrc, xdst):
    from contextlib import ExitStack
    nc, P, W, S, TC, NTC = K.nc, K.P, K.L[l], K.S, K.TC, K.NTC
    with ExitStack() as st:
        def sbt(name, shape, dt=F32):
            return st.enter_context(_sbm(nc, name, list(shape), dt)).ap()
        stg = sbt("o_stg", [128, 4, K.D])
        wpa = sbt("o_wpa", [128, 4, K.D], BF16)
        wpb = sbt("o_wpb", [128, 4, K.D], BF16)
        wo = sbt("o_wo", [128, 8, K.D], BF16)
        ya = [sbt(f"o_ya{i}", [128, 4, TC], BF16) for i in range(2)]
        yb = [sbt(f"o_yb{i}", [128, 4, TC], BF16) for i in range(2)]
        ga = [sbt(f"o_ga{i}", [128, 8, TC], BF16) for i in range(2)]
        gb = [sbt(f"o_gb{i}", [128, 8, TC], BF16) for i in range(2)]
        t1 = [sbt(f"o_t1{i}", [128, TC]) for i in range(2)]
        t2 = [sbt(f"o_t2{i}", [128, TC]) for i in range(2)]
        mg = sbt("o_mg", [128, 8, TC], BF16)
        xt = [sbt(f"o_xt{i}", [128, K.D]) for i in range(2)]
        xo = [sbt(f"o_xo{i}", [128, K.D]) for i in range(2)]
        psA = [st.enter_context(_psm(nc, f"opsA{i}", [128, 512], F32)).ap() for i in range(2)]
        psB = [st.enter_context(_psm(nc, f"opsB{i}", [128, 512], F32)).ap() for i in range(2)]
        psO = [st.enter_context(_psm(nc, f"opsO{i}", [128, 512], F32)).ap() for i in range(2)]
        for src, dst, dn in ((W["w_pa"], wpa, "o_wpa"), (W["w_pb"], wpb, "o_wpb")):
            P.dma("sp", stg, src.rearrange("(j p) n -> p j n", p=128), r=["o_stg"], w=["o_stg"])
            P.op("pool", lambda e: e.tensor_copy(out=dst, in_=stg), r=["o_stg"], w=[dn])
        for hf in range(2):
            P.dma("sp", stg, W["w_o"][hf * 512:(hf + 1) * 512, :].rearrange("(j p) n -> p j n", p=128), r=["o_stg"], w=["o_stg"])
            P.op("pool", lambda e: e.tensor_copy(out=wo[:, 4 * hf:4 * hf + 4, :], in_=stg), r=["o_stg"], w=["o_wo"])
        gate = K.modB[:, 2 * K.D:3 * K.D]
        cnt = 0
        xcnt = 0
        def load_chunk(tc):
            b = tc % 2
            tcs = slice(tc * TC, (tc + 1) * TC)
            P.dma("sp", ya[b], K.yagT.rearrange("(j p) t -> p j t", p=128)[:, :, tcs], r=["yagT"], w=[f"o_ya{b}"])
            P.dma("sp", yb[b], K.ybgT.rearrange("(j p) t -> p j t", p=128)[:, :, tcs], r=["ybgT"], w=[f"o_yb{b}"])
            P.dma("sp", ga[b], K.sgaT.rearrange("(j p) t -> p j t", p=128)[:, :, tcs], r=["sgaT"], w=[f"o_ga{b}"])
            P.dma("sp", gb[b], K.sgbT.rearrange("(j p) t -> p j t", p=128)[:, :, tcs], r=["sgbT"], w=[f"o_gb{b}"])

        def load_x(k):
            if k < S // 128:
                P.dma("sp", xt[k % 2], xsrc[k * 128:(k + 1) * 128, :], w=[f"o_xt{k % 2}"])
        load_chunk(0)
        load_x(0)
        for tc in range(NTC):
            b = tc % 2
            tcs = slice(tc * TC, (tc + 1) * TC)
            if tc + 1 < NTC:
                load_chunk(tc + 1)
            for ob in range(8):
                pb = cnt % 2
                cnt += 1
                oc = slice(ob * 128, (ob + 1) * 128)
                for j in range(4):
                    P.op("pe", lambda e: e.matmul(psA[pb][:, 0:TC], lhsT=wpa[:, j, oc], rhs=ya[b][:, j, :], start=(j == 0), stop=(j == 3)),
                         r=["o_wpa", f"o_ya{b}"], w=[f"opsA{pb}"])
                for j in range(4):
                    P.op("pe", lambda e: e.matmul(psB[pb][:, 0:TC], lhsT=wpb[:, j, oc], rhs=yb[b][:, j, :], start=(j == 0), stop=(j == 3)),
                         r=["o_wpb", f"o_yb{b}"], w=[f"opsB{pb}"])
                P.op("dve", lambda e: e.tensor_tensor(out=t1[pb], in0=psA[pb][:, 0:TC], in1=ga[b][:, ob, :], op=ALU.mult),
                     r=[f"opsA{pb}", f"o_ga{b}"], w=[f"o_t1{pb}"])
                P.op("dve", lambda e: e.tensor_tensor(out=t2[pb], in0=psB[pb][:, 0:TC], in1=gb[b][:, ob, :], op=ALU.mult),
                     r=[f"opsB{pb}", f"o_gb{b}"], w=[f"o_t2{pb}"])
                P.op("pool", lambda e: e.tensor_tensor(out=mg[:, ob, :], in0=t1[pb], in1=t2[pb], op=ALU.add),
                     r=[f"o_t1{pb}", f"o_t2{pb}"], w=["o_mg"])
            for ts in range(TC // 128):
                xb_ = xcnt % 2
                xcnt += 1
                rows = slice(tc * TC + ts * 128, tc * TC + (ts + 1) * 128)
                load_x(xcnt)
                for hf in range(2):
                    hc = slice(hf * 512, (hf + 1) * 512)
                    for ob in range(8):
                        P.op("pe", lambda e: e.matmul(psO[hf], lhsT=mg[:, ob, ts * 128:(ts + 1) * 128], rhs=wo[:, ob, hc], start=(ob == 0), stop=(ob == 7)),
                             r=["o_mg", "o_wo"], w=[f"opsO{hf}"])
                    P.op("dve", lambda e: e.tensor_tensor(out=xo[xb_][:, hc], in0=psO[hf], in1=gate[:, hc], op=ALU.mult),
                         r=[f"opsO{hf}", "modB"], w=[f"o_xo{xb_}"])
                P.op("pool", lambda e: e.tensor_tensor(out=xo[xb_], in0=xo[xb_], in1=xt[xb_], op=ALU.add),
                     r=[f"o_xo{xb_}", f"o_xt{xb_}"], w=[f"o_xo{xb_}"])
                P.dma("sp", xdst[rows, :], xo[xb_], r=[f"o_xo{xb_}"], w=["xs"])
        P.barrier()


def phase_final(K, xsrc):
    from contextlib import ExitStack
    nc, P, S, NT = K.nc, K.P, K.S, K.NT
    with ExitStack() as st:
        def sbt(name, shape, dt=F32):
            return st.enter_context(_sbm(nc, name, list(shape), dt)).ap()
        xt = [sbt(f"f_xt{i}", [128, K.D]) for i in range(2)]
        xo = [sbt(f"f_xo{i}", [128, K.D]) for i in range(2)]
        sq = sbt("f_sq", [128, K.D])
        fg = sbt("f_fg", [128, K.D])
        ss = sbt("f_ss", [128, 2])
        ps = [st.enter_context(_psm(nc, f"fps{i}", [128, 512], F32)).ap() for i in range(2)]
        P.dma("sp", K.row[0:1, 0:K.D], K.final_g, r=["row"], w=["row"])
        for hf in range(2):
            P.op("pe", lambda e: e.matmul(ps[hf], lhsT=K.ones[0:1, :], rhs=K.row[0:1, hf * 512:(hf + 1) * 512], start=True, stop=True),
                 r=["ones", "row"], w=[f"fps{hf}"])
            P.op("act", lambda e: e.activation(out=fg[:, hf * 512:(hf + 1) * 512], in_=ps[hf], func=AF.Copy), r=[f"fps{hf}"], w=["f_fg"])
        for i in range(NT):
            b = i % 2
            rows = slice(i * 128, (i + 1) * 128)
            P.dma("sp", xt[b], xsrc[rows, :], r=["xs"], w=[f"f_xt{b}"])
            P.op("act", lambda e: e.activation(out=sq, in_=xt[b], func=AF.Square, accum_out=ss[:, 0:1]), r=[f"f_xt{b}"], w=["f_sq", "f_ss"])
            P.op("dve", lambda e: e.tensor_scalar(out=ss[:, 1:2], in0=ss[:, 0:1], scalar1=1.0 / K.D, scalar2=K.EPS, op0=ALU.mult, op1=ALU.add),
                 r=["f_ss"], w=["f_ss"])
            P.op("act", lambda e: e.activation(out=ss[:, 1:2], in_=ss[:, 1:2], func=AF.Sqrt), r=["f_ss"], w=["f_ss"])
            P.op("dve", lambda e: e.reciprocal(out=ss[:, 1:2], in_=ss[:, 1:2]), r=["f_ss"], w=["f_ss"])
            P.op("dve", lambda e: e.scalar_tensor_tensor(out=xo[b], in0=xt[b], scalar=ss[:, 1:2], in1=fg, op0=ALU.mult, op1=ALU.mult),
                 r=[f"f_xt{b}", "f_ss", "f_fg"], w=[f"f_xo{b}"])
            P.dma("sp", K.out[rows, :], xo[b], r=[f"f_xo{b}"], is_output=True)
        P.barrier()


_CACHE = {}


def kernel(**inputs):
    S = int(np.asarray(inputs["x"]).shape[1])
    B = int(np.asarray(inputs["x"]).shape[0])
    depth = int(np.asarray(inputs["w_in"]).shape[0])
    topk = min(256, S // 4)
    key = (S, depth, topk)
    if key not in _CACHE:
        _CACHE[key] = build(S, topk, depth=depth)[0]
    nc = _CACHE[key]
    shared = shared_maps(inputs, depth)
    in_maps = [core_map(inputs, shared, b) for b in range(B)]
    res = run_bass_kernel_spmd(nc, in_maps, core_ids=list(range(B)))
    return np.stack([np.asarray(r["out"], np.float32) for r in res.results], axis=0)
```

```python
import numpy as np
import concourse.bass as bass
import concourse.mybir as mybir
from concourse.bass_utils import run_bass_kernel_spmd

F32 = mybir.dt.float32
BF16 = mybir.dt.bfloat16
AF = mybir.ActivationFunctionType
ALU = mybir.AluOpType
AX = mybir.AxisListType


class Prog:
    NDSEM = 14
    PSUM_PREFIXES = ("mps", "pps", "pst", "ppq", "rps", "dps", "ops", "fps")

    def __init__(self, nc):
        self.nc = nc
        self.eng = dict(pe=nc.tensor, act=nc.scalar, dve=nc.vector, pool=nc.gpsimd, sp=nc.sync)
        self.csem = {k: nc.alloc_semaphore(name=f"c_{k}") for k in self.eng}
        self.cnt = {k: 0 for k in self.eng}
        self.seen = {k: {} for k in self.eng}
        self.lastw = {}
        self.readers = {}
        self.dsem = {q: [nc.alloc_semaphore(name=f"d_{q}{i}") for i in range(self.NDSEM)]
                     for q in ("sp", "pool", "act")}
        self.dcnt = {q: [0] * self.NDSEM for q in self.dsem}
        self.drr = {q: 0 for q in self.dsem}
        self.uid = 0
        self.out_tokens = []
        self.ninstr = 0

    def _wait(self, e, tok):
        sem, val = tok
        if self.seen[e].get(sem.num, 0) >= val:
            return
        self.eng[e].wait_ge(sem, val)
        self.seen[e][sem.num] = val

    def _deps(self, e, r, w):
        for b in r:
            lw = self.lastw.get(b)
            if lw is not None and not (lw[0] == e == "pe"):
                self._wait(e, lw[1])
            if b.startswith(self.PSUM_PREFIXES):
                for re_, tok in self.readers.get(b, {}).items():
                    if re_ != e:
                        self._wait(e, tok)
        for b in w:
            lw = self.lastw.get(b)
            if lw is not None and not (lw[0] == e == "pe"):
                self._wait(e, lw[1])
            for re_, tok in self.readers.get(b, {}).items():
                if re_ != e or e == "pool":
                    self._wait(e, tok)

    def _commit(self, key, tok, r, w):
        for b in r:
            self.readers.setdefault(b, {})[key] = tok
        for b in w:
            self.lastw[b] = (key, tok)
            self.readers[b] = {}

    def op(self, e, fn, r=(), w=()):
        self._deps(e, r, w)
        ins = fn(self.eng[e])
        self.cnt[e] += 1
        ins.then_inc(self.csem[e], 1)
        self._commit(e, (self.csem[e], self.cnt[e]), r, w)
        self.ninstr += 1

    def dma(self, q, out, in_, r=(), w=(), is_output=False, **kw):
        i = self.drr[q]
        self.drr[q] = (i + 1) % self.NDSEM
        sem = self.dsem[q][i]
        if self.dcnt[q][i] > 0:
            self._wait(q, (sem, self.dcnt[q][i]))
        self._deps(q, r, w)
        ins = self.eng[q].dma_start(out=out, in_=in_, **kw)
        self.dcnt[q][i] += 16
        ins.then_inc(sem, 16)
        tok = (sem, self.dcnt[q][i])
        self.uid += 1
        self._commit(f"dma{self.uid}", tok, r, w)
        if is_output:
            self.out_tokens.append(tok)
        self.ninstr += 1

    def barrier(self):
        for e in self.eng:
            for o in self.eng:
                if o != e and self.cnt[o] > 0:
                    self._wait(e, (self.csem[o], self.cnt[o]))
            for q in self.dsem:
                for i, sem in enumerate(self.dsem[q]):
                    if self.dcnt[q][i] > 0:
                        self._wait(e, (sem, self.dcnt[q][i]))

    def finish(self):
        for q in self.dsem:
            for i, sem in enumerate(self.dsem[q]):
                if self.dcnt[q][i] > 0:
                    self._wait("sp", (sem, self.dcnt[q][i]))
        for e in ("pe", "act", "dve", "pool"):
            if self.cnt[e] > 0:
                self._wait("sp", (self.csem[e], self.cnt[e]))


D = 1024
NIN = 5960
C_R, C_K, C_V, C_WD, C_AD, C_ZA, C_Q, C_CKV, C_ZB, C_QI, C_KI, C_WI, C_GA, C_GB = (
    0, 512, 1024, 1536, 1600, 1664, 2176, 2688, 2816, 3328, 3840, 3904, 3912, 4936)
C0 = 0.6065306597126334
NEG = -1.0e30

COLS = dict(mu=(0, 13), w0=(13, 4), a0=(17, 4), kk=(21, 4), ka=(25, 4), rk=(29, 4),
            lg=(33, 4), lb=(37, 4), kvg=(41, 1))
NCOL = 42


def t5_bucket_np(dist):
    import math
    max_exact = 16
    d = np.maximum(dist, 1).astype(np.float32)
    large = max_exact + (np.log(d / np.float32(max_exact)) / np.float32(math.log(128 / max_exact))
                         * np.float32(32 - max_exact)).astype(np.int32)
    large = np.minimum(large, 31)
    return np.where(dist < max_exact, dist, large)


class Ctx:
    pass


_UNIQ = [0]


def _sbm(nc, name, shape, dt):
    _UNIQ[0] += 1
    return nc.sbuf_tensor(f"{name}_u{_UNIQ[0]}", shape, dt)


def _psm(nc, name, shape, dt):
    _UNIQ[0] += 1
    return nc.psum_tensor(f"{name}_u{_UNIQ[0]}", shape, dt)


def build(S, TOPK, depth=2, dbg=(), stop_after=None, stop_layer=0):
    from contextlib import ExitStack
    nc = bass.Bass("TRN2", target_bir_lowering=False)
    P = Prog(nc)
    NT = S // 128
    TC = min(512, S)
    NTC = S // TC
    EPS = 1e-6

    def din(name, shape, dt=F32):
        return nc.dram_tensor(name, list(shape), dt, kind="ExternalInput").ap()

    def scr(name, shape, dt=F32):
        kind = "ExternalOutput" if name in dbg else "Internal"
        return nc.dram_tensor(name, list(shape), dt, kind=kind).ap()

    x_in = din("x", [S, D])
    cT = din("cT", [128, 8])
    final_g = din("final_g", [1, D])
    ebias = din("ebias", [2, 128, 8, 128])
    eb31 = din("eb31", [128, 8, 128])
    L = []
    for l in range(depth):
        L.append(dict(
            ada_w=din(f"ada_w{l}", [D, 3 * D]), ada_b=din(f"ada_b{l}", [1, 3 * D]),
            norm_g=din(f"norm_g{l}", [1, D]), w_in=din(f"w_in{l}", [D, NIN]),
            cols=din(f"cols{l}", [128, NCOL]), w2=din(f"w2{l}", [64, 512]), a2=din(f"a2{l}", [64, 512]),
            wukT=din(f"wukT{l}", [128, 4, 128]), wuv=din(f"wuv{l}", [128, 512]),
            w_pa=din(f"w_pa{l}", [512, D]), w_pb=din(f"w_pb{l}", [512, D]), w_o=din(f"w_o{l}", [D, D])))
    out = nc.dram_tensor("out", [S, D], F32, kind="ExternalOutput").ap()
    xs = [scr(f"xs{i}", [S, D]) for i in range(2)]
    psT = scr("psT", [1664, S])
    szaT = scr("szaT", [512, S], BF16)
    szbT = scr("szbT", [512, S], BF16)
    sgaT = scr("sgaT", [D, S], BF16)
    sgbT = scr("sgbT", [D, S], BF16)
    qabsT = scr("qabsT", [8, 128, S], BF16)
    latT = scr("latT", [128, S], BF16)
    vext = scr("vext", [NT, 128, 520], BF16)
    qiT = scr("qiT", [512, S])
    kiT = scr("kiT", [64, S])
    witm = scr("witm", [128, NT * 8])
    yagT = scr("yagT", [512, S], BF16)
    ybgT = scr("ybgT", [512, S], BF16)

    def sb(name, shape, dt=F32):
        return nc.alloc_sbuf_tensor(name, list(shape), dt).ap()

    ones = sb("ones", [128, 128])
    identf = sb("identf", [128, 128])
    identb = sb("identb", [128, 128], BF16)
    bdiag = sb("bdiag", [128, 128])
    u2m = sb("u2m", [64, 128])
    slm = sb("slm", [64, 64])
    cact = sb("cact", [128, 8])
    cbc = sb("cbc", [128, 8, 128])
    modB = sb("modB", [128, 3 * D])
    Gt = sb("Gt", [128, D])
    cols = sb("cols", [128, NCOL])
    omka = sb("omka", [128, 4])
    row = sb("row", [1, 3 * D])

    reg_zero = nc.gpsimd.to_reg(0.0)
    reg_neg = nc.gpsimd.to_reg(NEG)
    P.op("pool", lambda e: e.memset(ones, 1.0), w=["ones"])
    P.op("pool", lambda e: e.affine_select(out=identf, in_=ones, pattern=[[-1, 128]], compare_op=ALU.is_equal,
                                           fill=reg_zero, base=0, channel_multiplier=1), r=["ones"], w=["identf"])
    P.op("pool", lambda e: e.tensor_copy(out=identb, in_=identf), r=["identf"], w=["identb"])
    P.op("pool", lambda e: e.memset(bdiag, 0.0), w=["bdiag"])
    P.op("pool", lambda e: e.memset(bdiag[0:64, 0:64], 1.0), w=["bdiag"])
    P.op("pool", lambda e: e.memset(bdiag[64:128, 64:128], 1.0), w=["bdiag"])
    P.op("pool", lambda e: e.affine_select(out=u2m[:, 0:64], in_=ones[0:64, 0:64], pattern=[[1, 64]], compare_op=ALU.is_ge,
                                           fill=reg_zero, base=-1, channel_multiplier=-1), r=["ones"], w=["u2m"])
    P.op("pool", lambda e: e.affine_select(out=u2m[:, 64:128], in_=ones[0:64, 0:64], pattern=[[1, 64]], compare_op=ALU.is_ge,
                                           fill=reg_zero, base=0, channel_multiplier=-1), r=["ones"], w=["u2m"])
    P.op("pool", lambda e: e.affine_select(out=slm, in_=ones[0:64, 0:64], pattern=[[-1, 64]], compare_op=ALU.is_ge,
                                           fill=reg_zero, base=-1, channel_multiplier=1), r=["ones"], w=["slm"])
    P.dma("sp", cact, cT, w=["cact"])
    P.op("act", lambda e: e.activation(out=cact, in_=cact, func=AF.Silu), r=["cact"], w=["cact"])
    for j in range(8):
        P.op("dve", lambda e, j=j: e.tensor_scalar(out=cbc[:, j, :], in0=ones, scalar1=cact[:, j:j + 1], scalar2=None,
                                                   op0=ALU.mult), r=["cact", "ones"], w=["cbc"])

    K = Ctx()
    K.__dict__.update(locals())
    K.D = D
    K.COLS = COLS
    for l in range(depth):
        xsrc = x_in if l == 0 else xs[(l - 1) % 2]
        xdst = xs[l % 2]
        phase_mod(K, l)
        if stop_after == "mod" and l == stop_layer:
            break
        phase_proj(K, l, xsrc)
        if stop_after == f"B{l}":
            break
        if stop_after == "proj" and l == stop_layer:
            break
        phase_rwkv(K, l)
        if stop_after == "rwkv" and l == stop_layer:
            break
        phase_dsa(K, l)
        if stop_after == "dsa" and l == stop_layer:
            break
        phase_out(K, l, xsrc, xdst)
        if stop_after == "out" and l == stop_layer:
            break
    else:
        phase_final(K, xs[(depth - 1) % 2])
    P.finish()
    return nc, P


def phase_mod(K, l):
    from contextlib import ExitStack
    nc, P, W = K.nc, K.P, K.L[l]
    with ExitStack() as st:
        wb = [st.enter_context(_sbm(nc, f"mw{i}", [128, 8, 512], F32)).ap() for i in range(2)]
        ps = [st.enter_context(_psm(nc, f"mps{i}", [128, 512], F32)).ap() for i in range(2)]
        P.dma("sp", K.cols, W["cols"], w=["cols"])
        P.dma("sp", K.row, W["ada_b"], w=["row"])
        P.op("dve", lambda e: e.tensor_scalar(out=K.omka, in0=K.cols[:, 25:29], scalar1=-1.0, scalar2=1.0,
                                              op0=ALU.mult, op1=ALU.add), r=["cols"], w=["omka"])
        for nb in range(6):
            b = nb % 2
            P.dma("sp", wb[b], W["ada_w"][:, nb * 512:(nb + 1) * 512].rearrange("(j p) n -> p j n", p=128), w=[f"mw{b}"])
            for j in range(8):
                P.op("pe", lambda e, j=j: e.matmul(ps[b], lhsT=K.cbc[:, j, :], rhs=wb[b][:, j, :], start=(j == 0), stop=False),
                     r=["cbc", f"mw{b}"], w=[f"mps{b}"])
            P.op("pe", lambda e: e.matmul(ps[b], lhsT=K.ones[0:1, :], rhs=K.row[0:1, nb * 512:(nb + 1) * 512], start=False, stop=True),
                 r=["ones", "row"], w=[f"mps{b}"])
            P.op("act", lambda e: e.activation(out=K.modB[:, nb * 512:(nb + 1) * 512], in_=ps[b], func=AF.Copy),
                 r=[f"mps{b}"], w=["modB"])
        P.dma("sp", K.row[0:1, 0:K.D], W["norm_g"], r=["row"], w=["row"])
        for hf in range(2):
            P.op("pe", lambda e: e.matmul(ps[hf], lhsT=K.ones[0:1, :], rhs=K.row[0:1, hf * 512:(hf + 1) * 512], start=True, stop=True),
                 r=["ones", "row"], w=[f"mps{hf}"])
            P.op("dve", lambda e: e.scalar_tensor_tensor(out=K.Gt[:, hf * 512:(hf + 1) * 512],
                                                         in0=K.modB[:, K.D + hf * 512:K.D + (hf + 1) * 512], scalar=1.0,
                                                         in1=ps[hf], op0=ALU.add, op1=ALU.mult),
                 r=["modB", f"mps{hf}"], w=["Gt"])


def proj_blocks():
    blks = []
    for i in range(13):
        blks.append((i * 128, 128, "shift", i))
    for i in range(4):
        blks.append((C_ZA + i * 128, 128, "sza", i))
    for i in range(4):
        blks.append((C_Q + i * 128, 128, "q", i))
    blks.append((C_CKV, 128, "ckv", 0))
    for i in range(4):
        blks.append((C_ZB + i * 128, 128, "szb", i))
    for i in range(8):
        blks.append((C_GA + i * 128, 128, "sga", i))
    for i in range(8):
        blks.append((C_GB + i * 128, 128, "sgb", i))
    return blks


def phase_proj(K, l, xsrc):
    from contextlib import ExitStack
    nc, P, W, S, NT, TC, NTC = K.nc, K.P, K.L[l], K.S, K.NT, K.TC, K.NTC
    with ExitStack() as st:
        def sbt(name, shape, dt=F32):
            return st.enter_context(_sbm(nc, name, list(shape), dt)).ap()
        hT = sbt("hT", [128, 8, S], BF16)
        ps = [st.enter_context(_psm(nc, f"pps{i}", [128, 512], F32)).ap() for i in range(4)]
        with ExitStack() as st2:
            def sb2(name, shape, dt=F32):
                return st2.enter_context(_sbm(nc, name, list(shape), dt)).ap()
            pst = [st2.enter_context(_psm(nc, f"pst{i}", [128, 512], F32)).ap() for i in range(2)]
            pq = [st2.enter_context(_psm(nc, f"ppq{i}", [128, 512], F32)).ap() for i in range(2)]
            xb = [sb2(f"xb{i}", [128, K.D]) for i in range(2)]
            t1s = [sb2(f"t1{i}", [128, K.D]) for i in range(2)]
            sqj = sb2("sqj", [128, K.D])
            sss = [sb2(f"ss{i}", [128, 2]) for i in range(2)]
            hTfs = [sb2(f"hTf{i}", [128, 8, 128]) for i in range(2)]
            wq = sb2("wq", [128, 8, 584])
            pqs = [sb2(f"pqs{i}", [128, 5, 128]) for i in range(2)]
            wit = sb2("wit", [128, NT * 8])
            tok = [sb2(f"tok{i}", [128, 584]) for i in range(2)]
            P.dma("sp", wq, W["w_in"][:, C_QI:C_QI + 584].rearrange("(j p) n -> p j n", p=128), w=["wq"])
            for i in range(NT):
                b = i % 2
                tcs = slice(i * 128, (i + 1) * 128)
                t1, ss, hTf = t1s[b], sss[b], hTfs[b]
                nt1, nss, nhf = f"t1{b}", f"ss{b}", f"hTf{b}"
                if i == 0:
                    P.dma("sp", xb[0], xsrc[0:128, :], w=["xb0"])
                if i + 1 < NT:
                    P.dma("sp", xb[1 - b], xsrc[(i + 1) * 128:(i + 2) * 128, :], w=[f"xb{1 - b}"])
                P.op("act", lambda e: e.activation(out=sqj, in_=xb[b], func=AF.Square, accum_out=ss[:, 0:1]),
                     r=[f"xb{b}"], w=["sqj", nss])
                P.op("dve", lambda e: e.tensor_scalar(out=ss[:, 1:2], in0=ss[:, 0:1], scalar1=1.0 / K.D, scalar2=K.EPS,
                                                      op0=ALU.mult, op1=ALU.add), r=[nss], w=[nss])
                P.op("act", lambda e: e.activation(out=ss[:, 1:2], in_=ss[:, 1:2], func=AF.Sqrt), r=[nss], w=[nss])
                P.op("dve", lambda e: e.reciprocal(out=ss[:, 1:2], in_=ss[:, 1:2]), r=[nss], w=[nss])
                P.op("dve", lambda e: e.scalar_tensor_tensor(out=t1, in0=xb[b], scalar=ss[:, 1:2], in1=K.Gt,
                                                             op0=ALU.mult, op1=ALU.mult), r=[f"xb{b}", nss, "Gt"], w=[nt1])
                P.op("dve", lambda e: e.tensor_tensor(out=t1, in0=t1, in1=K.modB[:, 0:K.D], op=ALU.add),
                     r=[nt1, "modB"], w=[nt1])
                for j in range(8):
                    P.op("pe", lambda e: e.matmul(pst[j // 4][:, (j % 4) * 128:(j % 4 + 1) * 128], lhsT=t1[:, j * 128:(j + 1) * 128],
                                                  rhs=K.identf, start=True, stop=True), r=[nt1, "identf"], w=[f"pst{j // 4}"])
                for hh in range(2):
                    src = pst[hh].rearrange("p (j t) -> p j t", j=4)
                    P.op("act", lambda e: e.activation(out=hT[:, 4 * hh:4 * hh + 4, tcs], in_=src, func=AF.Copy), r=[f"pst{hh}"], w=["hT"])
                    P.op("dve", lambda e: e.tensor_copy(out=hTf[:, 4 * hh:4 * hh + 4, :], in_=src), r=[f"pst{hh}"], w=[nhf])
                for j in range(8):
                    P.op("pe", lambda e: e.matmul(pq[0][:, 0:512], lhsT=hTf[:, j, :], rhs=wq[:, j, 0:512], start=(j == 0), stop=(j == 7)),
                         r=["wq", nhf], w=["ppq0"])
                    P.op("pe", lambda e: e.matmul(pq[1][:, 0:72], lhsT=hTf[:, j, :], rhs=wq[:, j, 512:584], start=(j == 0), stop=(j == 7)),
                         r=["wq", nhf], w=["ppq1"])
                P.op("act", lambda e: e.activation(out=tok[b][:, 0:512], in_=pq[0][:, 0:512], func=AF.Copy), r=["ppq0"], w=[f"tok{b}"])
                P.op("act", lambda e: e.activation(out=tok[b][:, 512:584], in_=pq[1][:, 0:72], func=AF.Copy), r=["ppq1"], w=[f"tok{b}"])
                P.op("dve", lambda e: e.tensor_copy(out=wit[:, i * 8:(i + 1) * 8], in_=tok[b][:, 576:584]), r=[f"tok{b}"], w=["wit"])
                for q_ in range(4):
                    P.op("pe", lambda e: e.matmul(ps[0][:, q_ * 128:(q_ + 1) * 128], lhsT=tok[b][:, q_ * 128:(q_ + 1) * 128], rhs=K.identf,
                                                  start=True, stop=True), r=[f"tok{b}", "identf"], w=["pps0"])
                P.op("pe", lambda e: e.matmul(ps[1][0:64, 0:128], lhsT=tok[b][:, 512:576], rhs=K.identf, start=True, stop=True),
                     r=[f"tok{b}", "identf"], w=["pps1"])
                P.op("act", lambda e: e.activation(out=pqs[b][:, 0:4, :], in_=ps[0].rearrange("p (q t) -> p q t", q=4), func=AF.Copy),
                     r=["pps0"], w=[f"pqs{b}"])
                P.op("dve", lambda e: e.tensor_copy(out=pqs[b][0:64, 4, :], in_=ps[1][0:64, 0:128]), r=["pps1"], w=[f"pqs{b}"])
                P.dma("sp", K.qiT.rearrange("(q p) t -> p q t", p=128)[:, :, tcs], pqs[b][:, 0:4, :], r=[f"pqs{b}"], w=["qiT"])
                P.dma("sp", K.kiT[:, tcs], pqs[b][0:64, 4, :], r=[f"pqs{b}"], w=["kiT"])
            P.dma("sp", K.witm, wit, r=["wit"], w=["witm"])
            P.barrier()
        if K.stop_after == f"B{l}":
            return
        wfs = [sbt(f"wfs{i}", [128, 8, 512]) for i in range(2)]
        wbf = [sbt(f"wbf{i}", [128, 8, 128], BF16) for i in range(2)]
        blkf = [sbt(f"blkf{i}", [128, S]) for i in range(2)]
        tmpf = [sbt("tmpf0", [128, S])]
        blkb = [sbt(f"blkb{i}", [128, S], BF16) for i in range(2)]
        wuk = blkf[0][:, 0:512].rearrange("p (a c) -> p a c", a=4)
        wukb = sbt("wukb", [128, 4, 128], BF16)
        wuv = blkf[1][:, 0:512]
        wuvb = sbt("wuvb", [128, 512], BF16)
        qa = [sbt(f"qa{i}", [128, TC], BF16) for i in range(2)]
        vx = [sbt(f"vx{i}", [128, 8, 65], BF16) for i in range(2)]
        P.dma("sp", wuk, W["wukT"], w=["blkf0"])
        P.op("pool", lambda e: e.tensor_copy(out=wukb, in_=wuk), r=["blkf0"], w=["wukb"])
        P.dma("sp", wuv, W["wuv"], w=["blkf1"])
        P.op("pool", lambda e: e.tensor_copy(out=wuvb, in_=wuv), r=["blkf1"], w=["wuvb"])
        for i in range(2):
            P.op("pool", lambda e: e.memset(vx[i], 1.0), w=[f"vx{i}"])
        pcount = 0
        qcount = 0
        blocks = proj_blocks()
        supers = []
        for (r0, r1) in ((0, C_QI), (C_GA, NIN)):
            c = r0
            while c < r1:
                supers.append((c, min(512, r1 - c)))
                c += 512
        sup_of = {}
        for si, (c0, cn) in enumerate(supers):
            for (cs, n, kind, idx) in blocks:
                if c0 <= cs < c0 + cn:
                    sup_of[cs] = si
        loaded = set()

        def load_super(si):
            if si in loaded or si >= len(supers):
                return
            loaded.add(si)
            c0, cn = supers[si]
            P.dma("sp", wfs[si % 2][:, :, 0:cn], W["w_in"][:, c0:c0 + cn].rearrange("(j p) n -> p j n", p=128), w=[f"wfs{si % 2}"])
        load_super(0)
        for bi, (cs, n, kind, idx) in enumerate(blocks):
            b = bi % 2
            si = sup_of[cs]
            load_super(si)
            if cs == supers[si][0]:
                load_super(si + 1)
            o = cs - supers[si][0]
            P.op("pool", lambda e: e.tensor_copy(out=wbf[b][:, :, 0:n], in_=wfs[si % 2][:, :, o:o + n]), r=[f"wfs{si % 2}"], w=[f"wbf{b}"])
            isb = kind in ("sza", "szb", "sga", "sgb", "q")
            dst, dname = (blkb[b], f"blkb{b}") if isb else (blkf[b], f"blkf{b}")
            func = AF.Silu if kind in ("sza", "szb") else (AF.Sigmoid if kind in ("sga", "sgb") else AF.Copy)
            for tc in range(NTC):
                pb = pcount % 2
                pcount += 1
                for j in range(8):
                    P.op("pe", lambda e: e.matmul(ps[pb][0:n, 0:TC], lhsT=wbf[b][:, j, 0:n], rhs=hT[:, j, tc * TC:(tc + 1) * TC],
                                                  start=(j == 0), stop=(j == 7)), r=[f"wbf{b}", "hT"], w=[f"pps{pb}"])
                P.op("act", lambda e: e.activation(out=dst[0:n, tc * TC:(tc + 1) * TC], in_=ps[pb][0:n, 0:TC], func=func),
                     r=[f"pps{pb}"], w=[dname])
            if kind == "shift":
                tm, tn = tmpf[0], "tmpf0"
                P.op("dve", lambda e: e.tensor_tensor(out=tm[:, 1:S], in0=dst[:, 0:S - 1], in1=dst[:, 1:S], op=ALU.subtract),
                     r=[dname], w=[tn])
                P.op("dve", lambda e: e.tensor_scalar(out=tm[:, 0:1], in0=dst[:, 0:1], scalar1=-1.0, scalar2=None, op0=ALU.mult),
                     r=[dname], w=[tn])
                P.op("dve", lambda e: e.scalar_tensor_tensor(out=tm, in0=tm, scalar=K.cols[:, idx:idx + 1], in1=dst,
                                                             op0=ALU.mult, op1=ALU.add), r=[tn, dname, "cols"], w=[tn])
                P.dma("sp", K.psT[cs:cs + 128, :], tm, r=[tn], w=["psT"])
            elif kind in ("sza", "szb", "sga", "sgb"):
                tgt = dict(sza=K.szaT, szb=K.szbT, sga=K.sgaT, sgb=K.sgbT)[kind]
                P.dma("sp", tgt[idx * 128:(idx + 1) * 128, :], dst, r=[dname], w=[kind + "T"])
            elif kind == "q":
                for hp in range(2):
                    for tc in range(NTC):
                        pb = 2 + qcount % 2
                        qb = qcount % 2
                        qcount += 1
                        P.op("pe", lambda e: e.matmul(ps[pb][:, 0:TC], lhsT=wukb[64 * hp:64 * hp + 64, idx, :],
                                                      rhs=dst[64 * hp:64 * hp + 64, tc * TC:(tc + 1) * TC], start=True, stop=True),
                             r=["wukb", dname], w=[f"pps{pb}"])
                        P.op("act", lambda e: e.activation(out=qa[qb], in_=ps[pb][:, 0:TC], func=AF.Copy, scale=0.125),
                             r=[f"pps{pb}"], w=[f"qa{qb}"])
                        P.dma("sp", K.qabsT[2 * idx + hp, :, tc * TC:(tc + 1) * TC], qa[qb], r=[f"qa{qb}"], w=["qabsT"])
            elif kind == "ckv":
                tm, tn = tmpf[0], "tmpf0"
                rs, rn = blkf[1 - b], f"blkf{1 - b}"
                P.op("act", lambda e: e.activation(out=tm, in_=dst, func=AF.Square), r=[dname], w=[tn])
                for tc in range(NTC):
                    pb = 2 + tc % 2
                    P.op("pe", lambda e: e.matmul(ps[pb][:, 0:TC], lhsT=K.ones, rhs=tm[:, tc * TC:(tc + 1) * TC], start=True, stop=True),
                         r=["ones", tn], w=[f"pps{pb}"])
                    P.op("act", lambda e: e.activation(out=rs[:, tc * TC:(tc + 1) * TC], in_=ps[pb][:, 0:TC], func=AF.Sqrt,
                                                       scale=1.0 / 128, bias=K.EPS), r=[f"pps{pb}"], w=[rn])
                P.op("dve", lambda e: e.reciprocal(out=rs, in_=rs), r=[rn], w=[rn])
                lb_, ln_ = blkb[b], f"blkb{b}"
                P.op("dve", lambda e: e.scalar_tensor_tensor(out=lb_, in0=dst, scalar=K.cols[:, 41:42], in1=rs,
                                                             op0=ALU.mult, op1=ALU.mult), r=[dname, rn, "cols"], w=[ln_])
                P.dma("sp", K.latT, lb_, r=[ln_], w=["latT"])
                for i in range(NT):
                    pb = 2 + i % 2
                    vb = i % 2
                    P.op("pe", lambda e: e.matmul(ps[pb], lhsT=lb_[:, i * 128:(i + 1) * 128], rhs=wuvb, start=True, stop=True),
                         r=[ln_, "wuvb"], w=[f"pps{pb}"])
                    P.op("act", lambda e: e.activation(out=vx[vb][:, :, 0:64], in_=ps[pb].rearrange("p (h d) -> p h d", h=8),
                                                       func=AF.Copy), r=[f"pps{pb}"], w=[f"vx{vb}"])
                    P.dma("sp", K.vext[i], vx[vb].rearrange("p h d -> p (h d)"), r=[f"vx{vb}"], w=["vext"])
        P.barrier()


def colpack(v):
    return np.ascontiguousarray(np.asarray(v, np.float32).reshape(-1, 128).T)


def shared_maps(inp, depth):
    m = {}
    m["final_g"] = np.asarray(inp["final_g"], np.float32).reshape(1, D)
    rb = np.asarray(inp["rel_bias"], np.float32)
    s_ = np.arange(128)[:, None]
    t_ = np.arange(128)[None, :]
    eb = np.zeros((2, 128, 8, 128), np.float32)
    for kind, off in ((0, 0), (1, 128)):
        dist = t_ - s_ + off
        bk = t5_bucket_np(np.maximum(dist, 0))
        eb[kind] = np.transpose(rb[bk], (0, 2, 1))
    m["ebias"] = eb
    m["eb31"] = np.ascontiguousarray(np.broadcast_to(rb[31][None, :, None], (128, 8, 128))).astype(np.float32)
    for l in range(depth):
        g = lambda k: np.asarray(inp[k][l], np.float32)
        m[f"ada_w{l}"] = g("ada_w")
        m[f"ada_b{l}"] = g("ada_b").reshape(1, -1)
        m[f"norm_g{l}"] = g("norm_g").reshape(1, -1)
        m[f"w_in{l}"] = g("w_in")
        m[f"cols{l}"] = np.ascontiguousarray(np.concatenate([
            colpack(g("shift_mu")), colpack(g("w0")), colpack(g("a0")), colpack(g("k_k")), colpack(g("k_a")),
            colpack(g("r_k").reshape(-1)), colpack(g("lnx_g")), colpack(g("lnx_b")), colpack(g("kv_norm_g"))], axis=1))
        m[f"w2{l}"] = g("w2")
        m[f"a2{l}"] = g("a2")
        wuk = g("w_uk")
        m[f"wukT{l}"] = np.ascontiguousarray(wuk.reshape(128, 4, 2, 64).transpose(2, 3, 1, 0).reshape(128, 4, 128))
        m[f"wuv{l}"] = g("w_uv").reshape(128, 512)
        m[f"w_pa{l}"] = g("w_pa")
        m[f"w_pb{l}"] = g("w_pb")
        m[f"w_o{l}"] = g("w_o")
    return m


def core_map(inp, shared, b):
    m = dict(shared)
    m["x"] = np.ascontiguousarray(np.asarray(inp["x"][b], np.float32))
    m["cT"] = colpack(np.asarray(inp["c"][b], np.float32))
    return m


def phase_rwkv(K, l):
    from contextlib import ExitStack
    nc, P, W, S = K.nc, K.P, K.L[l], K.S
    SEG = min(1024, S)
    NSEG = S // SEG
    NCH = SEG // 64
    CC = min(512, SEG)
    NCC = SEG // CC
    GN_EPS = 64e-5
    with ExitStack() as st:
        def sbt(name, shape, dt=F32):
            return st.enter_context(_sbm(nc, name, list(shape), dt)).ap()
        names = ["rT", "kT", "vT", "lwp", "ai", "kkn", "km", "bb", "Lp", "Lpe", "E1", "E2", "E3", "E4", "bonT", "tmpA", "OG", "cmask"]
        T = {n: sbt("r_" + n, [128, SEG]) for n in names}
        AR = sbt("r_AR", [128, 2, SEG])
        BK = sbt("r_BK", [128, 2, SEG])
        BKh = sbt("r_BKh", [128, 2, SEG])
        wdT = sbt("r_wdT", [64, SEG])
        adT = sbt("r_adT", [64, SEG])
        w2 = sbt("r_w2", [64, 512])
        a2 = sbt("r_a2", [64, 512])
        gz = sbt("r_gz", [128, SEG], BF16)
        ogb = sbt("r_ogb", [128, SEG], BF16)
        u2 = sbt("r_u2", [128, 128])
        sl = sbt("r_sl", [128, 64])
        NSLOT = 6
        tms = sbt("r_tms", [128, NCH, 192])
        MAKAs = sbt("r_MAKAs", [128, NCH, 256])
        TTs = sbt("r_TTs", [128, NCH, 64])
        Yall = sbt("r_Yall", [128, NCH, 64])
        Ysq = sbt("r_Ysq", [128, NCH, 64])
        gst = sbt("r_gst", [128, 4, NCH])
        Ps = [[sbt(f"r_P{i}s{k}", [128, 64]) for i in range(2)] for k in range(NSLOT)]
        Qs = [[sbt(f"r_Q{i}s{k}", [128, 64]) for i in range(2)] for k in range(NSLOT)]
        TTk = [[sbt(f"r_TT{i}s{k}", [128, 64]) for i in range(2)] for k in range(NSLOT)]
        Xs = sbt("r_X", [128, 64])
        Us = sbt("r_U", [128, 64])
        S0 = [sbt(f"r_S{i}", [128, 64]) for i in range(2)]
        psB = [st.enter_context(_psm(nc, f"rpsB{i}", [128, 512], F32)).ap() for i in range(NSLOT)]
        psX = [st.enter_context(_psm(nc, f"rpsX{i}", [128, 512], F32)).ap() for i in range(2)]
        psP = psB

        P.dma("sp", w2, W["w2"], w=["r_w2"])
        P.dma("sp", a2, W["a2"], w=["r_a2"])
        for par in range(2):
            pr = slice(64 * par, 64 * par + 64)
            P.dma("sp", u2[pr, :], K.u2m, r=["u2m"], w=["r_u2"])
            P.dma("sp", sl[pr, :], K.slm, r=["slm"], w=["r_sl"])
        cm = T["cmask"]
        P.op("pool", lambda e: e.memset(cm, 1.0), w=["r_cmask"])
        P.op("pool", lambda e: e.memset(cm.rearrange("p (c t) -> p c t", t=64)[:, :, 0:1], 0.0), w=["r_cmask"])

        def tt(out, in0, in1, op, r, w, eng="dve"):
            P.op(eng, lambda e: e.tensor_tensor(out=out, in0=in0, in1=in1, op=op), r=r, w=w)

        for hp in range(4):
            col = lambda nm: K.cols[:, K.COLS[nm][0] + hp:K.COLS[nm][0] + hp + 1]
            hc = slice(hp * 128, (hp + 1) * 128)
            for par in range(2):
                P.op("pool", lambda e: e.memset(S0[0][64 * par:64 * par + 64, :], 0.0), w=[f"r_S0_{par}"])
            for sg in range(NSEG):
                sc = slice(sg * SEG, (sg + 1) * SEG)
                P.dma("sp", T["rT"], K.psT[C_R + hp * 128:C_R + (hp + 1) * 128, sc], r=["psT"], w=["r_rT"])
                P.dma("sp", T["kT"], K.psT[C_K + hp * 128:C_K + (hp + 1) * 128, sc], r=["psT"], w=["r_kT"])
                P.dma("sp", T["vT"], K.psT[C_V + hp * 128:C_V + (hp + 1) * 128, sc], r=["psT"], w=["r_vT"])
                P.dma("sp", wdT, K.psT[C_WD:C_WD + 64, sc], r=["psT"], w=["r_wdT"])
                P.dma("sp", adT, K.psT[C_AD:C_AD + 64, sc], r=["psT"], w=["r_adT"])
                P.dma("sp", gz, K.szaT[hc, sc], r=["szaT"], w=["r_gz"])
                P.op("act", lambda e: e.activation(out=wdT, in_=wdT, func=AF.Tanh), r=["r_wdT"], w=["r_wdT"])
                for cc in range(NCC):
                    cs_ = slice(cc * CC, (cc + 1) * CC)
                    pb = cc % 2
                    P.op("pe", lambda e: e.matmul(psP[pb][:, 0:CC], lhsT=w2[:, hc], rhs=wdT[:, cs_], start=True, stop=True),
                         r=["r_w2", "r_wdT"], w=[f"rpsB{pb}"])
                    P.op("act", lambda e: e.activation(out=T["lwp"][:, cs_], in_=psP[pb][:, 0:CC], func=AF.Sigmoid, bias=col("w0")),
                         r=[f"rpsB{pb}", "cols"], w=["r_lwp"])
                    P.op("pe", lambda e: e.matmul(psP[pb][:, 0:CC], lhsT=a2[:, hc], rhs=adT[:, cs_], start=True, stop=True),
                         r=["r_a2", "r_adT"], w=[f"rpsB{pb}"])
                    P.op("act", lambda e: e.activation(out=T["ai"][:, cs_], in_=psP[pb][:, 0:CC], func=AF.Sigmoid, bias=col("a0")),
                         r=[f"rpsB{pb}", "cols"], w=["r_ai"])
                P.op("dve", lambda e: e.tensor_scalar(out=T["kkn"], in0=T["kT"], scalar1=col("kk"), scalar2=None, op0=ALU.mult),
                     r=["r_kT", "cols"], w=["r_kkn"])
                tt(T["tmpA"], T["kkn"], T["kkn"], ALU.mult, ["r_kkn"], ["r_tmpA"])
                for cc in range(NCC):
                    cs_ = slice(cc * CC, (cc + 1) * CC)
                    pb = cc % 2
                    P.op("pe", lambda e: e.matmul(psP[pb][:, 0:CC], lhsT=K.bdiag, rhs=T["tmpA"][:, cs_], start=True, stop=True),
                         r=["bdiag", "r_tmpA"], w=[f"rpsB{pb}"])
                    P.op("act", lambda e: e.activation(out=T["E4"][:, cs_], in_=psP[pb][:, 0:CC], func=AF.Sqrt),
                         r=[f"rpsB{pb}"], w=["r_E4"])
                P.op("dve", lambda e: e.tensor_scalar(out=T["E4"], in0=T["E4"], scalar1=1e-12, scalar2=None, op0=ALU.max),
                     r=["r_E4"], w=["r_E4"])
                P.op("dve", lambda e: e.reciprocal(out=T["E4"], in_=T["E4"]), r=["r_E4"], w=["r_E4"])
                tt(T["kkn"], T["kkn"], T["E4"], ALU.mult, ["r_kkn", "r_E4"], ["r_kkn"])
                P.op("dve", lambda e: e.tensor_scalar(out=T["tmpA"], in0=T["ai"], scalar1=col("ka"), scalar2=K.omka[:, hp:hp + 1],
                                                      op0=ALU.mult, op1=ALU.add), r=["r_ai", "cols", "omka"], w=["r_tmpA"])
                tt(T["km"], T["kT"], T["tmpA"], ALU.mult, ["r_kT", "r_tmpA"], ["r_km"])
                tt(T["bb"], T["kkn"], T["ai"], ALU.mult, ["r_kkn", "r_ai"], ["r_bb"])
                P.op("dve", lambda e: e.tensor_tensor_scan(out=T["Lp"], data0=cm, data1=T["lwp"], initial=0.0, op0=ALU.mult, op1=ALU.add),
                     r=["r_cmask", "r_lwp"], w=["r_Lp"])
                tt(T["Lpe"], T["Lp"], T["lwp"], ALU.subtract, ["r_Lp", "r_lwp"], ["r_Lpe"])
                Lp3 = T["Lp"].rearrange("p (c t) -> p c t", t=64)
                tt(T["tmpA"].rearrange("p (c t) -> p c t", t=64), Lp3[:, :, 63:64].broadcast_to([128, NCH, 64]), Lp3, ALU.subtract,
                   ["r_Lp"], ["r_tmpA"])
                P.op("act", lambda e: e.activation(out=T["E1"], in_=T["Lp"], func=AF.Exp, scale=-C0), r=["r_Lp"], w=["r_E1"])
                P.op("act", lambda e: e.activation(out=T["E2"], in_=T["Lp"], func=AF.Exp, scale=C0), r=["r_Lp"], w=["r_E2"])
                P.op("act", lambda e: e.activation(out=T["E3"], in_=T["Lpe"], func=AF.Exp, scale=-C0), r=["r_Lpe"], w=["r_E3"])
                P.op("act", lambda e: e.activation(out=T["E4"], in_=T["tmpA"], func=AF.Exp, scale=-C0), r=["r_tmpA"], w=["r_E4"])
                P.op("dve", lambda e: e.scalar_tensor_tensor(out=AR[:, 0, :], in0=T["kkn"], scalar=-1.0, in1=T["E3"], op0=ALU.mult, op1=ALU.mult),
                     r=["r_kkn", "r_E3"], w=["r_AR"])
                tt(AR[:, 1, :], T["rT"], T["E1"], ALU.mult, ["r_rT", "r_E1"], ["r_AR"])
                tt(BK[:, 0, :], T["bb"], T["E2"], ALU.mult, ["r_bb", "r_E2"], ["r_BK"])
                tt(BK[:, 1, :], T["km"], T["E2"], ALU.mult, ["r_km", "r_E2"], ["r_BK"])
                tt(BKh[:, 0, :], T["bb"], T["E4"], ALU.mult, ["r_bb", "r_E4"], ["r_BKh"])
                tt(BKh[:, 1, :], T["km"], T["E4"], ALU.mult, ["r_km", "r_E4"], ["r_BKh"])
                P.op("dve", lambda e: e.scalar_tensor_tensor(out=T["tmpA"], in0=T["rT"], scalar=col("rk"), in1=T["km"], op0=ALU.mult, op1=ALU.mult),
                     r=["r_rT", "r_km", "cols", "r_tmpA"], w=["r_tmpA"])
                for cc in range(NCC):
                    cs_ = slice(cc * CC, (cc + 1) * CC)
                    pb = cc % 2
                    P.op("pe", lambda e: e.matmul(psP[pb][:, 0:CC], lhsT=K.bdiag, rhs=T["tmpA"][:, cs_], start=True, stop=True),
                         r=["bdiag", "r_tmpA"], w=[f"rpsB{pb}"])
                    tt(T["bonT"][:, cs_], psP[pb][:, 0:CC], T["vT"][:, cs_], ALU.mult, [f"rpsB{pb}", "r_vT"], ["r_bonT"])
                done = [0, 0]

                def chain1(c, par, slot):
                    pr = slice(64 * par, 64 * par + 64)
                    cs_ = slice(c * 64, (c + 1) * 64)
                    B_, nB = psB[slot], f"rpsB{slot}"
                    sfx = f"_{par}_{c}"
                    idm = K.identf[pr, pr]
                    for q_, src, sn in ((0, T["vT"][pr, cs_], "r_vT"), (1, BKh[pr, 0, cs_], "r_BKh"), (2, BKh[pr, 1, cs_], "r_BKh")):
                        P.op("pe", lambda e: e.matmul(B_[pr, 320 + 64 * q_:384 + 64 * q_], lhsT=src, rhs=idm, start=True, stop=True),
                             r=[sn, "identf"], w=[nB])
                    P.op("pe", lambda e: e.matmul(B_[pr, 0:128], lhsT=BK[pr, 0, cs_], rhs=AR[pr, :, cs_], start=True, stop=True),
                         r=["r_BK", "r_AR"], w=[nB])
                    P.op("pe", lambda e: e.matmul(B_[pr, 128:256], lhsT=BK[pr, 1, cs_], rhs=AR[pr, :, cs_], start=True, stop=True),
                         r=["r_BK", "r_AR"], w=[nB])
                    P.op("pe", lambda e: e.matmul(B_[pr, 256:320], lhsT=AR[pr, 0, cs_], rhs=BK[pr, 0, cs_], start=True, stop=True),
                         r=["r_BK", "r_AR"], w=[nB])
                    yield
                    P.op("act", lambda e: e.activation(out=tms[pr, c, :], in_=B_[pr, 320:512], func=AF.Copy), r=[nB], w=["r_tms" + sfx])
                    tt(MAKAs[pr, c, :].rearrange("p (a t) -> p a t", a=2), B_[pr, 0:256].rearrange("p (a t) -> p a t", a=2),
                       u2[pr, :].unsqueeze(1).broadcast_to([64, 2, 128]), ALU.mult, [nB, "r_u2"], ["r_MAKA" + sfx])
                    tt(Qs[slot][0][pr], B_[pr, 256:320], sl[pr, :], ALU.mult, [nB, "r_sl"], [f"r_Q0s{slot}"])
                    tt(TTk[slot][0][pr], MAKAs[pr, c, 0:64], idm, ALU.add, ["r_MAKA" + sfx, "identf"], [f"r_TT0s{slot}"])
                    yield
                    p_prev, np_prev = MAKAs[pr, c, 0:64], "r_MAKA" + sfx
                    q_prev, nq_prev = Qs[slot][0][pr], f"r_Q0s{slot}"
                    for kq in range(1, 6):
                        pp = kq % 2
                        if kq < 5:
                            P.op("pe", lambda e: e.matmul(B_[pr, 0:64], lhsT=q_prev, rhs=p_prev, start=True, stop=True),
                                 r=[nq_prev, np_prev], w=[nB])
                        P.op("pe", lambda e: e.matmul(B_[pr, 64:128], lhsT=p_prev, rhs=q_prev, start=True, stop=True),
                             r=[nq_prev, np_prev], w=[nB])
                        yield
                        if kq < 5:
                            P.op("act", lambda e: e.activation(out=Ps[slot][pp][pr], in_=B_[pr, 0:64], func=AF.Copy),
                                 r=[nB], w=[f"r_P{pp}s{slot}"])
                        P.op("act", lambda e: e.activation(out=Qs[slot][pp][pr], in_=B_[pr, 64:128], func=AF.Copy),
                             r=[nB], w=[f"r_Q{pp}s{slot}"])
                        yield
                        t_old, nt_old = TTk[slot][(kq - 1) % 2][pr], f"r_TT{(kq - 1) % 2}s{slot}"
                        P.op("pe", lambda e: e.matmul(B_[pr, 128:192], lhsT=Qs[slot][pp][pr], rhs=t_old, start=True, stop=True),
                             r=[f"r_Q{pp}s{slot}", nt_old], w=[nB])
                        yield
                        if kq < 5:
                            tt(TTk[slot][kq % 2][pr], B_[pr, 128:192], t_old, ALU.add, [nB, nt_old], [f"r_TT{kq % 2}s{slot}"])
                        else:
                            tt(TTs[pr, c, :], B_[pr, 128:192], t_old, ALU.add, [nB, nt_old], ["r_TTs" + sfx])
                        yield
                        p_prev, np_prev = Ps[slot][pp][pr], f"r_P{pp}s{slot}"
                        q_prev, nq_prev = Qs[slot][pp][pr], f"r_Q{pp}s{slot}"
                    done[par] = max(done[par], c + 1)

                def chain2(par):
                    pr = slice(64 * par, 64 * par + 64)
                    X_, nX = psX[par], f"rpsX{par}"
                    for c in range(NCH):
                        while done[par] < min(NCH, c + 2):
                            yield
                        gi = sg * NCH + c
                        cs_ = slice(c * 64, (c + 1) * 64)
                        sfx = f"_{par}_{c}"
                        s_old, s_new = S0[gi % 2], S0[(gi + 1) % 2]
                        ns_old, ns_new = f"r_S{gi % 2}_{par}", f"r_S{(gi + 1) % 2}_{par}"
                        Vh, Bh, Kh = tms[pr, c, 0:64], tms[pr, c, 64:128], tms[pr, c, 128:192]
                        ArbT, AakT, ArkT = MAKAs[pr, c, 64:128], MAKAs[pr, c, 128:192], MAKAs[pr, c, 192:256]
                        P.op("pe", lambda e: e.matmul(X_[pr, 0:64], lhsT=AR[pr, 0, cs_], rhs=s_old[pr], start=True, stop=False),
                             r=["r_AR", ns_old], w=[nX])
                        P.op("pe", lambda e: e.matmul(X_[pr, 0:64], lhsT=AakT, rhs=Vh, start=False, stop=True),
                             r=["r_MAKA" + sfx, "r_tms" + sfx], w=[nX])
                        yield
                        P.op("act", lambda e: e.activation(out=Xs[pr], in_=X_[pr, 0:64], func=AF.Copy), r=[nX], w=[f"r_X_{par}"])
                        yield
                        P.op("pe", lambda e: e.matmul(X_[pr, 64:128], lhsT=TTs[pr, c, :], rhs=Xs[pr], start=True, stop=True),
                             r=["r_TTs" + sfx, f"r_X_{par}"], w=[nX])
                        yield
                        P.op("act", lambda e: e.activation(out=Us[pr], in_=X_[pr, 64:128], func=AF.Copy), r=[nX], w=[f"r_U_{par}"])
                        yield
                        P.op("pe", lambda e: e.matmul(X_[pr, 128:192], lhsT=AR[pr, 1, cs_], rhs=s_old[pr], start=True, stop=False),
                             r=["r_AR", ns_old], w=[nX])
                        P.op("pe", lambda e: e.matmul(X_[pr, 128:192], lhsT=ArbT, rhs=Us[pr], start=False, stop=False),
                             r=["r_MAKA" + sfx, f"r_U_{par}"], w=[nX])
                        P.op("pe", lambda e: e.matmul(X_[pr, 128:192], lhsT=ArkT, rhs=Vh, start=False, stop=True),
                             r=["r_MAKA" + sfx, "r_tms" + sfx], w=[nX])
                        P.op("pe", lambda e: e.matmul(X_[pr, 192:256], lhsT=Bh, rhs=Us[pr], start=True, stop=False),
                             r=["r_tms" + sfx, f"r_U_{par}"], w=[nX])
                        P.op("pe", lambda e: e.matmul(X_[pr, 192:256], lhsT=Kh, rhs=Vh, start=False, stop=True),
                             r=["r_tms" + sfx], w=[nX])
                        yield
                        P.op("dve", lambda e: e.scalar_tensor_tensor(out=s_new[pr], in0=s_old[pr], scalar=T["E1"][pr, c * 64 + 63:c * 64 + 64],
                                                                     in1=X_[pr, 192:256], op0=ALU.mult, op1=ALU.add),
                             r=[ns_old, "r_E1", nX], w=[ns_new])
                        P.op("dve", lambda e: e.tensor_copy(out=Yall[pr, c, :], in_=X_[pr, 128:192]), r=[nX], w=["r_Yall"])
                        yield

                pending = [(c, par) for c in range(NCH) for par in range(2)]
                free_slots = list(range(NSLOT))
                active = []
                for par in range(2):
                    active.append((chain2(par), None))
                while pending or active:
                    while pending and free_slots:
                        c, par = pending.pop(0)
                        sl_ = free_slots.pop(0)
                        active.append((chain1(c, par, sl_), sl_))
                    for g in list(active):
                        try:
                            next(g[0])
                        except StopIteration:
                            active.remove(g)
                            if g[1] is not None:
                                free_slots.append(g[1])
                P.op("dve", lambda e: e.tensor_reduce(out=gst[:, 0, :], in_=Yall, axis=AX.X, op=ALU.add), r=["r_Yall"], w=["r_gst"])
                P.op("dve", lambda e: e.tensor_scalar(out=gst[:, 1, :], in0=gst[:, 0, :], scalar1=-1.0 / 64, scalar2=None, op0=ALU.mult),
                     r=["r_gst"], w=["r_gst"])
                tt(Yall, Yall, gst[:, 1, :].unsqueeze(2).broadcast_to([128, NCH, 64]), ALU.add, ["r_Yall", "r_gst"], ["r_Yall"])
                tt(Ysq, Yall, Yall, ALU.mult, ["r_Yall"], ["r_Ysq"])
                P.op("dve", lambda e: e.tensor_reduce(out=gst[:, 2, :], in_=Ysq, axis=AX.X, op=ALU.add), r=["r_Ysq"], w=["r_gst"])
                P.op("dve", lambda e: e.tensor_scalar(out=gst[:, 3, :], in0=gst[:, 2, :], scalar1=1.0 / 64, scalar2=GN_EPS, op0=ALU.mult, op1=ALU.add),
                     r=["r_gst"], w=["r_gst"])
                P.op("act", lambda e: e.activation(out=gst[:, 3, :], in_=gst[:, 3, :], func=AF.Sqrt), r=["r_gst"], w=["r_gst"])
                P.op("dve", lambda e: e.reciprocal(out=gst[:, 3, :], in_=gst[:, 3, :]), r=["r_gst"], w=["r_gst"])
                tt(Yall, Yall, gst[:, 3, :].unsqueeze(2).broadcast_to([128, NCH, 64]), ALU.mult, ["r_Yall", "r_gst"], ["r_Yall"])
                for c in range(NCH):
                    for par in range(2):
                        pr = slice(64 * par, 64 * par + 64)
                        slot = (2 * c + par) % NSLOT
                        P.op("pe", lambda e: e.matmul(psB[slot][pr, 0:64], lhsT=Yall[pr, c, :], rhs=K.identf[pr, pr], start=True, stop=True),
                             r=["r_Yall", "identf"], w=[f"rpsB{slot}"])
                        P.op("act", lambda e: e.activation(out=T["OG"][pr, c * 64:(c + 1) * 64], in_=psB[slot][pr, 0:64], func=AF.Identity,
                                                           scale=col("lg")[pr], bias=col("lb")[pr]),
                             r=[f"rpsB{slot}", "cols"], w=["r_OG"])
                tt(T["OG"], T["OG"], T["bonT"], ALU.add, ["r_OG", "r_bonT"], ["r_OG"])
                tt(ogb, T["OG"], gz, ALU.mult, ["r_OG", "r_gz"], ["r_ogb"])
                P.dma("sp", K.yagT[hc, sc], ogb, r=["r_ogb"], w=["yagT"])
        P.barrier()


def phase_dsa(K, l):
    from contextlib import ExitStack
    nc, P, W, S, NT, TOPK = K.nc, K.P, K.L[l], K.S, K.NT, K.TOPK
    NIT = 20
    with ExitStack() as st:
        def sbt(name, shape, dt=F32):
            return st.enter_context(_sbm(nc, name, list(shape), dt)).ap()
        latS = sbt("d_lat", [128, S], BF16)
        ki2 = sbt("d_ki2", [128, S])
        vxs = sbt("d_vxs", [128, NT, 520], BF16)
        wis = sbt("d_wis", [128, NT * 8])
        EBM = [sbt(f"d_EBM{i}", [128, 8, 128]) for i in range(2)]
        b31 = sbt("d_b31", [128, 8, 128])
        qij = [sbt(f"d_qij{i}", [128, 4, 128]) for i in range(3)]
        qab = [sbt(f"d_qab{i}", [128, 8, 128], BF16) for i in range(3)]
        zbj = [sbt(f"d_zbj{i}", [128, 4, 128], BF16) for i in range(3)]
        accs = [sbt(f"d_acc{i}", [128, S]) for i in range(2)]
        maskfs = [sbt(f"d_maskf{i}", [128, S], BF16) for i in range(2)]
        maskT = [sbt(f"d_maskT{i}", [128, NT, 128], BF16) for i in range(3)]
        Pt = [sbt(f"d_Pt{i}", [128, 8, 128], BF16) for i in range(3)]
        rtmp = [sbt(f"d_rt{i}", [128, 512]) for i in range(3)]
        m8s = [sbt(f"d_m8{i}", [128, 8]) for i in range(2)]
        bss = [sbt(f"d_bs{i}", [128, 8]) for i in range(2)]
        p2 = sbt("d_p2", [128, NIT + 1])
        wcolss = [sbt(f"d_wcols{i}", [128, NIT + 1]) for i in range(2)]
        rec = sbt("d_rec", [128, 8])
        yb = sbt("d_yb", [128, 512])
        ybg = [sbt(f"d_ybg{i}", [128, 4, 128], BF16) for i in range(2)]
        psI = [st.enter_context(_psm(nc, f"dpsI{i}", [128, 512], F32)).ap() for i in range(3)]
        psM = st.enter_context(_psm(nc, "dpsM", [128, 1024], BF16)).ap()
        psQ = [st.enter_context(_psm(nc, f"dpsQ{i}", [128, 512], F32)).ap() for i in range(2)]
        psO = [st.enter_context(_psm(nc, f"dpsO{i}", [128, 512], F32)).ap() for i in range(2)]

        P.dma("sp", latS, K.latT, r=["latT"], w=["d_lat"])
        P.dma("sp", ki2[0:64, :], K.kiT, r=["kiT"], w=["d_ki2"])
        P.dma("sp", ki2[64:128, :], K.kiT, r=["kiT"], w=["d_ki2"])
        for i in range(NT):
            P.dma("sp", vxs[:, i, :], K.vext[i], r=["vext"], w=["d_vxs"])
        P.dma("sp", wis, K.witm, r=["witm"], w=["d_wis"])
        P.dma("sp", b31, K.eb31, w=["d_b31"])
        for kd in range(2):
            P.dma("sp", EBM[kd], K.ebias[kd], w=[f"d_EBM{kd}"])
            P.op("dve", lambda e: e.tensor_tensor(out=EBM[kd], in0=EBM[kd], in1=b31, op=ALU.subtract), r=[f"d_EBM{kd}", "d_b31"], w=[f"d_EBM{kd}"])
            P.op("act", lambda e: e.activation(out=EBM[kd], in_=EBM[kd], func=AF.Exp), r=[f"d_EBM{kd}"], w=[f"d_EBM{kd}"])
        P.op("pool", lambda e: e.affine_select(out=EBM[0], in_=EBM[0], pattern=[[0, 8], [1, 128]], compare_op=ALU.is_ge, fill=K.reg_zero,
                                               base=0, channel_multiplier=-1), r=["d_EBM0"], w=["d_EBM0"])
        for n in range(NIT + 1):
            P.op("pool", lambda e: e.memset(p2[:, n:n + 1], 2.0 ** -(n + 1)), w=["d_p2"])
        icount = [0]
        it_done = set()

        def chain_it(j):
            b = j % 3
            u = j % 2
            acc, nacc = accs[u], f"d_acc{u}"
            maskf, nmf = maskfs[u], f"d_maskf{u}"
            m8, nm8 = m8s[u], f"d_m8{u}"
            bs, nbs = bss[u], f"d_bs{u}"
            wcols, nwc = wcolss[u], f"d_wcols{u}"
            jc = slice(j * 128, (j + 1) * 128)
            Lk = (j + 1) * 128
            mT, nmT = maskT[b], f"d_maskT{b}"
            P.dma("sp", qij[b], K.qiT.rearrange("(hq p) t -> p hq t", p=128)[:, :, jc], r=["qiT"], w=[f"d_qij{b}"])
            P.dma("sp", qab[b], K.qabsT.rearrange("h c t -> c h t")[:, :, jc], r=["qabsT"], w=[f"d_qab{b}"])
            P.dma("sp", zbj[b], K.szbT.rearrange("(fb p) t -> p fb t", p=128)[:, :, jc], r=["szbT"], w=[f"d_zbj{b}"])
            if Lk <= TOPK:
                P.op("pool", lambda e: e.memset(mT[:, 0:j + 1, :], 1.0), w=[nmT])
                it_done.add(j)
                return
            for k0 in range(0, Lk, 512):
                k1 = min(Lk, k0 + 512)
                wd = k1 - k0
                for h in range(8):
                    par, hq = h % 2, h // 2
                    pr = slice(64 * par, 64 * par + 64)
                    ib = icount[0] % 3
                    icount[0] += 1
                    P.op("pe", lambda e: e.matmul(psI[ib][:, 0:wd], lhsT=qij[b][pr, hq, :], rhs=ki2[pr, k0:k1], start=True, stop=True),
                         r=[f"d_qij{b}", "d_ki2"], w=[f"dpsI{ib}"])
                    P.op("act", lambda e: e.activation(out=rtmp[ib][:, 0:wd], in_=psI[ib][:, 0:wd], func=AF.Relu),
                         r=[f"dpsI{ib}"], w=[f"d_rt{ib}"])
                    wcol = wis[:, j * 8 + h:j * 8 + h + 1]
                    if h == 0:
                        P.op("dve", lambda e: e.tensor_scalar(out=acc[:, k0:k1], in0=rtmp[ib][:, 0:wd], scalar1=wcol, scalar2=None, op0=ALU.mult),
                             r=[f"d_rt{ib}", "d_wis"], w=[nacc])
                    else:
                        P.op("dve", lambda e: e.scalar_tensor_tensor(out=acc[:, k0:k1], in0=rtmp[ib][:, 0:wd], scalar=wcol, in1=acc[:, k0:k1],
                                                                     op0=ALU.mult, op1=ALU.add), r=[f"d_rt{ib}", "d_wis", nacc], w=[nacc])
                    yield
            P.op("pool", lambda e: e.affine_select(out=acc[:, jc], in_=acc[:, jc], pattern=[[-1, 128]], compare_op=ALU.is_ge, fill=K.reg_neg,
                                                   base=0, channel_multiplier=1), r=[nacc], w=[nacc])
            lo0, w0, mid, cnt, sg = (bs[:, i:i + 1] for i in range(5))
            P.op("dve", lambda e: e.max(out=m8, in_=acc[:, 0:Lk]), r=[nacc], w=[nm8])
            P.op("dve", lambda e: e.tensor_reduce(out=lo0, in_=acc[:, 0:j * 128], axis=AX.X, op=ALU.min), r=[nacc], w=[nbs])
            yield
            P.op("dve", lambda e: e.tensor_tensor(out=w0, in0=m8[:, 0:1], in1=lo0, op=ALU.subtract), r=[nm8, nbs], w=[nbs])
            P.op("dve", lambda e: e.tensor_scalar(out=wcols, in0=p2, scalar1=w0, scalar2=None, op0=ALU.mult), r=["d_p2", nbs], w=[nwc])
            P.op("dve", lambda e: e.tensor_tensor(out=mid, in0=lo0, in1=wcols[:, 0:1], op=ALU.add), r=[nbs, nwc], w=[nbs])
            Ld = max(128, (int(0.56 * Lk) // 128) * 128)
            nA = Lk - Ld
            sgn, tot = bs[:, 5:6], bs[:, 6:7]
            for n in range(NIT):
                P.op("dve", lambda e: e.tensor_scalar(out=maskf[:, 0:Ld], in0=acc[:, 0:Ld], scalar1=mid, scalar2=None, op0=ALU.is_ge, op1=ALU.add,
                                                      accum_out=cnt), r=[nacc, nbs], w=[nmf, nbs + "c"])
                P.op("act", lambda e: e.activation(out=maskf[:, Ld:Lk], in_=acc[:, Ld:Lk], func=AF.Sign, scale=-1.0, bias=mid, accum_out=sgn),
                     r=[nacc, nbs], w=[nmf + "a", nbs + "s"])
                P.op("dve", lambda e: e.scalar_tensor_tensor(out=tot, in0=sgn, scalar=-0.5, in1=cnt, op0=ALU.mult, op1=ALU.add),
                     r=[nbs + "c", nbs + "s"], w=[nbs + "t"])
                P.op("dve", lambda e: e.tensor_scalar(out=sg, in0=tot, scalar1=TOPK - 0.5 - 0.5 * nA, scalar2=wcols[:, n:n + 1], op0=ALU.is_ge, op1=ALU.mult),
                     r=[nbs + "t", nwc], w=[nbs + "g"])
                sub = wcols[:, n + 1:n + 2] if n + 1 < NIT else wcols[:, n:n + 1]
                P.op("dve", lambda e: e.scalar_tensor_tensor(out=mid, in0=sg, scalar=sub, in1=mid, op0=ALU.subtract, op1=ALU.add),
                     r=[nbs + "g", nbs, nwc], w=[nbs])
                yield
            P.op("dve", lambda e: e.tensor_scalar(out=maskf[:, 0:Lk], in0=acc[:, 0:Lk], scalar1=mid, scalar2=None, op0=ALU.is_ge),
                 r=[nacc, nbs], w=[nmf, nmf + "a"])
            yield
            yield
            yield
            for i0 in range(0, j + 1, 8):
                i1 = min(j + 1, i0 + 8)
                for i in range(i0, i1):
                    P.op("pe", lambda e: e.transpose(out=psM[:, (i - i0) * 128:(i - i0 + 1) * 128], in_=maskf[:, i * 128:(i + 1) * 128],
                                                     identity=K.identb), r=[nmf, "identb"], w=["dpsM"])
                P.op("act", lambda e: e.activation(out=mT[:, i0:i1, :], in_=psM[:, 0:(i1 - i0) * 128].rearrange("p (i t) -> p i t", t=128),
                                                   func=AF.Copy), r=["dpsM"], w=[nmT])
                yield
            it_done.add(j)

        def chain_at(j):
            b = j % 3
            jc = slice(j * 128, (j + 1) * 128)
            mT, nmT = maskT[b], f"d_maskT{b}"
            while j not in it_done:
                yield

            def qk(i):
                for hh in range(2):
                    P.op("pe", lambda e: e.matmul(psQ[hh], lhsT=latS[:, i * 128:(i + 1) * 128], rhs=qab[b][:, 4 * hh:4 * hh + 4, :], start=True, stop=True),
                         r=["d_lat", f"d_qab{b}"], w=[f"dpsQ{hh}"])
            def pv(i):
                a = i % 3
                for h in range(8):
                    ob = h // 4
                    o0 = (h % 4) * 65
                    P.op("pe", lambda e: e.matmul(psO[ob][:, o0:o0 + 65], lhsT=Pt[a][:, h, :], rhs=vxs[:, i, h * 65:(h + 1) * 65],
                                                  start=(i == 0 and h % 4 == 0), stop=(i == j), skip_group_check=True),
                         r=[f"d_Pt{a}", "d_vxs"], w=[f"dpsO{ob}"])
            qk(0)
            for i in range(j + 1):
                a = i % 3
                for hh in range(2):
                    P.op("act", lambda e: e.activation(out=Pt[a][:, 4 * hh:4 * hh + 4, :], in_=psQ[hh].rearrange("p (h t) -> p h t", h=4), func=AF.Exp),
                         r=[f"dpsQ{hh}"], w=[f"d_Pt{a}"])
                kd = j - i
                if kd <= 1:
                    P.op("pool", lambda e: e.tensor_tensor(out=Pt[a], in0=Pt[a], in1=EBM[kd], op=ALU.mult), r=[f"d_Pt{a}", f"d_EBM{kd}"], w=[f"d_Pt{a}"])
                P.op("pool", lambda e: e.tensor_tensor(out=Pt[a], in0=Pt[a], in1=mT[:, i:i + 1, :].broadcast_to([128, 8, 128]), op=ALU.mult),
                     r=[f"d_Pt{a}", nmT], w=[f"d_Pt{a}"])
                if i >= 1:
                    pv(i - 1)
                if i + 1 <= j:
                    qk(i + 1)
                yield
            pv(j)
            yield
            for ob in range(2):
                o3 = psO[ob][:, 0:260].rearrange("p (h d) -> p h d", d=65)
                P.op("dve", lambda e: e.reciprocal(out=rec[:, 4 * ob:4 * ob + 4].unsqueeze(2), in_=o3[:, :, 64:65]), r=[f"dpsO{ob}"], w=["d_rec"])
                P.op("dve", lambda e: e.tensor_tensor(out=yb[:, 256 * ob:256 * ob + 256].rearrange("p (h d) -> p h d", d=64), in0=o3[:, :, 0:64],
                                                      in1=rec[:, 4 * ob:4 * ob + 4].unsqueeze(2).broadcast_to([128, 4, 64]), op=ALU.mult),
                     r=[f"dpsO{ob}", "d_rec"], w=["d_yb"])
            yield
            for fb in range(4):
                P.op("pe", lambda e: e.matmul(psI[0][:, fb * 128:(fb + 1) * 128], lhsT=yb[:, fb * 128:(fb + 1) * 128], rhs=K.identf, start=True, stop=True),
                     r=["d_yb", "identf"], w=["dpsI0"])
            P.op("dve", lambda e: e.tensor_tensor(out=ybg[j % 2], in0=psI[0].rearrange("p (f t) -> p f t", f=4), in1=zbj[b], op=ALU.mult),
                 r=["dpsI0", f"d_zbj{b}"], w=[f"d_ybg{j % 2}"])
            P.dma("sp", K.ybgT.rearrange("(fb p) t -> p fb t", p=128)[:, :, jc], ybg[j % 2], r=[f"d_ybg{j % 2}"], w=["ybgT"])

        it_gens = {}
        next_it = 0
        at_j = 0
        at_gen = None
        while at_j < NT:
            while next_it < NT and len(it_gens) < 2 and next_it <= at_j + 2:
                it_gens[next_it] = chain_it(next_it)
                next_it += 1
            for jj in sorted(it_gens):
                try:
                    next(it_gens[jj])
                except StopIteration:
                    del it_gens[jj]
            if at_gen is None and at_j in it_done:
                at_gen = chain_at(at_j)
            if at_gen is not None:
                try:
                    next(at_gen)
                except StopIteration:
                    at_gen = None
                    at_j += 1
        P.barrier()


def phase_out(K, l, xsrc, xdst):
    from contextlib import ExitStack
    nc, P, W, S, TC, NTC = K.nc, K.P, K.L[l], K.S, K.TC, K.NTC
    with ExitStack() as st:
        def sbt(name, shape, dt=F32):
            return st.enter_context(_sbm(nc, name, list(shape), dt)).ap()
        stg = sbt("o_stg", [128, 4, K.D])
        wpa = sbt("o_wpa", [128, 4, K.D], BF16)
        wpb = sbt("o_wpb", [128, 4, K.D], BF16)
        wo = sbt("o_wo", [128, 8, K.D], BF16)
        ya = [sbt(f"o_ya{i}", [128, 4, TC], BF16) for i in range(2)]
        yb = [sbt(f"o_yb{i}", [128, 4, TC], BF16) for i in range(2)]
        ga = [sbt(f"o_ga{i}", [128, 8, TC], BF16) for i in range(2)]
        gb = [sbt(f"o_gb{i}", [128, 8, TC], BF16) for i in range(2)]
        t1 = [sbt(f"o_t1{i}", [128, TC]) for i in range(2)]
        t2 = [sbt(f"o_t2{i}", [128, TC]) for i in range(2)]
        mg = sbt("o_mg", [128, 8, TC], BF16)
        xt = [sbt(f"o_xt{i}", [128, K.D]) for i in range(2)]
        xo = [sbt(f"o_xo{i}", [128, K.D]) for i in range(2)]
        psA = [st.enter_context(_psm(nc, f"opsA{i}", [128, 512], F32)).ap() for i in range(2)]
        psB = [st.enter_context(_psm(nc, f"opsB{i}", [128, 512], F32)).ap() for i in range(2)]
        psO = [st.enter_context(_psm(nc, f"opsO{i}", [128, 512], F32)).ap() for i in range(2)]
        for src, dst, dn in ((W["w_pa"], wpa, "o_wpa"), (W["w_pb"], wpb, "o_wpb")):
            P.dma("sp", stg, src.rearrange("(j p) n -> p j n", p=128), r=["o_stg"], w=["o_stg"])
            P.op("pool", lambda e: e.tensor_copy(out=dst, in_=stg), r=["o_stg"], w=[dn])
        for hf in range(2):
            P.dma("sp", stg, W["w_o"][hf * 512:(hf + 1) * 512, :].rearrange("(j p) n -> p j n", p=128), r=["o_stg"], w=["o_stg"])
            P.op("pool", lambda e: e.tensor_copy(out=wo[:, 4 * hf:4 * hf + 4, :], in_=stg), r=["o_stg"], w=["o_wo"])
        gate = K.modB[:, 2 * K.D:3 * K.D]
        cnt = 0
        xcnt = 0
        def load_chunk(tc):
            b = tc % 2
            tcs = slice(tc * TC, (tc + 1) * TC)
            P.dma("sp", ya[b], K.yagT.rearrange("(j p) t -> p j t", p=128)[:, :, tcs], r=["yagT"], w=[f"o_ya{b}"])
            P.dma("sp", yb[b], K.ybgT.rearrange("(j p) t -> p j t", p=128)[:, :, tcs], r=["ybgT"], w=[f"o_yb{b}"])
            P.dma("sp", ga[b], K.sgaT.rearrange("(j p) t -> p j t", p=128)[:, :, tcs], r=["sgaT"], w=[f"o_ga{b}"])
            P.dma("sp", gb[b], K.sgbT.rearrange("(j p) t -> p j t", p=128)[:, :, tcs], r=["sgbT"], w=[f"o_gb{b}"])

        def load_x(k):
            if k < S // 128:
                P.dma("sp", xt[k % 2], xsrc[k * 128:(k + 1) * 128, :], w=[f"o_xt{k % 2}"])
        load_chunk(0)
        load_x(0)
        for tc in range(NTC):
            b = tc % 2
            tcs = slice(tc * TC, (tc + 1) * TC)
            if tc + 1 < NTC:
                load_chunk(tc + 1)
            for ob in range(8):
                pb = cnt % 2
                cnt += 1
                oc = slice(ob * 128, (ob + 1) * 128)
                for j in range(4):
                    P.op("pe", lambda e: e.matmul(psA[pb][:, 0:TC], lhsT=wpa[:, j, oc], rhs=ya[b][:, j, :], start=(j == 0), stop=(j == 3)),
                         r=["o_wpa", f"o_ya{b}"], w=[f"opsA{pb}"])
                for j in range(4):
                    P.op("pe", lambda e: e.matmul(psB[pb][:, 0:TC], lhsT=wpb[:, j, oc], rhs=yb[b][:, j, :], start=(j == 0), stop=(j == 3)),
                         r=["o_wpb", f"o_yb{b}"], w=[f"opsB{pb}"])
                P.op("dve", lambda e: e.tensor_tensor(out=t1[pb], in0=psA[pb][:, 0:TC], in1=ga[b][:, ob, :], op=ALU.mult),
                     r=[f"opsA{pb}", f"o_ga{b}"], w=[f"o_t1{pb}"])
                P.op("dve", lambda e: e.tensor_tensor(out=t2[pb], in0=psB[pb][:, 0:TC], in1=gb[b][:, ob, :], op=ALU.mult),
                     r=[f"opsB{pb}", f"o_gb{b}"], w=[f"o_t2{pb}"])
                P.op("pool", lambda e: e.tensor_tensor(out=mg[:, ob, :], in0=t1[pb], in1=t2[pb], op=ALU.add),
                     r=[f"o_t1{pb}", f"o_t2{pb}"], w=["o_mg"])
            for ts in range(TC // 128):
                xb_ = xcnt % 2
                xcnt += 1
                rows = slice(tc * TC + ts * 128, tc * TC + (ts + 1) * 128)
                load_x(xcnt)
                for hf in range(2):
                    hc = slice(hf * 512, (hf + 1) * 512)
                    for ob in range(8):
                        P.op("pe", lambda e: e.matmul(psO[hf], lhsT=mg[:, ob, ts * 128:(ts + 1) * 128], rhs=wo[:, ob, hc], start=(ob == 0), stop=(ob == 7)),
                             r=["o_mg", "o_wo"], w=[f"opsO{hf}"])
                    P.op("dve", lambda e: e.tensor_tensor(out=xo[xb_][:, hc], in0=psO[hf], in1=gate[:, hc], op=ALU.mult),
                         r=[f"opsO{hf}", "modB"], w=[f"o_xo{xb_}"])
                P.op("pool", lambda e: e.tensor_tensor(out=xo[xb_], in0=xo[xb_], in1=xt[xb_], op=ALU.add),
                     r=[f"o_xo{xb_}", f"o_xt{xb_}"], w=[f"o_xo{xb_}"])
                P.dma("sp", xdst[rows, :], xo[xb_], r=[f"o_xo{xb_}"], w=["xs"])
        P.barrier()


def phase_final(K, xsrc):
    from contextlib import ExitStack
    nc, P, S, NT = K.nc, K.P, K.S, K.NT
    with ExitStack() as st:
        def sbt(name, shape, dt=F32):
            return st.enter_context(_sbm(nc, name, list(shape), dt)).ap()
        xt = [sbt(f"f_xt{i}", [128, K.D]) for i in range(2)]
        xo = [sbt(f"f_xo{i}", [128, K.D]) for i in range(2)]
        sq = sbt("f_sq", [128, K.D])
        fg = sbt("f_fg", [128, K.D])
        ss = sbt("f_ss", [128, 2])
        ps = [st.enter_context(_psm(nc, f"fps{i}", [128, 512], F32)).ap() for i in range(2)]
        P.dma("sp", K.row[0:1, 0:K.D], K.final_g, r=["row"], w=["row"])
        for hf in range(2):
            P.op("pe", lambda e: e.matmul(ps[hf], lhsT=K.ones[0:1, :], rhs=K.row[0:1, hf * 512:(hf + 1) * 512], start=True, stop=True),
                 r=["ones", "row"], w=[f"fps{hf}"])
            P.op("act", lambda e: e.activation(out=fg[:, hf * 512:(hf + 1) * 512], in_=ps[hf], func=AF.Copy), r=[f"fps{hf}"], w=["f_fg"])
        for i in range(NT):
            b = i % 2
            rows = slice(i * 128, (i + 1) * 128)
            P.dma("sp", xt[b], xsrc[rows, :], r=["xs"], w=[f"f_xt{b}"])
            P.op("act", lambda e: e.activation(out=sq, in_=xt[b], func=AF.Square, accum_out=ss[:, 0:1]), r=[f"f_xt{b}"], w=["f_sq", "f_ss"])
            P.op("dve", lambda e: e.tensor_scalar(out=ss[:, 1:2], in0=ss[:, 0:1], scalar1=1.0 / K.D, scalar2=K.EPS, op0=ALU.mult, op1=ALU.add),
                 r=["f_ss"], w=["f_ss"])
            P.op("act", lambda e: e.activation(out=ss[:, 1:2], in_=ss[:, 1:2], func=AF.Sqrt), r=["f_ss"], w=["f_ss"])
            P.op("dve", lambda e: e.reciprocal(out=ss[:, 1:2], in_=ss[:, 1:2]), r=["f_ss"], w=["f_ss"])
            P.op("dve", lambda e: e.scalar_tensor_tensor(out=xo[b], in0=xt[b], scalar=ss[:, 1:2], in1=fg, op0=ALU.mult, op1=ALU.mult),
                 r=[f"f_xt{b}", "f_ss", "f_fg"], w=[f"f_xo{b}"])
            P.dma("sp", K.out[rows, :], xo[b], r=[f"f_xo{b}"], is_output=True)
        P.barrier()


_CACHE = {}


def kernel(**inputs):
    S = int(np.asarray(inputs["x"]).shape[1])
    B = int(np.asarray(inputs["x"]).shape[0])
    depth = int(np.asarray(inputs["w_in"]).shape[0])
    topk = min(256, S // 4)
    key = (S, depth, topk)
    if key not in _CACHE:
        _CACHE[key] = build(S, topk, depth=depth)[0]
    nc = _CACHE[key]
    shared = shared_maps(inputs, depth)
    in_maps = [core_map(inputs, shared, b) for b in range(B)]
    res = run_bass_kernel_spmd(nc, in_maps, core_ids=list(range(B)))
    return np.stack([np.asarray(r["out"], np.float32) for r in res.results], axis=0)
```

```python
import numpy as np
import concourse.bass as bass
import concourse.mybir as mybir
from concourse.bass_utils import run_bass_kernel_spmd

F32 = mybir.dt.float32
BF16 = mybir.dt.bfloat16
AF = mybir.ActivationFunctionType
ALU = mybir.AluOpType
AX = mybir.AxisListType


class Prog:
    NDSEM = 14
    PSUM_PREFIXES = ("mps", "pps", "pst", "ppq", "rps", "dps", "ops", "fps")

    def __init__(self, nc):
        self.nc = nc
        self.eng = dict(pe=nc.tensor, act=nc.scalar, dve=nc.vector, pool=nc.gpsimd, sp=nc.sync)
        self.csem = {k: nc.alloc_semaphore(name=f"c_{k}") for k in self.eng}
        self.cnt = {k: 0 for k in self.eng}
        self.seen = {k: {} for k in self.eng}
        self.lastw = {}
        self.readers = {}
        self.dsem = {q: [nc.alloc_semaphore(name=f"d_{q}{i}") for i in range(self.NDSEM)]
                     for q in ("sp", "pool", "act")}
        self.dcnt = {q: [0] * self.NDSEM for q in self.dsem}
        self.drr = {q: 0 for q in self.dsem}
        self.uid = 0
        self.out_tokens = []
        self.ninstr = 0

    def _wait(self, e, tok):
        sem, val = tok
        if self.seen[e].get(sem.num, 0) >= val:
            return
        self.eng[e].wait_ge(sem, val)
        self.seen[e][sem.num] = val

    def _deps(self, e, r, w):
        for b in r:
            lw = self.lastw.get(b)
            if lw is not None and not (lw[0] == e == "pe"):
                self._wait(e, lw[1])
            if b.startswith(self.PSUM_PREFIXES):
                for re_, tok in self.readers.get(b, {}).items():
                    if re_ != e:
                        self._wait(e, tok)
        for b in w:
            lw = self.lastw.get(b)
            if lw is not None and not (lw[0] == e == "pe"):
                self._wait(e, lw[1])
            for re_, tok in self.readers.get(b, {}).items():
                if re_ != e or e == "pool":
                    self._wait(e, tok)

    def _commit(self, key, tok, r, w):
        for b in r:
            self.readers.setdefault(b, {})[key] = tok
        for b in w:
            self.lastw[b] = (key, tok)
            self.readers[b] = {}

    def op(self, e, fn, r=(), w=()):
        self._deps(e, r, w)
        ins = fn(self.eng[e])
        self.cnt[e] += 1
        ins.then_inc(self.csem[e], 1)
        self._commit(e, (self.csem[e], self.cnt[e]), r, w)
        self.ninstr += 1

    def dma(self, q, out, in_, r=(), w=(), is_output=False, **kw):
        i = self.drr[q]
        self.drr[q] = (i + 1) % self.NDSEM
        sem = self.dsem[q][i]
        if self.dcnt[q][i] > 0:
            self._wait(q, (sem, self.dcnt[q][i]))
        self._deps(q, r, w)
        ins = self.eng[q].dma_start(out=out, in_=in_, **kw)
        self.dcnt[q][i] += 16
        ins.then_inc(sem, 16)
        tok = (sem, self.dcnt[q][i])
        self.uid += 1
        self._commit(f"dma{self.uid}", tok, r, w)
        if is_output:
            self.out_tokens.append(tok)
        self.ninstr += 1

    def barrier(self):
        for e in self.eng:
            for o in self.eng:
                if o != e and self.cnt[o] > 0:
                    self._wait(e, (self.csem[o], self.cnt[o]))
            for q in self.dsem:
                for i, sem in enumerate(self.dsem[q]):
                    if self.dcnt[q][i] > 0:
                        self._wait(e, (sem, self.dcnt[q][i]))

    def finish(self):
        for q in self.dsem:
            for i, sem in enumerate(self.dsem[q]):
                if self.dcnt[q][i] > 0:
                    self._wait("sp", (sem, self.dcnt[q][i]))
        for e in ("pe", "act", "dve", "pool"):
            if self.cnt[e] > 0:
                self._wait("sp", (self.csem[e], self.cnt[e]))


D = 1024
NIN = 5960
C_R, C_K, C_V, C_WD, C_AD, C_ZA, C_Q, C_CKV, C_ZB, C_QI, C_KI, C_WI, C_GA, C_GB = (
    0, 512, 1024, 1536, 1600, 1664, 2176, 2688, 2816, 3328, 3840, 3904, 3912, 4936)
C0 = 0.6065306597126334
NEG = -1.0e30

COLS = dict(mu=(0, 13), w0=(13, 4), a0=(17, 4), kk=(21, 4), ka=(25, 4), rk=(29, 4),
            lg=(33, 4), lb=(37, 4), kvg=(41, 1))
NCOL = 42


def t5_bucket_np(dist):
    import math
    max_exact = 16
    d = np.maximum(dist, 1).astype(np.float32)
    large = max_exact + (np.log(d / np.float32(max_exact)) / np.float32(math.log(128 / max_exact))
                         * np.float32(32 - max_exact)).astype(np.int32)
    large = np.minimum(large, 31)
    return np.where(dist < max_exact, dist, large)


class Ctx:
    pass


_UNIQ = [0]


def _sbm(nc, name, shape, dt):
    _UNIQ[0] += 1
    return nc.sbuf_tensor(f"{name}_u{_UNIQ[0]}", shape, dt)


def _psm(nc, name, shape, dt):
    _UNIQ[0] += 1
    return nc.psum_tensor(f"{name}_u{_UNIQ[0]}", shape, dt)


def build(S, TOPK, depth=2, dbg=(), stop_after=None, stop_layer=0):
    from contextlib import ExitStack
    nc = bass.Bass("TRN2", target_bir_lowering=False)
    P = Prog(nc)
    NT = S // 128
    TC = min(512, S)
    NTC = S // TC
    EPS = 1e-6

    def din(name, shape, dt=F32):
        return nc.dram_tensor(name, list(shape), dt, kind="ExternalInput").ap()

    def scr(name, shape, dt=F32):
        kind = "ExternalOutput" if name in dbg else "Internal"
        return nc.dram_tensor(name, list(shape), dt, kind=kind).ap()

    x_in = din("x", [S, D])
    cT = din("cT", [128, 8])
    final_g = din("final_g", [1, D])
    ebias = din("ebias", [2, 128, 8, 128])
    eb31 = din("eb31", [128, 8, 128])
    L = []
    for l in range(depth):
        L.append(dict(
            ada_w=din(f"ada_w{l}", [D, 3 * D]), ada_b=din(f"ada_b{l}", [1, 3 * D]),
            norm_g=din(f"norm_g{l}", [1, D]), w_in=din(f"w_in{l}", [D, NIN]),
            cols=din(f"cols{l}", [128, NCOL]), w2=din(f"w2{l}", [64, 512]), a2=din(f"a2{l}", [64, 512]),
            wukT=din(f"wukT{l}", [128, 4, 128]), wuv=din(f"wuv{l}", [128, 512]),
            w_pa=din(f"w_pa{l}", [512, D]), w_pb=din(f"w_pb{l}", [512, D]), w_o=din(f"w_o{l}", [D, D])))
    out = nc.dram_tensor("out", [S, D], F32, kind="ExternalOutput").ap()
    xs = [scr(f"xs{i}", [S, D]) for i in range(2)]
    psT = scr("psT", [1664, S])
    szaT = scr("szaT", [512, S], BF16)
    szbT = scr("szbT", [512, S], BF16)
    sgaT = scr("sgaT", [D, S], BF16)
    sgbT = scr("sgbT", [D, S], BF16)
    qabsT = scr("qabsT", [8, 128, S], BF16)
    latT = scr("latT", [128, S], BF16)
    vext = scr("vext", [NT, 128, 520], BF16)
    qiT = scr("qiT", [512, S])
    kiT = scr("kiT", [64, S])
    witm = scr("witm", [128, NT * 8])
    yagT = scr("yagT", [512, S], BF16)
    ybgT = scr("ybgT", [512, S], BF16)

    def sb(name, shape, dt=F32):
        return nc.alloc_sbuf_tensor(name, list(shape), dt).ap()

    ones = sb("ones", [128, 128])
    identf = sb("identf", [128, 128])
    identb = sb("identb", [128, 128], BF16)
    bdiag = sb("bdiag", [128, 128])
    u2m = sb("u2m", [64, 128])
    slm = sb("slm", [64, 64])
    cact = sb("cact", [128, 8])
    cbc = sb("cbc", [128, 8, 128])
    modB = sb("modB", [128, 3 * D])
    Gt = sb("Gt", [128, D])
    cols = sb("cols", [128, NCOL])
    omka = sb("omka", [128, 4])
    row = sb("row", [1, 3 * D])

    reg_zero = nc.gpsimd.to_reg(0.0)
    reg_neg = nc.gpsimd.to_reg(NEG)
    P.op("pool", lambda e: e.memset(ones, 1.0), w=["ones"])
    P.op("pool", lambda e: e.affine_select(out=identf, in_=ones, pattern=[[-1, 128]], compare_op=ALU.is_equal,
                                           fill=reg_zero, base=0, channel_multiplier=1), r=["ones"], w=["identf"])
    P.op("pool", lambda e: e.tensor_copy(out=identb, in_=identf), r=["identf"], w=["identb"])
    P.op("pool", lambda e: e.memset(bdiag, 0.0), w=["bdiag"])
    P.op("pool", lambda e: e.memset(bdiag[0:64, 0:64], 1.0), w=["bdiag"])
    P.op("pool", lambda e: e.memset(bdiag[64:128, 64:128], 1.0), w=["bdiag"])
    P.op("pool", lambda e: e.affine_select(out=u2m[:, 0:64], in_=ones[0:64, 0:64], pattern=[[1, 64]], compare_op=ALU.is_ge,
                                           fill=reg_zero, base=-1, channel_multiplier=-1), r=["ones"], w=["u2m"])
    P.op("pool", lambda e: e.affine_select(out=u2m[:, 64:128], in_=ones[0:64, 0:64], pattern=[[1, 64]], compare_op=ALU.is_ge,
                                           fill=reg_zero, base=0, channel_multiplier=-1), r=["ones"], w=["u2m"])
    P.op("pool", lambda e: e.affine_select(out=slm, in_=ones[0:64, 0:64], pattern=[[-1, 64]], compare_op=ALU.is_ge,
                                           fill=reg_zero, base=-1, channel_multiplier=1), r=["ones"], w=["slm"])
    P.dma("sp", cact, cT, w=["cact"])
    P.op("act", lambda e: e.activation(out=cact, in_=cact, func=AF.Silu), r=["cact"], w=["cact"])
    for j in range(8):
        P.op("dve", lambda e, j=j: e.tensor_scalar(out=cbc[:, j, :], in0=ones, scalar1=cact[:, j:j + 1], scalar2=None,
                                                   op0=ALU.mult), r=["cact", "ones"], w=["cbc"])

    K = Ctx()
    K.__dict__.update(locals())
    K.D = D
    K.COLS = COLS
    for l in range(depth):
        xsrc = x_in if l == 0 else xs[(l - 1) % 2]
        xdst = xs[l % 2]
        phase_mod(K, l)
        if stop_after == "mod" and l == stop_layer:
            break
        phase_proj(K, l, xsrc)
        if stop_after == f"B{l}":
            break
        if stop_after == "proj" and l == stop_layer:
            break
        phase_rwkv(K, l)
        if stop_after == "rwkv" and l == stop_layer:
            break
        phase_dsa(K, l)
        if stop_after == "dsa" and l == stop_layer:
            break
        phase_out(K, l, xsrc, xdst)
        if stop_after == "out" and l == stop_layer:
            break
    else:
        phase_final(K, xs[(depth - 1) % 2])
    P.finish()
    return nc, P


def phase_mod(K, l):
    from contextlib import ExitStack
    nc, P, W = K.nc, K.P, K.L[l]
    with ExitStack() as st:
        wb = [st.enter_context(_sbm(nc, f"mw{i}", [128, 8, 512], F32)).ap() for i in range(2)]
        ps = [st.enter_context(_psm(nc, f"mps{i}", [128, 512], F32)).ap() for i in range(2)]
        P.dma("sp", K.cols, W["cols"], w=["cols"])
        P.dma("sp", K.row, W["ada_b"], w=["row"])
        P.op("dve", lambda e: e.tensor_scalar(out=K.omka, in0=K.cols[:, 25:29], scalar1=-1.0, scalar2=1.0,
                                              op0=ALU.mult, op1=ALU.add), r=["cols"], w=["omka"])
        for nb in range(6):
            b = nb % 2
            P.dma("sp", wb[b], W["ada_w"][:, nb * 512:(nb + 1) * 512].rearrange("(j p) n -> p j n", p=128), w=[f"mw{b}"])
            for j in range(8):
                P.op("pe", lambda e, j=j: e.matmul(ps[b], lhsT=K.cbc[:, j, :], rhs=wb[b][:, j, :], start=(j == 0), stop=False),
                     r=["cbc", f"mw{b}"], w=[f"mps{b}"])
            P.op("pe", lambda e: e.matmul(ps[b], lhsT=K.ones[0:1, :], rhs=K.row[0:1, nb * 512:(nb + 1) * 512], start=False, stop=True),
                 r=["ones", "row"], w=[f"mps{b}"])
            P.op("act", lambda e: e.activation(out=K.modB[:, nb * 512:(nb + 1) * 512], in_=ps[b], func=AF.Copy),
                 r=[f"mps{b}"], w=["modB"])
        P.dma("sp", K.row[0:1, 0:K.D], W["norm_g"], r=["row"], w=["row"])
        for hf in range(2):
            P.op("pe", lambda e: e.matmul(ps[hf], lhsT=K.ones[0:1, :], rhs=K.row[0:1, hf * 512:(hf + 1) * 512], start=True, stop=True),
                 r=["ones", "row"], w=[f"mps{hf}"])
            P.op("dve", lambda e: e.scalar_tensor_tensor(out=K.Gt[:, hf * 512:(hf + 1) * 512],
                                                         in0=K.modB[:, K.D + hf * 512:K.D + (hf + 1) * 512], scalar=1.0,
                                                         in1=ps[hf], op0=ALU.add, op1=ALU.mult),
                 r=["modB", f"mps{hf}"], w=["Gt"])


def proj_blocks():
    blks = []
    for i in range(13):
        blks.append((i * 128, 128, "shift", i))
    for i in range(4):
        blks.append((C_ZA + i * 128, 128, "sza", i))
    for i in range(4):
        blks.append((C_Q + i * 128, 128, "q", i))
    blks.append((C_CKV, 128, "ckv", 0))
    for i in range(4):
        blks.append((C_ZB + i * 128, 128, "szb", i))
    for i in range(8):
        blks.append((C_GA + i * 128, 128, "sga", i))
    for i in range(8):
        blks.append((C_GB + i * 128, 128, "sgb", i))
    return blks


def phase_proj(K, l, xsrc):
    from contextlib import ExitStack
    nc, P, W, S, NT, TC, NTC = K.nc, K.P, K.L[l], K.S, K.NT, K.TC, K.NTC
    with ExitStack() as st:
        def sbt(name, shape, dt=F32):
            return st.enter_context(_sbm(nc, name, list(shape), dt)).ap()
        hT = sbt("hT", [128, 8, S], BF16)
        ps = [st.enter_context(_psm(nc, f"pps{i}", [128, 512], F32)).ap() for i in range(4)]
        with ExitStack() as st2:
            def sb2(name, shape, dt=F32):
                return st2.enter_context(_sbm(nc, name, list(shape), dt)).ap()
            pst = [st2.enter_context(_psm(nc, f"pst{i}", [128, 512], F32)).ap() for i in range(2)]
            pq = [st2.enter_context(_psm(nc, f"ppq{i}", [128, 512], F32)).ap() for i in range(2)]
            xb = [sb2(f"xb{i}", [128, K.D]) for i in range(2)]
            t1s = [sb2(f"t1{i}", [128, K.D]) for i in range(2)]
            sqj = sb2("sqj", [128, K.D])
            sss = [sb2(f"ss{i}", [128, 2]) for i in range(2)]
            hTfs = [sb2(f"hTf{i}", [128, 8, 128]) for i in range(2)]
            wq = sb2("wq", [128, 8, 584])
            pqs = [sb2(f"pqs{i}", [128, 5, 128]) for i in range(2)]
            wit = sb2("wit", [128, NT * 8])
            tok = [sb2(f"tok{i}", [128, 584]) for i in range(2)]
            P.dma("sp", wq, W["w_in"][:, C_QI:C_QI + 584].rearrange("(j p) n -> p j n", p=128), w=["wq"])
            for i in range(NT):
                b = i % 2
                tcs = slice(i * 128, (i + 1) * 128)
                t1, ss, hTf = t1s[b], sss[b], hTfs[b]
                nt1, nss, nhf = f"t1{b}", f"ss{b}", f"hTf{b}"
                if i == 0:
                    P.dma("sp", xb[0], xsrc[0:128, :], w=["xb0"])
                if i + 1 < NT:
                    P.dma("sp", xb[1 - b], xsrc[(i + 1) * 128:(i + 2) * 128, :], w=[f"xb{1 - b}"])
                P.op("act", lambda e: e.activation(out=sqj, in_=xb[b], func=AF.Square, accum_out=ss[:, 0:1]),
                     r=[f"xb{b}"], w=["sqj", nss])
                P.op("dve", lambda e: e.tensor_scalar(out=ss[:, 1:2], in0=ss[:, 0:1], scalar1=1.0 / K.D, scalar2=K.EPS,
                                                      op0=ALU.mult, op1=ALU.add), r=[nss], w=[nss])
                P.op("act", lambda e: e.activation(out=ss[:, 1:2], in_=ss[:, 1:2], func=AF.Sqrt), r=[nss], w=[nss])
                P.op("dve", lambda e: e.reciprocal(out=ss[:, 1:2], in_=ss[:, 1:2]), r=[nss], w=[nss])
                P.op("dve", lambda e: e.scalar_tensor_tensor(out=t1, in0=xb[b], scalar=ss[:, 1:2], in1=K.Gt,
                                                             op0=ALU.mult, op1=ALU.mult), r=[f"xb{b}", nss, "Gt"], w=[nt1])
                P.op("dve", lambda e: e.tensor_tensor(out=t1, in0=t1, in1=K.modB[:, 0:K.D], op=ALU.add),
                     r=[nt1, "modB"], w=[nt1])
                for j in range(8):
                    P.op("pe", lambda e: e.matmul(pst[j // 4][:, (j % 4) * 128:(j % 4 + 1) * 128], lhsT=t1[:, j * 128:(j + 1) * 128],
                                                  rhs=K.identf, start=True, stop=True), r=[nt1, "identf"], w=[f"pst{j // 4}"])
                for hh in range(2):
                    src = pst[hh].rearrange("p (j t) -> p j t", j=4)
                    P.op("act", lambda e: e.activation(out=hT[:, 4 * hh:4 * hh + 4, tcs], in_=src, func=AF.Copy), r=[f"pst{hh}"], w=["hT"])
                    P.op("dve", lambda e: e.tensor_copy(out=hTf[:, 4 * hh:4 * hh + 4, :], in_=src), r=[f"pst{hh}"], w=[nhf])
                for j in range(8):
                    P.op("pe", lambda e: e.matmul(pq[0][:, 0:512], lhsT=hTf[:, j, :], rhs=wq[:, j, 0:512], start=(j == 0), stop=(j == 7)),
                         r=["wq", nhf], w=["ppq0"])
                    P.op("pe", lambda e: e.matmul(pq[1][:, 0:72], lhsT=hTf[:, j, :], rhs=wq[:, j, 512:584], start=(j == 0), stop=(j == 7)),
                         r=["wq", nhf], w=["ppq1"])
                P.op("act", lambda e: e.activation(out=tok[b][:, 0:512], in_=pq[0][:, 0:512], func=AF.Copy), r=["ppq0"], w=[f"tok{b}"])
                P.op("act", lambda e: e.activation(out=tok[b][:, 512:584], in_=pq[1][:, 0:72], func=AF.Copy), r=["ppq1"], w=[f"tok{b}"])
                P.op("dve", lambda e: e.tensor_copy(out=wit[:, i * 8:(i + 1) * 8], in_=tok[b][:, 576:584]), r=[f"tok{b}"], w=["wit"])
                for q_ in range(4):
                    P.op("pe", lambda e: e.matmul(ps[0][:, q_ * 128:(q_ + 1) * 128], lhsT=tok[b][:, q_ * 128:(q_ + 1) * 128], rhs=K.identf,
                                                  start=True, stop=True), r=[f"tok{b}", "identf"], w=["pps0"])
                P.op("pe", lambda e: e.matmul(ps[1][0:64, 0:128], lhsT=tok[b][:, 512:576], rhs=K.identf, start=True, stop=True),
                     r=[f"tok{b}", "identf"], w=["pps1"])
                P.op("act", lambda e: e.activation(out=pqs[b][:, 0:4, :], in_=ps[0].rearrange("p (q t) -> p q t", q=4), func=AF.Copy),
                     r=["pps0"], w=[f"pqs{b}"])
                P.op("dve", lambda e: e.tensor_copy(out=pqs[b][0:64, 4, :], in_=ps[1][0:64, 0:128]), r=["pps1"], w=[f"pqs{b}"])
                P.dma("sp", K.qiT.rearrange("(q p) t -> p q t", p=128)[:, :, tcs], pqs[b][:, 0:4, :], r=[f"pqs{b}"], w=["qiT"])
                P.dma("sp", K.kiT[:, tcs], pqs[b][0:64, 4, :], r=[f"pqs{b}"], w=["kiT"])
            P.dma("sp", K.witm, wit, r=["wit"], w=["witm"])
            P.barrier()
        if K.stop_after == f"B{l}":
            return
        wfs = [sbt(f"wfs{i}", [128, 8, 512]) for i in range(2)]
        wbf = [sbt(f"wbf{i}", [128, 8, 128], BF16) for i in range(2)]
        blkf = [sbt(f"blkf{i}", [128, S]) for i in range(2)]
        tmpf = [sbt("tmpf0", [128, S])]
        blkb = [sbt(f"blkb{i}", [128, S], BF16) for i in range(2)]
        wuk = blkf[0][:, 0:512].rearrange("p (a c) -> p a c", a=4)
        wukb = sbt("wukb", [128, 4, 128], BF16)
        wuv = blkf[1][:, 0:512]
        wuvb = sbt("wuvb", [128, 512], BF16)
        qa = [sbt(f"qa{i}", [128, TC], BF16) for i in range(2)]
        vx = [sbt(f"vx{i}", [128, 8, 65], BF16) for i in range(2)]
        P.dma("sp", wuk, W["wukT"], w=["blkf0"])
        P.op("pool", lambda e: e.tensor_copy(out=wukb, in_=wuk), r=["blkf0"], w=["wukb"])
        P.dma("sp", wuv, W["wuv"], w=["blkf1"])
        P.op("pool", lambda e: e.tensor_copy(out=wuvb, in_=wuv), r=["blkf1"], w=["wuvb"])
        for i in range(2):
            P.op("pool", lambda e: e.memset(vx[i], 1.0), w=[f"vx{i}"])
        pcount = 0
        qcount = 0
        blocks = proj_blocks()
        supers = []
        for (r0, r1) in ((0, C_QI), (C_GA, NIN)):
            c = r0
            while c < r1:
                supers.append((c, min(512, r1 - c)))
                c += 512
        sup_of = {}
        for si, (c0, cn) in enumerate(supers):
            for (cs, n, kind, idx) in blocks:
                if c0 <= cs < c0 + cn:
                    sup_of[cs] = si
        loaded = set()

        def load_super(si):
            if si in loaded or si >= len(supers):
                return
            loaded.add(si)
            c0, cn = supers[si]
            P.dma("sp", wfs[si % 2][:, :, 0:cn], W["w_in"][:, c0:c0 + cn].rearrange("(j p) n -> p j n", p=128), w=[f"wfs{si % 2}"])
        load_super(0)
        for bi, (cs, n, kind, idx) in enumerate(blocks):
            b = bi % 2
            si = sup_of[cs]
            load_super(si)
            if cs == supers[si][0]:
                load_super(si + 1)
            o = cs - supers[si][0]
            P.op("pool", lambda e: e.tensor_copy(out=wbf[b][:, :, 0:n], in_=wfs[si % 2][:, :, o:o + n]), r=[f"wfs{si % 2}"], w=[f"wbf{b}"])
            isb = kind in ("sza", "szb", "sga", "sgb", "q")
            dst, dname = (blkb[b], f"blkb{b}") if isb else (blkf[b], f"blkf{b}")
            func = AF.Silu if kind in ("sza", "szb") else (AF.Sigmoid if kind in ("sga", "sgb") else AF.Copy)
            for tc in range(NTC):
                pb = pcount % 2
                pcount += 1
                for j in range(8):
                    P.op("pe", lambda e: e.matmul(ps[pb][0:n, 0:TC], lhsT=wbf[b][:, j, 0:n], rhs=hT[:, j, tc * TC:(tc + 1) * TC],
                                                  start=(j == 0), stop=(j == 7)), r=[f"wbf{b}", "hT"], w=[f"pps{pb}"])
                P.op("act", lambda e: e.activation(out=dst[0:n, tc * TC:(tc + 1) * TC], in_=ps[pb][0:n, 0:TC], func=func),
                     r=[f"pps{pb}"], w=[dname])
            if kind == "shift":
                tm, tn = tmpf[0], "tmpf0"
                P.op("dve", lambda e: e.tensor_tensor(out=tm[:, 1:S], in0=dst[:, 0:S - 1], in1=dst[:, 1:S], op=ALU.subtract),
                     r=[dname], w=[tn])
                P.op("dve", lambda e: e.tensor_scalar(out=tm[:, 0:1], in0=dst[:, 0:1], scalar1=-1.0, scalar2=None, op0=ALU.mult),
                     r=[dname], w=[tn])
                P.op("dve", lambda e: e.scalar_tensor_tensor(out=tm, in0=tm, scalar=K.cols[:, idx:idx + 1], in1=dst,
                                                             op0=ALU.mult, op1=ALU.add), r=[tn, dname, "cols"], w=[tn])
                P.dma("sp", K.psT[cs:cs + 128, :], tm, r=[tn], w=["psT"])
            elif kind in ("sza", "szb", "sga", "sgb"):
                tgt = dict(sza=K.szaT, szb=K.szbT, sga=K.sgaT, sgb=K.sgbT)[kind]
                P.dma("sp", tgt[idx * 128:(idx + 1) * 128, :], dst, r=[dname], w=[kind + "T"])
            elif kind == "q":
                for hp in range(2):
                    for tc in range(NTC):
                        pb = 2 + qcount % 2
                        qb = qcount % 2
                        qcount += 1
                        P.op("pe", lambda e: e.matmul(ps[pb][:, 0:TC], lhsT=wukb[64 * hp:64 * hp + 64, idx, :],
                                                      rhs=dst[64 * hp:64 * hp + 64, tc * TC:(tc + 1) * TC], start=True, stop=True),
                             r=["wukb", dname], w=[f"pps{pb}"])
                        P.op("act", lambda e: e.activation(out=qa[qb], in_=ps[pb][:, 0:TC], func=AF.Copy, scale=0.125),
                             r=[f"pps{pb}"], w=[f"qa{qb}"])
                        P.dma("sp", K.qabsT[2 * idx + hp, :, tc * TC:(tc + 1) * TC], qa[qb], r=[f"qa{qb}"], w=["qabsT"])
            elif kind == "ckv":
                tm, tn = tmpf[0], "tmpf0"
                rs, rn = blkf[1 - b], f"blkf{1 - b}"
                P.op("act", lambda e: e.activation(out=tm, in_=dst, func=AF.Square), r=[dname], w=[tn])
                for tc in range(NTC):
                    pb = 2 + tc % 2
                    P.op("pe", lambda e: e.matmul(ps[pb][:, 0:TC], lhsT=K.ones, rhs=tm[:, tc * TC:(tc + 1) * TC], start=True, stop=True),
                         r=["ones", tn], w=[f"pps{pb}"])
                    P.op("act", lambda e: e.activation(out=rs[:, tc * TC:(tc + 1) * TC], in_=ps[pb][:, 0:TC], func=AF.Sqrt,
                                                       scale=1.0 / 128, bias=K.EPS), r=[f"pps{pb}"], w=[rn])
                P.op("dve", lambda e: e.reciprocal(out=rs, in_=rs), r=[rn], w=[rn])
                lb_, ln_ = blkb[b], f"blkb{b}"
                P.op("dve", lambda e: e.scalar_tensor_tensor(out=lb_, in0=dst, scalar=K.cols[:, 41:42], in1=rs,
                                                             op0=ALU.mult, op1=ALU.mult), r=[dname, rn, "cols"], w=[ln_])
                P.dma("sp", K.latT, lb_, r=[ln_], w=["latT"])
                for i in range(NT):
                    pb = 2 + i % 2
                    vb = i % 2
                    P.op("pe", lambda e: e.matmul(ps[pb], lhsT=lb_[:, i * 128:(i + 1) * 128], rhs=wuvb, start=True, stop=True),
                         r=[ln_, "wuvb"], w=[f"pps{pb}"])
                    P.op("act", lambda e: e.activation(out=vx[vb][:, :, 0:64], in_=ps[pb].rearrange("p (h d) -> p h d", h=8),
                                                       func=AF.Copy), r=[f"pps{pb}"], w=[f"vx{vb}"])
                    P.dma("sp", K.vext[i], vx[vb].rearrange("p h d -> p (h d)"), r=[f"vx{vb}"], w=["vext"])
        P.barrier()


def colpack(v):
    return np.ascontiguousarray(np.asarray(v, np.float32).reshape(-1, 128).T)


def shared_maps(inp, depth):
    m = {}
    m["final_g"] = np.asarray(inp["final_g"], np.float32).reshape(1, D)
    rb = np.asarray(inp["rel_bias"], np.float32)
    s_ = np.arange(128)[:, None]
    t_ = np.arange(128)[None, :]
    eb = np.zeros((2, 128, 8, 128), np.float32)
    for kind, off in ((0, 0), (1, 128)):
        dist = t_ - s_ + off
        bk = t5_bucket_np(np.maximum(dist, 0))
        eb[kind] = np.transpose(rb[bk], (0, 2, 1))
    m["ebias"] = eb
    m["eb31"] = np.ascontiguousarray(np.broadcast_to(rb[31][None, :, None], (128, 8, 128))).astype(np.float32)
    for l in range(depth):
        g = lambda k: np.asarray(inp[k][l], np.float32)
        m[f"ada_w{l}"] = g("ada_w")
        m[f"ada_b{l}"] = g("ada_b").reshape(1, -1)
        m[f"norm_g{l}"] = g("norm_g").reshape(1, -1)
        m[f"w_in{l}"] = g("w_in")
        m[f"cols{l}"] = np.ascontiguousarray(np.concatenate([
            colpack(g("shift_mu")), colpack(g("w0")), colpack(g("a0")), colpack(g("k_k")), colpack(g("k_a")),
            colpack(g("r_k").reshape(-1)), colpack(g("lnx_g")), colpack(g("lnx_b")), colpack(g("kv_norm_g"))], axis=1))
        m[f"w2{l}"] = g("w2")
        m[f"a2{l}"] = g("a2")
        wuk = g("w_uk")
        m[f"wukT{l}"] = np.ascontiguousarray(wuk.reshape(128, 4, 2, 64).transpose(2, 3, 1, 0).reshape(128, 4, 128))
        m[f"wuv{l}"] = g("w_uv").reshape(128, 512)
        m[f"w_pa{l}"] = g("w_pa")
        m[f"w_pb{l}"] = g("w_pb")
        m[f"w_o{l}"] = g("w_o")
    return m


def core_map(inp, shared, b):
    m = dict(shared)
    m["x"] = np.ascontiguousarray(np.asarray(inp["x"][b], np.float32))
    m["cT"] = colpack(np.asarray(inp["c"][b], np.float32))
    return m


def phase_rwkv(K, l):
    from contextlib import ExitStack
    nc, P, W, S = K.nc, K.P, K.L[l], K.S
    SEG = min(1024, S)
    NSEG = S // SEG
    NCH = SEG // 64
    CC = min(512, SEG)
    NCC = SEG // CC
    GN_EPS = 64e-5
    with ExitStack() as st:
        def sbt(name, shape, dt=F32):
            return st.enter_context(_sbm(nc, name, list(shape), dt)).ap()
        names = ["lwp", "ai", "kkn", "km", "bb", "Lp", "Lpe", "E1", "E4", "bonT", "tmpA", "OG", "cmask"]
        T = {n: sbt("r_" + n, [128, SEG]) for n in names}
        AR = sbt("r_AR", [128, 2, SEG])
        BK = sbt("r_BK", [128, 2, SEG])
        BKh = sbt("r_BKh", [128, 2, SEG])
        inb = [dict(rT=sbt(f"r_rT{i}", [128, SEG]), kT=sbt(f"r_kT{i}", [128, SEG]), vT=sbt(f"r_vT{i}", [128, SEG]),
                    wdT=sbt(f"r_wdT{i}", [64, SEG]), adT=sbt(f"r_adT{i}", [64, SEG]), gz=sbt(f"r_gz{i}", [128, SEG], BF16)) for i in range(2)]
        w2 = sbt("r_w2", [64, 512])
        a2 = sbt("r_a2", [64, 512])
        ogb = sbt("r_ogb", [128, SEG], BF16)
        u2 = sbt("r_u2", [128, 128])
        sl = sbt("r_sl", [128, 64])
        NSLOT = 6
        tms = sbt("r_tms", [128, NCH, 192])
        MAKAs = sbt("r_MAKAs", [128, NCH, 256])
        TTs = sbt("r_TTs", [128, NCH, 64])
        Yall = T["Lpe"].rearrange("p (c v) -> p c v", v=64)
        Ysq = T["tmpA"].rearrange("p (c v) -> p c v", v=64)
        gst = sbt("r_gst", [128, 4, NCH])
        Ps = [[sbt(f"r_P{i}s{k}", [128, 64], BF16) for i in range(2)] for k in range(NSLOT)]
        Qs = [[sbt(f"r_Q{i}s{k}", [128, 64], BF16) for i in range(2)] for k in range(NSLOT)]
        Q0s = [sbt(f"r_Q0f{k}", [128, 64]) for k in range(NSLOT)]
        TTk = [[sbt(f"r_TT{i}s{k}", [128, 64]) for i in range(2)] for k in range(NSLOT)]
        TTb = [[sbt(f"r_TTb{i}s{k}", [128, 64], BF16) for i in range(2)] for k in range(NSLOT)]
        Xs = sbt("r_X", [128, 64])
        Us = sbt("r_U", [128, 64])
        S0 = [sbt(f"r_S{i}", [128, 64]) for i in range(2)]
        psB = [st.enter_context(_psm(nc, f"rpsB{i}", [128, 512], F32)).ap() for i in range(NSLOT)]
        psX = [st.enter_context(_psm(nc, f"rpsX{i}", [128, 512], F32)).ap() for i in range(2)]
        psP = psB

        P.dma("sp", w2, W["w2"], w=["r_w2"])
        P.dma("sp", a2, W["a2"], w=["r_a2"])
        for par in range(2):
            pr = slice(64 * par, 64 * par + 64)
            P.dma("sp", u2[pr, :], K.u2m, r=["u2m"], w=["r_u2"])
            P.dma("sp", sl[pr, :], K.slm, r=["slm"], w=["r_sl"])
        cm = T["cmask"]
        P.op("pool", lambda e: e.memset(cm, 1.0), w=["r_cmask"])
        P.op("pool", lambda e: e.memset(cm.rearrange("p (c t) -> p c t", t=64)[:, :, 0:1], 0.0), w=["r_cmask"])

        def tt(out, in0, in1, op, r, w, eng="dve"):
            P.op(eng, lambda e: e.tensor_tensor(out=out, in0=in0, in1=in1, op=op), r=r, w=w)

        def load_inputs(ibuf, hp_, sg_):
            sc_ = slice(sg_ * SEG, (sg_ + 1) * SEG)
            hc_ = slice(hp_ * 128, (hp_ + 1) * 128)
            d = inb[ibuf]
            P.dma("sp", d["rT"], K.psT[C_R + hp_ * 128:C_R + (hp_ + 1) * 128, sc_], r=["psT"], w=[f"r_rT{ibuf}"])
            P.dma("sp", d["kT"], K.psT[C_K + hp_ * 128:C_K + (hp_ + 1) * 128, sc_], r=["psT"], w=[f"r_kT{ibuf}"])
            P.dma("sp", d["vT"], K.psT[C_V + hp_ * 128:C_V + (hp_ + 1) * 128, sc_], r=["psT"], w=[f"r_vT{ibuf}"])
            P.dma("sp", d["wdT"], K.psT[C_WD:C_WD + 64, sc_], r=["psT"], w=[f"r_wdT{ibuf}"])
            P.dma("sp", d["adT"], K.psT[C_AD:C_AD + 64, sc_], r=["psT"], w=[f"r_adT{ibuf}"])
            P.dma("sp", d["gz"], K.szaT[hc_, sc_], r=["szaT"], w=[f"r_gz{ibuf}"])
        seq_i = 0
        for hp in range(4):
            col = lambda nm: K.cols[:, K.COLS[nm][0] + hp:K.COLS[nm][0] + hp + 1]
            hc = slice(hp * 128, (hp + 1) * 128)
            for par in range(2):
                P.op("pool", lambda e: e.memset(S0[0][64 * par:64 * par + 64, :], 0.0), w=[f"r_S0_{par}"])
            for sg in range(NSEG):
                sc = slice(sg * SEG, (sg + 1) * SEG)
                ib_ = seq_i % 2
                T["rT"], T["kT"], T["vT"] = inb[ib_]["rT"], inb[ib_]["kT"], inb[ib_]["vT"]
                wdT, adT, gz = inb[ib_]["wdT"], inb[ib_]["adT"], inb[ib_]["gz"]
                nsfx = str(ib_)
                if seq_i == 0:
                    load_inputs(0, hp, sg)
                nxt = seq_i + 1
                if nxt < 4 * NSEG:
                    load_inputs(nxt % 2, nxt // NSEG, nxt % NSEG)
                seq_i += 1
                P.op("act", lambda e: e.activation(out=wdT, in_=wdT, func=AF.Tanh), r=["r_wdT" + nsfx], w=["r_wdT" + nsfx])
                for cc in range(NCC):
                    cs_ = slice(cc * CC, (cc + 1) * CC)
                    pb = cc % 2
                    P.op("pe", lambda e: e.matmul(psP[pb][:, 0:CC], lhsT=w2[:, hc], rhs=wdT[:, cs_], start=True, stop=True),
                         r=["r_w2", "r_wdT" + nsfx], w=[f"rpsB{pb}"])
                    P.op("act", lambda e: e.activation(out=T["lwp"][:, cs_], in_=psP[pb][:, 0:CC], func=AF.Sigmoid, bias=col("w0")),
                         r=[f"rpsB{pb}", "cols"], w=["r_lwp"])
                    P.op("pe", lambda e: e.matmul(psP[pb][:, 0:CC], lhsT=a2[:, hc], rhs=adT[:, cs_], start=True, stop=True),
                         r=["r_a2", "r_adT" + nsfx], w=[f"rpsB{pb}"])
                    P.op("act", lambda e: e.activation(out=T["ai"][:, cs_], in_=psP[pb][:, 0:CC], func=AF.Sigmoid, bias=col("a0")),
                         r=[f"rpsB{pb}", "cols"], w=["r_ai"])
                P.op("dve", lambda e: e.tensor_scalar(out=T["kkn"], in0=T["kT"], scalar1=col("kk"), scalar2=None, op0=ALU.mult),
                     r=["r_kT" + nsfx, "cols"], w=["r_kkn"])
                tt(T["tmpA"], T["kkn"], T["kkn"], ALU.mult, ["r_kkn"], ["r_tmpA"])
                for cc in range(NCC):
                    cs_ = slice(cc * CC, (cc + 1) * CC)
                    pb = cc % 2
                    P.op("pe", lambda e: e.matmul(psP[pb][:, 0:CC], lhsT=K.bdiag, rhs=T["tmpA"][:, cs_], start=True, stop=True),
                         r=["bdiag", "r_tmpA"], w=[f"rpsB{pb}"])
                    P.op("act", lambda e: e.activation(out=T["E4"][:, cs_], in_=psP[pb][:, 0:CC], func=AF.Sqrt),
                         r=[f"rpsB{pb}"], w=["r_E4"])
                P.op("dve", lambda e: e.tensor_scalar(out=T["E4"], in0=T["E4"], scalar1=1e-12, scalar2=None, op0=ALU.max),
                     r=["r_E4"], w=["r_E4"])
                P.op("dve", lambda e: e.reciprocal(out=T["E4"], in_=T["E4"]), r=["r_E4"], w=["r_E4"])
                tt(T["kkn"], T["kkn"], T["E4"], ALU.mult, ["r_kkn", "r_E4"], ["r_kkn"])
                P.op("dve", lambda e: e.tensor_scalar(out=T["tmpA"], in0=T["ai"], scalar1=col("ka"), scalar2=K.omka[:, hp:hp + 1],
                                                      op0=ALU.mult, op1=ALU.add), r=["r_ai", "cols", "omka"], w=["r_tmpA"])
                tt(T["km"], T["kT"], T["tmpA"], ALU.mult, ["r_kT" + nsfx, "r_tmpA"], ["r_km"])
                tt(T["bb"], T["kkn"], T["ai"], ALU.mult, ["r_kkn", "r_ai"], ["r_bb"])
                P.op("dve", lambda e: e.tensor_tensor_scan(out=T["Lp"], data0=cm, data1=T["lwp"], initial=0.0, op0=ALU.mult, op1=ALU.add),
                     r=["r_cmask", "r_lwp"], w=["r_Lp"])
                tt(T["Lpe"], T["Lp"], T["lwp"], ALU.subtract, ["r_Lp", "r_lwp"], ["r_Lpe"])
                Lp3 = T["Lp"].rearrange("p (c t) -> p c t", t=64)
                tt(T["tmpA"].rearrange("p (c t) -> p c t", t=64), Lp3[:, :, 63:64].broadcast_to([128, NCH, 64]), Lp3, ALU.subtract,
                   ["r_Lp"], ["r_tmpA"])
                P.op("act", lambda e: e.activation(out=T["E1"], in_=T["Lp"], func=AF.Exp, scale=-C0), r=["r_Lp"], w=["r_E1"])
                P.op("act", lambda e: e.activation(out=BK[:, 1, :], in_=T["Lp"], func=AF.Exp, scale=C0), r=["r_Lp"], w=["r_BK"])
                P.op("act", lambda e: e.activation(out=AR[:, 0, :], in_=T["Lpe"], func=AF.Exp, scale=-C0), r=["r_Lpe"], w=["r_AR"])
                P.op("act", lambda e: e.activation(out=BKh[:, 1, :], in_=T["tmpA"], func=AF.Exp, scale=-C0), r=["r_tmpA"], w=["r_BKh"])
                P.op("dve", lambda e: e.scalar_tensor_tensor(out=AR[:, 0, :], in0=T["kkn"], scalar=-1.0, in1=AR[:, 0, :], op0=ALU.mult, op1=ALU.mult),
                     r=["r_kkn", "r_AR"], w=["r_AR"])
                tt(AR[:, 1, :], T["rT"], T["E1"], ALU.mult, ["r_rT" + nsfx, "r_E1"], ["r_AR"])
                tt(BK[:, 0, :], T["bb"], BK[:, 1, :], ALU.mult, ["r_bb", "r_BK"], ["r_BK"])
                tt(BK[:, 1, :], T["km"], BK[:, 1, :], ALU.mult, ["r_km", "r_BK"], ["r_BK"])
                tt(BKh[:, 0, :], T["bb"], BKh[:, 1, :], ALU.mult, ["r_bb", "r_BKh"], ["r_BKh"])
                tt(BKh[:, 1, :], T["km"], BKh[:, 1, :], ALU.mult, ["r_km", "r_BKh"], ["r_BKh"])
                P.op("dve", lambda e: e.scalar_tensor_tensor(out=T["tmpA"], in0=T["rT"], scalar=col("rk"), in1=T["km"], op0=ALU.mult, op1=ALU.mult),
                     r=["r_rT" + nsfx, "r_km", "cols", "r_tmpA"], w=["r_tmpA"])
                for cc in range(NCC):
                    cs_ = slice(cc * CC, (cc + 1) * CC)
                    pb = cc % 2
                    P.op("pe", lambda e: e.matmul(psP[pb][:, 0:CC], lhsT=K.bdiag, rhs=T["tmpA"][:, cs_], start=True, stop=True),
                         r=["bdiag", "r_tmpA"], w=[f"rpsB{pb}"])
                    tt(T["bonT"][:, cs_], psP[pb][:, 0:CC], T["vT"][:, cs_], ALU.mult, [f"rpsB{pb}", "r_vT" + nsfx], ["r_bonT"])
                done = [0, 0]

                def chain1(c, par, slot):
                    pr = slice(64 * par, 64 * par + 64)
                    cs_ = slice(c * 64, (c + 1) * 64)
                    B_, nB = psB[slot], f"rpsB{slot}"
                    sfx = f"_{par}_{c}"
                    idm = K.identf[pr, pr]
                    for q_, src, sn in ((0, T["vT"][pr, cs_], "r_vT" + nsfx), (1, BKh[pr, 0, cs_], "r_BKh"), (2, BKh[pr, 1, cs_], "r_BKh")):
                        P.op("pe", lambda e: e.matmul(B_[pr, 320 + 64 * q_:384 + 64 * q_], lhsT=src, rhs=idm, start=True, stop=True),
                             r=[sn, "identf"], w=[nB])
                    P.op("pe", lambda e: e.matmul(B_[pr, 0:128], lhsT=BK[pr, 0, cs_], rhs=AR[pr, :, cs_], start=True, stop=True),
                         r=["r_BK", "r_AR"], w=[nB])
                    P.op("pe", lambda e: e.matmul(B_[pr, 128:256], lhsT=BK[pr, 1, cs_], rhs=AR[pr, :, cs_], start=True, stop=True),
                         r=["r_BK", "r_AR"], w=[nB])
                    P.op("pe", lambda e: e.matmul(B_[pr, 256:320], lhsT=AR[pr, 0, cs_], rhs=BK[pr, 0, cs_], start=True, stop=True),
                         r=["r_BK", "r_AR"], w=[nB])
                    yield
                    P.op("act", lambda e: e.activation(out=tms[pr, c, :], in_=B_[pr, 320:512], func=AF.Copy), r=[nB], w=["r_tms" + sfx])
                    tt(MAKAs[pr, c, :].rearrange("p (a t) -> p a t", a=2), B_[pr, 0:256].rearrange("p (a t) -> p a t", a=2),
                       u2[pr, :].unsqueeze(1).broadcast_to([64, 2, 128]), ALU.mult, [nB, "r_u2"], ["r_MAKA" + sfx])
                    tt(Q0s[slot][pr], B_[pr, 256:320], sl[pr, :], ALU.mult, [nB, "r_sl"], [f"r_Q0f{slot}"])
                    tt(TTk[slot][0][pr], MAKAs[pr, c, 0:64], idm, ALU.add, ["r_MAKA" + sfx, "identf"], [f"r_TT0s{slot}"])
                    P.op("pool", lambda e: e.tensor_copy(out=TTb[slot][0][pr], in_=TTk[slot][0][pr]), r=[f"r_TT0s{slot}"], w=[f"r_TTb0s{slot}"])
                    yield
                    p_prev, np_prev = MAKAs[pr, c, 0:64], "r_MAKA" + sfx
                    q_prev, nq_prev = Q0s[slot][pr], f"r_Q0f{slot}"
                    for kq in range(1, 6):
                        pp = kq % 2
                        if kq < 5:
                            P.op("pe", lambda e: e.matmul(B_[pr, 0:64], lhsT=q_prev, rhs=p_prev, start=True, stop=True),
                                 r=[nq_prev, np_prev], w=[nB])
                        P.op("pe", lambda e: e.matmul(B_[pr, 64:128], lhsT=p_prev, rhs=q_prev, start=True, stop=True),
                             r=[nq_prev, np_prev], w=[nB])
                        yield
                        if kq < 5:
                            P.op("act", lambda e: e.activation(out=Ps[slot][pp][pr], in_=B_[pr, 0:64], func=AF.Copy),
                                 r=[nB], w=[f"r_P{pp}s{slot}"])
                        P.op("act", lambda e: e.activation(out=Qs[slot][pp][pr], in_=B_[pr, 64:128], func=AF.Copy),
                             r=[nB], w=[f"r_Q{pp}s{slot}"])
                        yield
                        ob = (kq - 1) % 2
                        t_old, nt_old = TTk[slot][ob][pr], f"r_TT{ob}s{slot}"
                        P.op("pe", lambda e: e.matmul(B_[pr, 128:192], lhsT=Qs[slot][pp][pr], rhs=TTb[slot][ob][pr], start=True, stop=True),
                             r=[f"r_Q{pp}s{slot}", f"r_TTb{ob}s{slot}"], w=[nB])
                        yield
                        if kq < 5:
                            tt(TTk[slot][kq % 2][pr], B_[pr, 128:192], t_old, ALU.add, [nB, nt_old], [f"r_TT{kq % 2}s{slot}"])
                            P.op("pool", lambda e: e.tensor_copy(out=TTb[slot][kq % 2][pr], in_=TTk[slot][kq % 2][pr]),
                                 r=[f"r_TT{kq % 2}s{slot}"], w=[f"r_TTb{kq % 2}s{slot}"])
                        else:
                            tt(TTs[pr, c, :], B_[pr, 128:192], t_old, ALU.add, [nB, nt_old], ["r_TTs" + sfx])
                        yield
                        p_prev, np_prev = Ps[slot][pp][pr], f"r_P{pp}s{slot}"
                        q_prev, nq_prev = Qs[slot][pp][pr], f"r_Q{pp}s{slot}"
                    done[par] = max(done[par], c + 1)

                def chain2(par):
                    pr = slice(64 * par, 64 * par + 64)
                    X_, nX = psX[par], f"rpsX{par}"
                    for c in range(NCH):
                        while done[par] < min(NCH, c + 2):
                            yield
                        gi = sg * NCH + c
                        cs_ = slice(c * 64, (c + 1) * 64)
                        sfx = f"_{par}_{c}"
                        s_old, s_new = S0[gi % 2], S0[(gi + 1) % 2]
                        ns_old, ns_new = f"r_S{gi % 2}_{par}", f"r_S{(gi + 1) % 2}_{par}"
                        Vh, Bh, Kh = tms[pr, c, 0:64], tms[pr, c, 64:128], tms[pr, c, 128:192]
                        ArbT, AakT, ArkT = MAKAs[pr, c, 64:128], MAKAs[pr, c, 128:192], MAKAs[pr, c, 192:256]
                        P.op("pe", lambda e: e.matmul(X_[pr, 0:64], lhsT=AR[pr, 0, cs_], rhs=s_old[pr], start=True, stop=False),
                             r=["r_AR", ns_old], w=[nX])
                        P.op("pe", lambda e: e.matmul(X_[pr, 0:64], lhsT=AakT, rhs=Vh, start=False, stop=True),
                             r=["r_MAKA" + sfx, "r_tms" + sfx], w=[nX])
                        yield
                        P.op("act", lambda e: e.activation(out=Xs[pr], in_=X_[pr, 0:64], func=AF.Copy), r=[nX], w=[f"r_X_{par}"])
                        yield
                        P.op("pe", lambda e: e.matmul(X_[pr, 64:128], lhsT=TTs[pr, c, :], rhs=Xs[pr], start=True, stop=True),
                             r=["r_TTs" + sfx, f"r_X_{par}"], w=[nX])
                        yield
                        P.op("act", lambda e: e.activation(out=Us[pr], in_=X_[pr, 64:128], func=AF.Copy), r=[nX], w=[f"r_U_{par}"])
                        yield
                        P.op("pe", lambda e: e.matmul(X_[pr, 128:192], lhsT=AR[pr, 1, cs_], rhs=s_old[pr], start=True, stop=False),
                             r=["r_AR", ns_old], w=[nX])
                        P.op("pe", lambda e: e.matmul(X_[pr, 128:192], lhsT=ArbT, rhs=Us[pr], start=False, stop=False),
                             r=["r_MAKA" + sfx, f"r_U_{par}"], w=[nX])
                        P.op("pe", lambda e: e.matmul(X_[pr, 128:192], lhsT=ArkT, rhs=Vh, start=False, stop=True),
                             r=["r_MAKA" + sfx, "r_tms" + sfx], w=[nX])
                        P.op("pe", lambda e: e.matmul(X_[pr, 192:256], lhsT=Bh, rhs=Us[pr], start=True, stop=False),
                             r=["r_tms" + sfx, f"r_U_{par}"], w=[nX])
                        P.op("pe", lambda e: e.matmul(X_[pr, 192:256], lhsT=Kh, rhs=Vh, start=False, stop=True),
                             r=["r_tms" + sfx], w=[nX])
                        yield
                        P.op("dve", lambda e: e.scalar_tensor_tensor(out=s_new[pr], in0=s_old[pr], scalar=T["E1"][pr, c * 64 + 63:c * 64 + 64],
                                                                     in1=X_[pr, 192:256], op0=ALU.mult, op1=ALU.add),
                             r=[ns_old, "r_E1", nX], w=[ns_new])
                        P.op("dve", lambda e: e.tensor_copy(out=Yall[pr, c, :], in_=X_[pr, 128:192]), r=[nX], w=["r_Lpe"])
                        yield

                pending = [(c, par) for c in range(NCH) for par in range(2)]
                free_slots = list(range(NSLOT))
                active = []
                for par in range(2):
                    active.append((chain2(par), None))
                while pending or active:
                    while pending and free_slots:
                        c, par = pending.pop(0)
                        sl_ = free_slots.pop(0)
                        active.append((chain1(c, par, sl_), sl_))
                    for g in list(active):
                        try:
                            next(g[0])
                        except StopIteration:
                            active.remove(g)
                            if g[1] is not None:
                                free_slots.append(g[1])
                P.op("dve", lambda e: e.tensor_reduce(out=gst[:, 0, :], in_=Yall, axis=AX.X, op=ALU.add), r=["r_Lpe"], w=["r_gst"])
                P.op("dve", lambda e: e.tensor_scalar(out=gst[:, 1, :], in0=gst[:, 0, :], scalar1=-1.0 / 64, scalar2=None, op0=ALU.mult),
                     r=["r_gst"], w=["r_gst"])
                tt(Yall, Yall, gst[:, 1, :].unsqueeze(2).broadcast_to([128, NCH, 64]), ALU.add, ["r_Lpe", "r_gst"], ["r_Lpe"])
                tt(Ysq, Yall, Yall, ALU.mult, ["r_Lpe"], ["r_tmpA"])
                P.op("dve", lambda e: e.tensor_reduce(out=gst[:, 2, :], in_=Ysq, axis=AX.X, op=ALU.add), r=["r_tmpA"], w=["r_gst"])
                P.op("dve", lambda e: e.tensor_scalar(out=gst[:, 3, :], in0=gst[:, 2, :], scalar1=1.0 / 64, scalar2=GN_EPS, op0=ALU.mult, op1=ALU.add),
                     r=["r_gst"], w=["r_gst"])
                P.op("act", lambda e: e.activation(out=gst[:, 3, :], in_=gst[:, 3, :], func=AF.Sqrt), r=["r_gst"], w=["r_gst"])
                P.op("dve", lambda e: e.reciprocal(out=gst[:, 3, :], in_=gst[:, 3, :]), r=["r_gst"], w=["r_gst"])
                tt(Yall, Yall, gst[:, 3, :].unsqueeze(2).broadcast_to([128, NCH, 64]), ALU.mult, ["r_Lpe", "r_gst"], ["r_Lpe"])
                for c in range(NCH):
                    for par in range(2):
                        pr = slice(64 * par, 64 * par + 64)
                        slot = (2 * c + par) % NSLOT
                        P.op("pe", lambda e: e.matmul(psB[slot][pr, 0:64], lhsT=Yall[pr, c, :], rhs=K.identf[pr, pr], start=True, stop=True),
                             r=["r_Lpe", "identf"], w=[f"rpsB{slot}"])
                        P.op("act", lambda e: e.activation(out=T["OG"][pr, c * 64:(c + 1) * 64], in_=psB[slot][pr, 0:64], func=AF.Identity,
                                                           scale=col("lg")[pr], bias=col("lb")[pr]),
                             r=[f"rpsB{slot}", "cols"], w=["r_OG"])
                tt(T["OG"], T["OG"], T["bonT"], ALU.add, ["r_OG", "r_bonT"], ["r_OG"])
                tt(ogb, T["OG"], gz, ALU.mult, ["r_OG", "r_gz" + nsfx], ["r_ogb"])
                P.dma("sp", K.yagT[hc, sc], ogb, r=["r_ogb"], w=["yagT"])
        P.barrier()


def phase_dsa(K, l):
    from contextlib import ExitStack
    nc, P, W, S, NT, TOPK = K.nc, K.P, K.L[l], K.S, K.NT, K.TOPK
    NIT = 20
    with ExitStack() as st:
        def sbt(name, shape, dt=F32):
            return st.enter_context(_sbm(nc, name, list(shape), dt)).ap()
        latS = sbt("d_lat", [128, S], BF16)
        ki2 = sbt("d_ki2", [128, S])
        vxs = sbt("d_vxs", [128, NT, 520], BF16)
        wis = sbt("d_wis", [128, NT * 8])
        EBM = [sbt(f"d_EBM{i}", [128, 8, 128]) for i in range(2)]
        b31 = sbt("d_b31", [128, 8, 128])
        qij = [sbt(f"d_qij{i}", [128, 4, 128]) for i in range(3)]
        qab = [sbt(f"d_qab{i}", [128, 8, 128], BF16) for i in range(3)]
        zbj = [sbt(f"d_zbj{i}", [128, 4, 128], BF16) for i in range(3)]
        accs = [sbt(f"d_acc{i}", [128, S]) for i in range(2)]
        maskfs = [sbt(f"d_maskf{i}", [128, S], BF16) for i in range(2)]
        maskT = [sbt(f"d_maskT{i}", [128, NT, 128], BF16) for i in range(3)]
        Pt = [sbt(f"d_Pt{i}", [128, 8, 128], BF16) for i in range(3)]
        rtmp = [sbt(f"d_rt{i}", [128, 512]) for i in range(3)]
        m8s = [sbt(f"d_m8{i}", [128, 8]) for i in range(2)]
        bss = [sbt(f"d_bs{i}", [128, 8]) for i in range(2)]
        p2 = sbt("d_p2", [128, NIT + 1])
        wcolss = [sbt(f"d_wcols{i}", [128, NIT + 1]) for i in range(2)]
        rec = sbt("d_rec", [128, 8])
        yb = sbt("d_yb", [128, 512])
        ybg = [sbt(f"d_ybg{i}", [128, 4, 128], BF16) for i in range(2)]
        psI = [st.enter_context(_psm(nc, f"dpsI{i}", [128, 512], F32)).ap() for i in range(3)]
        psM = st.enter_context(_psm(nc, "dpsM", [128, 1024], BF16)).ap()
        psQ = [st.enter_context(_psm(nc, f"dpsQ{i}", [128, 512], F32)).ap() for i in range(2)]
        psO = [st.enter_context(_psm(nc, f"dpsO{i}", [128, 512], F32)).ap() for i in range(2)]

        P.dma("sp", latS, K.latT, r=["latT"], w=["d_lat"])
        P.dma("sp", ki2[0:64, :], K.kiT, r=["kiT"], w=["d_ki2"])
        P.dma("sp", ki2[64:128, :], K.kiT, r=["kiT"], w=["d_ki2"])
        for i in range(NT):
            P.dma("sp", vxs[:, i, :], K.vext[i], r=["vext"], w=["d_vxs"])
        P.dma("sp", wis, K.witm, r=["witm"], w=["d_wis"])
        P.dma("sp", b31, K.eb31, w=["d_b31"])
        for kd in range(2):
            P.dma("sp", EBM[kd], K.ebias[kd], w=[f"d_EBM{kd}"])
            P.op("dve", lambda e: e.tensor_tensor(out=EBM[kd], in0=EBM[kd], in1=b31, op=ALU.subtract), r=[f"d_EBM{kd}", "d_b31"], w=[f"d_EBM{kd}"])
            P.op("act", lambda e: e.activation(out=EBM[kd], in_=EBM[kd], func=AF.Exp), r=[f"d_EBM{kd}"], w=[f"d_EBM{kd}"])
        P.op("pool", lambda e: e.affine_select(out=EBM[0], in_=EBM[0], pattern=[[0, 8], [1, 128]], compare_op=ALU.is_ge, fill=K.reg_zero,
                                               base=0, channel_multiplier=-1), r=["d_EBM0"], w=["d_EBM0"])
        for n in range(NIT + 1):
            P.op("pool", lambda e: e.memset(p2[:, n:n + 1], 2.0 ** -(n + 1)), w=["d_p2"])
        icount = [0]
        it_done = set()

        def chain_it(j):
            b = j % 3
            u = j % 2
            acc, nacc = accs[u], f"d_acc{u}"
            maskf, nmf = maskfs[u], f"d_maskf{u}"
            m8, nm8 = m8s[u], f"d_m8{u}"
            bs, nbs = bss[u], f"d_bs{u}"
            wcols, nwc = wcolss[u], f"d_wcols{u}"
            jc = slice(j * 128, (j + 1) * 128)
            Lk = (j + 1) * 128
            mT, nmT = maskT[b], f"d_maskT{b}"
            P.dma("sp", qij[b], K.qiT.rearrange("(hq p) t -> p hq t", p=128)[:, :, jc], r=["qiT"], w=[f"d_qij{b}"])
            P.dma("sp", qab[b], K.qabsT.rearrange("h c t -> c h t")[:, :, jc], r=["qabsT"], w=[f"d_qab{b}"])
            P.dma("sp", zbj[b], K.szbT.rearrange("(fb p) t -> p fb t", p=128)[:, :, jc], r=["szbT"], w=[f"d_zbj{b}"])
            if Lk <= TOPK:
                P.op("pool", lambda e: e.memset(mT[:, 0:j + 1, :], 1.0), w=[nmT])
                it_done.add(j)
                return
            for k0 in range(0, Lk, 512):
                k1 = min(Lk, k0 + 512)
                wd = k1 - k0
                for h in range(8):
                    par, hq = h % 2, h // 2
                    pr = slice(64 * par, 64 * par + 64)
                    ib = icount[0] % 3
                    icount[0] += 1
                    P.op("pe", lambda e: e.matmul(psI[ib][:, 0:wd], lhsT=qij[b][pr, hq, :], rhs=ki2[pr, k0:k1], start=True, stop=True),
                         r=[f"d_qij{b}", "d_ki2"], w=[f"dpsI{ib}"])
                    P.op("act", lambda e: e.activation(out=rtmp[ib][:, 0:wd], in_=psI[ib][:, 0:wd], func=AF.Relu),
                         r=[f"dpsI{ib}"], w=[f"d_rt{ib}"])
                    wcol = wis[:, j * 8 + h:j * 8 + h + 1]
                    if h == 0:
                        P.op("dve", lambda e: e.tensor_scalar(out=acc[:, k0:k1], in0=rtmp[ib][:, 0:wd], scalar1=wcol, scalar2=None, op0=ALU.mult),
                             r=[f"d_rt{ib}", "d_wis"], w=[nacc])
                    else:
                        P.op("dve", lambda e: e.scalar_tensor_tensor(out=acc[:, k0:k1], in0=rtmp[ib][:, 0:wd], scalar=wcol, in1=acc[:, k0:k1],
                                                                     op0=ALU.mult, op1=ALU.add), r=[f"d_rt{ib}", "d_wis", nacc], w=[nacc])
                    yield
            P.op("pool", lambda e: e.affine_select(out=acc[:, jc], in_=acc[:, jc], pattern=[[-1, 128]], compare_op=ALU.is_ge, fill=K.reg_neg,
                                                   base=0, channel_multiplier=1), r=[nacc], w=[nacc])
            lo0, w0, mid, cnt, sg = (bs[:, i:i + 1] for i in range(5))
            P.op("dve", lambda e: e.max(out=m8, in_=acc[:, 0:Lk]), r=[nacc], w=[nm8])
            P.op("dve", lambda e: e.tensor_reduce(out=lo0, in_=acc[:, 0:j * 128], axis=AX.X, op=ALU.min), r=[nacc], w=[nbs])
            yield
            P.op("dve", lambda e: e.tensor_tensor(out=w0, in0=m8[:, 0:1], in1=lo0, op=ALU.subtract), r=[nm8, nbs], w=[nbs])
            P.op("dve", lambda e: e.tensor_scalar(out=wcols, in0=p2, scalar1=w0, scalar2=None, op0=ALU.mult), r=["d_p2", nbs], w=[nwc])
            P.op("dve", lambda e: e.tensor_tensor(out=mid, in0=lo0, in1=wcols[:, 0:1], op=ALU.add), r=[nbs, nwc], w=[nbs])
            Ld = max(128, (int(0.56 * Lk) // 128) * 128)
            nA = Lk - Ld
            sgn, tot = bs[:, 5:6], bs[:, 6:7]
            for n in range(NIT):
                P.op("dve", lambda e: e.tensor_scalar(out=maskf[:, 0:Ld], in0=acc[:, 0:Ld], scalar1=mid, scalar2=None, op0=ALU.is_ge, op1=ALU.add,
                                                      accum_out=cnt), r=[nacc, nbs], w=[nmf, nbs + "c"])
                P.op("act", lambda e: e.activation(out=maskf[:, Ld:Lk], in_=acc[:, Ld:Lk], func=AF.Sign, scale=-1.0, bias=mid, accum_out=sgn),
                     r=[nacc, nbs], w=[nmf + "a", nbs + "s"])
                P.op("dve", lambda e: e.scalar_tensor_tensor(out=tot, in0=sgn, scalar=-0.5, in1=cnt, op0=ALU.mult, op1=ALU.add),
                     r=[nbs + "c", nbs + "s"], w=[nbs + "t"])
                P.op("dve", lambda e: e.tensor_scalar(out=sg, in0=tot, scalar1=TOPK - 0.5 - 0.5 * nA, scalar2=wcols[:, n:n + 1], op0=ALU.is_ge, op1=ALU.mult),
                     r=[nbs + "t", nwc], w=[nbs + "g"])
                sub = wcols[:, n + 1:n + 2] if n + 1 < NIT else wcols[:, n:n + 1]
                P.op("dve", lambda e: e.scalar_tensor_tensor(out=mid, in0=sg, scalar=sub, in1=mid, op0=ALU.subtract, op1=ALU.add),
                     r=[nbs + "g", nbs, nwc], w=[nbs])
                yield
            P.op("dve", lambda e: e.tensor_scalar(out=maskf[:, 0:Lk], in0=acc[:, 0:Lk], scalar1=mid, scalar2=None, op0=ALU.is_ge),
                 r=[nacc, nbs], w=[nmf, nmf + "a"])
            yield
            yield
            yield
            for i0 in range(0, j + 1, 8):
                i1 = min(j + 1, i0 + 8)
                for i in range(i0, i1):
                    P.op("pe", lambda e: e.transpose(out=psM[:, (i - i0) * 128:(i - i0 + 1) * 128], in_=maskf[:, i * 128:(i + 1) * 128],
                                                     identity=K.identb), r=[nmf, "identb"], w=["dpsM"])
                P.op("act", lambda e: e.activation(out=mT[:, i0:i1, :], in_=psM[:, 0:(i1 - i0) * 128].rearrange("p (i t) -> p i t", t=128),
                                                   func=AF.Copy), r=["dpsM"], w=[nmT])
                yield
            it_done.add(j)

        def chain_at(j):
            b = j % 3
            jc = slice(j * 128, (j + 1) * 128)
            mT, nmT = maskT[b], f"d_maskT{b}"
            while j not in it_done:
                yield

            def qk(i):
                for hh in range(2):
                    P.op("pe", lambda e: e.matmul(psQ[hh], lhsT=latS[:, i * 128:(i + 1) * 128], rhs=qab[b][:, 4 * hh:4 * hh + 4, :], start=True, stop=True),
                         r=["d_lat", f"d_qab{b}"], w=[f"dpsQ{hh}"])
            def pv(i):
                a = i % 3
                for h in range(8):
                    ob = h // 4
                    o0 = (h % 4) * 65
                    P.op("pe", lambda e: e.matmul(psO[ob][:, o0:o0 + 65], lhsT=Pt[a][:, h, :], rhs=vxs[:, i, h * 65:(h + 1) * 65],
                                                  start=(i == 0 and h % 4 == 0), stop=(i == j), skip_group_check=True),
                         r=[f"d_Pt{a}", "d_vxs"], w=[f"dpsO{ob}"])
            qk(0)
            for i in range(j + 1):
                a = i % 3
                for hh in range(2):
                    P.op("act", lambda e: e.activation(out=Pt[a][:, 4 * hh:4 * hh + 4, :], in_=psQ[hh].rearrange("p (h t) -> p h t", h=4), func=AF.Exp),
                         r=[f"dpsQ{hh}"], w=[f"d_Pt{a}"])
                kd = j - i
                if kd <= 1:
                    P.op("pool", lambda e: e.tensor_tensor(out=Pt[a], in0=Pt[a], in1=EBM[kd], op=ALU.mult), r=[f"d_Pt{a}", f"d_EBM{kd}"], w=[f"d_Pt{a}"])
                P.op("pool", lambda e: e.tensor_tensor(out=Pt[a], in0=Pt[a], in1=mT[:, i:i + 1, :].broadcast_to([128, 8, 128]), op=ALU.mult),
                     r=[f"d_Pt{a}", nmT], w=[f"d_Pt{a}"])
                if i >= 1:
                    pv(i - 1)
                if i + 1 <= j:
                    qk(i + 1)
                yield
            pv(j)
            yield
            for ob in range(2):
                o3 = psO[ob][:, 0:260].rearrange("p (h d) -> p h d", d=65)
                P.op("dve", lambda e: e.reciprocal(out=rec[:, 4 * ob:4 * ob + 4].unsqueeze(2), in_=o3[:, :, 64:65]), r=[f"dpsO{ob}"], w=["d_rec"])
                P.op("dve", lambda e: e.tensor_tensor(out=yb[:, 256 * ob:256 * ob + 256].rearrange("p (h d) -> p h d", d=64), in0=o3[:, :, 0:64],
                                                      in1=rec[:, 4 * ob:4 * ob + 4].unsqueeze(2).broadcast_to([128, 4, 64]), op=ALU.mult),
                     r=[f"dpsO{ob}", "d_rec"], w=["d_yb"])
            yield
            for fb in range(4):
                P.op("pe", lambda e: e.matmul(psI[0][:, fb * 128:(fb + 1) * 128], lhsT=yb[:, fb * 128:(fb + 1) * 128], rhs=K.identf, start=True, stop=True),
                     r=["d_yb", "identf"], w=["dpsI0"])
            P.op("dve", lambda e: e.tensor_tensor(out=ybg[j % 2], in0=psI[0].rearrange("p (f t) -> p f t", f=4), in1=zbj[b], op=ALU.mult),
                 r=["dpsI0", f"d_zbj{b}"], w=[f"d_ybg{j % 2}"])
            P.dma("sp", K.ybgT.rearrange("(fb p) t -> p fb t", p=128)[:, :, jc], ybg[j % 2], r=[f"d_ybg{j % 2}"], w=["ybgT"])

        it_gens = {}
        next_it = 0
        at_j = 0
        at_gen = None
        while at_j < NT:
            while next_it < NT and len(it_gens) < 2 and next_it <= at_j + 2:
                it_gens[next_it] = chain_it(next_it)
                next_it += 1
            for jj in sorted(it_gens):
                try:
                    next(it_gens[jj])
                except StopIteration:
                    del it_gens[jj]
            if at_gen is None and at_j in it_done:
                at_gen = chain_at(at_j)
            if at_gen is not None:
                try:
                    next(at_gen)
                except StopIteration:
                    at_gen = None
                    at_j += 1
        P.barrier()


def phase_out(K, l, xsrc, xdst):
    from contextlib import ExitStack
    nc, P, W, S, TC, NTC = K.nc, K.P, K.L[l], K.S, K.TC, K.NTC
    with ExitStack() as st:
        def sbt(name, shape, dt=F32):
            return st.enter_context(_sbm(nc, name, list(shape), dt)).ap()
        stg = sbt("o_stg", [128, 4, K.D])
        wpa = sbt("o_wpa", [128, 4, K.D], BF16)
        wpb = sbt("o_wpb", [128, 4, K.D], BF16)
        wo = sbt("o_wo", [128, 8, K.D], BF16)
        ya = [sbt(f"o_ya{i}", [128, 4, TC], BF16) for i in range(2)]
        yb = [sbt(f"o_yb{i}", [128, 4, TC], BF16) for i in range(2)]
        ga = [sbt(f"o_ga{i}", [128, 8, TC], BF16) for i in range(2)]
        gb = [sbt(f"o_gb{i}", [128, 8, TC], BF16) for i in range(2)]
        t1 = [sbt(f"o_t1{i}", [128, TC]) for i in range(2)]
        t2 = [sbt(f"o_t2{i}", [128, TC]) for i in range(2)]
        mg = sbt("o_mg", [128, 8, TC], BF16)
        xt = [sbt(f"o_xt{i}", [128, K.D]) for i in range(2)]
        xo = [sbt(f"o_xo{i}", [128, K.D]) for i in range(2)]
        psA = [st.enter_context(_psm(nc, f"opsA{i}", [128, 512], F32)).ap() for i in range(2)]
        psB = [st.enter_context(_psm(nc, f"opsB{i}", [128, 512], F32)).ap() for i in range(2)]
        psO = [st.enter_context(_psm(nc, f"opsO{i}", [128, 512], F32)).ap() for i in range(2)]
        for src, dst, dn in ((W["w_pa"], wpa, "o_wpa"), (W["w_pb"], wpb, "o_wpb")):
            P.dma("sp", stg, src.rearrange("(j p) n -> p j n", p=128), r=["o_stg"], w=["o_stg"])
            P.op("pool", lambda e: e.tensor_copy(out=dst, in_=stg), r=["o_stg"], w=[dn])
        for hf in range(2):
            P.dma("sp", stg, W["w_o"][hf * 512:(hf + 1) * 512, :].rearrange("(j p) n -> p j n", p=128), r=["o_stg"], w=["o_stg"])
            P.op("pool", lambda e: e.tensor_copy(out=wo[:, 4 * hf:4 * hf + 4, :], in_=stg), r=["o_stg"], w=["o_wo"])
        gate = K.modB[:, 2 * K.D:3 * K.D]
        cnt = 0
        xcnt = 0
        def load_chunk(tc):
            b = tc % 2
            tcs = slice(tc * TC, (tc + 1) * TC)
            P.dma("sp", ya[b], K.yagT.rearrange("(j p) t -> p j t", p=128)[:, :, tcs], r=["yagT"], w=[f"o_ya{b}"])
            P.dma("sp", yb[b], K.ybgT.rearrange("(j p) t -> p j t", p=128)[:, :, tcs], r=["ybgT"], w=[f"o_yb{b}"])
            P.dma("sp", ga[b], K.sgaT.rearrange("(j p) t -> p j t", p=128)[:, :, tcs], r=["sgaT"], w=[f"o_ga{b}"])
            P.dma("sp", gb[b], K.sgbT.rearrange("(j p) t -> p j t", p=128)[:, :, tcs], r=["sgbT"], w=[f"o_gb{b}"])

        def load_x(k):
            if k < S // 128:
                P.dma("sp", xt[k % 2], xsrc[k * 128:(k + 1) * 128, :], w=[f"o_xt{k % 2}"])
        load_chunk(0)
        load_x(0)
        for tc in range(NTC):
            b = tc % 2
            tcs = slice(tc * TC, (tc + 1) * TC)
            if tc + 1 < NTC:
                load_chunk(tc + 1)
            for ob in range(8):
                pb = cnt % 2
                cnt += 1
                oc = slice(ob * 128, (ob + 1) * 128)
                for j in range(4):
                    P.op("pe", lambda e: e.matmul(psA[pb][:, 0:TC], lhsT=wpa[:, j, oc], rhs=ya[b][:, j, :], start=(j == 0), stop=(j == 3)),
                         r=["o_wpa", f"o_ya{b}"], w=[f"opsA{pb}"])
                for j in range(4):
                    P.op("pe", lambda e: e.matmul(psB[pb][:, 0:TC], lhsT=wpb[:, j, oc], rhs=yb[b][:, j, :], start=(j == 0), stop=(j == 3)),
                         r=["o_wpb", f"o_yb{b}"], w=[f"opsB{pb}"])
                P.op("dve", lambda e: e.tensor_tensor(out=t1[pb], in0=psA[pb][:, 0:TC], in1=ga[b][:, ob, :], op=ALU.mult),
                     r=[f"opsA{pb}", f"o_ga{b}"], w=[f"o_t1{pb}"])
                P.op("dve", lambda e: e.tensor_tensor(out=t2[pb], in0=psB[pb][:, 0:TC], in1=gb[b][:, ob, :], op=ALU.mult),
                     r=[f"opsB{pb}", f"o_gb{b}"], w=[f"o_t2{pb}"])
                P.op("pool", lambda e: e.tensor_tensor(out=mg[:, ob, :], in0=t1[pb], in1=t2[pb], op=ALU.add),
                     r=[f"o_t1{pb}", f"o_t2{pb}"], w=["o_mg"])
            for ts in range(TC // 128):
                xb_ = xcnt % 2
                xcnt += 1
                rows = slice(tc * TC + ts * 128, tc * TC + (ts + 1) * 128)
                load_x(xcnt)
                for hf in range(2):
                    hc = slice(hf * 512, (hf + 1) * 512)
                    for ob in range(8):
                        P.op("pe", lambda e: e.matmul(psO[hf], lhsT=mg[:, ob, ts * 128:(ts + 1) * 128], rhs=wo[:, ob, hc], start=(ob == 0), stop=(ob == 7)),
                             r=["o_mg", "o_wo"], w=[f"opsO{hf}"])
                    P.op("dve", lambda e: e.tensor_tensor(out=xo[xb_][:, hc], in0=psO[hf], in1=gate[:, hc], op=ALU.mult),
                         r=[f"opsO{hf}", "modB"], w=[f"o_xo{xb_}"])
                P.op("pool", lambda e: e.tensor_tensor(out=xo[xb_], in0=xo[xb_], in1=xt[xb_], op=ALU.add),
                     r=[f"o_xo{xb_}", f"o_xt{xb_}"], w=[f"o_xo{xb_}"])
                P.dma("sp", xdst[rows, :], xo[xb_], r=[f"o_xo{xb_}"], w=["xs"])
        P.barrier()


def phase_final(K, xsrc):
    from contextlib import ExitStack
    nc, P, S, NT = K.nc, K.P, K.S, K.NT
    with ExitStack() as st:
        def sbt(name, shape, dt=F32):
            return st.enter_context(_sbm(nc, name, list(shape), dt)).ap()
        xt = [sbt(f"f_xt{i}", [128, K.D]) for i in range(2)]
        xo = [sbt(f"f_xo{i}", [128, K.D]) for i in range(2)]
        sq = sbt("f_sq", [128, K.D])
        fg = sbt("f_fg", [128, K.D])
        ss = sbt("f_ss", [128, 2])
        ps = [st.enter_context(_psm(nc, f"fps{i}", [128, 512], F32)).ap() for i in range(2)]
        P.dma("sp", K.row[0:1, 0:K.D], K.final_g, r=["row"], w=["row"])
        for hf in range(2):
            P.op("pe", lambda e: e.matmul(ps[hf], lhsT=K.ones[0:1, :], rhs=K.row[0:1, hf * 512:(hf + 1) * 512], start=True, stop=True),
                 r=["ones", "row"], w=[f"fps{hf}"])
            P.op("act", lambda e: e.activation(out=fg[:, hf * 512:(hf + 1) * 512], in_=ps[hf], func=AF.Copy), r=[f"fps{hf}"], w=["f_fg"])
        for i in range(NT):
            b = i % 2
            rows = slice(i * 128, (i + 1) * 128)
            P.dma("sp", xt[b], xsrc[rows, :], r=["xs"], w=[f"f_xt{b}"])
            P.op("act", lambda e: e.activation(out=sq, in_=xt[b], func=AF.Square, accum_out=ss[:, 0:1]), r=[f"f_xt{b}"], w=["f_sq", "f_ss"])
            P.op("dve", lambda e: e.tensor_scalar(out=ss[:, 1:2], in0=ss[:, 0:1], scalar1=1.0 / K.D, scalar2=K.EPS, op0=ALU.mult, op1=ALU.add),
                 r=["f_ss"], w=["f_ss"])
            P.op("act", lambda e: e.activation(out=ss[:, 1:2], in_=ss[:, 1:2], func=AF.Sqrt), r=["f_ss"], w=["f_ss"])
            P.op("dve", lambda e: e.reciprocal(out=ss[:, 1:2], in_=ss[:, 1:2]), r=["f_ss"], w=["f_ss"])
            P.op("dve", lambda e: e.scalar_tensor_tensor(out=xo[b], in0=xt[b], scalar=ss[:, 1:2], in1=fg, op0=ALU.mult, op1=ALU.mult),
                 r=[f"f_xt{b}", "f_ss", "f_fg"], w=[f"f_xo{b}"])
            P.dma("sp", K.out[rows, :], xo[b], r=[f"f_xo{b}"], is_output=True)
        P.barrier()


_CACHE = {}


def kernel(**inputs):
    S = int(np.asarray(inputs["x"]).shape[1])
    B = int(np.asarray(inputs["x"]).shape[0])
    depth = int(np.asarray(inputs["w_in"]).shape[0])
    topk = min(256, S // 4)
    key = (S, depth, topk)
    if key not in _CACHE:
        _CACHE[key] = build(S, topk, depth=depth)[0]
    nc = _CACHE[key]
    shared = shared_maps(inputs, depth)
    in_maps = [core_map(inputs, shared, b) for b in range(B)]
    res = run_bass_kernel_spmd(nc, in_maps, core_ids=list(range(B)))
    return np.stack([np.asarray(r["out"], np.float32) for r in res.results], axis=0)
```

```python
import numpy as np
import concourse.bass as bass
import concourse.mybir as mybir
from concourse.bass_utils import run_bass_kernel_spmd

F32 = mybir.dt.float32
BF16 = mybir.dt.bfloat16
AF = mybir.ActivationFunctionType
ALU = mybir.AluOpType
AX = mybir.AxisListType


class Prog:
    NDSEM = 14
    PSUM_PREFIXES = ("mps", "pps", "pst", "ppq", "rps", "dps", "ops", "fps")

    def __init__(self, nc):
        self.nc = nc
        self.eng = dict(pe=nc.tensor, act=nc.scalar, dve=nc.vector, pool=nc.gpsimd, sp=nc.sync)
        self.csem = {k: nc.alloc_semaphore(name=f"c_{k}") for k in self.eng}
        self.cnt = {k: 0 for k in self.eng}
        self.seen = {k: {} for k in self.eng}
        self.lastw = {}
        self.readers = {}
        self.dsem = {q: [nc.alloc_semaphore(name=f"d_{q}{i}") for i in range(self.NDSEM)]
                     for q in ("sp", "pool", "act")}
        self.dcnt = {q: [0] * self.NDSEM for q in self.dsem}
        self.drr = {q: 0 for q in self.dsem}
        self.uid = 0
        self.out_tokens = []
        self.ninstr = 0

    def _wait(self, e, tok):
        sem, val = tok
        if self.seen[e].get(sem.num, 0) >= val:
            return
        self.eng[e].wait_ge(sem, val)
        self.seen[e][sem.num] = val

    def _deps(self, e, r, w):
        for b in r:
            lw = self.lastw.get(b)
            if lw is not None and not (lw[0] == e == "pe"):
                self._wait(e, lw[1])
            if b.startswith(self.PSUM_PREFIXES):
                for re_, tok in self.readers.get(b, {}).items():
                    if re_ != e:
                        self._wait(e, tok)
        for b in w:
            lw = self.lastw.get(b)
            if lw is not None and not (lw[0] == e == "pe"):
                self._wait(e, lw[1])
            for re_, tok in self.readers.get(b, {}).items():
                if re_ != e or e == "pool":
                    self._wait(e, tok)

    def _commit(self, key, tok, r, w):
        for b in r:
            self.readers.setdefault(b, {})[key] = tok
        for b in w:
            self.lastw[b] = (key, tok)
            self.readers[b] = {}

    def op(self, e, fn, r=(), w=()):
        self._deps(e, r, w)
        ins = fn(self.eng[e])
        self.cnt[e] += 1
        ins.then_inc(self.csem[e], 1)
        self._commit(e, (self.csem[e], self.cnt[e]), r, w)
        self.ninstr += 1

    def dma(self, q, out, in_, r=(), w=(), is_output=False, **kw):
        i = self.drr[q]
        self.drr[q] = (i + 1) % self.NDSEM
        sem = self.dsem[q][i]
        if self.dcnt[q][i] > 0:
            self._wait(q, (sem, self.dcnt[q][i]))
        self._deps(q, r, w)
        ins = self.eng[q].dma_start(out=out, in_=in_, **kw)
        self.dcnt[q][i] += 16
        ins.then_inc(sem, 16)
        tok = (sem, self.dcnt[q][i])
        self.uid += 1
        self._commit(f"dma{self.uid}", tok, r, w)
        if is_output:
            self.out_tokens.append(tok)
        self.ninstr += 1

    def barrier(self):
        for e in self.eng:
            for o in self.eng:
                if o != e and self.cnt[o] > 0:
                    self._wait(e, (self.csem[o], self.cnt[o]))
            for q in self.dsem:
                for i, sem in enumerate(self.dsem[q]):
                    if self.dcnt[q][i] > 0:
                        self._wait(e, (sem, self.dcnt[q][i]))

    def finish(self):
        for q in self.dsem:
            for i, sem in enumerate(self.dsem[q]):
                if self.dcnt[q][i] > 0:
                    self._wait("sp", (sem, self.dcnt[q][i]))
        for e in ("pe", "act", "dve", "pool"):
            if self.cnt[e] > 0:
                self._wait("sp", (self.csem[e], self.cnt[e]))


D = 1024
NIN = 5960
C_R, C_K, C_V, C_WD, C_AD, C_ZA, C_Q, C_CKV, C_ZB, C_QI, C_KI, C_WI, C_GA, C_GB = (
    0, 512, 1024, 1536, 1600, 1664, 2176, 2688, 2816, 3328, 3840, 3904, 3912, 4936)
C0 = 0.6065306597126334
NEG = -1.0e30

COLS = dict(mu=(0, 13), w0=(13, 4), a0=(17, 4), kk=(21, 4), ka=(25, 4), rk=(29, 4),
            lg=(33, 4), lb=(37, 4), kvg=(41, 1))
NCOL = 42


def t5_bucket_np(dist):
    import math
    max_exact = 16
    d = np.maximum(dist, 1).astype(np.float32)
    large = max_exact + (np.log(d / np.float32(max_exact)) / np.float32(math.log(128 / max_exact))
                         * np.float32(32 - max_exact)).astype(np.int32)
    large = np.minimum(large, 31)
    return np.where(dist < max_exact, dist, large)


class Ctx:
    pass


_UNIQ = [0]


def _sbm(nc, name, shape, dt):
    _UNIQ[0] += 1
    return nc.sbuf_tensor(f"{name}_u{_UNIQ[0]}", shape, dt)


def _psm(nc, name, shape, dt):
    _UNIQ[0] += 1
    return nc.psum_tensor(f"{name}_u{_UNIQ[0]}", shape, dt)


def build(S, TOPK, depth=2, dbg=(), stop_after=None, stop_layer=0):
    from contextlib import ExitStack
    nc = bass.Bass("TRN2", target_bir_lowering=False)
    P = Prog(nc)
    NT = S // 128
    TC = min(512, S)
    NTC = S // TC
    EPS = 1e-6

    def din(name, shape, dt=F32):
        return nc.dram_tensor(name, list(shape), dt, kind="ExternalInput").ap()

    def scr(name, shape, dt=F32):
        kind = "ExternalOutput" if name in dbg else "Internal"
        return nc.dram_tensor(name, list(shape), dt, kind=kind).ap()

    x_in = din("x", [S, D])
    cT = din("cT", [128, 8])
    final_g = din("final_g", [1, D])
    ebias = din("ebias", [2, 128, 8, 128])
    eb31 = din("eb31", [128, 8, 128])
    L = []
    for l in range(depth):
        L.append(dict(
            ada_w=din(f"ada_w{l}", [D, 3 * D]), ada_b=din(f"ada_b{l}", [1, 3 * D]),
            norm_g=din(f"norm_g{l}", [1, D]), w_in=din(f"w_in{l}", [D, NIN]),
            cols=din(f"cols{l}", [128, NCOL]), w2=din(f"w2{l}", [64, 512]), a2=din(f"a2{l}", [64, 512]),
            wukT=din(f"wukT{l}", [128, 4, 128]), wuv=din(f"wuv{l}", [128, 512]),
            w_pa=din(f"w_pa{l}", [512, D]), w_pb=din(f"w_pb{l}", [512, D]), w_o=din(f"w_o{l}", [D, D])))
    out = nc.dram_tensor("out", [S, D], F32, kind="ExternalOutput").ap()
    xs = [scr(f"xs{i}", [S, D]) for i in range(2)]
    psT = scr("psT", [1664, S])
    szaT = scr("szaT", [512, S], BF16)
    szbT = scr("szbT", [512, S], BF16)
    sgaT = scr("sgaT", [D, S], BF16)
    sgbT = scr("sgbT", [D, S], BF16)
    qabsT = scr("qabsT", [8, 128, S], BF16)
    latT = scr("latT", [128, S], BF16)
    vext = scr("vext", [NT, 128, 520], BF16)
    qiT = scr("qiT", [512, S])
    kiT = scr("kiT", [64, S])
    witm = scr("witm", [128, NT * 8])
    yagT = scr("yagT", [512, S], BF16)
    ybgT = scr("ybgT", [512, S], BF16)

    def sb(name, shape, dt=F32):
        return nc.alloc_sbuf_tensor(name, list(shape), dt).ap()

    ones = sb("ones", [128, 128])
    identf = sb("identf", [128, 128])
    identb = sb("identb", [128, 128], BF16)
    bdiag = sb("bdiag", [128, 128])
    u2m = sb("u2m", [64, 128])
    slm = sb("slm", [64, 64])
    cact = sb("cact", [128, 8])
    cbc = sb("cbc", [128, 8, 128])
    modB = sb("modB", [128, 3 * D])
    Gt = sb("Gt", [128, D])
    cols = sb("cols", [128, NCOL])
    omka = sb("omka", [128, 4])
    row = sb("row", [1, 3 * D])

    reg_zero = nc.gpsimd.to_reg(0.0)
    reg_neg = nc.gpsimd.to_reg(NEG)
    P.op("pool", lambda e: e.memset(ones, 1.0), w=["ones"])
    P.op("pool", lambda e: e.affine_select(out=identf, in_=ones, pattern=[[-1, 128]], compare_op=ALU.is_equal,
                                           fill=reg_zero, base=0, channel_multiplier=1), r=["ones"], w=["identf"])
    P.op("pool", lambda e: e.tensor_copy(out=identb, in_=identf), r=["identf"], w=["identb"])
    P.op("pool", lambda e: e.memset(bdiag, 0.0), w=["bdiag"])
    P.op("pool", lambda e: e.memset(bdiag[0:64, 0:64], 1.0), w=["bdiag"])
    P.op("pool", lambda e: e.memset(bdiag[64:128, 64:128], 1.0), w=["bdiag"])
    P.op("pool", lambda e: e.affine_select(out=u2m[:, 0:64], in_=ones[0:64, 0:64], pattern=[[1, 64]], compare_op=ALU.is_ge,
                                           fill=reg_zero, base=-1, channel_multiplier=-1), r=["ones"], w=["u2m"])
    P.op("pool", lambda e: e.affine_select(out=u2m[:, 64:128], in_=ones[0:64, 0:64], pattern=[[1, 64]], compare_op=ALU.is_ge,
                                           fill=reg_zero, base=0, channel_multiplier=-1), r=["ones"], w=["u2m"])
    P.op("pool", lambda e: e.affine_select(out=slm, in_=ones[0:64, 0:64], pattern=[[-1, 64]], compare_op=ALU.is_ge,
                                           fill=reg_zero, base=-1, channel_multiplier=1), r=["ones"], w=["slm"])
    P.dma("sp", cact, cT, w=["cact"])
    P.op("act", lambda e: e.activation(out=cact, in_=cact, func=AF.Silu), r=["cact"], w=["cact"])
    for j in range(8):
        P.op("dve", lambda e, j=j: e.tensor_scalar(out=cbc[:, j, :], in0=ones, scalar1=cact[:, j:j + 1], scalar2=None,
                                                   op0=ALU.mult), r=["cact", "ones"], w=["cbc"])

    K = Ctx()
    K.__dict__.update(locals())
    K.D = D
    K.COLS = COLS
    for l in range(depth):
        xsrc = x_in if l == 0 else xs[(l - 1) % 2]
        xdst = xs[l % 2]
        phase_mod(K, l)
        if stop_after == "mod" and l == stop_layer:
            break
        phase_proj(K, l, xsrc)
        if stop_after == f"B{l}":
            break
        if stop_after == "proj" and l == stop_layer:
            break
        phase_rwkv(K, l)
        if stop_after == "rwkv" and l == stop_layer:
            break
        phase_dsa(K, l)
        if stop_after == "dsa" and l == stop_layer:
            break
        phase_out(K, l, xsrc, xdst)
        if stop_after == "out" and l == stop_layer:
            break
    else:
        phase_final(K, xs[(depth - 1) % 2])
    P.finish()
    return nc, P


def phase_mod(K, l):
    from contextlib import ExitStack
    nc, P, W = K.nc, K.P, K.L[l]
    with ExitStack() as st:
        wb = [st.enter_context(_sbm(nc, f"mw{i}", [128, 8, 512], F32)).ap() for i in range(2)]
        ps = [st.enter_context(_psm(nc, f"mps{i}", [128, 512], F32)).ap() for i in range(2)]
        P.dma("sp", K.cols, W["cols"], w=["cols"])
        P.dma("sp", K.row, W["ada_b"], w=["row"])
        P.op("dve", lambda e: e.tensor_scalar(out=K.omka, in0=K.cols[:, 25:29], scalar1=-1.0, scalar2=1.0,
                                              op0=ALU.mult, op1=ALU.add), r=["cols"], w=["omka"])
        for nb in range(6):
            b = nb % 2
            P.dma("sp", wb[b], W["ada_w"][:, nb * 512:(nb + 1) * 512].rearrange("(j p) n -> p j n", p=128), w=[f"mw{b}"])
            for j in range(8):
                P.op("pe", lambda e, j=j: e.matmul(ps[b], lhsT=K.cbc[:, j, :], rhs=wb[b][:, j, :], start=(j == 0), stop=False),
                     r=["cbc", f"mw{b}"], w=[f"mps{b}"])
            P.op("pe", lambda e: e.matmul(ps[b], lhsT=K.ones[0:1, :], rhs=K.row[0:1, nb * 512:(nb + 1) * 512], start=False, stop=True),
                 r=["ones", "row"], w=[f"mps{b}"])
            P.op("act", lambda e: e.activation(out=K.modB[:, nb * 512:(nb + 1) * 512], in_=ps[b], func=AF.Copy),
                 r=[f"mps{b}"], w=["modB"])
        P.dma("sp", K.row[0:1, 0:K.D], W["norm_g"], r=["row"], w=["row"])
        for hf in range(2):
            P.op("pe", lambda e: e.matmul(ps[hf], lhsT=K.ones[0:1, :], rhs=K.row[0:1, hf * 512:(hf + 1) * 512], start=True, stop=True),
                 r=["ones", "row"], w=[f"mps{hf}"])
            P.op("dve", lambda e: e.scalar_tensor_tensor(out=K.Gt[:, hf * 512:(hf + 1) * 512],
                                                         in0=K.modB[:, K.D + hf * 512:K.D + (hf + 1) * 512], scalar=1.0,
                                                         in1=ps[hf], op0=ALU.add, op1=ALU.mult),
                 r=["modB", f"mps{hf}"], w=["Gt"])


def proj_blocks():
    blks = []
    for i in range(13):
        blks.append((i * 128, 128, "shift", i))
    for i in range(4):
        blks.append((C_ZA + i * 128, 128, "sza", i))
    for i in range(4):
        blks.append((C_Q + i * 128, 128, "q", i))
    blks.append((C_CKV, 128, "ckv", 0))
    for i in range(4):
        blks.append((C_ZB + i * 128, 128, "szb", i))
    for i in range(8):
        blks.append((C_GA + i * 128, 128, "sga", i))
    for i in range(8):
        blks.append((C_GB + i * 128, 128, "sgb", i))
    return blks


def phase_proj(K, l, xsrc):
    from contextlib import ExitStack
    nc, P, W, S, NT, TC, NTC = K.nc, K.P, K.L[l], K.S, K.NT, K.TC, K.NTC
    with ExitStack() as st:
        def sbt(name, shape, dt=F32):
            return st.enter_context(_sbm(nc, name, list(shape), dt)).ap()
        hT = sbt("hT", [128, 8, S], BF16)
        ps = [st.enter_context(_psm(nc, f"pps{i}", [128, 512], F32)).ap() for i in range(4)]
        with ExitStack() as st2:
            def sb2(name, shape, dt=F32):
                return st2.enter_context(_sbm(nc, name, list(shape), dt)).ap()
            pst = [st2.enter_context(_psm(nc, f"pst{i}", [128, 512], F32)).ap() for i in range(2)]
            pq = [st2.enter_context(_psm(nc, f"ppq{i}", [128, 512], F32)).ap() for i in range(2)]
            xb = [sb2(f"xb{i}", [128, K.D]) for i in range(2)]
            t1s = [sb2(f"t1{i}", [128, K.D]) for i in range(2)]
            sqj = sb2("sqj", [128, K.D])
            sss = [sb2(f"ss{i}", [128, 2]) for i in range(2)]
            hTfs = [sb2(f"hTf{i}", [128, 8, 128]) for i in range(2)]
            wq = sb2("wq", [128, 8, 584])
            pqs = [sb2(f"pqs{i}", [128, 5, 128]) for i in range(2)]
            wit = sb2("wit", [128, NT * 8])
            tok = [sb2(f"tok{i}", [128, 584]) for i in range(2)]
            P.dma("sp", wq, W["w_in"][:, C_QI:C_QI + 584].rearrange("(j p) n -> p j n", p=128), w=["wq"])
            for i in range(NT):
                b = i % 2
                tcs = slice(i * 128, (i + 1) * 128)
                t1, ss, hTf = t1s[b], sss[b], hTfs[b]
                nt1, nss, nhf = f"t1{b}", f"ss{b}", f"hTf{b}"
                if i == 0:
                    P.dma("sp", xb[0], xsrc[0:128, :], w=["xb0"])
                if i + 1 < NT:
                    P.dma("sp", xb[1 - b], xsrc[(i + 1) * 128:(i + 2) * 128, :], w=[f"xb{1 - b}"])
                P.op("act", lambda e: e.activation(out=sqj, in_=xb[b], func=AF.Square, accum_out=ss[:, 0:1]),
                     r=[f"xb{b}"], w=["sqj", nss])
                P.op("dve", lambda e: e.tensor_scalar(out=ss[:, 1:2], in0=ss[:, 0:1], scalar1=1.0 / K.D, scalar2=K.EPS,
                                                      op0=ALU.mult, op1=ALU.add), r=[nss], w=[nss])
                P.op("act", lambda e: e.activation(out=ss[:, 1:2], in_=ss[:, 1:2], func=AF.Sqrt), r=[nss], w=[nss])
                P.op("dve", lambda e: e.reciprocal(out=ss[:, 1:2], in_=ss[:, 1:2]), r=[nss], w=[nss])
                P.op("dve", lambda e: e.scalar_tensor_tensor(out=t1, in0=xb[b], scalar=ss[:, 1:2], in1=K.Gt,
                                                             op0=ALU.mult, op1=ALU.mult), r=[f"xb{b}", nss, "Gt"], w=[nt1])
                P.op("dve", lambda e: e.tensor_tensor(out=t1, in0=t1, in1=K.modB[:, 0:K.D], op=ALU.add),
                     r=[nt1, "modB"], w=[nt1])
                for j in range(8):
                    P.op("pe", lambda e: e.matmul(pst[j // 4][:, (j % 4) * 128:(j % 4 + 1) * 128], lhsT=t1[:, j * 128:(j + 1) * 128],
                                                  rhs=K.identf, start=True, stop=True), r=[nt1, "identf"], w=[f"pst{j // 4}"])
                for hh in range(2):
                    src = pst[hh].rearrange("p (j t) -> p j t", j=4)
                    P.op("act", lambda e: e.activation(out=hT[:, 4 * hh:4 * hh + 4, tcs], in_=src, func=AF.Copy), r=[f"pst{hh}"], w=["hT"])
                    P.op("dve", lambda e: e.tensor_copy(out=hTf[:, 4 * hh:4 * hh + 4, :], in_=src), r=[f"pst{hh}"], w=[nhf])
                for j in range(8):
                    P.op("pe", lambda e: e.matmul(pq[0][:, 0:512], lhsT=hTf[:, j, :], rhs=wq[:, j, 0:512], start=(j == 0), stop=(j == 7)),
                         r=["wq", nhf], w=["ppq0"])
                    P.op("pe", lambda e: e.matmul(pq[1][:, 0:72], lhsT=hTf[:, j, :], rhs=wq[:, j, 512:584], start=(j == 0), stop=(j == 7)),
                         r=["wq", nhf], w=["ppq1"])
                P.op("act", lambda e: e.activation(out=tok[b][:, 0:512], in_=pq[0][:, 0:512], func=AF.Copy), r=["ppq0"], w=[f"tok{b}"])
                P.op("act", lambda e: e.activation(out=tok[b][:, 512:584], in_=pq[1][:, 0:72], func=AF.Copy), r=["ppq1"], w=[f"tok{b}"])
                P.op("dve", lambda e: e.tensor_copy(out=wit[:, i * 8:(i + 1) * 8], in_=tok[b][:, 576:584]), r=[f"tok{b}"], w=["wit"])
                for q_ in range(4):
                    P.op("pe", lambda e: e.matmul(ps[0][:, q_ * 128:(q_ + 1) * 128], lhsT=tok[b][:, q_ * 128:(q_ + 1) * 128], rhs=K.identf,
                                                  start=True, stop=True), r=[f"tok{b}", "identf"], w=["pps0"])
                P.op("pe", lambda e: e.matmul(ps[1][0:64, 0:128], lhsT=tok[b][:, 512:576], rhs=K.identf, start=True, stop=True),
                     r=[f"tok{b}", "identf"], w=["pps1"])
                P.op("act", lambda e: e.activation(out=pqs[b][:, 0:4, :], in_=ps[0].rearrange("p (q t) -> p q t", q=4), func=AF.Copy),
                     r=["pps0"], w=[f"pqs{b}"])
                P.op("dve", lambda e: e.tensor_copy(out=pqs[b][0:64, 4, :], in_=ps[1][0:64, 0:128]), r=["pps1"], w=[f"pqs{b}"])
                P.dma("sp", K.qiT.rearrange("(q p) t -> p q t", p=128)[:, :, tcs], pqs[b][:, 0:4, :], r=[f"pqs{b}"], w=["qiT"])
                P.dma("sp", K.kiT[:, tcs], pqs[b][0:64, 4, :], r=[f"pqs{b}"], w=["kiT"])
            P.dma("sp", K.witm, wit, r=["wit"], w=["witm"])
            P.barrier()
        if K.stop_after == f"B{l}":
            return
        wfs = [sbt(f"wfs{i}", [128, 8, 512]) for i in range(2)]
        wbf = [sbt(f"wbf{i}", [128, 8, 128], BF16) for i in range(2)]
        blkf = [sbt(f"blkf{i}", [128, S]) for i in range(2)]
        tmpf = [sbt("tmpf0", [128, S])]
        blkb = [sbt(f"blkb{i}", [128, S], BF16) for i in range(2)]
        wuk = blkf[0][:, 0:512].rearrange("p (a c) -> p a c", a=4)
        wukb = sbt("wukb", [128, 4, 128], BF16)
        wuv = blkf[1][:, 0:512]
        wuvb = sbt("wuvb", [128, 512], BF16)
        qa = [sbt(f"qa{i}", [128, TC], BF16) for i in range(2)]
        vx = [sbt(f"vx{i}", [128, 8, 65], BF16) for i in range(2)]
        P.dma("sp", wuk, W["wukT"], w=["blkf0"])
        P.op("pool", lambda e: e.tensor_copy(out=wukb, in_=wuk), r=["blkf0"], w=["wukb"])
        P.dma("sp", wuv, W["wuv"], w=["blkf1"])
        P.op("pool", lambda e: e.tensor_copy(out=wuvb, in_=wuv), r=["blkf1"], w=["wuvb"])
        for i in range(2):
            P.op("pool", lambda e: e.memset(vx[i], 1.0), w=[f"vx{i}"])
        pcount = 0
        qcount = 0
        blocks = proj_blocks()
        supers = []
        for (r0, r1) in ((0, C_QI), (C_GA, NIN)):
            c = r0
            while c < r1:
                supers.append((c, min(512, r1 - c)))
                c += 512
        sup_of = {}
        for si, (c0, cn) in enumerate(supers):
            for (cs, n, kind, idx) in blocks:
                if c0 <= cs < c0 + cn:
                    sup_of[cs] = si
        loaded = set()

        def load_super(si):
            if si in loaded or si >= len(supers):
                return
            loaded.add(si)
            c0, cn = supers[si]
            P.dma("sp", wfs[si % 2][:, :, 0:cn], W["w_in"][:, c0:c0 + cn].rearrange("(j p) n -> p j n", p=128), w=[f"wfs{si % 2}"])
        load_super(0)
        for bi, (cs, n, kind, idx) in enumerate(blocks):
            b = bi % 2
            si = sup_of[cs]
            load_super(si)
            if cs == supers[si][0]:
                load_super(si + 1)
            o = cs - supers[si][0]
            P.op("pool", lambda e: e.tensor_copy(out=wbf[b][:, :, 0:n], in_=wfs[si % 2][:, :, o:o + n]), r=[f"wfs{si % 2}"], w=[f"wbf{b}"])
            isb = kind in ("sza", "szb", "sga", "sgb", "q")
            dst, dname = (blkb[b], f"blkb{b}") if isb else (blkf[b], f"blkf{b}")
            func = AF.Silu if kind in ("sza", "szb") else (AF.Sigmoid if kind in ("sga", "sgb") else AF.Copy)
            for tc in range(NTC):
                pb = pcount % 2
                pcount += 1
                for j in range(8):
                    P.op("pe", lambda e: e.matmul(ps[pb][0:n, 0:TC], lhsT=wbf[b][:, j, 0:n], rhs=hT[:, j, tc * TC:(tc + 1) * TC],
                                                  start=(j == 0), stop=(j == 7)), r=[f"wbf{b}", "hT"], w=[f"pps{pb}"])
                P.op("act", lambda e: e.activation(out=dst[0:n, tc * TC:(tc + 1) * TC], in_=ps[pb][0:n, 0:TC], func=func),
                     r=[f"pps{pb}"], w=[dname])
            if kind == "shift":
                tm, tn = tmpf[0], "tmpf0"
                P.op("dve", lambda e: e.tensor_tensor(out=tm[:, 1:S], in0=dst[:, 0:S - 1], in1=dst[:, 1:S], op=ALU.subtract),
                     r=[dname], w=[tn])
                P.op("dve", lambda e: e.tensor_scalar(out=tm[:, 0:1], in0=dst[:, 0:1], scalar1=-1.0, scalar2=None, op0=ALU.mult),
                     r=[dname], w=[tn])
                P.op("dve", lambda e: e.scalar_tensor_tensor(out=tm, in0=tm, scalar=K.cols[:, idx:idx + 1], in1=dst,
                                                             op0=ALU.mult, op1=ALU.add), r=[tn, dname, "cols"], w=[tn])
                P.dma("sp", K.psT[cs:cs + 128, :], tm, r=[tn], w=["psT"])
            elif kind in ("sza", "szb", "sga", "sgb"):
                tgt = dict(sza=K.szaT, szb=K.szbT, sga=K.sgaT, sgb=K.sgbT)[kind]
                P.dma("sp", tgt[idx * 128:(idx + 1) * 128, :], dst, r=[dname], w=[kind + "T"])
            elif kind == "q":
                for hp in range(2):
                    for tc in range(NTC):
                        pb = 2 + qcount % 2
                        qb = qcount % 2
                        qcount += 1
                        P.op("pe", lambda e: e.matmul(ps[pb][:, 0:TC], lhsT=wukb[64 * hp:64 * hp + 64, idx, :],
                                                      rhs=dst[64 * hp:64 * hp + 64, tc * TC:(tc + 1) * TC], start=True, stop=True),
                             r=["wukb", dname], w=[f"pps{pb}"])
                        P.op("act", lambda e: e.activation(out=qa[qb], in_=ps[pb][:, 0:TC], func=AF.Copy, scale=0.125),
                             r=[f"pps{pb}"], w=[f"qa{qb}"])
                        P.dma("sp", K.qabsT[2 * idx + hp, :, tc * TC:(tc + 1) * TC], qa[qb], r=[f"qa{qb}"], w=["qabsT"])
            elif kind == "ckv":
                tm, tn = tmpf[0], "tmpf0"
                rs, rn = blkf[1 - b], f"blkf{1 - b}"
                P.op("act", lambda e: e.activation(out=tm, in_=dst, func=AF.Square), r=[dname], w=[tn])
                for tc in range(NTC):
                    pb = 2 + tc % 2
                    P.op("pe", lambda e: e.matmul(ps[pb][:, 0:TC], lhsT=K.ones, rhs=tm[:, tc * TC:(tc + 1) * TC], start=True, stop=True),
                         r=["ones", tn], w=[f"pps{pb}"])
                    P.op("act", lambda e: e.activation(out=rs[:, tc * TC:(tc + 1) * TC], in_=ps[pb][:, 0:TC], func=AF.Sqrt,
                                                       scale=1.0 / 128, bias=K.EPS), r=[f"pps{pb}"], w=[rn])
                P.op("dve", lambda e: e.reciprocal(out=rs, in_=rs), r=[rn], w=[rn])
                lb_, ln_ = blkb[b], f"blkb{b}"
                P.op("dve", lambda e: e.scalar_tensor_tensor(out=lb_, in0=dst, scalar=K.cols[:, 41:42], in1=rs,
                                                             op0=ALU.mult, op1=ALU.mult), r=[dname, rn, "cols"], w=[ln_])
                P.dma("sp", K.latT, lb_, r=[ln_], w=["latT"])
                for i in range(NT):
                    pb = 2 + i % 2
                    vb = i % 2
                    P.op("pe", lambda e: e.matmul(ps[pb], lhsT=lb_[:, i * 128:(i + 1) * 128], rhs=wuvb, start=True, stop=True),
                         r=[ln_, "wuvb"], w=[f"pps{pb}"])
                    P.op("act", lambda e: e.activation(out=vx[vb][:, :, 0:64], in_=ps[pb].rearrange("p (h d) -> p h d", h=8),
                                                       func=AF.Copy), r=[f"pps{pb}"], w=[f"vx{vb}"])
                    P.dma("sp", K.vext[i], vx[vb].rearrange("p h d -> p (h d)"), r=[f"vx{vb}"], w=["vext"])
        P.barrier()


def colpack(v):
    return np.ascontiguousarray(np.asarray(v, np.float32).reshape(-1, 128).T)


def shared_maps(inp, depth):
    m = {}
    m["final_g"] = np.asarray(inp["final_g"], np.float32).reshape(1, D)
    rb = np.asarray(inp["rel_bias"], np.float32)
    s_ = np.arange(128)[:, None]
    t_ = np.arange(128)[None, :]
    eb = np.zeros((2, 128, 8, 128), np.float32)
    for kind, off in ((0, 0), (1, 128)):
        dist = t_ - s_ + off
        bk = t5_bucket_np(np.maximum(dist, 0))
        eb[kind] = np.transpose(rb[bk], (0, 2, 1))
    m["ebias"] = eb
    m["eb31"] = np.ascontiguousarray(np.broadcast_to(rb[31][None, :, None], (128, 8, 128))).astype(np.float32)
    for l in range(depth):
        g = lambda k: np.asarray(inp[k][l], np.float32)
        m[f"ada_w{l}"] = g("ada_w")
        m[f"ada_b{l}"] = g("ada_b").reshape(1, -1)
        m[f"norm_g{l}"] = g("norm_g").reshape(1, -1)
        m[f"w_in{l}"] = g("w_in")
        m[f"cols{l}"] = np.ascontiguousarray(np.concatenate([
            colpack(g("shift_mu")), colpack(g("w0")), colpack(g("a0")), colpack(g("k_k")), colpack(g("k_a")),
            colpack(g("r_k").reshape(-1)), colpack(g("lnx_g")), colpack(g("lnx_b")), colpack(g("kv_norm_g"))], axis=1))
        m[f"w2{l}"] = g("w2")
        m[f"a2{l}"] = g("a2")
        wuk = g("w_uk")
        m[f"wukT{l}"] = np.ascontiguousarray(wuk.reshape(128, 4, 2, 64).transpose(2, 3, 1, 0).reshape(128, 4, 128))
        m[f"wuv{l}"] = g("w_uv").reshape(128, 512)
        m[f"w_pa{l}"] = g("w_pa")
        m[f"w_pb{l}"] = g("w_pb")
        m[f"w_o{l}"] = g("w_o")
    return m


def core_map(inp, shared, b):
    m = dict(shared)
    m["x"] = np.ascontiguousarray(np.asarray(inp["x"][b], np.float32))
    m["cT"] = colpack(np.asarray(inp["c"][b], np.float32))
    return m


def phase_rwkv(K, l):
    from contextlib import ExitStack
    nc, P, W, S = K.nc, K.P, K.L[l], K.S
    SEG = min(1024, S)
    NSEG = S // SEG
    NCH = SEG // 64
    CC = min(512, SEG)
    NCC = SEG // CC
    GN_EPS = 64e-5
    with ExitStack() as st:
        def sbt(name, shape, dt=F32):
            return st.enter_context(_sbm(nc, name, list(shape), dt)).ap()
        names = ["lwp", "ai", "kkn", "km", "bb", "Lp", "Lpe", "E1", "E4", "bonT", "tmpA", "OG", "cmask"]
        T = {n: sbt("r_" + n, [128, SEG]) for n in names}
        AR = sbt("r_AR", [128, 2, SEG])
        BK = sbt("r_BK", [128, 2, SEG])
        BKh = sbt("r_BKh", [128, 2, SEG])
        inb = [dict(rT=sbt(f"r_rT{i}", [128, SEG]), kT=sbt(f"r_kT{i}", [128, SEG]), vT=sbt(f"r_vT{i}", [128, SEG]),
                    wdT=sbt(f"r_wdT{i}", [64, SEG]), adT=sbt(f"r_adT{i}", [64, SEG]), gz=sbt(f"r_gz{i}", [128, SEG], BF16)) for i in range(2)]
        w2 = sbt("r_w2", [64, 512])
        a2 = sbt("r_a2", [64, 512])
        ogb = sbt("r_ogb", [128, SEG], BF16)
        u2 = sbt("r_u2", [128, 128])
        sl = sbt("r_sl", [128, 64])
        NSLOT = 6
        tms = sbt("r_tms", [128, NCH, 192])
        MAKAs = sbt("r_MAKAs", [128, NCH, 256])
        TTs = sbt("r_TTs", [128, NCH, 64])
        Yall = T["Lpe"].rearrange("p (c v) -> p c v", v=64)
        Ysq = T["tmpA"].rearrange("p (c v) -> p c v", v=64)
        gst = sbt("r_gst", [128, 4, NCH])
        Ps = [[sbt(f"r_P{i}s{k}", [128, 64], BF16) for i in range(2)] for k in range(NSLOT)]
        Qs = [[sbt(f"r_Q{i}s{k}", [128, 64], BF16) for i in range(2)] for k in range(NSLOT)]
        Q0s = [sbt(f"r_Q0f{k}", [128, 64]) for k in range(NSLOT)]
        TTk = [[sbt(f"r_TT{i}s{k}", [128, 64]) for i in range(2)] for k in range(NSLOT)]
        TTb = [[sbt(f"r_TTb{i}s{k}", [128, 64], BF16) for i in range(2)] for k in range(NSLOT)]
        Xs = sbt("r_X", [128, 64])
        Us = sbt("r_U", [128, 64])
        S0 = [sbt(f"r_S{i}", [128, 64]) for i in range(2)]
        psB = [st.enter_context(_psm(nc, f"rpsB{i}", [128, 512], F32)).ap() for i in range(NSLOT)]
        psX = [st.enter_context(_psm(nc, f"rpsX{i}", [128, 512], F32)).ap() for i in range(2)]
        psP = psB

        P.dma("sp", w2, W["w2"], w=["r_w2"])
        P.dma("sp", a2, W["a2"], w=["r_a2"])
        for par in range(2):
            pr = slice(64 * par, 64 * par + 64)
            P.dma("sp", u2[pr, :], K.u2m, r=["u2m"], w=["r_u2"])
            P.dma("sp", sl[pr, :], K.slm, r=["slm"], w=["r_sl"])
        cm = T["cmask"]
        P.op("pool", lambda e: e.memset(cm, 1.0), w=["r_cmask"])
        P.op("pool", lambda e: e.memset(cm.rearrange("p (c t) -> p c t", t=64)[:, :, 0:1], 0.0), w=["r_cmask"])

        def tt(out, in0, in1, op, r, w, eng="dve"):
            P.op(eng, lambda e: e.tensor_tensor(out=out, in0=in0, in1=in1, op=op), r=r, w=w)

        def load_inputs(ibuf, hp_, sg_):
            sc_ = slice(sg_ * SEG, (sg_ + 1) * SEG)
            hc_ = slice(hp_ * 128, (hp_ + 1) * 128)
            d = inb[ibuf]
            P.dma("sp", d["rT"], K.psT[C_R + hp_ * 128:C_R + (hp_ + 1) * 128, sc_], r=["psT"], w=[f"r_rT{ibuf}"])
            P.dma("sp", d["kT"], K.psT[C_K + hp_ * 128:C_K + (hp_ + 1) * 128, sc_], r=["psT"], w=[f"r_kT{ibuf}"])
            P.dma("sp", d["vT"], K.psT[C_V + hp_ * 128:C_V + (hp_ + 1) * 128, sc_], r=["psT"], w=[f"r_vT{ibuf}"])
            P.dma("sp", d["wdT"], K.psT[C_WD:C_WD + 64, sc_], r=["psT"], w=[f"r_wdT{ibuf}"])
            P.dma("sp", d["adT"], K.psT[C_AD:C_AD + 64, sc_], r=["psT"], w=[f"r_adT{ibuf}"])
            P.dma("sp", d["gz"], K.szaT[hc_, sc_], r=["szaT"], w=[f"r_gz{ibuf}"])
        seq_i = 0
        for hp in range(4):
            col = lambda nm: K.cols[:, K.COLS[nm][0] + hp:K.COLS[nm][0] + hp + 1]
            hc = slice(hp * 128, (hp + 1) * 128)
            for par in range(2):
                P.op("pool", lambda e: e.memset(S0[0][64 * par:64 * par + 64, :], 0.0), w=[f"r_S0_{par}"])
            for sg in range(NSEG):
                sc = slice(sg * SEG, (sg + 1) * SEG)
                ib_ = seq_i % 2
                T["rT"], T["kT"], T["vT"] = inb[ib_]["rT"], inb[ib_]["kT"], inb[ib_]["vT"]
                wdT, adT, gz = inb[ib_]["wdT"], inb[ib_]["adT"], inb[ib_]["gz"]
                nsfx = str(ib_)
                if seq_i == 0:
                    load_inputs(0, hp, sg)
                nxt = seq_i + 1
                if nxt < 4 * NSEG:
                    load_inputs(nxt % 2, nxt // NSEG, nxt % NSEG)
                seq_i += 1
                P.op("act", lambda e: e.activation(out=wdT, in_=wdT, func=AF.Tanh), r=["r_wdT" + nsfx], w=["r_wdT" + nsfx])
                for cc in range(NCC):
                    cs_ = slice(cc * CC, (cc + 1) * CC)
                    pb = cc % 2
                    P.op("pe", lambda e: e.matmul(psP[pb][:, 0:CC], lhsT=w2[:, hc], rhs=wdT[:, cs_], start=True, stop=True),
                         r=["r_w2", "r_wdT" + nsfx], w=[f"rpsB{pb}"])
                    P.op("act", lambda e: e.activation(out=T["lwp"][:, cs_], in_=psP[pb][:, 0:CC], func=AF.Sigmoid, bias=col("w0")),
                         r=[f"rpsB{pb}", "cols"], w=["r_lwp"])
                    P.op("pe", lambda e: e.matmul(psP[pb][:, 0:CC], lhsT=a2[:, hc], rhs=adT[:, cs_], start=True, stop=True),
                         r=["r_a2", "r_adT" + nsfx], w=[f"rpsB{pb}"])
                    P.op("act", lambda e: e.activation(out=T["ai"][:, cs_], in_=psP[pb][:, 0:CC], func=AF.Sigmoid, bias=col("a0")),
                         r=[f"rpsB{pb}", "cols"], w=["r_ai"])
                P.op("dve", lambda e: e.tensor_scalar(out=T["kkn"], in0=T["kT"], scalar1=col("kk"), scalar2=None, op0=ALU.mult),
                     r=["r_kT" + nsfx, "cols"], w=["r_kkn"])
                tt(T["tmpA"], T["kkn"], T["kkn"], ALU.mult, ["r_kkn"], ["r_tmpA"])
                for cc in range(NCC):
                    cs_ = slice(cc * CC, (cc + 1) * CC)
                    pb = cc % 2
                    P.op("pe", lambda e: e.matmul(psP[pb][:, 0:CC], lhsT=K.bdiag, rhs=T["tmpA"][:, cs_], start=True, stop=True),
                         r=["bdiag", "r_tmpA"], w=[f"rpsB{pb}"])
                    P.op("act", lambda e: e.activation(out=T["E4"][:, cs_], in_=psP[pb][:, 0:CC], func=AF.Sqrt),
                         r=[f"rpsB{pb}"], w=["r_E4"])
                P.op("dve", lambda e: e.tensor_scalar(out=T["E4"], in0=T["E4"], scalar1=1e-12, scalar2=None, op0=ALU.max),
                     r=["r_E4"], w=["r_E4"])
                P.op("dve", lambda e: e.reciprocal(out=T["E4"], in_=T["E4"]), r=["r_E4"], w=["r_E4"])
                tt(T["kkn"], T["kkn"], T["E4"], ALU.mult, ["r_kkn", "r_E4"], ["r_kkn"])
                P.op("dve", lambda e: e.tensor_scalar(out=T["tmpA"], in0=T["ai"], scalar1=col("ka"), scalar2=K.omka[:, hp:hp + 1],
                                                      op0=ALU.mult, op1=ALU.add), r=["r_ai", "cols", "omka"], w=["r_tmpA"])
                tt(T["km"], T["kT"], T["tmpA"], ALU.mult, ["r_kT" + nsfx, "r_tmpA"], ["r_km"])
                tt(T["bb"], T["kkn"], T["ai"], ALU.mult, ["r_kkn", "r_ai"], ["r_bb"])
                P.op("dve", lambda e: e.tensor_tensor_scan(out=T["Lp"], data0=cm, data1=T["lwp"], initial=0.0, op0=ALU.mult, op1=ALU.add),
                     r=["r_cmask", "r_lwp"], w=["r_Lp"])
                tt(T["Lpe"], T["Lp"], T["lwp"], ALU.subtract, ["r_Lp", "r_lwp"], ["r_Lpe"])
                Lp3 = T["Lp"].rearrange("p (c t) -> p c t", t=64)
                tt(T["tmpA"].rearrange("p (c t) -> p c t", t=64), Lp3[:, :, 63:64].broadcast_to([128, NCH, 64]), Lp3, ALU.subtract,
                   ["r_Lp"], ["r_tmpA"])
                P.op("act", lambda e: e.activation(out=T["E1"], in_=T["Lp"], func=AF.Exp, scale=-C0), r=["r_Lp"], w=["r_E1"])
                P.op("act", lambda e: e.activation(out=BK[:, 1, :], in_=T["Lp"], func=AF.Exp, scale=C0), r=["r_Lp"], w=["r_BK"])
                P.op("act", lambda e: e.activation(out=AR[:, 0, :], in_=T["Lpe"], func=AF.Exp, scale=-C0), r=["r_Lpe"], w=["r_AR"])
                P.op("act", lambda e: e.activation(out=BKh[:, 1, :], in_=T["tmpA"], func=AF.Exp, scale=-C0), r=["r_tmpA"], w=["r_BKh"])
                P.op("dve", lambda e: e.scalar_tensor_tensor(out=AR[:, 0, :], in0=T["kkn"], scalar=-1.0, in1=AR[:, 0, :], op0=ALU.mult, op1=ALU.mult),
                     r=["r_kkn", "r_AR"], w=["r_AR"])
                tt(AR[:, 1, :], T["rT"], T["E1"], ALU.mult, ["r_rT" + nsfx, "r_E1"], ["r_AR"])
                tt(BK[:, 0, :], T["bb"], BK[:, 1, :], ALU.mult, ["r_bb", "r_BK"], ["r_BK"])
                tt(BK[:, 1, :], T["km"], BK[:, 1, :], ALU.mult, ["r_km", "r_BK"], ["r_BK"])
                tt(BKh[:, 0, :], T["bb"], BKh[:, 1, :], ALU.mult, ["r_bb", "r_BKh"], ["r_BKh"])
                tt(BKh[:, 1, :], T["km"], BKh[:, 1, :], ALU.mult, ["r_km", "r_BKh"], ["r_BKh"])
                P.op("dve", lambda e: e.scalar_tensor_tensor(out=T["tmpA"], in0=T["rT"], scalar=col("rk"), in1=T["km"], op0=ALU.mult, op1=ALU.mult),
                     r=["r_rT" + nsfx, "r_km", "cols", "r_tmpA"], w=["r_tmpA"])
                for cc in range(NCC):
                    cs_ = slice(cc * CC, (cc + 1) * CC)
                    pb = cc % 2
                    P.op("pe", lambda e: e.matmul(psP[pb][:, 0:CC], lhsT=K.bdiag, rhs=T["tmpA"][:, cs_], start=True, stop=True),
                         r=["bdiag", "r_tmpA"], w=[f"rpsB{pb}"])
                    tt(T["bonT"][:, cs_], psP[pb][:, 0:CC], T["vT"][:, cs_], ALU.mult, [f"rpsB{pb}", "r_vT" + nsfx], ["r_bonT"])
                done = [0, 0]

                def chain1(c, par, slot):
                    pr = slice(64 * par, 64 * par + 64)
                    cs_ = slice(c * 64, (c + 1) * 64)
                    B_, nB = psB[slot], f"rpsB{slot}"
                    sfx = f"_{par}_{c}"
                    idm = K.identf[pr, pr]
                    for q_, src, sn in ((0, T["vT"][pr, cs_], "r_vT" + nsfx), (1, BKh[pr, 0, cs_], "r_BKh"), (2, BKh[pr, 1, cs_], "r_BKh")):
                        P.op("pe", lambda e: e.matmul(B_[pr, 320 + 64 * q_:384 + 64 * q_], lhsT=src, rhs=idm, start=True, stop=True),
                             r=[sn, "identf"], w=[nB])
                    P.op("pe", lambda e: e.matmul(B_[pr, 0:128], lhsT=BK[pr, 0, cs_], rhs=AR[pr, :, cs_], start=True, stop=True),
                         r=["r_BK", "r_AR"], w=[nB])
                    P.op("pe", lambda e: e.matmul(B_[pr, 128:256], lhsT=BK[pr, 1, cs_], rhs=AR[pr, :, cs_], start=True, stop=True),
                         r=["r_BK", "r_AR"], w=[nB])
                    P.op("pe", lambda e: e.matmul(B_[pr, 256:320], lhsT=AR[pr, 0, cs_], rhs=BK[pr, 0, cs_], start=True, stop=True),
                         r=["r_BK", "r_AR"], w=[nB])
                    yield
                    P.op("act", lambda e: e.activation(out=tms[pr, c, :], in_=B_[pr, 320:512], func=AF.Copy), r=[nB], w=["r_tms" + sfx])
                    tt(MAKAs[pr, c, :].rearrange("p (a t) -> p a t", a=2), B_[pr, 0:256].rearrange("p (a t) -> p a t", a=2),
                       u2[pr, :].unsqueeze(1).broadcast_to([64, 2, 128]), ALU.mult, [nB, "r_u2"], ["r_MAKA" + sfx])
                    tt(Q0s[slot][pr], B_[pr, 256:320], sl[pr, :], ALU.mult, [nB, "r_sl"], [f"r_Q0f{slot}"])
                    tt(TTk[slot][0][pr], MAKAs[pr, c, 0:64], idm, ALU.add, ["r_MAKA" + sfx, "identf"], [f"r_TT0s{slot}"])
                    P.op("pool", lambda e: e.tensor_copy(out=TTb[slot][0][pr], in_=TTk[slot][0][pr]), r=[f"r_TT0s{slot}"], w=[f"r_TTb0s{slot}"])
                    yield
                    p_prev, np_prev = MAKAs[pr, c, 0:64], "r_MAKA" + sfx
                    q_prev, nq_prev = Q0s[slot][pr], f"r_Q0f{slot}"
                    for kq in range(1, 6):
                        pp = kq % 2
                        if kq < 5:
                            P.op("pe", lambda e: e.matmul(B_[pr, 0:64], lhsT=q_prev, rhs=p_prev, start=True, stop=True),
                                 r=[nq_prev, np_prev], w=[nB])
                        P.op("pe", lambda e: e.matmul(B_[pr, 64:128], lhsT=p_prev, rhs=q_prev, start=True, stop=True),
                             r=[nq_prev, np_prev], w=[nB])
                        yield
                        if kq < 5:
                            P.op("act", lambda e: e.activation(out=Ps[slot][pp][pr], in_=B_[pr, 0:64], func=AF.Copy),
                                 r=[nB], w=[f"r_P{pp}s{slot}"])
                        P.op("act", lambda e: e.activation(out=Qs[slot][pp][pr], in_=B_[pr, 64:128], func=AF.Copy),
                             r=[nB], w=[f"r_Q{pp}s{slot}"])
                        yield
                        ob = (kq - 1) % 2
                        t_old, nt_old = TTk[slot][ob][pr], f"r_TT{ob}s{slot}"
                        P.op("pe", lambda e: e.matmul(B_[pr, 128:192], lhsT=Qs[slot][pp][pr], rhs=TTb[slot][ob][pr], start=True, stop=True),
                             r=[f"r_Q{pp}s{slot}", f"r_TTb{ob}s{slot}"], w=[nB])
                        yield
                        if kq < 5:
                            tt(TTk[slot][kq % 2][pr], B_[pr, 128:192], t_old, ALU.add, [nB, nt_old], [f"r_TT{kq % 2}s{slot}"])
                            P.op("pool", lambda e: e.tensor_copy(out=TTb[slot][kq % 2][pr], in_=TTk[slot][kq % 2][pr]),
                                 r=[f"r_TT{kq % 2}s{slot}"], w=[f"r_TTb{kq % 2}s{slot}"])
                        else:
                            tt(TTs[pr, c, :], B_[pr, 128:192], t_old, ALU.add, [nB, nt_old], ["r_TTs" + sfx])
                        yield
                        p_prev, np_prev = Ps[slot][pp][pr], f"r_P{pp}s{slot}"
                        q_prev, nq_prev = Qs[slot][pp][pr], f"r_Q{pp}s{slot}"
                    done[par] = max(done[par], c + 1)

                def chain2(par):
                    pr = slice(64 * par, 64 * par + 64)
                    X_, nX = psX[par], f"rpsX{par}"
                    for c in range(NCH):
                        while done[par] < min(NCH, c + 2):
                            yield
                        gi = sg * NCH + c
                        cs_ = slice(c * 64, (c + 1) * 64)
                        sfx = f"_{par}_{c}"
                        s_old, s_new = S0[gi % 2], S0[(gi + 1) % 2]
                        ns_old, ns_new = f"r_S{gi % 2}_{par}", f"r_S{(gi + 1) % 2}_{par}"
                        Vh, Bh, Kh = tms[pr, c, 0:64], tms[pr, c, 64:128], tms[pr, c, 128:192]
                        ArbT, AakT, ArkT = MAKAs[pr, c, 64:128], MAKAs[pr, c, 128:192], MAKAs[pr, c, 192:256]
                        P.op("pe", lambda e: e.matmul(X_[pr, 0:64], lhsT=AR[pr, 0, cs_], rhs=s_old[pr], start=True, stop=False),
                             r=["r_AR", ns_old], w=[nX])
                        P.op("pe", lambda e: e.matmul(X_[pr, 0:64], lhsT=AakT, rhs=Vh, start=False, stop=True),
                             r=["r_MAKA" + sfx, "r_tms" + sfx], w=[nX])
                        yield
                        P.op("act", lambda e: e.activation(out=Xs[pr], in_=X_[pr, 0:64], func=AF.Copy), r=[nX], w=[f"r_X_{par}"])
                        yield
                        P.op("pe", lambda e: e.matmul(X_[pr, 64:128], lhsT=TTs[pr, c, :], rhs=Xs[pr], start=True, stop=True),
                             r=["r_TTs" + sfx, f"r_X_{par}"], w=[nX])
                        yield
                        P.op("act", lambda e: e.activation(out=Us[pr], in_=X_[pr, 64:128], func=AF.Copy), r=[nX], w=[f"r_U_{par}"])
                        yield
                        P.op("pe", lambda e: e.matmul(X_[pr, 128:192], lhsT=AR[pr, 1, cs_], rhs=s_old[pr], start=True, stop=False),
                             r=["r_AR", ns_old], w=[nX])
                        P.op("pe", lambda e: e.matmul(X_[pr, 128:192], lhsT=ArbT, rhs=Us[pr], start=False, stop=False),
                             r=["r_MAKA" + sfx, f"r_U_{par}"], w=[nX])
                        P.op("pe", lambda e: e.matmul(X_[pr, 128:192], lhsT=ArkT, rhs=Vh, start=False, stop=True),
                             r=["r_MAKA" + sfx, "r_tms" + sfx], w=[nX])
                        P.op("pe", lambda e: e.matmul(X_[pr, 192:256], lhsT=Bh, rhs=Us[pr], start=True, stop=False),
                             r=["r_tms" + sfx, f"r_U_{par}"], w=[nX])
                        P.op("pe", lambda e: e.matmul(X_[pr, 192:256], lhsT=Kh, rhs=Vh, start=False, stop=True),
                             r=["r_tms" + sfx], w=[nX])
                        yield
                        P.op("dve", lambda e: e.scalar_tensor_tensor(out=s_new[pr], in0=s_old[pr], scalar=T["E1"][pr, c * 64 + 63:c * 64 + 64],
                                                                     in1=X_[pr, 192:256], op0=ALU.mult, op1=ALU.add),
                             r=[ns_old, "r_E1", nX], w=[ns_new])
                        P.op("dve", lambda e: e.tensor_copy(out=Yall[pr, c, :], in_=X_[pr, 128:192]), r=[nX], w=["r_Lpe"])
                        yield

                pending = [(c, par) for c in range(NCH) for par in range(2)]
                free_slots = list(range(NSLOT))
                active = []
                for par in range(2):
                    active.append((chain2(par), None))
                while pending or active:
                    while pending and free_slots:
                        c, par = pending.pop(0)
                        sl_ = free_slots.pop(0)
                        active.append((chain1(c, par, sl_), sl_))
                    for g in list(active):
                        try:
                            next(g[0])
                        except StopIteration:
                            active.remove(g)
                            if g[1] is not None:
                                free_slots.append(g[1])
                P.op("dve", lambda e: e.tensor_reduce(out=gst[:, 0, :], in_=Yall, axis=AX.X, op=ALU.add), r=["r_Lpe"], w=["r_gst"])
                P.op("dve", lambda e: e.tensor_scalar(out=gst[:, 1, :], in0=gst[:, 0, :], scalar1=-1.0 / 64, scalar2=None, op0=ALU.mult),
                     r=["r_gst"], w=["r_gst"])
                tt(Yall, Yall, gst[:, 1, :].unsqueeze(2).broadcast_to([128, NCH, 64]), ALU.add, ["r_Lpe", "r_gst"], ["r_Lpe"])
                tt(Ysq, Yall, Yall, ALU.mult, ["r_Lpe"], ["r_tmpA"])
                P.op("dve", lambda e: e.tensor_reduce(out=gst[:, 2, :], in_=Ysq, axis=AX.X, op=ALU.add), r=["r_tmpA"], w=["r_gst"])
                P.op("dve", lambda e: e.tensor_scalar(out=gst[:, 3, :], in0=gst[:, 2, :], scalar1=1.0 / 64, scalar2=GN_EPS, op0=ALU.mult, op1=ALU.add),
                     r=["r_gst"], w=["r_gst"])
                P.op("act", lambda e: e.activation(out=gst[:, 3, :], in_=gst[:, 3, :], func=AF.Sqrt), r=["r_gst"], w=["r_gst"])
                P.op("dve", lambda e: e.reciprocal(out=gst[:, 3, :], in_=gst[:, 3, :]), r=["r_gst"], w=["r_gst"])
                tt(Yall, Yall, gst[:, 3, :].unsqueeze(2).broadcast_to([128, NCH, 64]), ALU.mult, ["r_Lpe", "r_gst"], ["r_Lpe"])
                for c in range(NCH):
                    for par in range(2):
                        pr = slice(64 * par, 64 * par + 64)
                        slot = (2 * c + par) % NSLOT
                        P.op("pe", lambda e: e.matmul(psB[slot][pr, 0:64], lhsT=Yall[pr, c, :], rhs=K.identf[pr, pr], start=True, stop=True),
                             r=["r_Lpe", "identf"], w=[f"rpsB{slot}"])
                        P.op("act", lambda e: e.activation(out=T["OG"][pr, c * 64:(c + 1) * 64], in_=psB[slot][pr, 0:64], func=AF.Identity,
                                                           scale=col("lg")[pr], bias=col("lb")[pr]),
                             r=[f"rpsB{slot}", "cols"], w=["r_OG"])
                tt(T["OG"], T["OG"], T["bonT"], ALU.add, ["r_OG", "r_bonT"], ["r_OG"])
                tt(ogb, T["OG"], gz, ALU.mult, ["r_OG", "r_gz" + nsfx], ["r_ogb"])
                P.dma("sp", K.yagT[hc, sc], ogb, r=["r_ogb"], w=["yagT"])
        P.barrier()


def phase_dsa(K, l):
    from contextlib import ExitStack
    nc, P, W, S, NT, TOPK = K.nc, K.P, K.L[l], K.S, K.NT, K.TOPK
    NIT = 20
    with ExitStack() as st:
        def sbt(name, shape, dt=F32):
            return st.enter_context(_sbm(nc, name, list(shape), dt)).ap()
        latS = sbt("d_lat", [128, S], BF16)
        ki2 = sbt("d_ki2", [128, S])
        vxs = sbt("d_vxs", [128, NT, 520], BF16)
        wis = sbt("d_wis", [128, NT * 8])
        EBM = [sbt(f"d_EBM{i}", [128, 8, 128]) for i in range(2)]
        b31 = sbt("d_b31", [128, 8, 128])
        qij = [sbt(f"d_qij{i}", [128, 4, 128]) for i in range(3)]
        qab = [sbt(f"d_qab{i}", [128, 8, 128], BF16) for i in range(3)]
        zbj = [sbt(f"d_zbj{i}", [128, 4, 128], BF16) for i in range(3)]
        accs = [sbt(f"d_acc{i}", [128, S]) for i in range(2)]
        maskfs = [sbt(f"d_maskf{i}", [128, S], BF16) for i in range(2)]
        maskT = [sbt(f"d_maskT{i}", [128, NT, 128], BF16) for i in range(3)]
        Pt = [sbt(f"d_Pt{i}", [128, 8, 128], BF16) for i in range(3)]
        rtmp = [sbt(f"d_rt{i}", [128, 512]) for i in range(3)]
        m8s = [sbt(f"d_m8{i}", [128, 8]) for i in range(2)]
        bss = [sbt(f"d_bs{i}", [128, 8]) for i in range(2)]
        p2 = sbt("d_p2", [128, NIT + 1])
        wcolss = [sbt(f"d_wcols{i}", [128, NIT + 1]) for i in range(2)]
        rec = sbt("d_rec", [128, 8])
        yb = sbt("d_yb", [128, 512])
        ybg = [sbt(f"d_ybg{i}", [128, 4, 128], BF16) for i in range(2)]
        psI = [st.enter_context(_psm(nc, f"dpsI{i}", [128, 512], F32)).ap() for i in range(3)]
        psM = st.enter_context(_psm(nc, "dpsM", [128, 1024], BF16)).ap()
        psQ = [st.enter_context(_psm(nc, f"dpsQ{i}", [128, 512], F32)).ap() for i in range(2)]
        psO = [st.enter_context(_psm(nc, f"dpsO{i}", [128, 512], F32)).ap() for i in range(2)]

        P.dma("sp", latS, K.latT, r=["latT"], w=["d_lat"])
        P.dma("sp", ki2[0:64, :], K.kiT, r=["kiT"], w=["d_ki2"])
        P.dma("sp", ki2[64:128, :], K.kiT, r=["kiT"], w=["d_ki2"])
        for i in range(NT):
            P.dma("sp", vxs[:, i, :], K.vext[i], r=["vext"], w=["d_vxs"])
        P.dma("sp", wis, K.witm, r=["witm"], w=["d_wis"])
        P.dma("sp", b31, K.eb31, w=["d_b31"])
        for kd in range(2):
            P.dma("sp", EBM[kd], K.ebias[kd], w=[f"d_EBM{kd}"])
            P.op("dve", lambda e: e.tensor_tensor(out=EBM[kd], in0=EBM[kd], in1=b31, op=ALU.subtract), r=[f"d_EBM{kd}", "d_b31"], w=[f"d_EBM{kd}"])
            P.op("act", lambda e: e.activation(out=EBM[kd], in_=EBM[kd], func=AF.Exp), r=[f"d_EBM{kd}"], w=[f"d_EBM{kd}"])
        P.op("pool", lambda e: e.affine_select(out=EBM[0], in_=EBM[0], pattern=[[0, 8], [1, 128]], compare_op=ALU.is_ge, fill=K.reg_zero,
                                               base=0, channel_multiplier=-1), r=["d_EBM0"], w=["d_EBM0"])
        for n in range(NIT + 1):
            P.op("pool", lambda e: e.memset(p2[:, n:n + 1], 2.0 ** -(n + 1)), w=["d_p2"])
        icount = [0]
        it_done = set()

        def chain_it(j):
            b = j % 3
            u = j % 2
            acc, nacc = accs[u], f"d_acc{u}"
            maskf, nmf = maskfs[u], f"d_maskf{u}"
            m8, nm8 = m8s[u], f"d_m8{u}"
            bs, nbs = bss[u], f"d_bs{u}"
            wcols, nwc = wcolss[u], f"d_wcols{u}"
            jc = slice(j * 128, (j + 1) * 128)
            Lk = (j + 1) * 128
            mT, nmT = maskT[b], f"d_maskT{b}"
            P.dma("sp", qij[b], K.qiT.rearrange("(hq p) t -> p hq t", p=128)[:, :, jc], r=["qiT"], w=[f"d_qij{b}"])
            P.dma("sp", qab[b], K.qabsT.rearrange("h c t -> c h t")[:, :, jc], r=["qabsT"], w=[f"d_qab{b}"])
            P.dma("sp", zbj[b], K.szbT.rearrange("(fb p) t -> p fb t", p=128)[:, :, jc], r=["szbT"], w=[f"d_zbj{b}"])
            if Lk <= TOPK:
                P.op("pool", lambda e: e.memset(mT[:, 0:j + 1, :], 1.0), w=[nmT])
                it_done.add(j)
                return
            for k0 in range(0, Lk, 512):
                k1 = min(Lk, k0 + 512)
                wd = k1 - k0
                for h in range(8):
                    par, hq = h % 2, h // 2
                    pr = slice(64 * par, 64 * par + 64)
                    ib = icount[0] % 3
                    icount[0] += 1
                    P.op("pe", lambda e: e.matmul(psI[ib][:, 0:wd], lhsT=qij[b][pr, hq, :], rhs=ki2[pr, k0:k1], start=True, stop=True),
                         r=[f"d_qij{b}", "d_ki2"], w=[f"dpsI{ib}"])
                    P.op("act", lambda e: e.activation(out=rtmp[ib][:, 0:wd], in_=psI[ib][:, 0:wd], func=AF.Relu),
                         r=[f"dpsI{ib}"], w=[f"d_rt{ib}"])
                    wcol = wis[:, j * 8 + h:j * 8 + h + 1]
                    if h == 0:
                        P.op("dve", lambda e: e.tensor_scalar(out=acc[:, k0:k1], in0=rtmp[ib][:, 0:wd], scalar1=wcol, scalar2=None, op0=ALU.mult),
                             r=[f"d_rt{ib}", "d_wis"], w=[nacc])
                    else:
                        P.op("dve", lambda e: e.scalar_tensor_tensor(out=acc[:, k0:k1], in0=rtmp[ib][:, 0:wd], scalar=wcol, in1=acc[:, k0:k1],
                                                                     op0=ALU.mult, op1=ALU.add), r=[f"d_rt{ib}", "d_wis", nacc], w=[nacc])
                    yield
            P.op("pool", lambda e: e.affine_select(out=acc[:, jc], in_=acc[:, jc], pattern=[[-1, 128]], compare_op=ALU.is_ge, fill=K.reg_neg,
                                                   base=0, channel_multiplier=1), r=[nacc], w=[nacc])
            lo0, w0, mid, cnt, sg = (bs[:, i:i + 1] for i in range(5))
            P.op("dve", lambda e: e.max(out=m8, in_=acc[:, 0:Lk]), r=[nacc], w=[nm8])
            P.op("dve", lambda e: e.tensor_reduce(out=lo0, in_=acc[:, 0:j * 128], axis=AX.X, op=ALU.min), r=[nacc], w=[nbs])
            yield
            P.op("dve", lambda e: e.tensor_tensor(out=w0, in0=m8[:, 0:1], in1=lo0, op=ALU.subtract), r=[nm8, nbs], w=[nbs])
            P.op("dve", lambda e: e.tensor_scalar(out=wcols, in0=p2, scalar1=w0, scalar2=None, op0=ALU.mult), r=["d_p2", nbs], w=[nwc])
            P.op("dve", lambda e: e.tensor_tensor(out=mid, in0=lo0, in1=wcols[:, 0:1], op=ALU.add), r=[nbs, nwc], w=[nbs])
            Ld = max(128, (int(0.56 * Lk) // 128) * 128)
            nA = Lk - Ld
            sgn, tot = bs[:, 5:6], bs[:, 6:7]
            for n in range(NIT):
                P.op("dve", lambda e: e.tensor_scalar(out=maskf[:, 0:Ld], in0=acc[:, 0:Ld], scalar1=mid, scalar2=None, op0=ALU.is_ge, op1=ALU.add,
                                                      accum_out=cnt), r=[nacc, nbs], w=[nmf, nbs + "c"])
                P.op("act", lambda e: e.activation(out=maskf[:, Ld:Lk], in_=acc[:, Ld:Lk], func=AF.Sign, scale=-1.0, bias=mid, accum_out=sgn),
                     r=[nacc, nbs], w=[nmf + "a", nbs + "s"])
                P.op("dve", lambda e: e.scalar_tensor_tensor(out=tot, in0=sgn, scalar=-0.5, in1=cnt, op0=ALU.mult, op1=ALU.add),
                     r=[nbs + "c", nbs + "s"], w=[nbs + "t"])
                P.op("dve", lambda e: e.tensor_scalar(out=sg, in0=tot, scalar1=TOPK - 0.5 - 0.5 * nA, scalar2=wcols[:, n:n + 1], op0=ALU.is_ge, op1=ALU.mult),
                     r=[nbs + "t", nwc], w=[nbs + "g"])
                sub = wcols[:, n + 1:n + 2] if n + 1 < NIT else wcols[:, n:n + 1]
                P.op("dve", lambda e: e.scalar_tensor_tensor(out=mid, in0=sg, scalar=sub, in1=mid, op0=ALU.subtract, op1=ALU.add),
                     r=[nbs + "g", nbs, nwc], w=[nbs])
                yield
            P.op("dve", lambda e: e.tensor_scalar(out=maskf[:, 0:Lk], in0=acc[:, 0:Lk], scalar1=mid, scalar2=None, op0=ALU.is_ge),
                 r=[nacc, nbs], w=[nmf, nmf + "a"])
            yield
            yield
            yield
            for i0 in range(0, j + 1, 8):
                i1 = min(j + 1, i0 + 8)
                for i in range(i0, i1):
                    P.op("pe", lambda e: e.transpose(out=psM[:, (i - i0) * 128:(i - i0 + 1) * 128], in_=maskf[:, i * 128:(i + 1) * 128],
                                                     identity=K.identb), r=[nmf, "identb"], w=["dpsM"])
                P.op("act", lambda e: e.activation(out=mT[:, i0:i1, :], in_=psM[:, 0:(i1 - i0) * 128].rearrange("p (i t) -> p i t", t=128),
                                                   func=AF.Copy), r=["dpsM"], w=[nmT])
                yield
            it_done.add(j)

        def chain_at(j):
            b = j % 3
            jc = slice(j * 128, (j + 1) * 128)
            mT, nmT = maskT[b], f"d_maskT{b}"
            while j not in it_done:
                yield

            def qk(i):
                for hh in range(2):
                    P.op("pe", lambda e: e.matmul(psQ[hh], lhsT=latS[:, i * 128:(i + 1) * 128], rhs=qab[b][:, 4 * hh:4 * hh + 4, :], start=True, stop=True),
                         r=["d_lat", f"d_qab{b}"], w=[f"dpsQ{hh}"])
            def pv(i):
                a = i % 3
                for h in range(8):
                    ob = h // 4
                    o0 = (h % 4) * 65
                    P.op("pe", lambda e: e.matmul(psO[ob][:, o0:o0 + 65], lhsT=Pt[a][:, h, :], rhs=vxs[:, i, h * 65:(h + 1) * 65],
                                                  start=(i == 0 and h % 4 == 0), stop=(i == j), skip_group_check=True),
                         r=[f"d_Pt{a}", "d_vxs"], w=[f"dpsO{ob}"])
            qk(0)
            for i in range(j + 1):
                a = i % 3
                for hh in range(2):
                    P.op("act", lambda e: e.activation(out=Pt[a][:, 4 * hh:4 * hh + 4, :], in_=psQ[hh].rearrange("p (h t) -> p h t", h=4), func=AF.Exp),
                         r=[f"dpsQ{hh}"], w=[f"d_Pt{a}"])
                kd = j - i
                if kd <= 1:
                    P.op("pool", lambda e: e.tensor_tensor(out=Pt[a], in0=Pt[a], in1=EBM[kd], op=ALU.mult), r=[f"d_Pt{a}", f"d_EBM{kd}"], w=[f"d_Pt{a}"])
                P.op("pool", lambda e: e.tensor_tensor(out=Pt[a], in0=Pt[a], in1=mT[:, i:i + 1, :].broadcast_to([128, 8, 128]), op=ALU.mult),
                     r=[f"d_Pt{a}", nmT], w=[f"d_Pt{a}"])
                if i >= 1:
                    pv(i - 1)
                if i + 1 <= j:
                    qk(i + 1)
                yield
            pv(j)
            yield
            for ob in range(2):
                o3 = psO[ob][:, 0:260].rearrange("p (h d) -> p h d", d=65)
                P.op("dve", lambda e: e.reciprocal(out=rec[:, 4 * ob:4 * ob + 4].unsqueeze(2), in_=o3[:, :, 64:65]), r=[f"dpsO{ob}"], w=["d_rec"])
                P.op("dve", lambda e: e.tensor_tensor(out=yb[:, 256 * ob:256 * ob + 256].rearrange("p (h d) -> p h d", d=64), in0=o3[:, :, 0:64],
                                                      in1=rec[:, 4 * ob:4 * ob + 4].unsqueeze(2).broadcast_to([128, 4, 64]), op=ALU.mult),
                     r=[f"dpsO{ob}", "d_rec"], w=["d_yb"])
            yield
            for fb in range(4):
                P.op("pe", lambda e: e.matmul(psI[0][:, fb * 128:(fb + 1) * 128], lhsT=yb[:, fb * 128:(fb + 1) * 128], rhs=K.identf, start=True, stop=True),
                     r=["d_yb", "identf"], w=["dpsI0"])
            P.op("dve", lambda e: e.tensor_tensor(out=ybg[j % 2], in0=psI[0].rearrange("p (f t) -> p f t", f=4), in1=zbj[b], op=ALU.mult),
                 r=["dpsI0", f"d_zbj{b}"], w=[f"d_ybg{j % 2}"])
            P.dma("sp", K.ybgT.rearrange("(fb p) t -> p fb t", p=128)[:, :, jc], ybg[j % 2], r=[f"d_ybg{j % 2}"], w=["ybgT"])

        it_gens = {}
        next_it = 0
        at_j = 0
        at_gen = None
        while at_j < NT:
            while next_it < NT and len(it_gens) < 2 and next_it <= at_j + 2:
                it_gens[next_it] = chain_it(next_it)
                next_it += 1
            for jj in sorted(it_gens):
                try:
                    next(it_gens[jj])
                    next(it_gens[jj])
                except StopIteration:
                    del it_gens[jj]
            if at_gen is None and at_j in it_done:
                at_gen = chain_at(at_j)
            if at_gen is not None:
                try:
                    next(at_gen)
                except StopIteration:
                    at_gen = None
                    at_j += 1
        P.barrier()


def phase_out(K, l, xsrc, xdst):
    from contextlib import ExitStack
    nc, P, W, S, TC, NTC = K.nc, K.P, K.L[l], K.S, K.TC, K.NTC
    with ExitStack() as st:
        def sbt(name, shape, dt=F32):
            return st.enter_context(_sbm(nc, name, list(shape), dt)).ap()
        stg = sbt("o_stg", [128, 4, K.D])
        wpa = sbt("o_wpa", [128, 4, K.D], BF16)
        wpb = sbt("o_wpb", [128, 4, K.D], BF16)
        wo = sbt("o_wo", [128, 8, K.D], BF16)
        ya = [sbt(f"o_ya{i}", [128, 4, TC], BF16) for i in range(2)]
        yb = [sbt(f"o_yb{i}", [128, 4, TC], BF16) for i in range(2)]
        ga = [sbt(f"o_ga{i}", [128, 8, TC], BF16) for i in range(2)]
        gb = [sbt(f"o_gb{i}", [128, 8, TC], BF16) for i in range(2)]
        t1 = [sbt(f"o_t1{i}", [128, TC]) for i in range(2)]
        t2 = [sbt(f"o_t2{i}", [128, TC]) for i in range(2)]
        mg = sbt("o_mg", [128, 8, TC], BF16)
        xt = [sbt(f"o_xt{i}", [128, K.D]) for i in range(2)]
        xo = [sbt(f"o_xo{i}", [128, K.D]) for i in range(2)]
        psA = [st.enter_context(_psm(nc, f"opsA{i}", [128, 512], F32)).ap() for i in range(2)]
        psB = [st.enter_context(_psm(nc, f"opsB{i}", [128, 512], F32)).ap() for i in range(2)]
        psO = [st.enter_context(_psm(nc, f"opsO{i}", [128, 512], F32)).ap() for i in range(2)]
        for src, dst, dn in ((W["w_pa"], wpa, "o_wpa"), (W["w_pb"], wpb, "o_wpb")):
            P.dma("sp", stg, src.rearrange("(j p) n -> p j n", p=128), r=["o_stg"], w=["o_stg"])
            P.op("pool", lambda e: e.tensor_copy(out=dst, in_=stg), r=["o_stg"], w=[dn])
        for hf in range(2):
            P.dma("sp", stg, W["w_o"][hf * 512:(hf + 1) * 512, :].rearrange("(j p) n -> p j n", p=128), r=["o_stg"], w=["o_stg"])
            P.op("pool", lambda e: e.tensor_copy(out=wo[:, 4 * hf:4 * hf + 4, :], in_=stg), r=["o_stg"], w=["o_wo"])
        gate = K.modB[:, 2 * K.D:3 * K.D]
        cnt = 0
        xcnt = 0
        def load_chunk(tc):
            b = tc % 2
            tcs = slice(tc * TC, (tc + 1) * TC)
            P.dma("sp", ya[b], K.yagT.rearrange("(j p) t -> p j t", p=128)[:, :, tcs], r=["yagT"], w=[f"o_ya{b}"])
            P.dma("sp", yb[b], K.ybgT.rearrange("(j p) t -> p j t", p=128)[:, :, tcs], r=["ybgT"], w=[f"o_yb{b}"])
            P.dma("sp", ga[b], K.sgaT.rearrange("(j p) t -> p j t", p=128)[:, :, tcs], r=["sgaT"], w=[f"o_ga{b}"])
            P.dma("sp", gb[b], K.sgbT.rearrange("(j p) t -> p j t", p=128)[:, :, tcs], r=["sgbT"], w=[f"o_gb{b}"])

        def load_x(k):
            if k < S // 128:
                P.dma("sp", xt[k % 2], xsrc[k * 128:(k + 1) * 128, :], w=[f"o_xt{k % 2}"])
        load_chunk(0)
        load_x(0)
        for tc in range(NTC):
            b = tc % 2
            tcs = slice(tc * TC, (tc + 1) * TC)
            if tc + 1 < NTC:
                load_chunk(tc + 1)
            for ob in range(8):
                pb = cnt % 2
                cnt += 1
                oc = slice(ob * 128, (ob + 1) * 128)
                for j in range(4):
                    P.op("pe", lambda e: e.matmul(psA[pb][:, 0:TC], lhsT=wpa[:, j, oc], rhs=ya[b][:, j, :], start=(j == 0), stop=(j == 3)),
                         r=["o_wpa", f"o_ya{b}"], w=[f"opsA{pb}"])
                for j in range(4):
                    P.op("pe", lambda e: e.matmul(psB[pb][:, 0:TC], lhsT=wpb[:, j, oc], rhs=yb[b][:, j, :], start=(j == 0), stop=(j == 3)),
                         r=["o_wpb", f"o_yb{b}"], w=[f"opsB{pb}"])
                P.op("dve", lambda e: e.tensor_tensor(out=t1[pb], in0=psA[pb][:, 0:TC], in1=ga[b][:, ob, :], op=ALU.mult),
                     r=[f"opsA{pb}", f"o_ga{b}"], w=[f"o_t1{pb}"])
                P.op("dve", lambda e: e.tensor_tensor(out=t2[pb], in0=psB[pb][:, 0:TC], in1=gb[b][:, ob, :], op=ALU.mult),
                     r=[f"opsB{pb}", f"o_gb{b}"], w=[f"o_t2{pb}"])
                P.op("pool", lambda e: e.tensor_tensor(out=mg[:, ob, :], in0=t1[pb], in1=t2[pb], op=ALU.add),
                     r=[f"o_t1{pb}", f"o_t2{pb}"], w=["o_mg"])
            for ts in range(TC // 128):
                xb_ = xcnt % 2
                xcnt += 1
                rows = slice(tc * TC + ts * 128, tc * TC + (ts + 1) * 128)
                load_x(xcnt)
                for hf in range(2):
                    hc = slice(hf * 512, (hf + 1) * 512)
                    for ob in range(8):
                        P.op("pe", lambda e: e.matmul(psO[hf], lhsT=mg[:, ob, ts * 128:(ts + 1) * 128], rhs=wo[:, ob, hc], start=(ob == 0), stop=(ob == 7)),
                             r=["o_mg", "o_wo"], w=[f"opsO{hf}"])
                    P.op("dve", lambda e: e.tensor_tensor(out=xo[xb_][:, hc], in0=psO[hf], in1=gate[:, hc], op=ALU.mult),
                         r=[f"opsO{hf}", "modB"], w=[f"o_xo{xb_}"])
                P.op("pool", lambda e: e.tensor_tensor(out=xo[xb_], in0=xo[xb_], in1=xt[xb_], op=ALU.add),
                     r=[f"o_xo{xb_}", f"o_xt{xb_}"], w=[f"o_xo{xb_}"])
                P.dma("sp", xdst[rows, :], xo[xb_], r=[f"o_xo{xb_}"], w=["xs"])
        P.barrier()


def phase_final(K, xsrc):
    from contextlib import ExitStack
    nc, P, S, NT = K.nc, K.P, K.S, K.NT
    with ExitStack() as st:
        def sbt(name, shape, dt=F32):
            return st.enter_context(_sbm(nc, name, list(shape), dt)).ap()
        xt = [sbt(f"f_xt{i}", [128, K.D]) for i in range(2)]
        xo = [sbt(f"f_xo{i}", [128, K.D]) for i in range(2)]
        sq = sbt("f_sq", [128, K.D])
        fg = sbt("f_fg", [128, K.D])
        ss = sbt("f_ss", [128, 2])
        ps = [st.enter_context(_psm(nc, f"fps{i}", [128, 512], F32)).ap() for i in range(2)]
        P.dma("sp", K.row[0:1, 0:K.D], K.final_g, r=["row"], w=["row"])
        for hf in range(2):
            P.op("pe", lambda e: e.matmul(ps[hf], lhsT=K.ones[0:1, :], rhs=K.row[0:1, hf * 512:(hf + 1) * 512], start=True, stop=True),
                 r=["ones", "row"], w=[f"fps{hf}"])
            P.op("act", lambda e: e.activation(out=fg[:, hf * 512:(hf + 1) * 512], in_=ps[hf], func=AF.Copy), r=[f"fps{hf}"], w=["f_fg"])
        for i in range(NT):
            b = i % 2
            rows = slice(i * 128, (i + 1) * 128)
            P.dma("sp", xt[b], xsrc[rows, :], r=["xs"], w=[f"f_xt{b}"])
            P.op("act", lambda e: e.activation(out=sq, in_=xt[b], func=AF.Square, accum_out=ss[:, 0:1]), r=[f"f_xt{b}"], w=["f_sq", "f_ss"])
            P.op("dve", lambda e: e.tensor_scalar(out=ss[:, 1:2], in0=ss[:, 0:1], scalar1=1.0 / K.D, scalar2=K.EPS, op0=ALU.mult, op1=ALU.add),
                 r=["f_ss"], w=["f_ss"])
            P.op("act", lambda e: e.activation(out=ss[:, 1:2], in_=ss[:, 1:2], func=AF.Sqrt), r=["f_ss"], w=["f_ss"])
            P.op("dve", lambda e: e.reciprocal(out=ss[:, 1:2], in_=ss[:, 1:2]), r=["f_ss"], w=["f_ss"])
            P.op("dve", lambda e: e.scalar_tensor_tensor(out=xo[b], in0=xt[b], scalar=ss[:, 1:2], in1=fg, op0=ALU.mult, op1=ALU.mult),
                 r=[f"f_xt{b}", "f_ss", "f_fg"], w=[f"f_xo{b}"])
            P.dma("sp", K.out[rows, :], xo[b], r=[f"f_xo{b}"], is_output=True)
        P.barrier()


_CACHE = {}


def kernel(**inputs):
    S = int(np.asarray(inputs["x"]).shape[1])
    B = int(np.asarray(inputs["x"]).shape[0])
    depth = int(np.asarray(inputs["w_in"]).shape[0])
    topk = min(256, S // 4)
    key = (S, depth, topk)
    if key not in _CACHE:
        _CACHE[key] = build(S, topk, depth=depth)[0]
    nc = _CACHE[key]
    shared = shared_maps(inputs, depth)
    in_maps = [core_map(inputs, shared, b) for b in range(B)]
    res = run_bass_kernel_spmd(nc, in_maps, core_ids=list(range(B)))
    return np.stack([np.asarray(r["out"], np.float32) for r in res.results], axis=0)
```
